# Optimizing a Trainium2 kernel written in Bass

```python
import math
import jax
import jax.numpy as jnp
from jax import lax
import numpy as np

D_MODEL = 1024
BATCH = 16
SEQ = 2048
DEPTH = 2

GRID_W = 64
CTX_LEN = 256
NORM_EPS = 1e-6
NA_HEADS = 4
NA_HEAD_DIM = 64
NA_ROWS = 8
NA_COLS = 16
NA_QCOLS = 16
NA_KCOLS = NA_QCOLS + NA_COLS
NA_WIDTH = NA_HEADS * NA_HEAD_DIM
DF_HEADS = 4
DF_HEAD_DIM = 32
DF_WIDTH = DF_HEADS * 2 * DF_HEAD_DIM
DF_Q_BLOCK = 128
ROPE_BASE = 10000.0
SSM_HEADS = 8
SSM_HEAD_DIM = 64
SSM_D_INNER = SSM_HEADS * SSM_HEAD_DIM
SSM_GROUPS = 2
SSM_STATE = 128
SSM_CONV = 5
SSM_CHUNK = 128
SSM_CONV_DIM = SSM_D_INNER + 2 * SSM_GROUPS * SSM_STATE
D_MIX = NA_WIDTH + DF_WIDTH + SSM_D_INNER
IN_SPLITS = (NA_WIDTH, NA_WIDTH, NA_WIDTH, DF_WIDTH, DF_WIDTH, DF_WIDTH, SSM_D_INNER, SSM_CONV_DIM, 2 * SSM_HEADS)
IN_COLS = sum(IN_SPLITS)
D_FF = 2816
FFN_CONV = 3

kernel_name = 'hymba_style_na_diff_ssd_block'


def rms_norm(x, gain):
    xf = x.astype(jnp.float32)
    y = xf * lax.rsqrt(jnp.mean(xf * xf, axis=-1, keepdims=True) + NORM_EPS)
    return (y * gain.astype(jnp.float32)).astype(x.dtype)


def depthwise_conv(x, w, b):
    y = lax.conv_general_dilated(x, w[:, None, :].astype(x.dtype), window_strides=(1,), padding='SAME',
                                 dimension_numbers=('NWC', 'WIO', 'NWC'), feature_group_count=x.shape[-1])
    return y + b


def axial_rope_tables(n, dim):
    per_axis = dim // 2
    inv_freq = ROPE_BASE ** (-jnp.arange(0, per_axis, 2, dtype=jnp.float32) / per_axis)
    t = jnp.arange(n, dtype=jnp.int32)
    pos = jnp.stack([t // GRID_W, t % GRID_W], axis=-1).astype(jnp.float32)
    ang = pos[:, :, None] * inv_freq
    ang = jnp.concatenate([ang, ang], axis=-1).reshape(n, dim)
    return jnp.cos(ang), jnp.sin(ang)


def apply_axial_rope(x, cos, sin):
    k = x.shape[-1] // 4
    xf = x.astype(jnp.float32)
    xr = xf.reshape(*x.shape[:-1], 2, 2, k)
    rot = jnp.concatenate([-xr[..., 1:, :], xr[..., :1, :]], axis=-2).reshape(x.shape)
    shape = (x.shape[1],) + (1,) * (x.ndim - 3) + (x.shape[-1],)
    return (xf * cos.reshape(shape) + rot * sin.reshape(shape)).astype(x.dtype)


def dense_attention(q, k, v):
    s = jnp.einsum('bqhd,bkhd->bhqk', q, k).astype(jnp.float32) * q.shape[-1] ** -0.5
    p = jax.nn.softmax(s, axis=-1).astype(v.dtype)
    return jnp.einsum('bhqk,bkhd->bqhd', p, v)


def neighbourhood_attention(q, k, v, k_ctx, v_ctx, rpb):
    b, n, h, d = q.shape
    rows = n // GRID_W
    wr = min(NA_ROWS, rows)
    ncb = GRID_W // NA_QCOLS
    r = np.arange(rows)
    key_rows = np.clip(r - wr // 2, 0, rows - wr)[:, None] + np.arange(wr)
    jb = np.arange(ncb)
    key_cols = np.clip(jb * NA_QCOLS - NA_COLS // 2, 0, GRID_W - NA_KCOLS)[:, None] + np.arange(NA_KCOLS)
    q_cols = jb[:, None] * NA_QCOLS + np.arange(NA_QCOLS)
    win_start = np.clip(q_cols - NA_COLS // 2, 0, GRID_W - NA_COLS)
    col_ok = (key_cols[:, None, :] >= win_start[..., None]) & (key_cols[:, None, :] < win_start[..., None] + NA_COLS)
    nk = wr * NA_KCOLS
    idx = (key_rows[:, None, :, None] * GRID_W + key_cols[None, :, None, :]).reshape(rows, ncb, nk)
    dr = key_rows - r[:, None] + NA_ROWS - 1
    dc = np.clip(key_cols[:, None, :] - q_cols[:, :, None], 1 - NA_COLS, NA_COLS - 1) + NA_COLS - 1
    bias = rpb[:, dr[:, None, None, :, None], dc[None, :, :, None, :]].astype(jnp.float32)
    bias = jnp.where(col_ok[None, None, :, :, None, :], bias, -jnp.inf).reshape(h, rows, ncb, NA_QCOLS, nk)
    qb = q.reshape(b, rows, ncb, NA_QCOLS, h, d)
    kg, vg = k[:, idx], v[:, idx]
    scale = d ** -0.5
    s_loc = jnp.einsum('brjqhd,brjkhd->bhrjqk', qb, kg).astype(jnp.float32) * scale + bias
    s_ctx = jnp.einsum('brjqhd,bkhd->bhrjqk', qb, k_ctx).astype(jnp.float32) * scale
    p = jax.nn.softmax(jnp.concatenate([s_loc, s_ctx], axis=-1), axis=-1).astype(v.dtype)
    out = (jnp.einsum('bhrjqk,brjkhd->brjqhd', p[..., :nk], vg)
           + jnp.einsum('bhrjqk,bkhd->brjqhd', p[..., nk:], v_ctx))
    return out.reshape(b, n, h, d)


def diff_attend(q, k, v, lam):
    s = jnp.einsum('bqhmd,bkhmd->bhmqk', q, k).astype(jnp.float32) * DF_HEAD_DIM ** -0.5
    p = jax.nn.softmax(s, axis=-1)
    a = p[:, :, 0] - lam * p[:, :, 1]
    return jnp.einsum('bhqk,bkhe->bqhe', a.astype(v.dtype), v)


def diff_mixer(q, k, v, qc, kc, vc, cos, sin, q_gain, k_gain, lam_vecs, subln, layer, ctx_out):
    heads = lambda t: t.reshape(*t.shape[:-1], DF_HEADS, 2, DF_HEAD_DIM)
    vheads = lambda t: t.reshape(*t.shape[:-1], DF_HEADS, 2 * DF_HEAD_DIM)
    q = apply_axial_rope(rms_norm(heads(q), q_gain), cos, sin)
    k = apply_axial_rope(rms_norm(heads(k), k_gain), cos, sin)
    kc = rms_norm(heads(kc), k_gain)
    v, vc = vheads(v), vheads(vc)
    lam_init = 0.8 - 0.6 * math.exp(-0.3 * layer)
    lv = lam_vecs.astype(jnp.float32)
    lam = jnp.exp(jnp.sum(lv[0] * lv[1])) - jnp.exp(jnp.sum(lv[2] * lv[3])) + lam_init
    k_all = jnp.concatenate([k, kc], axis=1)
    v_all = jnp.concatenate([v, vc], axis=1)
    b, n = q.shape[:2]
    nb = n // DF_Q_BLOCK
    qblocks = q.reshape(b, nb, DF_Q_BLOCK, DF_HEADS, 2, DF_HEAD_DIM).swapaxes(0, 1)
    y = lax.map(lambda qb: diff_attend(qb, k_all, v_all, lam), qblocks)
    y = y.swapaxes(0, 1).reshape(b, n, DF_HEADS, 2 * DF_HEAD_DIM)
    finish = lambda t: (rms_norm(t, subln) * (1.0 - lam_init)).reshape(*t.shape[:2], DF_WIDTH)
    yc = finish(diff_attend(rms_norm(heads(qc), q_gain), kc, vc, lam)) if ctx_out else None
    return finish(y), yc


def ssd_chunked(x, dt, a, bm, cm, h0, return_y):
    b, n, nh, p = x.shape
    g, ns = bm.shape[2], bm.shape[3]
    e = nh // g
    nc, q = n // SSM_CHUNK, SSM_CHUNK
    f32 = jnp.float32
    dtf = dt.astype(f32)
    xdt = (x.astype(f32) * dtf[..., None]).reshape(b, nc, q, g, e, p)
    la = (dtf * a.astype(f32)).reshape(b, nc, q, g, e).transpose(0, 3, 4, 1, 2)
    bm = bm.astype(f32).reshape(b, nc, q, g, ns)
    cm = cm.astype(f32).reshape(b, nc, q, g, ns)
    cs = jnp.cumsum(la, axis=-1)
    decay_to_end = jnp.exp(cs[..., -1:] - cs)
    states = jnp.einsum('bcjgn,bgecj,bcjgep->bcgepn', bm, decay_to_end, xdt)
    chunk_decay = jnp.exp(cs[..., -1])

    def step(s, inp):
        st, dec = inp
        return s * dec[..., None, None] + st, s

    final, prev = lax.scan(step, h0.astype(f32).reshape(b, g, e, p, ns),
                           (states.swapaxes(0, 1), jnp.moveaxis(chunk_decay, -1, 0)))
    final = final.reshape(b, nh, p, ns)
    if not return_y:
        return None, final
    prev = prev.swapaxes(0, 1)
    lower = np.tril(np.ones((q, q), dtype=bool))
    lmat = jnp.exp(jnp.where(lower, cs[..., :, None] - cs[..., None, :], -jnp.inf))
    cb = jnp.einsum('bcign,bcjgn->bgcij', cm, bm)
    y_diag = jnp.einsum('bgcij,bgecij,bcjgep->bcigep', cb, lmat, xdt)
    y_off = jnp.einsum('bcign,bcgepn,bgeci->bcigep', cm, prev, jnp.exp(cs))
    return (y_diag + y_off).reshape(b, n, nh, p).astype(x.dtype), final


def ssm_mixer(z, xbc, dt_raw, zc, xbc_c, dt_raw_c, conv_w, conv_b, dt_bias, a_log, d_skip, norm_gain, ctx_out):
    def prep(xbc_, dt_raw_):
        u = jax.nn.silu(depthwise_conv(xbc_, conv_w, conv_b))
        xs, bm, cm = jnp.split(u, [SSM_D_INNER, SSM_D_INNER + SSM_GROUPS * SSM_STATE], axis=-1)
        bsz, n = u.shape[:2]
        xs = xs.reshape(bsz, n, SSM_HEADS, SSM_HEAD_DIM)
        bm = bm.reshape(bsz, n, SSM_GROUPS, SSM_STATE)
        cm = cm.reshape(bsz, n, SSM_GROUPS, SSM_STATE)
        dt = jax.nn.softplus(dt_raw_.reshape(bsz, n, 2, SSM_HEADS) + dt_bias)
        return xs, bm, cm, dt

    def finish(y, xs, zz):
        y = y + xs * d_skip[:, None]
        return rms_norm(y.reshape(*y.shape[:2], SSM_D_INNER) * jax.nn.silu(zz), norm_gain)

    flip = lambda t: t[:, ::-1]
    a = -jnp.exp(a_log.astype(jnp.float32))
    xs, bm, cm, dt = prep(xbc, dt_raw)
    xc, bc, cc, dtc = prep(xbc_c, dt_raw_c)
    h0 = jnp.zeros((xc.shape[0], SSM_HEADS, SSM_HEAD_DIM, SSM_STATE), jnp.float32)
    yc_f, s_f = ssd_chunked(xc, dtc[:, :, 0], a[0], bc, cc, h0, ctx_out)
    yc_b, s_b = ssd_chunked(flip(xc), flip(dtc[:, :, 1]), a[1], flip(bc), flip(cc), h0, ctx_out)
    y_f, _ = ssd_chunked(xs, dt[:, :, 0], a[0], bm, cm, s_f, True)
    y_b, _ = ssd_chunked(flip(xs), flip(dt[:, :, 1]), a[1], flip(bm), flip(cm), s_b, True)
    y = finish(y_f + flip(y_b), xs, z)
    yc = finish(yc_f + flip(yc_b), xc, zc) if ctx_out else None
    return y, yc


def token_mixers(h, hc, cos, sin, layer, ctx_out, w_in, na_q_gain, na_k_gain, na_rpb, df_q_gain, df_k_gain,
                 df_lambda, df_subln, ssm_conv_w, ssm_conv_b, ssm_dt_bias, ssm_a_log, ssm_d, ssm_norm):
    cuts = [int(v) for v in np.cumsum(IN_SPLITS)[:-1]]
    na_q, na_k, na_v, df_q, df_k, df_v, s_z, s_xbc, s_dt = jnp.split(h @ w_in, cuts, axis=-1)
    na_qc, na_kc, na_vc, df_qc, df_kc, df_vc, s_zc, s_xbcc, s_dtc = jnp.split(hc @ w_in, cuts, axis=-1)
    b, n = h.shape[:2]
    nc = hc.shape[1]
    na_heads = lambda t: t.reshape(*t.shape[:-1], NA_HEADS, NA_HEAD_DIM)
    kc, vc = rms_norm(na_heads(na_kc), na_k_gain), na_heads(na_vc)
    y_na = neighbourhood_attention(rms_norm(na_heads(na_q), na_q_gain), rms_norm(na_heads(na_k), na_k_gain),
                                   na_heads(na_v), kc, vc, na_rpb).reshape(b, n, NA_WIDTH)
    y_df, yc_df = diff_mixer(df_q, df_k, df_v, df_qc, df_kc, df_vc, cos, sin, df_q_gain, df_k_gain,
                             df_lambda, df_subln, layer, ctx_out)
    y_ssm, yc_ssm = ssm_mixer(s_z, s_xbc, s_dt, s_zc, s_xbcc, s_dtc, ssm_conv_w, ssm_conv_b, ssm_dt_bias,
                              ssm_a_log, ssm_d, ssm_norm, ctx_out)
    y = jnp.concatenate([y_na, y_df, y_ssm], axis=-1)
    if not ctx_out:
        return y, None
    yc_na = dense_attention(rms_norm(na_heads(na_qc), na_q_gain), kc, vc).reshape(b, nc, NA_WIDTH)
    return y, jnp.concatenate([yc_na, yc_df, yc_ssm], axis=-1)


def conv_glu(h, w_up, conv_w, conv_b, w_down):
    gate, val = jnp.split(h @ w_up, 2, axis=-1)
    return (jax.nn.silu(depthwise_conv(gate, conv_w, conv_b)) * val) @ w_down


def setup_inputs(seed: int = 0) -> dict:
    key = jax.random.key(seed)
    ks = iter(jax.random.split(key, 40))
    nrm = lambda shape, scale: jax.random.normal(next(ks), shape, jnp.float32) * scale
    gain = lambda shape: 1.0 + nrm(shape, 0.02)
    dt0 = jnp.exp(jax.random.uniform(next(ks), (DEPTH, 2, SSM_HEADS), jnp.float32,
                                     minval=math.log(1e-3), maxval=math.log(1e-1)))
    a0 = jax.random.uniform(next(ks), (DEPTH, 2, SSM_HEADS), jnp.float32, minval=1.0, maxval=16.0)
    return {
        'x': nrm((BATCH, SEQ, D_MODEL), 1.0),
        'c': nrm((BATCH, D_MODEL), 1.0),
        'ctx': nrm((BATCH, CTX_LEN, D_MODEL), 1.0),
        'c_ctx': nrm((D_MODEL,), 1.0),
        'w_ada': nrm((DEPTH, D_MODEL, 6 * D_MODEL), 0.5 * D_MODEL ** -0.5),
        'b_ada': nrm((DEPTH, 6 * D_MODEL), 0.02),
        'g_mix': gain((DEPTH, D_MODEL)),
        'g_ffn': gain((DEPTH, D_MODEL)),
        'w_in': nrm((DEPTH, D_MODEL, IN_COLS), D_MODEL ** -0.5),
        'na_q_gain': gain((DEPTH, NA_HEAD_DIM)),
        'na_k_gain': gain((DEPTH, NA_HEAD_DIM)),
        'na_rpb': nrm((DEPTH, NA_HEADS, 2 * NA_ROWS - 1, 2 * NA_COLS - 1), 0.02),
        'df_q_gain': gain((DEPTH, DF_HEAD_DIM)),
        'df_k_gain': gain((DEPTH, DF_HEAD_DIM)),
        'df_lambda': nrm((DEPTH, 4, DF_HEAD_DIM), 0.1),
        'df_subln': gain((DEPTH, 2 * DF_HEAD_DIM)),
        'ssm_conv_w': nrm((DEPTH, SSM_CONV, SSM_CONV_DIM), SSM_CONV ** -0.5),
        'ssm_conv_b': nrm((DEPTH, SSM_CONV_DIM), 0.02),
        'ssm_dt_bias': dt0 + jnp.log(-jnp.expm1(-dt0)),
        'ssm_a_log': jnp.log(a0),
        'ssm_d': gain((DEPTH, SSM_HEADS)),
        'ssm_norm': gain((DEPTH, SSM_D_INNER)),
        'w_out': nrm((DEPTH, D_MIX, D_MODEL), D_MIX ** -0.5),
        'ffn_w_up': nrm((DEPTH, D_MODEL, 2 * D_FF), D_MODEL ** -0.5),
        'ffn_conv_w': nrm((DEPTH, FFN_CONV, D_FF), FFN_CONV ** -0.5),
        'ffn_conv_b': nrm((DEPTH, D_FF), 0.02),
        'ffn_w_down': nrm((DEPTH, D_FF, D_MODEL), D_FF ** -0.5),
    }


def reference(x, c, ctx, c_ctx, w_ada, b_ada, g_mix, g_ffn, w_in, na_q_gain, na_k_gain, na_rpb,
              df_q_gain, df_k_gain, df_lambda, df_subln, ssm_conv_w, ssm_conv_b, ssm_dt_bias, ssm_a_log,
              ssm_d, ssm_norm, w_out, ffn_w_up, ffn_conv_w, ffn_conv_b, ffn_w_down):
    cos, sin = axial_rope_tables(x.shape[1], DF_HEAD_DIM)
    for l in range(DEPTH):
        ctx_out = l < DEPTH - 1
        sh1, sc1, g1, sh2, sc2, g2 = [m[:, None, :] for m in
                                      jnp.split(jax.nn.silu(c) @ w_ada[l] + b_ada[l], 6, axis=-1)]
        csh1, csc1, cg1, csh2, csc2, cg2 = jnp.split(jax.nn.silu(c_ctx) @ w_ada[l] + b_ada[l], 6, axis=-1)
        h = rms_norm(x, g_mix[l]) * (1.0 + sc1) + sh1
        hc = rms_norm(ctx, g_mix[l]) * (1.0 + csc1) + csh1
        y, yc = token_mixers(h, hc, cos, sin, l, ctx_out, w_in[l], na_q_gain[l], na_k_gain[l], na_rpb[l],
                             df_q_gain[l], df_k_gain[l], df_lambda[l], df_subln[l], ssm_conv_w[l], ssm_conv_b[l],
                             ssm_dt_bias[l], ssm_a_log[l], ssm_d[l], ssm_norm[l])
        x = x + g1 * (y @ w_out[l])
        h = rms_norm(x, g_ffn[l]) * (1.0 + sc2) + sh2
        x = x + g2 * conv_glu(h, ffn_w_up[l], ffn_conv_w[l], ffn_conv_b[l], ffn_w_down[l])
        if ctx_out:
            ctx = ctx + cg1 * (yc @ w_out[l])
            hc = rms_norm(ctx, g_ffn[l]) * (1.0 + csc2) + csh2
            ctx = ctx + cg2 * conv_glu(hc, ffn_w_up[l], ffn_conv_w[l], ffn_conv_b[l], ffn_w_down[l])
    return x
```

```python
import math
from contextlib import ExitStack

import numpy as np
import concourse.bass as bass
import concourse.mybir as mybir
from concourse.bass_utils import run_bass_kernel_spmd

F32 = mybir.dt.float32
BF16 = mybir.dt.bfloat16
AF = mybir.ActivationFunctionType
ALU = mybir.AluOpType
AX = mybir.AxisListType

NCORES = 8
NB = 2
L = 2048
LC = 256
T = L + LC
D = 1024
DEPTH = 2
DFF = 2816
NF = DFF // 128
EPS = 1e-6
NEG = -30000.0


class Buf:
    __slots__ = ("name", "w", "r", "dsem", "t", "persist")

    def __init__(self, name, t=None, persist=False):
        self.name = name
        self.w = {}
        self.r = {}
        self.dsem = None
        self.t = t
        self.persist = persist

    def __getitem__(self, idx):
        return self.t[idx]


class Sched:
    ENG = ("pe", "act", "dve", "pool", "sp")

    def __init__(self, nc, es, ndsem=48):
        self.nc = nc
        self.ges = es
        self.es = es
        self.eng = {"pe": nc.tensor, "act": nc.scalar, "dve": nc.vector, "pool": nc.gpsimd, "sp": nc.sync}
        self.sem = {k: es.enter_context(nc.semaphore("c_" + k)) for k in self.ENG}
        self.cnt = {k: 0 for k in self.ENG}
        self.seen = {k: {} for k in self.ENG}
        self.prog = {k: [] for k in self.ENG}
        self.dsems = [es.enter_context(nc.semaphore("d%d" % i)) for i in range(ndsem)]
        self.dval = [0] * ndsem
        self.dfree = list(range(ndsem))
        self.phase_bufs = []
        self.uid = 0
        self.ps_rr = 0
        self.PS = []

    def sb(self, name, shape, dt, persist=False):
        self.uid += 1
        es = self.ges if persist else self.es
        t = es.enter_context(self.nc.sbuf_tensor("%s_%d" % (name, self.uid), list(shape), dt))
        b = Buf(name, t, persist)
        if not persist:
            self.phase_bufs.append(b)
        return b

    def dram(self, name, shape, dt, kind="Internal"):
        t = self.nc.dram_tensor(name, list(shape), dt, kind=kind)
        return Buf(name, t, True)

    def psum_init(self):
        for i in range(8):
            t = self.ges.enter_context(self.nc.psum_tensor("psb%d" % i, [128, 512], F32))
            self.PS.append(Buf("ps%d" % i, t, True))

    def ps(self):
        b = self.PS[self.ps_rr % 6]
        self.ps_rr += 1
        return b

    def psacc(self, i):
        return self.PS[6 + (i % 2)]

    def _deps(self, e, reads, writes):
        d = {}
        own = "c_" + e

        def add(tokdict, is_read_set):
            for key, (sem, val) in tokdict.items():
                if key == own and (e == "pe" or is_read_set):
                    continue
                if d.get(key, (None, 0))[1] < val:
                    d[key] = (sem, val)

        for b in reads:
            add(b.w, False)
        for b in writes:
            add(b.w, False)
            add(b.r, True)
        return d

    def _wait(self, e, d):
        seen = self.seen[e]
        for key, (sem, val) in d.items():
            if seen.get(key, 0) >= val:
                continue
            self.prog[e].append(("w", sem, val))
            seen[key] = val

    def I(self, e, name, *args, rd=(), wr=(), **kw):
        d = self._deps(e, rd, wr)
        self._wait(e, d)
        self.cnt[e] += 1
        self.prog[e].append(("i", name, args, kw, self.sem[e], 1))
        key = "c_" + e
        tok = (self.sem[e], self.cnt[e])
        for b in rd:
            b.r[key] = tok
        for b in wr:
            b.w[key] = tok

    def dma(self, e, out_ap, in_ap, dst, src, **kw):
        if dst.dsem is None:
            dst.dsem = self.dfree.pop()
        i = dst.dsem
        d = self._deps(e, [src], [dst])
        self._wait(e, d)
        self.dval[i] += 16
        self.prog[e].append(("i", "dma_start", (), dict(out=out_ap, in_=in_ap, **kw), self.dsems[i], 16))
        key = "d%d" % i
        tok = (self.dsems[i], self.dval[i])
        src.r[key] = tok
        dst.w[key] = tok

    def drain(self):
        d = {}
        for i, v in enumerate(self.dval):
            if v:
                d["d%d" % i] = (self.dsems[i], v)
        self._wait("sp", d)

    def emit(self):
        self.drain()
        prog = self.prog
        with self.nc.Block() as block:
            def mk(e):
                def body(g):
                    for it in prog[e]:
                        if it[0] == "w":
                            g.wait_ge(it[1], it[2])
                        else:
                            getattr(g, it[1])(*it[2], **it[3]).then_inc(it[4], it[5])
                return body
            block.tensor(mk("pe"))
            block.scalar(mk("act"))
            block.vector(mk("dve"))
            block.gpsimd(mk("pool"))
            block.sync(mk("sp"))
        self.prog = {k: [] for k in self.ENG}
        for b in self.phase_bufs:
            if b.dsem is not None:
                self.dfree.append(b.dsem)
                b.dsem = None
        self.phase_bufs = []

    class _Phase:
        def __init__(self, s):
            self.s = s

        def __enter__(self):
            self.es = ExitStack()
            self.es.__enter__()
            self.s.es = self.es
            return self

        def __exit__(self, *a):
            if a[0] is None:
                self.s.emit()
            self.s.es = self.s.ges
            return self.es.__exit__(*a)

    def phase(self):
        return Sched._Phase(self)


def dap(buf, off, dims):
    return bass.AP(buf.t, off, [list(d) for d in dims])


class Builder:
    def __init__(self, dbg=None):
        self.dbg = dbg or {}
        self.nc = bass.Bass("TRN2", target_bir_lowering=False)
        self.outs = []

    def declare(self, s):
        I = lambda n, sh, dt=F32: s.dram(n, sh, dt, kind="ExternalInput")
        self.x = I("x", [NB, L, D])
        self.ctx = I("ctx", [NB, LC, D])
        self.cT = I("cT", [128, 8, 3])
        self.w_ada = I("w_ada", [DEPTH, D, 6 * D])
        self.b_ada = I("b_ada", [DEPTH, 6 * D])
        self.g_mixc = I("g_mixc", [DEPTH, 128, 8])
        self.g_ffnc = I("g_ffnc", [DEPTH, 128, 8])
        self.w_in = I("w_in", [DEPTH, D, 3088])
        self.w_out = I("w_out", [DEPTH, D, D])
        self.w_up = I("w_up", [DEPTH, D, 2 * DFF])
        self.w_down = I("w_down", [DEPTH, DFF, D])
        self.na_qg = I("na_qg", [DEPTH, 64])
        self.na_kg = I("na_kg", [DEPTH, 64])
        self.rpb_t = I("rpb_t", [DEPTH, 4, 15, 64, 64])
        self.na_mask = I("na_mask", [128, 64])
        self.df_qg = I("df_qg", [DEPTH, 32])
        self.df_kg = I("df_kg", [DEPTH, 32])
        self.df_lam = I("df_lam", [DEPTH, 4, 32])
        self.df_subln = I("df_subln", [DEPTH, 64])
        self.rope_cos = I("rope_cos", [L, 32])
        self.rope_sin = I("rope_sin", [L, 32])
        self.conv_wc = I("conv_wc", [DEPTH, 128, 8, 5])
        self.conv_bc = I("conv_bc", [DEPTH, 128, 8])
        self.dt_bias = I("dt_bias", [DEPTH, 16])
        self.a_log = I("a_log", [DEPTH, 16])
        self.ssm_d = I("ssm_d", [DEPTH, 8])
        self.ssm_norm = I("ssm_norm", [DEPTH, 512])
        self.fconv_wc = I("fconv_wc", [DEPTH, 128, NF, 3])
        self.fconv_bc = I("fconv_bc", [DEPTH, 128, NF])
        self.ident_d = I("ident", [128, 128])
        self.tri_d = I("tri", [5, 128, 128])
        self.out = s.dram("out", [NB, L, D], F32, kind="ExternalOutput")
        dk = "ExternalOutput" if self.dbg.get("dump") else "Internal"
        self.modrow_d = s.dram("modrow_d", [DEPTH, 3, 6 * D], F32, kind=dk)
        self.XA = [s.dram("XA%d" % l, [NB, T, D], F32, kind=dk) for l in range(DEPTH)]
        self.XB = s.dram("XB0", [NB, T, D], F32, kind=dk)
        self.Yd = s.dram("Yd", [NB, T, D], F32, kind=dk)
        self.Zd = s.dram("Zd", [T, 512], F32, kind=dk)
        self.ATd = s.dram("ATd", [NF, 128, T], BF16, kind=dk)
        self.hTd = s.dram("hTd", [128, 8, T], BF16, kind=dk) if self.dbg.get("dump") else None
        self.Yin = I("Yin", [DEPTH, NB, T, D]) if self.dbg.get("feed_y") else None

    def build(self):
        nc = self.nc
        with ExitStack() as es:
            s = Sched(nc, es)
            self.s = s
            self.declare(s)
            s.psum_init()
            self.ident = s.sb("ident", [128, 128], F32, persist=True)
            self.identb = s.sb("identb", [128, 128], BF16, persist=True)
            self.hT = s.sb("hT", [128, 8, T], BF16, persist=True)
            self.AB = [[s.sb("AB%d%d" % (l, i), [128, 8, 3], F32, persist=True) for i in range(4)] for l in range(DEPTH)]
            self.BT = s.sb("BT", [128, 4, 14, 64], BF16, persist=True)
            ph = self.dbg.get("phases")
            with s.phase():
                s.dma("sp", self.ident[:], self.ident_d.t.ap(), self.ident, self.ident_d)
                s.I("act", "activation", out=self.identb[:], in_=self.ident[:], func=AF.Copy, rd=[self.ident], wr=[self.identb])
                self.phase_adaln()
            for l in self.dbg.get("layers", range(DEPTH)):
                last = l == DEPTH - 1
                nt = 16 if last else 18
                if ph is None or "na" in ph:
                    with s.phase():
                        self.phase_na_bias(l)
                for b in range(NB):
                    if l == 0:
                        src = lambda t, b=b: (self.x.t.ap()[b, t * 128:(t + 1) * 128, :], self.x) if t < 16 else \
                            (self.ctx.t.ap()[b, (t - 16) * 128:(t - 15) * 128, :], self.ctx)
                    else:
                        src = lambda t, b=b: (self.XB.t.ap()[b, t * 128:(t + 1) * 128, :], self.XB)
                    with s.phase():
                        self.phase_norm(src, self.AB[l][0], self.AB[l][1], b, 18)
                    if self.dbg.get("dump") and l == self.dbg.get("dump_l", 0) and b == 0 and self.dbg.get("dump_h") == "mix":
                        with s.phase():
                            s.dma("sp", self.hTd.t.ap(), self.hT[:], self.hTd, self.hT)
                    if ph is None or "na" in ph:
                        with s.phase():
                            self.phase_na(l, b)
                    if ph is None or "df" in ph:
                        with s.phase():
                            self.phase_df(l, b)
                    if ph is None or "ssm" in ph:
                        self.phase_ssm(l, b)
                    if ph is None or "out" in ph:
                        with s.phase():
                            self.phase_out(l, b, src, nt)
                    if ph is None or "ffn" in ph:
                        srcA = lambda t, b=b, l=l: (self.XA[l].t.ap()[b, t * 128:(t + 1) * 128, :], self.XA[l])
                        with s.phase():
                            self.phase_norm(srcA, self.AB[l][2], self.AB[l][3], b, nt)
                        with s.phase():
                            self.phase_ffn_up(l, b, nt)
                        with s.phase():
                            self.phase_ffn_down(l, b, srcA, nt)
        return nc

    def phase_adaln(self):
        s = self.s
        cT = s.sb("cT", [128, 8, 3], F32)
        siluT = s.sb("siluT", [128, 8, 3], F32)
        s.dma("sp", cT[:], self.cT.t.ap(), cT, self.cT)
        s.I("act", "activation", out=siluT[:], in_=cT[:], func=AF.Silu, rd=[cT], wr=[siluT])
        wa = [s.sb("wa%d" % i, [128, 8, 512], F32) for i in range(3)]
        for l in range(DEPTH):
            brow = s.sb("brow%d" % l, [3, 6 * D], F32)
            modrow = s.sb("modrow%d" % l, [3, 6 * D], F32)
            s.dma("sp", brow[:], dap(self.b_ada, l * 6 * D, [[0, 3], [1, 6 * D]]), brow, self.b_ada)
            for j in range(12):
                w = wa[j % 3]
                s.dma("sp", w[:], self.w_ada.t.ap()[l, :, j * 512:(j + 1) * 512].rearrange("(k p) n -> p k n", p=128), w, self.w_ada)
                pm = s.ps()
                for k in range(8):
                    s.I("pe", "matmul", pm[0:3, :], siluT[:, k, :], w[:, k, :], start=(k == 0), stop=(k == 7), rd=[siluT, w], wr=[pm])
                s.I("dve", "tensor_tensor", out=modrow[:, j * 512:(j + 1) * 512], in0=pm[0:3, :], in1=brow[:, j * 512:(j + 1) * 512], op=ALU.add,
                    rd=[pm, brow], wr=[modrow])
            s.dma("sp", self.modrow_d.t.ap()[l], modrow[:], self.modrow_d, modrow)
            pT = s.ps()
            for c in range(48):
                s.I("pe", "transpose", pT[:, c * 3:(c + 1) * 3], modrow[0:3, c * 128:(c + 1) * 128], self.ident[0:3, 0:3], rd=[modrow, self.ident], wr=[pT])
            modcol = s.sb("modcol%d" % l, [128, 48, 3], F32)
            s.I("dve", "tensor_copy", modcol[:].rearrange("p a b -> p (a b)"), pT[:, 0:144], rd=[pT], wr=[modcol])
            gm = s.sb("gm%d" % l, [128, 8, 1], F32)
            gf = s.sb("gf%d" % l, [128, 8, 1], F32)
            s.dma("sp", gm[:, :, 0], self.g_mixc.t.ap()[l], gm, self.g_mixc)
            s.dma("sp", gf[:, :, 0], self.g_ffnc.t.ap()[l], gf, self.g_ffnc)
            A1, B1, A2, B2 = self.AB[l]
            for (A, Bv, g, sc0, sh0) in ((A1, B1, gm, 8, 0), (A2, B2, gf, 32, 24)):
                s.I("dve", "scalar_tensor_tensor", out=A[:], in0=modcol[:, sc0:sc0 + 8, :], scalar=1.0, in1=g[:].to_broadcast([128, 8, 3]),
                    op0=ALU.add, op1=ALU.mult, rd=[modcol, g], wr=[A])
                s.I("dve", "tensor_copy", Bv[:], modcol[:, sh0:sh0 + 8, :], rd=[modcol], wr=[Bv])

    def phase_norm(self, src, A, Bv, b, ntiles):
        s = self.s
        xt = [s.sb("nxt%d" % i, [128, D], F32) for i in range(4)]
        xn = [s.sb("nxn%d" % i, [128, D], F32) for i in range(8)]
        junk = s.sb("njunk", [128, D], BF16)
        st = s.sb("nst", [128, 24, 3], F32)
        ngroups = (ntiles + 3) // 4
        for g in range(ngroups):
            tiles = list(range(4 * g, min(4 * g + 4, ntiles)))
            j = b if tiles[0] < 16 else 2
            for ti, t in enumerate(tiles):
                ap, sbuf = src(t)
                x_ = xt[t % 4]
                n_ = xn[t % 8]
                s.dma("sp", x_[:], ap, x_, sbuf)
                s.I("act", "activation", out=junk[:], in_=x_[:], func=AF.Square, accum_out=st[:, t, 0:1], rd=[x_], wr=[junk, st])
                s.I("act", "activation", out=st[:, t, 1:2], in_=st[:, t, 0:1], func=AF.Sqrt, scale=1.0 / D, bias=EPS, rd=[st], wr=[st])
                s.I("dve", "reciprocal", st[:, t, 2:3], st[:, t, 1:2], rd=[st], wr=[st])
                s.I("dve", "tensor_scalar", out=n_[:], in0=x_[:], scalar1=st[:, t, 2:3], scalar2=None, op0=ALU.mult, rd=[x_, st], wr=[n_])
            n = len(tiles) * 128
            for c in range(8):
                p = s.ps()
                for ti, t in enumerate(tiles):
                    n_ = xn[t % 8]
                    s.I("pe", "transpose", p[:, ti * 128:(ti + 1) * 128], n_[:, c * 128:(c + 1) * 128], self.ident[:], rd=[n_, self.ident], wr=[p])
                s.I("act", "activation", out=self.hT[:, c, tiles[0] * 128:tiles[0] * 128 + n], in_=p[:, 0:n], func=AF.Identity,
                    scale=A[:, c, j:j + 1], bias=Bv[:, c, j:j + 1], rd=[p, A, Bv], wr=[self.hT])

    def load_w(self, name, src_buf, src_ap, shape):
        s = self.s
        w = s.sb(name, shape, BF16)
        s.dma("pool", w[:], src_ap, w, src_buf)
        return w

    def bcast_row(self, name, src_buf, off, n, dt=F32, parts=128):
        s = self.s
        t = s.sb(name, [parts, n], dt)
        s.dma("sp", t[:], dap(src_buf, off, [[0, parts], [1, n]]), t, src_buf)
        return t

    def phase_out(self, l, b, src, ntiles):
        s = self.s
        w = self.load_w("wout", self.w_out, self.w_out.t.ap()[l].rearrange("(k p) n -> p k n", p=128), [128, 8, D])
        G = {}
        G[b] = self.bcast_row("g1b", self.modrow_d, (l * 3 + b) * 6 * D + 2 * D, D)
        if ntiles > 16:
            G[2] = self.bcast_row("g1c", self.modrow_d, (l * 3 + 2) * 6 * D + 2 * D, D)
        yt = [s.sb("oyt%d" % i, [128, D], F32) for i in range(8)]
        xr = [s.sb("oxr%d" % i, [128, D], F32) for i in range(4)]
        yT = [s.sb("oyT%d" % i, [128, 8, 512], BF16) for i in range(2)]
        tmp = [s.sb("otmp%d" % i, [128, D], F32) for i in range(2)]
        xo = [s.sb("oxo%d" % i, [128, D], F32) for i in range(2)]
        ngroups = (ntiles + 3) // 4
        for g in range(ngroups):
            tiles = list(range(4 * g, min(4 * g + 4, ntiles)))
            j = b if tiles[0] < 16 else 2
            for t in tiles:
                y_ = yt[t % 8]
                if self.Yin is not None:
                    s.dma("sp", y_[:], self.Yin.t.ap()[l, b, t * 128:(t + 1) * 128, :], y_, self.Yin)
                else:
                    s.dma("sp", y_[:], self.Yd.t.ap()[b, t * 128:(t + 1) * 128, :], y_, self.Yd)
                ap, sbuf = src(t)
                s.dma("sp", xr[t % 4][:], ap, xr[t % 4], sbuf)
            yT_ = yT[g % 2]
            n = len(tiles) * 128
            for c in range(8):
                p = s.ps()
                for ti, t in enumerate(tiles):
                    s.I("pe", "transpose", p[:, ti * 128:(ti + 1) * 128], yt[t % 8][:, c * 128:(c + 1) * 128], self.ident[:], rd=[yt[t % 8], self.ident], wr=[p])
                s.I("act", "activation", out=yT_[:, c, 0:n], in_=p[:, 0:n], func=AF.Copy, rd=[p], wr=[yT_])
            for ti, t in enumerate(tiles):
                tm = tmp[t % 2]
                xo_ = xo[t % 2]
                for hf in range(2):
                    p = s.ps()
                    for k in range(8):
                        s.I("pe", "matmul", p[:, :], yT_[:, k, ti * 128:(ti + 1) * 128], w[:, k, hf * 512:(hf + 1) * 512], start=(k == 0), stop=(k == 7),
                            rd=[yT_, w], wr=[p])
                    s.I("dve", "tensor_tensor", out=tm[:, hf * 512:(hf + 1) * 512], in0=p[:, :], in1=G[j][:, hf * 512:(hf + 1) * 512], op=ALU.mult,
                        rd=[p, G[j]], wr=[tm])
                s.I("pool", "tensor_tensor", out=xo_[:], in0=tm[:], in1=xr[t % 4][:], op=ALU.add, rd=[tm, xr[t % 4]], wr=[xo_])
                s.dma("sp", self.XA[l].t.ap()[b, t * 128:(t + 1) * 128, :], xo_[:], self.XA[l], xo_)

    def phase_ffn_up(self, l, b, ntiles):
        s = self.s
        groups = [(0, 512), (512, 512), (1024, 512), (1536, 512)]
        if ntiles > 16:
            groups.append((2048, 256))
        ntok = ntiles * 128
        wu = [s.sb("wu%d" % i, [128, 8, 256], BF16) for i in range(3)]
        Gl = [s.sb("fGl%d" % i, [128, L + 2], F32) for i in range(2)]
        Gc = [s.sb("fGc%d" % i, [128, LC + 2], F32) for i in range(2)]
        V = [s.sb("fV%d" % i, [128, T], F32) for i in range(2)]
        acc = [s.sb("facc%d" % i, [128, T], F32) for i in range(2)]
        at = [s.sb("fat%d" % i, [128, T], BF16) for i in range(2)]
        cw = s.sb("fcw", [128, NF, 3], F32)
        cb = s.sb("fcb", [128, NF], F32)
        s.dma("sp", cw[:], self.fconv_wc.t.ap()[l], cw, self.fconv_wc)
        s.dma("sp", cb[:], self.fconv_bc.t.ap()[l], cb, self.fconv_bc)
        for i in range(2):
            s.I("pool", "memset", Gl[i][:], 0.0, wr=[Gl[i]])
            s.I("pool", "memset", Gc[i][:], 0.0, wr=[Gc[i]])
        wup = self.w_up.t.ap()[l]

        def loadw(f):
            w = wu[f % 3]
            s.dma("pool", w[:, :, 0:128], wup[:, f * 128:(f + 1) * 128].rearrange("(k p) n -> p k n", p=128), w, self.w_up)
            s.dma("pool", w[:, :, 128:256], wup[:, DFF + f * 128:DFF + (f + 1) * 128].rearrange("(k p) n -> p k n", p=128), w, self.w_up)

        loadw(0)
        for f in range(NF):
            if f + 1 < NF:
                loadw(f + 1)
            w = wu[f % 3]
            gl, gc, v, a, o = Gl[f % 2], Gc[f % 2], V[f % 2], acc[f % 2], at[f % 2]
            for (t0, n) in groups:
                pa = s.ps()
                pb = s.ps()
                for k in range(8):
                    s.I("pe", "matmul", pa[:, 0:n], w[:, k, 0:128], self.hT[:, k, t0:t0 + n], start=(k == 0), stop=(k == 7), rd=[w, self.hT], wr=[pa])
                for k in range(8):
                    s.I("pe", "matmul", pb[:, 0:n], w[:, k, 128:256], self.hT[:, k, t0:t0 + n], start=(k == 0), stop=(k == 7), rd=[w, self.hT], wr=[pb])
                if t0 < L:
                    s.I("act", "activation", out=gl[:, 1 + t0:1 + t0 + n], in_=pa[:, 0:n], func=AF.Copy, rd=[pa], wr=[gl])
                else:
                    s.I("act", "activation", out=gc[:, 1:1 + n], in_=pa[:, 0:n], func=AF.Copy, rd=[pa], wr=[gc])
                s.I("act", "activation", out=v[:, t0:t0 + n], in_=pb[:, 0:n], func=AF.Copy, rd=[pb], wr=[v])
            segs = [(gl, 0, L)] + ([(gc, L, LC)] if ntiles > 16 else [])
            for (gb, o0, n) in segs:
                s.I("dve", "tensor_scalar", out=a[:, o0:o0 + n], in0=gb[:, 0:n], scalar1=cw[:, f, 0:1], scalar2=None, op0=ALU.mult, rd=[gb, cw], wr=[a])
                for k in (1, 2):
                    s.I("dve", "scalar_tensor_tensor", out=a[:, o0:o0 + n], in0=gb[:, k:k + n], scalar=cw[:, f, k:k + 1], in1=a[:, o0:o0 + n],
                        op0=ALU.mult, op1=ALU.add, rd=[gb, cw, a], wr=[a])
            s.I("act", "activation", out=a[:, 0:ntok], in_=a[:, 0:ntok], func=AF.Silu, bias=cb[:, f:f + 1], scale=1.0, rd=[a, cb], wr=[a])
            s.I("pool", "tensor_tensor", out=o[:, 0:ntok], in0=a[:, 0:ntok], in1=v[:, 0:ntok], op=ALU.mult, rd=[a, v], wr=[o])
            s.dma("sp", self.ATd.t.ap()[f, :, 0:ntok], o[:, 0:ntok], self.ATd, o)

    def phase_ffn_down(self, l, b, src, ntiles):
        s = self.s
        last = l == DEPTH - 1
        w = self.load_w("wdn", self.w_down, self.w_down.t.ap()[l].rearrange("(f p) n -> p f n", p=128), [128, NF, D])
        G = {}
        G[b] = self.bcast_row("g2b", self.modrow_d, (l * 3 + b) * 6 * D + 5 * D, D)
        if ntiles > 16:
            G[2] = self.bcast_row("g2c", self.modrow_d, (l * 3 + 2) * 6 * D + 5 * D, D)
        aT = [s.sb("daT%d" % i, [128, NF, 512], BF16) for i in range(2)]
        xr = [s.sb("dxr%d" % i, [128, D], F32) for i in range(4)]
        tmp = [s.sb("dtmp%d" % i, [128, D], F32) for i in range(2)]
        xo = [s.sb("dxo%d" % i, [128, D], F32) for i in range(2)]
        ngroups = (ntiles + 3) // 4
        for g in range(ngroups):
            tiles = list(range(4 * g, min(4 * g + 4, ntiles)))
            j = b if tiles[0] < 16 else 2
            n = len(tiles) * 128
            a_ = aT[g % 2]
            s.dma("sp", a_[:, :, 0:n], self.ATd.t.ap()[:, :, tiles[0] * 128:tiles[0] * 128 + n].rearrange("f p t -> p f t"), a_, self.ATd)
            for t in tiles:
                ap, sbuf = src(t)
                s.dma("sp", xr[t % 4][:], ap, xr[t % 4], sbuf)
            for ti, t in enumerate(tiles):
                tm = tmp[t % 2]
                xo_ = xo[t % 2]
                for hf in range(2):
                    p = s.ps()
                    for f in range(NF):
                        s.I("pe", "matmul", p[:, :], a_[:, f, ti * 128:(ti + 1) * 128], w[:, f, hf * 512:(hf + 1) * 512], start=(f == 0), stop=(f == NF - 1),
                            rd=[a_, w], wr=[p])
                    s.I("dve", "tensor_tensor", out=tm[:, hf * 512:(hf + 1) * 512], in0=p[:, :], in1=G[j][:, hf * 512:(hf + 1) * 512], op=ALU.mult,
                        rd=[p, G[j]], wr=[tm])
                s.I("pool", "tensor_tensor", out=xo_[:], in0=tm[:], in1=xr[t % 4][:], op=ALU.add, rd=[tm, xr[t % 4]], wr=[xo_])
                if last:
                    s.dma("sp", self.out.t.ap()[b, t * 128:(t + 1) * 128, :], xo_[:], self.out, xo_)
                else:
                    s.dma("sp", self.XB.t.ap()[b, t * 128:(t + 1) * 128, :], xo_[:], self.XB, xo_)

    def group_norm(self, p, ngrp, gd, gains, out, sq, st, view):
        s = self.s
        n = ngrp * gd
        s.I("act", "activation", out=sq[:, 0:n], in_=p[:, 0:n], func=AF.Square, rd=[p], wr=[sq])
        s.I("dve", "tensor_reduce", out=st[:, 0:ngrp, 0], in_=sq[:, 0:n].rearrange("p (g d) -> p g d", d=gd), axis=AX.X, op=ALU.add, rd=[sq], wr=[st])
        s.I("act", "activation", out=st[:, 0:ngrp, 1], in_=st[:, 0:ngrp, 0], func=AF.Sqrt, scale=1.0 / gd, bias=EPS, rd=[st], wr=[st])
        s.I("dve", "reciprocal", st[:, 0:ngrp, 2], st[:, 0:ngrp, 1], rd=[st], wr=[st])
        s.I("dve", "tensor_tensor", out=sq[:, 0:n].rearrange("p (g d) -> p g d", d=gd), in0=p[:, 0:n].rearrange("p (g d) -> p g d", d=gd),
            in1=st[:, 0:ngrp, 2:3].to_broadcast([128, ngrp, gd]), op=ALU.mult, rd=[p, st], wr=[sq])
        s.I("pool", "tensor_tensor", out=out, in0=view(sq[:, 0:n]), in1=gains, op=ALU.mult, rd=[sq] + self._gn_rd, wr=self._gn_wr)

    def phase_na_bias(self, l):
        s = self.s
        bt32 = s.sb("bt32", [128, 4, 14, 64], F32)
        mk = s.sb("namask", [128, 64], F32)
        s.dma("sp", mk[:], self.na_mask.t.ap(), mk, self.na_mask)
        for h in range(4):
            for half in range(2):
                s.dma("sp", bt32[64 * half:64 * half + 64, h, :, :], self.rpb_t.t.ap()[l, h, half:half + 14].rearrange("d k q -> k d q"), bt32, self.rpb_t)
        s.I("dve", "tensor_tensor", out=bt32[:].rearrange("p h d q -> p (h d) q"), in0=bt32[:].rearrange("p h d q -> p (h d) q"),
            in1=mk[:].rearrange("p (o q) -> p o q", o=1).to_broadcast([128, 56, 64]), op=ALU.add, rd=[bt32, mk], wr=[bt32])
        s.I("dve", "tensor_scalar", out=self.BT[:].rearrange("p h d q -> p (h d q)"), in0=bt32[:].rearrange("p h d q -> p (h d q)"), scalar1=8.0, scalar2=None,
            op0=ALU.mult, rd=[bt32], wr=[self.BT])

    def phase_na(self, l, b):
        s = self.s
        last = l == DEPTH - 1
        w = self.load_w("wna", self.w_in, self.w_in.t.ap()[l, :, 0:768].rearrange("(k p) n -> p k n", p=128), [128, 8, 768])
        gains = s.sb("nagain", [128, 2, 1, 64], F32)
        s.dma("sp", gains[:, 0, 0, :], dap(self.na_qg, l * 64, [[0, 128], [1, 64]]), gains, self.na_qg)
        s.dma("sp", gains[:, 1, 0, :], dap(self.na_kg, l * 64, [[0, 128], [1, 64]]), gains, self.na_kg)
        QKT = s.sb("naQKT", [128, 4, T], BF16)
        VE = s.sb("naVE", [128, 18, 4, 65], BF16)
        VO = s.sb("naVO", [128, 15, 4, 65], BF16)
        s.I("pool", "memset", VE[:], 1.0, wr=[VE])
        s.I("pool", "memset", VO[:], 1.0, wr=[VO])
        qn = [s.sb("naqn%d" % i, [128, 2, 4, 64], F32) for i in range(2)]
        gsq = [s.sb("nagsq%d" % i, [128, 512], F32) for i in range(2)]
        gst = [s.sb("nagst%d" % i, [128, 16, 3], F32) for i in range(2)]
        for t in range(18):
            pq = s.ps()
            pv = s.ps()
            for k in range(8):
                s.I("pe", "matmul", pq[:, :], self.hT[:, k, t * 128:(t + 1) * 128], w[:, k, 0:512], start=(k == 0), stop=(k == 7), rd=[self.hT, w], wr=[pq])
            for k in range(8):
                s.I("pe", "matmul", pv[:, 0:256], self.hT[:, k, t * 128:(t + 1) * 128], w[:, k, 512:768], start=(k == 0), stop=(k == 7), rd=[self.hT, w], wr=[pv])
            q_ = qn[t % 2]
            self._gn_rd = [gains]
            self._gn_wr = [q_]
            self.group_norm(pq, 8, 64, gains[:].to_broadcast([128, 2, 4, 64]), q_[:], gsq[t % 2], gst[t % 2],
                            lambda ap: ap.rearrange("p (a h d) -> p a h d", a=2, h=4))
            pt = s.ps()
            qf = q_[:].rearrange("p a h d -> p (a h d)")
            for cc in range(4):
                s.I("pe", "transpose", pt[:, cc * 128:(cc + 1) * 128], qf[:, cc * 128:(cc + 1) * 128], self.ident[:], rd=[q_, self.ident], wr=[pt])
            s.I("act", "activation", out=QKT[:, :, t * 128:(t + 1) * 128], in_=pt[:, :].rearrange("p (c t) -> p c t", c=4), func=AF.Copy, rd=[pt], wr=[QKT])
            s.I("dve", "tensor_copy", VE[:, t, :, 0:64], pv[:, 0:256].rearrange("p (h d) -> p h d", h=4), rd=[pv], wr=[VE])
        for i in range(15):
            pv = s.ps()
            for k in range(8):
                s.I("pe", "matmul", pv[:, 0:256], self.hT[:, k, 64 + i * 128:64 + (i + 1) * 128], w[:, k, 512:768], start=(k == 0), stop=(k == 7), rd=[self.hT, w], wr=[pv])
            s.I("dve", "tensor_copy", VO[:, i, :, 0:64], pv[:, 0:256].rearrange("p (h d) -> p h d", h=4), rd=[pv], wr=[VO])
        PT = [s.sb("naPT%d" % i, [128, 6, 64], BF16) for i in range(4)]
        rec = [s.sb("narec%d" % i, [128, 4, 1], F32) for i in range(2)]
        yo = [s.sb("nayo%d" % i, [128, 4, 64], F32) for i in range(2)]
        it = 0
        for rp in range(16):
            po = s.psacc(rp)
            for rr in range(2):
                r = 2 * rp + rr
                R0 = min(max(r - 4, 0), 24)
                for h in range(4):
                    pair, base = h // 2, 64 * (h % 2)
                    pS = s.ps()
                    q_ap = QKT[base:base + 64, pair, r * 64:(r + 1) * 64]
                    for ci in range(4):
                        kr = R0 + 2 * ci
                        d = kr - r + 7
                        s.I("pe", "matmul", pS[:, ci * 64:(ci + 1) * 64], QKT[base:base + 64, 2 + pair, kr * 64:kr * 64 + 128], q_ap, start=True, stop=False,
                            rd=[QKT], wr=[pS])
                        s.I("pe", "matmul", pS[:, ci * 64:(ci + 1) * 64], self.identb[:], self.BT[:, h, d, :], start=False, stop=True,
                            rd=[self.identb, self.BT], wr=[pS])
                    for cc in range(2):
                        s.I("pe", "matmul", pS[:, (4 + cc) * 64:(5 + cc) * 64], QKT[base:base + 64, 2 + pair, L + cc * 128:L + (cc + 1) * 128], q_ap,
                            start=True, stop=True, rd=[QKT], wr=[pS])
                    P_ = PT[it % 4]
                    it += 1
                    s.I("act", "activation", out=P_[:].rearrange("p c q -> p (c q)"), in_=pS[:, 0:384], func=AF.Exp, scale=0.125, rd=[pS], wr=[P_])
                    for c in range(6):
                        if c < 4:
                            kr = R0 + 2 * c
                            vb, v_ap = (VE, VE[:, kr // 2, h, :]) if kr % 2 == 0 else (VO, VO[:, (kr - 1) // 2, h, :])
                        else:
                            vb, v_ap = VE, VE[:, 16 + (c - 4), h, :]
                        s.I("pe", "matmul", po[64 * rr:64 * rr + 64, h * 65:(h + 1) * 65], P_[:, c, :], v_ap, start=(c == 0), stop=(c == 5), rd=[P_, vb], wr=[po])
            rc, y_ = rec[rp % 2], yo[rp % 2]
            pov = po[:, 0:260].rearrange("p (h e) -> p h e", e=65)
            s.I("dve", "reciprocal", rc[:], pov[:, :, 64:65], rd=[po], wr=[rc])
            s.I("dve", "tensor_tensor", out=y_[:], in0=pov[:, :, 0:64], in1=rc[:].to_broadcast([128, 4, 64]), op=ALU.mult, rd=[po, rc], wr=[y_])
            s.dma("sp", self.Yd.t.ap()[b, rp * 128:(rp + 1) * 128, 0:256], y_[:].rearrange("p h d -> p (h d)"), self.Yd, y_)
        if not last:
            PTc = [s.sb("naPTc%d" % i, [128, 2, 256], BF16) for i in range(2)]
            pos = [s.psacc(0), s.psacc(1)]
            for h in range(4):
                pair, base = h // 2, 64 * (h % 2)
                pS = s.ps()
                for cc in range(2):
                    s.I("pe", "matmul", pS[:, cc * 256:(cc + 1) * 256], QKT[base:base + 64, 2 + pair, L + cc * 128:L + (cc + 1) * 128],
                        QKT[base:base + 64, pair, L:L + 256], start=True, stop=True, rd=[QKT], wr=[pS])
                P_ = PTc[h % 2]
                s.I("act", "activation", out=P_[:].rearrange("p c q -> p (c q)"), in_=pS[:, :], func=AF.Exp, scale=0.125, rd=[pS], wr=[P_])
                for qt in range(2):
                    for cc in range(2):
                        s.I("pe", "matmul", pos[qt][:, h * 65:(h + 1) * 65], P_[:, cc, qt * 128:(qt + 1) * 128], VE[:, 16 + cc, h, :], start=(cc == 0), stop=(cc == 1),
                            rd=[P_, VE], wr=[pos[qt]])
            for qt in range(2):
                rc, y_ = rec[qt], yo[qt]
                pov = pos[qt][:, 0:260].rearrange("p (h e) -> p h e", e=65)
                s.I("dve", "reciprocal", rc[:], pov[:, :, 64:65], rd=[pos[qt]], wr=[rc])
                s.I("dve", "tensor_tensor", out=y_[:], in0=pov[:, :, 0:64], in1=rc[:].to_broadcast([128, 4, 64]), op=ALU.mult, rd=[pos[qt], rc], wr=[y_])
                s.dma("sp", self.Yd.t.ap()[b, L + qt * 128:L + (qt + 1) * 128, 0:256], y_[:].rearrange("p h d -> p (h d)"), self.Yd, y_)

    def phase_df(self, l, b):
        s = self.s
        last = l == DEPTH - 1
        lam_init = 0.8 - 0.6 * math.exp(-0.3 * l)
        w = self.load_w("wdf", self.w_in, self.w_in.t.ap()[l, :, 768:1536].rearrange("(k p) n -> p k n", p=128), [128, 8, 768])
        gains = s.sb("dfgain", [128, 2, 1, 32], F32)
        s.dma("sp", gains[:, 0, 0, :], dap(self.df_qg, l * 32, [[0, 128], [1, 32]]), gains, self.df_qg)
        s.dma("sp", gains[:, 1, 0, :], dap(self.df_kg, l * 32, [[0, 128], [1, 32]]), gains, self.df_kg)
        COS = s.sb("dfcos", [128, 16, 32], F32)
        SIN = s.sb("dfsin", [128, 16, 32], F32)
        s.dma("sp", COS[:], self.rope_cos.t.ap().rearrange("(t p) d -> p t d", p=128), COS, self.rope_cos)
        s.dma("sp", SIN[:], self.rope_sin.t.ap().rearrange("(t p) d -> p t d", p=128), SIN, self.rope_sin)
        lv = s.sb("dflv", [128, 4, 32], F32)
        s.dma("sp", lv[:].rearrange("p a d -> p (a d)"), dap(self.df_lam, l * 128, [[0, 128], [1, 128]]), lv, self.df_lam)
        lp = s.sb("dflp", [128, 2, 32], F32)
        ls = s.sb("dfls", [128, 8], F32)
        s.I("dve", "tensor_tensor", out=lp[:, 0, :], in0=lv[:, 0, :], in1=lv[:, 1, :], op=ALU.mult, rd=[lv], wr=[lp])
        s.I("dve", "tensor_tensor", out=lp[:, 1, :], in0=lv[:, 2, :], in1=lv[:, 3, :], op=ALU.mult, rd=[lv], wr=[lp])
        s.I("dve", "tensor_reduce", out=ls[:, 0:2], in_=lp[:], axis=AX.X, op=ALU.add, rd=[lp], wr=[ls])
        s.I("act", "activation", out=ls[:, 2:4], in_=ls[:, 0:2], func=AF.Exp, rd=[ls], wr=[ls])
        s.I("dve", "scalar_tensor_tensor", out=ls[:, 4:5], in0=ls[:, 3:4], scalar=-lam_init, in1=ls[:, 2:3], op0=ALU.add, op1=ALU.subtract, rd=[ls], wr=[ls])
        neglam = ls[:, 4:5]
        sub = s.sb("dfsub", [128, 1, 64], F32)
        s.dma("sp", sub[:, 0, :], dap(self.df_subln, l * 64, [[0, 128], [1, 64]]), sub, self.df_subln)
        s.I("dve", "tensor_scalar", out=sub[:], in0=sub[:], scalar1=1.0 - lam_init, scalar2=None, op0=ALU.mult, rd=[sub], wr=[sub])

        QZ = s.sb("dfQZ", [128, 2, 2, T], BF16)
        KT = s.sb("dfKT", [128, 2, T], BF16)
        VD = s.sb("dfVD", [128, 18, 4, 65], BF16)
        s.I("pool", "memset", VD[:], 1.0, wr=[VD])
        qn = [s.sb("dfqn%d" % i, [128, 16, 32], F32) for i in range(2)]
        gsq = [s.sb("dfgsq%d" % i, [128, 512], F32) for i in range(2)]
        gst = [s.sb("dfgst%d" % i, [128, 16, 3], F32) for i in range(2)]
        t1 = [s.sb("dft1%d" % i, [128, 16, 32], F32) for i in range(2)]
        t2 = [s.sb("dft2%d" % i, [128, 16, 32], F32) for i in range(2)]
        qz = [s.sb("dfqz%d" % i, [128, 3, 256], F32) for i in range(2)]
        for i in range(2):
            s.I("pool", "memset", qz[i][:], 0.0, wr=[qz[i]])
        for t in range(18):
            pq = s.ps()
            pv = s.ps()
            for k in range(8):
                s.I("pe", "matmul", pq[:, :], self.hT[:, k, t * 128:(t + 1) * 128], w[:, k, 0:512], start=(k == 0), stop=(k == 7), rd=[self.hT, w], wr=[pq])
            for k in range(8):
                s.I("pe", "matmul", pv[:, 0:256], self.hT[:, k, t * 128:(t + 1) * 128], w[:, k, 512:768], start=(k == 0), stop=(k == 7), rd=[self.hT, w], wr=[pv])
            q_ = qn[t % 2]
            self._gn_rd = [gains]
            self._gn_wr = [q_]
            self.group_norm(pq, 16, 32, gains[:].to_broadcast([128, 2, 8, 32]), q_[:].rearrange("p (a h) d -> p a h d", a=2), gsq[t % 2], gst[t % 2],
                            lambda ap: ap.rearrange("p (a h d) -> p a h d", a=2, h=8))
            z_ = qz[t % 2]
            if t < 16:
                a_, b_ = t1[t % 2], t2[t % 2]
                s.I("pool", "tensor_tensor", out=a_[:], in0=q_[:], in1=COS[:, t:t + 1, :].to_broadcast([128, 16, 32]), op=ALU.mult, rd=[q_, COS], wr=[a_])
                q5 = q_[:].rearrange("p g (a f e) -> p g a f e", a=2, f=2)
                b5 = b_[:].rearrange("p g (a f e) -> p g a f e", a=2, f=2)
                s5 = SIN[:, t:t + 1, :].to_broadcast([128, 16, 32]).rearrange("p g (a f e) -> p g a f e", a=2, f=2)
                s.I("dve", "tensor_tensor", out=b5[:, :, :, 0, :], in0=q5[:, :, :, 1, :], in1=s5[:, :, :, 0, :], op=ALU.mult, rd=[q_, SIN], wr=[b_])
                s.I("dve", "tensor_tensor", out=b5[:, :, :, 1, :], in0=q5[:, :, :, 0, :], in1=s5[:, :, :, 1, :], op=ALU.mult, rd=[q_, SIN], wr=[b_])
                srcs = (a_, b_)
            else:
                srcs = (q_,)

            def comb(out_ap, sel):
                if len(srcs) == 2:
                    s.I("pool", "tensor_tensor", out=out_ap, in0=sel(srcs[0]), in1=sel(srcs[1]), op=ALU.add, rd=list(srcs), wr=[z_])
                else:
                    s.I("pool", "tensor_copy", out_ap, sel(srcs[0]), rd=list(srcs), wr=[z_])
            for m in range(2):
                comb(z_[:, m, :].rearrange("p (h m d) -> p h m d", h=4, m=2)[:, :, m, :],
                     lambda bf: bf[:, 0:8, :].rearrange("p (h m) d -> p h m d", m=2)[:, :, m, :])
            comb(z_[:, 2, :].rearrange("p (g d) -> p g d", d=32), lambda bf: bf[:, 8:16, :])
            pt = s.ps()
            pt2 = s.ps()
            for m in range(2):
                for pr in range(2):
                    s.I("pe", "transpose", pt[:, (m * 2 + pr) * 128:(m * 2 + pr + 1) * 128], z_[:, m, pr * 128:(pr + 1) * 128], self.ident[:], rd=[z_, self.ident], wr=[pt])
            for pr in range(2):
                s.I("pe", "transpose", pt2[:, pr * 128:(pr + 1) * 128], z_[:, 2, pr * 128:(pr + 1) * 128], self.ident[:], rd=[z_, self.ident], wr=[pt2])
            s.I("act", "activation", out=QZ[:, :, :, t * 128:(t + 1) * 128], in_=pt[:, :].rearrange("p (m c t) -> p m c t", m=2, c=2), func=AF.Copy, rd=[pt], wr=[QZ])
            s.I("act", "activation", out=KT[:, :, t * 128:(t + 1) * 128], in_=pt2[:, 0:256].rearrange("p (c t) -> p c t", c=2), func=AF.Copy, rd=[pt2], wr=[KT])
            s.I("dve", "tensor_copy", VD[:, t, :, 0:64], pv[:, 0:256].rearrange("p (h d) -> p h d", h=4), rd=[pv], wr=[VD])
        PT = [[s.sb("dfPT%d%d" % (i, m), [128, 18, 512], BF16) for m in range(2)] for i in range(2)]
        rec = s.sb("dfrec", [128, 2, 4, 1], F32)
        o0 = s.sb("dfo0", [128, 4, 64], F32)
        o1 = s.sb("dfo1", [128, 4, 64], F32)
        yd = [s.sb("dfyd%d" % i, [128, 4, 64], F32) for i in range(2)]
        sq = s.sb("dfsq2", [128, 4, 64], F32)
        st = s.sb("dfst2", [128, 4, 3], F32)
        blocks = [(qb * 512, 512, list(range(18))) for qb in range(4)]
        if not last:
            blocks.append((L, 256, [16, 17]))
        it = 0
        for h in range(4):
            pair, base = h // 2, 64 * (h % 2)
            for (q0, nq, kcs) in blocks:
                P_ = PT[it % 2]
                it += 1
                for m in range(2):
                    for kc in kcs:
                        pS = s.ps()
                        s.I("pe", "matmul", pS[:, 0:nq], KT[base:base + 64, pair, kc * 128:(kc + 1) * 128], QZ[base:base + 64, m, pair, q0:q0 + nq], start=True, stop=True,
                            rd=[KT, QZ], wr=[pS])
                        s.I("act", "activation", out=P_[m][:, kc, 0:nq], in_=pS[:, 0:nq], func=AF.Exp, scale=32.0 ** -0.5, rd=[pS], wr=[P_[m]])
                po = [s.psacc(0), s.psacc(1)]
                nqs = nq // 128
                for m in range(2):
                    for qs in range(nqs):
                        for i, kc in enumerate(kcs):
                            s.I("pe", "matmul", po[m][:, qs * 65:(qs + 1) * 65], P_[m][:, kc, qs * 128:(qs + 1) * 128], VD[:, kc, h, :], start=(i == 0), stop=(i == len(kcs) - 1),
                                rd=[P_[m], VD], wr=[po[m]])
                pv0 = po[0][:, 0:nqs * 65].rearrange("p (q e) -> p q e", e=65)
                pv1 = po[1][:, 0:nqs * 65].rearrange("p (q e) -> p q e", e=65)
                y_ = yd[it % 2]
                s.I("dve", "reciprocal", rec[:, 0, 0:nqs, :], pv0[:, :, 64:65], rd=[po[0]], wr=[rec])
                s.I("dve", "reciprocal", rec[:, 1, 0:nqs, :], pv1[:, :, 64:65], rd=[po[1]], wr=[rec])
                s.I("dve", "tensor_tensor", out=o0[:, 0:nqs, :], in0=pv0[:, :, 0:64], in1=rec[:, 0, 0:nqs, :].to_broadcast([128, nqs, 64]), op=ALU.mult, rd=[po[0], rec], wr=[o0])
                s.I("dve", "tensor_tensor", out=o1[:, 0:nqs, :], in0=pv1[:, :, 0:64], in1=rec[:, 1, 0:nqs, :].to_broadcast([128, nqs, 64]), op=ALU.mult, rd=[po[1], rec], wr=[o1])
                s.I("dve", "scalar_tensor_tensor", out=o0[:, 0:nqs, :], in0=o1[:, 0:nqs, :], scalar=neglam, in1=o0[:, 0:nqs, :], op0=ALU.mult, op1=ALU.add, rd=[o1, o0, ls], wr=[o0])
                s.I("pool", "tensor_tensor", out=sq[:, 0:nqs, :], in0=o0[:, 0:nqs, :], in1=o0[:, 0:nqs, :], op=ALU.mult, rd=[o0], wr=[sq])
                s.I("dve", "tensor_reduce", out=st[:, 0:nqs, 0], in_=sq[:, 0:nqs, :], axis=AX.X, op=ALU.add, rd=[sq], wr=[st])
                s.I("act", "activation", out=st[:, 0:nqs, 1], in_=st[:, 0:nqs, 0], func=AF.Sqrt, scale=1.0 / 64, bias=EPS, rd=[st], wr=[st])
                s.I("dve", "reciprocal", st[:, 0:nqs, 2], st[:, 0:nqs, 1], rd=[st], wr=[st])
                s.I("dve", "tensor_tensor", out=sq[:, 0:nqs, :], in0=o0[:, 0:nqs, :], in1=st[:, 0:nqs, 2:3].to_broadcast([128, nqs, 64]), op=ALU.mult, rd=[o0, st], wr=[sq])
                s.I("pool", "tensor_tensor", out=y_[:, 0:nqs, :], in0=sq[:, 0:nqs, :], in1=sub[:].to_broadcast([128, nqs, 64]), op=ALU.mult, rd=[sq, sub], wr=[y_])
                s.dma("sp", self.Yd.t.ap()[b, q0:q0 + nq, 256 + h * 64:256 + (h + 1) * 64].rearrange("(q p) d -> p q d", p=128), y_[:, 0:nqs, :], self.Yd, y_)

    def phase_ssm(self, l, b):
        s = self.s
        last = l == DEPTH - 1
        ntl = 16 if last else 18
        with ExitStack() as mid:
            def msb(name, shape, dt):
                s.uid += 1
                t = mid.enter_context(self.nc.sbuf_tensor("%s_%d" % (name, s.uid), list(shape), dt))
                return Buf(name, t, True)
            XS = msb("ssXS", [128, 18, 512], F32)
            BMT = msb("ssBMT", [128, 2, T], BF16)
            CMT = msb("ssCMT", [128, 2, T], BF16)
            BM = msb("ssBM", [128, 18, 2, 128], BF16)
            DT = msb("ssDT", [128, 18, 16], F32)
            LA = msb("ssLA", [128, 18, 16], F32)
            with s.phase():
                self.ssm_prep(l, b, XS, BMT, CMT, BM, DT, LA)
            with s.phase():
                self.ssm_scan(l, b, XS, BMT, CMT, BM, DT, LA, ntl)
            for bb in (XS, BMT, CMT, BM, DT, LA):
                if bb.dsem is not None:
                    s.dfree.append(bb.dsem)
                    bb.dsem = None

    def ssm_prep(self, l, b, XS, BMT, CMT, BM, DT, LA):
        s = self.s
        w = self.load_w("wss", self.w_in, self.w_in.t.ap()[l, :, 1536:3088].rearrange("(k p) n -> p k n", p=128), [128, 8, 1552])
        dtb = self.bcast_row("ssdtb", self.dt_bias, l * 16, 16)
        alog = self.bcast_row("ssalog", self.a_log, l * 16, 16)
        A = s.sb("ssA", [128, 16], F32)
        s.I("act", "activation", out=A[:], in_=alog[:], func=AF.Exp, rd=[alog], wr=[A])
        s.I("dve", "tensor_scalar", out=A[:], in0=A[:], scalar1=-1.0, scalar2=None, op0=ALU.mult, rd=[A], wr=[A])
        cw = s.sb("sscw", [128, 8, 5], F32)
        cb = s.sb("sscb", [128, 8], F32)
        s.dma("sp", cw[:], self.conv_wc.t.ap()[l], cw, self.conv_wc)
        s.dma("sp", cb[:], self.conv_bc.t.ap()[l], cb, self.conv_bc)
        zs = [s.sb("sszs%d" % i, [128, 512], F32) for i in range(2)]
        tmp = s.sb("sstmp", [128, 18, 16], F32)
        for t in range(18):
            pz = s.ps()
            pd = s.ps()
            for k in range(8):
                s.I("pe", "matmul", pz[:, :], self.hT[:, k, t * 128:(t + 1) * 128], w[:, k, 0:512], start=(k == 0), stop=(k == 7), rd=[self.hT, w], wr=[pz])
            for k in range(8):
                s.I("pe", "matmul", pd[:, 0:16], self.hT[:, k, t * 128:(t + 1) * 128], w[:, k, 1536:1552], start=(k == 0), stop=(k == 7), rd=[self.hT, w], wr=[pd])
            z_ = zs[t % 2]
            s.I("act", "activation", out=z_[:], in_=pz[:, :], func=AF.Silu, rd=[pz], wr=[z_])
            s.dma("sp", self.Zd.t.ap()[t * 128:(t + 1) * 128, :], z_[:], self.Zd, z_)
            s.I("dve", "tensor_tensor", out=tmp[:, t, :], in0=pd[:, 0:16], in1=dtb[:], op=ALU.add, rd=[pd, dtb], wr=[tmp])
        s.I("act", "activation", out=tmp[:], in_=tmp[:], func=AF.Exp, rd=[tmp], wr=[tmp])
        s.I("act", "activation", out=DT[:], in_=tmp[:], func=AF.Ln, bias=1.0, scale=1.0, rd=[tmp], wr=[DT])
        s.I("dve", "tensor_tensor", out=LA[:], in0=DT[:], in1=A[:].rearrange("p (o d) -> p o d", o=1).to_broadcast([128, 18, 16]), op=ALU.mult, rd=[DT, A], wr=[LA])
        Gl = [s.sb("ssGl%d" % i, [128, L + 4], F32) for i in range(1)]
        Gc = [s.sb("ssGc%d" % i, [128, LC + 4], F32) for i in range(1)]
        acc = [s.sb("ssacc%d" % i, [128, T], F32) for i in range(1)]
        for i in range(1):
            s.I("pool", "memset", Gl[i][:], 0.0, wr=[Gl[i]])
            s.I("pool", "memset", Gc[i][:], 0.0, wr=[Gc[i]])
        groups = [(0, 512), (512, 512), (1024, 512), (1536, 512), (2048, 256)]
        for c in range(8):
            gl, gc, a = Gl[0], Gc[0], acc[0]
            for (t0, n) in groups:
                p = s.ps()
                for k in range(8):
                    s.I("pe", "matmul", p[:, 0:n], w[:, k, 512 + c * 128:512 + (c + 1) * 128], self.hT[:, k, t0:t0 + n], start=(k == 0), stop=(k == 7), rd=[w, self.hT], wr=[p])
                if t0 < L:
                    s.I("act", "activation", out=gl[:, 2 + t0:2 + t0 + n], in_=p[:, 0:n], func=AF.Copy, rd=[p], wr=[gl])
                else:
                    s.I("act", "activation", out=gc[:, 2:2 + n], in_=p[:, 0:n], func=AF.Copy, rd=[p], wr=[gc])
            for (gb, o0, n) in ((gl, 0, L), (gc, L, LC)):
                s.I("dve", "tensor_scalar", out=a[:, o0:o0 + n], in0=gb[:, 0:n], scalar1=cw[:, c, 0:1], scalar2=None, op0=ALU.mult, rd=[gb, cw], wr=[a])
                for k in range(1, 5):
                    s.I("dve", "scalar_tensor_tensor", out=a[:, o0:o0 + n], in0=gb[:, k:k + n], scalar=cw[:, c, k:k + 1], in1=a[:, o0:o0 + n],
                        op0=ALU.mult, op1=ALU.add, rd=[gb, cw, a], wr=[a])
            if c < 4 or c in (4, 5):
                s.I("act", "activation", out=a[:], in_=a[:], func=AF.Silu, bias=cb[:, c:c + 1], scale=1.0, rd=[a, cb], wr=[a])
            if c < 4:
                for g4 in range(5):
                    tiles = list(range(4 * g4, min(4 * g4 + 4, 18)))
                    p = s.ps()
                    for ti, t in enumerate(tiles):
                        s.I("pe", "transpose", p[:, ti * 128:(ti + 1) * 128], a[:, t * 128:(t + 1) * 128], self.ident[:], rd=[a, self.ident], wr=[p])
                    s.I("act", "activation", out=XS[:, tiles[0]:tiles[0] + len(tiles), c * 128:(c + 1) * 128], in_=p[:, 0:len(tiles) * 128].rearrange("p (t d) -> p t d", d=128),
                        func=AF.Copy, rd=[p], wr=[XS])
            elif c in (4, 5):
                g = c - 4
                s.I("pool", "tensor_copy", BMT[:, g, :], a[:], rd=[a], wr=[BMT])
                for g4 in range(5):
                    tiles = list(range(4 * g4, min(4 * g4 + 4, 18)))
                    p = s.ps()
                    for ti, t in enumerate(tiles):
                        s.I("pe", "transpose", p[:, ti * 128:(ti + 1) * 128], a[:, t * 128:(t + 1) * 128], self.ident[:], rd=[a, self.ident], wr=[p])
                    s.I("act", "activation", out=BM[:, tiles[0]:tiles[0] + len(tiles), g, :], in_=p[:, 0:len(tiles) * 128].rearrange("p (t d) -> p t d", d=128),
                        func=AF.Copy, rd=[p], wr=[BM])
            else:
                g = c - 6
                s.I("act", "activation", out=CMT[:, g, :], in_=a[:], func=AF.Silu, bias=cb[:, c:c + 1], scale=1.0, rd=[a, cb], wr=[CMT])

    def ssm_scan(self, l, b, XS, BMT, CMT, BM, DT, LA, ntl):
        s = self.s
        TRI = s.sb("ssTRI", [128, 5, 128], F32)
        s.dma("sp", TRI[:], self.tri_d.t.ap().rearrange("a p n -> p a n"), TRI, self.tri_d)
        SL, SU, LE, GE, ON = (TRI[:, i, :] for i in range(5))
        Y = s.sb("ssY", [128, 18, 512], F32)
        s.I("pool", "memset", Y[:], 0.0, wr=[Y])
        ST = [[s.sb("ssST%d%d" % (d, g), [128, 4, 64], F32) for g in range(2)] for d in range(2)]
        STb = [[s.sb("ssSTb%d%d" % (d, g), [128, 256], BF16) for g in range(2)] for d in range(2)]
        for d in range(2):
            for g in range(2):
                s.I("pool", "memset", ST[d][g][:], 0.0, wr=[ST[d][g]])
                s.I("pool", "memset", STb[d][g][:], 0.0, wr=[STb[d][g]])
        EX = [s.sb("ssEX%d" % i, [128, 3, 8], F32) for i in range(2)]
        XDT = [s.sb("ssXDT%d" % i, [128, 8, 64], BF16) for i in range(2)]
        XDD = [s.sb("ssXDD%d" % i, [128, 8, 64], BF16) for i in range(2)]
        cbm = [s.sb("sscbm%d" % i, [128, 2, 128], F32) for i in range(2)]
        laM = [s.sb("sslaM%d" % i, [128, 128], F32) for i in range(4)]
        Lt = [s.sb("ssLt%d" % i, [128, 128], F32) for i in range(4)]
        Wm = [s.sb("ssW%d" % i, [128, 128], BF16) for i in range(4)]
        tm1 = [s.sb("sstm1%d" % i, [128, 4, 64], F32) for i in range(2)]
        yt = [s.sb("ssyt%d" % i, [128, 4, 64], F32) for i in range(2)]
        tm2 = [s.sb("sstm2%d" % i, [128, 4, 64], F32) for i in range(2)]
        order = [(0, 16), (1, 17), (0, 17), (1, 16)]
        for i in range(16):
            order.append((0, i))
            order.append((1, 15 - i))
        cbcache = {}
        it = 0
        hh = 0
        for (dr, t) in order:
            want_y = t < ntl
            ex = EX[it % 2]
            xdt, xdd = XDT[it % 2], XDD[it % 2]
            it += 1
            la8 = LA[:, t, dr * 8:(dr + 1) * 8]
            pe_ = s.ps()
            m1, m2 = (SL, LE) if dr == 0 else (SU, GE)
            s.I("pe", "matmul", pe_[:, 0:8], m1, la8, start=True, stop=True, rd=[TRI, LA], wr=[pe_])
            s.I("pe", "matmul", pe_[:, 8:16], m2, la8, start=True, stop=True, rd=[TRI, LA], wr=[pe_])
            s.I("pe", "matmul", pe_[:, 16:24], ON, la8, start=True, stop=True, rd=[TRI, LA], wr=[pe_])
            s.I("act", "activation", out=ex[:].rearrange("p a h -> p (a h)"), in_=pe_[:, 0:24], func=AF.Exp, rd=[pe_], wr=[ex])
            s.I("pool", "tensor_tensor", out=xdt[:], in0=XS[:, t, :].rearrange("p (h d) -> p h d", d=64),
                in1=DT[:, t, dr * 8:(dr + 1) * 8].rearrange("p (h o) -> p h o", o=1).to_broadcast([128, 8, 64]), op=ALU.mult, rd=[XS, DT], wr=[xdt])
            s.I("pool", "tensor_tensor", out=xdd[:], in0=xdt[:], in1=ex[:, 0, :].rearrange("p (h o) -> p h o", o=1).to_broadcast([128, 8, 64]), op=ALU.mult,
                rd=[xdt, ex], wr=[xdd])
            for g in range(2):
                if want_y:
                    cb_ = cbm[g]
                    pc = s.ps()
                    s.I("pe", "matmul", pc[:, 0:128], BMT[:, g, t * 128:(t + 1) * 128], CMT[:, g, t * 128:(t + 1) * 128], start=True, stop=True, rd=[BMT, CMT], wr=[pc])
                    s.I("dve", "tensor_tensor", out=cb_[:, dr, :], in0=pc[:, 0:128], in1=(LE if dr == 0 else GE), op=ALU.mult, rd=[pc, TRI], wr=[cb_])
                    py = s.psacc(g)
                    for e in range(4):
                        hd = dr * 8 + g * 4 + e
                        lm, lt, wm = laM[hh % 4], Lt[hh % 4], Wm[hh % 4]
                        hh += 1
                        s.I("pool", "tensor_scalar", out=lm[:], in0=m1, scalar1=LA[:, t, hd:hd + 1], scalar2=None, op0=ALU.mult, rd=[TRI, LA], wr=[lm])
                        pd_ = s.ps()
                        s.I("pe", "matmul", pd_[:, 0:128], lm[:], m2, start=True, stop=True, rd=[lm, TRI], wr=[pd_])
                        s.I("act", "activation", out=lt[:], in_=pd_[:, 0:128], func=AF.Exp, rd=[pd_], wr=[lt])
                        s.I("pool", "tensor_tensor", out=wm[:], in0=lt[:], in1=cb_[:, dr, :], op=ALU.mult, rd=[lt, cb_], wr=[wm])
                        s.I("pe", "matmul", py[:, e * 64:(e + 1) * 64], wm[:], xdt[:, g * 4 + e, :], start=True, stop=True, rd=[wm, xdt], wr=[py])
                    po = s.ps()
                    s.I("pe", "matmul", po[:, 0:256], CMT[:, g, t * 128:(t + 1) * 128], STb[dr][g][:], start=True, stop=True, rd=[CMT, STb[dr][g]], wr=[po])
                    a_, y_ = tm1[g], yt[g]
                    s.I("dve", "tensor_tensor", out=a_[:], in0=po[:, 0:256].rearrange("p (h d) -> p h d", d=64),
                        in1=ex[:, 1, g * 4:(g + 1) * 4].rearrange("p (h o) -> p h o", o=1).to_broadcast([128, 4, 64]), op=ALU.mult, rd=[po, ex], wr=[a_])
                    s.I("dve", "tensor_tensor", out=y_[:], in0=py[:, 0:256].rearrange("p (h d) -> p h d", d=64), in1=a_[:], op=ALU.add, rd=[py, a_], wr=[y_])
                    yv = Y[:, t, g * 256:(g + 1) * 256].rearrange("p (h d) -> p h d", d=64)
                    s.I("pool", "tensor_tensor", out=yv, in0=yv, in1=y_[:], op=ALU.add, rd=[Y, y_], wr=[Y])
                pst = s.ps()
                s.I("pe", "matmul", pst[:, 0:256], BM[:, t, g, :], xdd[:, g * 4:(g + 1) * 4, :], start=True, stop=True, rd=[BM, xdd], wr=[pst])
                c_ = tm2[g]
                s.I("pool", "tensor_tensor", out=c_[:], in0=ST[dr][g][:], in1=ex[:, 2, g * 4:(g + 1) * 4].rearrange("p (h o) -> p h o", o=1).to_broadcast([128, 4, 64]),
                    op=ALU.mult, rd=[ST[dr][g], ex], wr=[c_])
                s.I("dve", "tensor_tensor", out=ST[dr][g][:], in0=pst[:, 0:256].rearrange("p (h d) -> p h d", d=64), in1=c_[:], op=ALU.add, rd=[pst, c_], wr=[ST[dr][g]])
                s.I("act", "activation", out=STb[dr][g][:], in_=ST[dr][g][:].rearrange("p h d -> p (h d)"), func=AF.Copy, rd=[ST[dr][g]], wr=[STb[dr][g]])
        dsk = self.bcast_row("ssdsk", self.ssm_d, l * 8, 8)
        ng = self.bcast_row("ssng", self.ssm_norm, l * 512, 512)
        zt = [s.sb("sszt%d" % i, [128, 512], F32) for i in range(2)]
        u = [s.sb("ssu%d" % i, [128, 512], F32) for i in range(2)]
        junk = s.sb("ssjunk", [128, 512], BF16)
        st = s.sb("ssfst", [128, 18, 3], F32)
        for t in range(ntl):
            z_, u_ = zt[t % 2], u[t % 2]
            s.dma("sp", z_[:], self.Zd.t.ap()[t * 128:(t + 1) * 128, :], z_, self.Zd)
            s.I("dve", "tensor_tensor", out=u_[:].rearrange("p (h d) -> p h d", d=64), in0=XS[:, t, :].rearrange("p (h d) -> p h d", d=64),
                in1=dsk[:].rearrange("p (h o) -> p h o", o=1).to_broadcast([128, 8, 64]), op=ALU.mult, rd=[XS, dsk], wr=[u_])
            s.I("pool", "tensor_tensor", out=u_[:], in0=u_[:], in1=Y[:, t, :], op=ALU.add, rd=[u_, Y], wr=[u_])
            s.I("dve", "tensor_tensor", out=u_[:], in0=u_[:], in1=z_[:], op=ALU.mult, rd=[u_, z_], wr=[u_])
            s.I("act", "activation", out=junk[:], in_=u_[:], func=AF.Square, accum_out=st[:, t, 0:1], rd=[u_], wr=[junk, st])
            s.I("act", "activation", out=st[:, t, 1:2], in_=st[:, t, 0:1], func=AF.Sqrt, scale=1.0 / 512, bias=EPS, rd=[st], wr=[st])
            s.I("dve", "reciprocal", st[:, t, 2:3], st[:, t, 1:2], rd=[st], wr=[st])
            s.I("dve", "scalar_tensor_tensor", out=u_[:], in0=u_[:], scalar=st[:, t, 2:3], in1=ng[:], op0=ALU.mult, op1=ALU.mult, rd=[u_, st, ng], wr=[u_])
            s.dma("sp", self.Yd.t.ap()[b, t * 128:(t + 1) * 128, 512:1024], u_[:], self.Yd, u_)


def _consts():
    ident = np.eye(128, dtype=np.float32)
    i = np.arange(128)
    SL = (i[:, None] > i[None, :]).astype(np.float32)
    SU = (i[:, None] < i[None, :]).astype(np.float32)
    LE = (i[:, None] <= i[None, :]).astype(np.float32)
    GE = (i[:, None] >= i[None, :]).astype(np.float32)
    ON = np.ones((128, 128), np.float32)
    tri = np.stack([SL, SU, LE, GE, ON]).astype(np.float32)
    qc = np.arange(64)
    ws = np.clip(qc - 8, 0, 48)
    kc = np.arange(64)
    ok = (kc[:, None] >= ws[None, :]) & (kc[:, None] < ws[None, :] + 16)
    m = np.where(ok, 0.0, NEG).astype(np.float32)
    na_mask = np.concatenate([m, m], axis=0)
    per_axis = 16
    inv_freq = (10000.0 ** (-np.arange(0, per_axis, 2, dtype=np.float32) / per_axis)).astype(np.float32)
    t = np.arange(L)
    pos = np.stack([t // 64, t % 64], axis=-1).astype(np.float32)
    ang = pos[:, :, None] * inv_freq
    ang = np.concatenate([ang, ang], axis=-1).reshape(L, 32)
    cos = np.cos(ang).astype(np.float32)
    sin = np.sin(ang).astype(np.float32).reshape(L, 2, 2, 8).copy()
    sin[:, :, 0, :] *= -1.0
    return ident, tri, na_mask, cos, sin.reshape(L, 32)


def make_in_maps(inp):
    f = lambda a: np.ascontiguousarray(np.asarray(a, dtype=np.float32))
    ident, tri, na_mask, cos, sin = _consts()
    colv = lambda v, n: f(np.asarray(v).reshape(DEPTH, n, 128).transpose(0, 2, 1))
    idx = np.clip(np.arange(64)[:, None] - np.arange(64)[None, :], -15, 15) + 15
    shared = {
        "w_ada": f(inp["w_ada"]), "b_ada": f(inp["b_ada"]),
        "g_mixc": colv(inp["g_mix"], 8), "g_ffnc": colv(inp["g_ffn"], 8),
        "w_in": f(inp["w_in"]), "w_out": f(inp["w_out"]), "w_up": f(inp["ffn_w_up"]), "w_down": f(inp["ffn_w_down"]),
        "na_qg": f(inp["na_q_gain"]), "na_kg": f(inp["na_k_gain"]),
        "rpb_t": f(np.asarray(inp["na_rpb"])[..., idx]), "na_mask": na_mask,
        "df_qg": f(inp["df_q_gain"]), "df_kg": f(inp["df_k_gain"]), "df_lam": f(inp["df_lambda"]), "df_subln": f(inp["df_subln"]),
        "rope_cos": cos, "rope_sin": sin,
        "conv_wc": f(np.asarray(inp["ssm_conv_w"]).reshape(DEPTH, 5, 8, 128).transpose(0, 3, 2, 1)),
        "conv_bc": colv(inp["ssm_conv_b"], 8),
        "dt_bias": f(np.asarray(inp["ssm_dt_bias"]).reshape(DEPTH, 16)), "a_log": f(np.asarray(inp["ssm_a_log"]).reshape(DEPTH, 16)),
        "ssm_d": f(inp["ssm_d"]), "ssm_norm": f(inp["ssm_norm"]),
        "fconv_wc": f(np.asarray(inp["ffn_conv_w"]).reshape(DEPTH, 3, NF, 128).transpose(0, 3, 2, 1)),
        "fconv_bc": colv(inp["ffn_conv_b"], NF),
        "ident": ident, "tri": tri,
    }
    maps = []
    x, c, ctx, c_ctx = (np.asarray(inp[k]) for k in ("x", "c", "ctx", "c_ctx"))
    for i in range(NCORES):
        sl = slice(i * NB, (i + 1) * NB)
        c3 = np.concatenate([c[sl], c_ctx[None, :]], axis=0)
        cT = f(c3.reshape(3, 8, 128).transpose(2, 1, 0))
        m = dict(shared)
        m.update({"x": f(x[sl]), "ctx": f(ctx[sl]), "cT": cT})
        maps.append(m)
    return maps


_NC_CACHE = {}


def kernel(**inputs):
    if "nc" not in _NC_CACHE:
        _NC_CACHE["nc"] = Builder().build()
    nc = _NC_CACHE["nc"]
    maps = make_in_maps(inputs)
    res = run_bass_kernel_spmd(nc, maps, core_ids=list(range(NCORES)))
    return np.concatenate([np.asarray(r["out"]) for r in res.results], axis=0).astype(np.float32)
```

```python
import math
from contextlib import ExitStack

import numpy as np
import concourse.bass as bass
import concourse.mybir as mybir
from concourse.bass_utils import run_bass_kernel_spmd

F32 = mybir.dt.float32
BF16 = mybir.dt.bfloat16
AF = mybir.ActivationFunctionType
ALU = mybir.AluOpType
AX = mybir.AxisListType

NCORES = 8
NB = 2
L = 2048
LC = 256
T = L + LC
D = 1024
DEPTH = 2
DFF = 2816
NF = DFF // 128
EPS = 1e-6
NEG = -30000.0


class Buf:
    __slots__ = ("name", "w", "r", "dsem", "t", "persist")

    def __init__(self, name, t=None, persist=False):
        self.name = name
        self.w = {}
        self.r = {}
        self.dsem = None
        self.t = t
        self.persist = persist

    def __getitem__(self, idx):
        return self.t[idx]


class Sched:
    ENG = ("pe", "act", "dve", "pool", "sp")

    def __init__(self, nc, es, ndsem=48):
        self.nc = nc
        self.ges = es
        self.es = es
        self.eng = {"pe": nc.tensor, "act": nc.scalar, "dve": nc.vector, "pool": nc.gpsimd, "sp": nc.sync}
        self.sem = {k: es.enter_context(nc.semaphore("c_" + k)) for k in self.ENG}
        self.cnt = {k: 0 for k in self.ENG}
        self.seen = {k: {} for k in self.ENG}
        self.prog = {k: [] for k in self.ENG}
        self.dsems = [es.enter_context(nc.semaphore("d%d" % i)) for i in range(ndsem)]
        self.dval = [0] * ndsem
        self.dfree = list(range(ndsem))
        self.phase_bufs = []
        self.uid = 0
        self.ps_rr = 0
        self.PS = []

    def sb(self, name, shape, dt, persist=False):
        self.uid += 1
        es = self.ges if persist else self.es
        t = es.enter_context(self.nc.sbuf_tensor("%s_%d" % (name, self.uid), list(shape), dt))
        b = Buf(name, t, persist)
        if not persist:
            self.phase_bufs.append(b)
        return b

    def dram(self, name, shape, dt, kind="Internal"):
        t = self.nc.dram_tensor(name, list(shape), dt, kind=kind)
        return Buf(name, t, True)

    def psum_init(self):
        for i in range(8):
            t = self.ges.enter_context(self.nc.psum_tensor("psb%d" % i, [128, 512], F32))
            self.PS.append(Buf("ps%d" % i, t, True))

    def ps(self):
        b = self.PS[self.ps_rr % 6]
        self.ps_rr += 1
        return b

    def psacc(self, i):
        return self.PS[6 + (i % 2)]

    def _deps(self, e, reads, writes):
        d = {}
        own = "c_" + e

        def add(tokdict, is_read_set):
            for key, (sem, val) in tokdict.items():
                if key == own and (e == "pe" or is_read_set):
                    continue
                if d.get(key, (None, 0))[1] < val:
                    d[key] = (sem, val)

        for b in reads:
            add(b.w, False)
        for b in writes:
            add(b.w, False)
            add(b.r, True)
        return d

    def _wait(self, e, d):
        seen = self.seen[e]
        for key, (sem, val) in d.items():
            if seen.get(key, 0) >= val:
                continue
            self.prog[e].append(("w", sem, val))
            seen[key] = val

    def I(self, e, name, *args, rd=(), wr=(), **kw):
        d = self._deps(e, rd, wr)
        self._wait(e, d)
        self.cnt[e] += 1
        self.prog[e].append(("i", name, args, kw, self.sem[e], 1))
        key = "c_" + e
        tok = (self.sem[e], self.cnt[e])
        for b in rd:
            b.r[key] = tok
        for b in wr:
            b.w[key] = tok

    def dma(self, e, out_ap, in_ap, dst, src, **kw):
        if dst.dsem is None:
            dst.dsem = self.dfree.pop()
        i = dst.dsem
        d = self._deps(e, [src], [dst])
        self._wait(e, d)
        self.dval[i] += 16
        self.prog[e].append(("i", "dma_start", (), dict(out=out_ap, in_=in_ap, **kw), self.dsems[i], 16))
        key = "d%d" % i
        tok = (self.dsems[i], self.dval[i])
        src.r[key] = tok
        dst.w[key] = tok

    def drain(self):
        d = {}
        for i, v in enumerate(self.dval):
            if v:
                d["d%d" % i] = (self.dsems[i], v)
        self._wait("sp", d)

    def emit(self):
        self.drain()
        prog = self.prog
        with self.nc.Block() as block:
            def mk(e):
                def body(g):
                    for it in prog[e]:
                        if it[0] == "w":
                            g.wait_ge(it[1], it[2])
                        else:
                            getattr(g, it[1])(*it[2], **it[3]).then_inc(it[4], it[5])
                return body
            block.tensor(mk("pe"))
            block.scalar(mk("act"))
            block.vector(mk("dve"))
            block.gpsimd(mk("pool"))
            block.sync(mk("sp"))
        self.prog = {k: [] for k in self.ENG}
        for b in self.phase_bufs:
            if b.dsem is not None:
                self.dfree.append(b.dsem)
                b.dsem = None
        self.phase_bufs = []

    class _Phase:
        def __init__(self, s):
            self.s = s

        def __enter__(self):
            self.es = ExitStack()
            self.es.__enter__()
            self.s.es = self.es
            return self

        def __exit__(self, *a):
            if a[0] is None:
                self.s.emit()
            self.s.es = self.s.ges
            return self.es.__exit__(*a)

    def phase(self):
        return Sched._Phase(self)


def dap(buf, off, dims):
    return bass.AP(buf.t, off, [list(d) for d in dims])


class Builder:
    def __init__(self, dbg=None):
        self.dbg = dbg or {}
        self.nc = bass.Bass("TRN2", target_bir_lowering=False)
        self.outs = []

    def declare(self, s):
        I = lambda n, sh, dt=F32: s.dram(n, sh, dt, kind="ExternalInput")
        self.x = I("x", [NB, L, D])
        self.ctx = I("ctx", [NB, LC, D])
        self.cT = I("cT", [128, 8, 3])
        self.w_ada = I("w_ada", [DEPTH, D, 6 * D])
        self.b_ada = I("b_ada", [DEPTH, 6 * D])
        self.g_mixc = I("g_mixc", [DEPTH, 128, 8])
        self.g_ffnc = I("g_ffnc", [DEPTH, 128, 8])
        self.w_in = I("w_in", [DEPTH, D, 3088])
        self.w_out = I("w_out", [DEPTH, D, D])
        self.w_up = I("w_up", [DEPTH, D, 2 * DFF])
        self.w_down = I("w_down", [DEPTH, DFF, D])
        self.na_qg = I("na_qg", [DEPTH, 64])
        self.na_kg = I("na_kg", [DEPTH, 64])
        self.rpb_t = I("rpb_t", [DEPTH, 4, 15, 64, 64])
        self.na_mask = I("na_mask", [128, 64])
        self.df_qg = I("df_qg", [DEPTH, 32])
        self.df_kg = I("df_kg", [DEPTH, 32])
        self.df_lam = I("df_lam", [DEPTH, 4, 32])
        self.df_subln = I("df_subln", [DEPTH, 64])
        self.rope_cos = I("rope_cos", [L, 32])
        self.rope_sin = I("rope_sin", [L, 32])
        self.conv_wc = I("conv_wc", [DEPTH, 128, 8, 5])
        self.conv_bc = I("conv_bc", [DEPTH, 128, 8])
        self.dt_bias = I("dt_bias", [DEPTH, 16])
        self.a_log = I("a_log", [DEPTH, 16])
        self.ssm_d = I("ssm_d", [DEPTH, 8])
        self.ssm_norm = I("ssm_norm", [DEPTH, 512])
        self.fconv_wc = I("fconv_wc", [DEPTH, 128, NF, 3])
        self.fconv_bc = I("fconv_bc", [DEPTH, 128, NF])
        self.ident_d = I("ident", [128, 128])
        self.tri_d = I("tri", [5, 128, 128])
        self.negm_d = I("negm", [128, 2, 128])
        self.sel_d = I("sel", [8, 8, 128])
        self.out = s.dram("out", [NB, L, D], F32, kind="ExternalOutput")
        dk = "ExternalOutput" if self.dbg.get("dump") else "Internal"
        self.modrow_d = s.dram("modrow_d", [DEPTH, 3, 6 * D], F32, kind=dk)
        self.XA = [s.dram("XA%d" % l, [NB, T, D], F32, kind=dk) for l in range(DEPTH)]
        self.XB = s.dram("XB0", [NB, T, D], F32, kind=dk)
        self.Yd = s.dram("Yd", [NB, T, D], F32, kind=dk)
        self.Zd = s.dram("Zd", [T, 512], F32, kind=dk)
        self.ATd = s.dram("ATd", [NF, 128, T], BF16, kind=dk)
        self.hTd = s.dram("hTd", [128, 8, T], BF16, kind=dk) if self.dbg.get("dump") else None
        self.Yin = I("Yin", [DEPTH, NB, T, D]) if self.dbg.get("feed_y") else None

    def build(self):
        nc = self.nc
        with ExitStack() as es:
            s = Sched(nc, es)
            self.s = s
            self.declare(s)
            s.psum_init()
            self.ident = s.sb("ident", [128, 128], F32, persist=True)
            self.identb = s.sb("identb", [128, 128], BF16, persist=True)
            self.hT = s.sb("hT", [128, 8, T], BF16, persist=True)
            self.AB = [[s.sb("AB%d%d" % (l, i), [128, 8, 3], F32, persist=True) for i in range(4)] for l in range(DEPTH)]
            self.BT = s.sb("BT", [128, 4, 14, 64], BF16, persist=True)
            ph = self.dbg.get("phases")
            with s.phase():
                s.dma("sp", self.ident[:], self.ident_d.t.ap(), self.ident, self.ident_d)
                s.I("act", "activation", out=self.identb[:], in_=self.ident[:], func=AF.Copy, rd=[self.ident], wr=[self.identb])
                self.phase_adaln()
            for l in self.dbg.get("layers", range(DEPTH)):
                last = l == DEPTH - 1
                nt = 16 if last else 18
                if ph is None or "na" in ph:
                    with s.phase():
                        self.phase_na_bias(l)
                for b in range(NB):
                    if l == 0:
                        src = lambda t, b=b: (self.x.t.ap()[b, t * 128:(t + 1) * 128, :], self.x) if t < 16 else \
                            (self.ctx.t.ap()[b, (t - 16) * 128:(t - 15) * 128, :], self.ctx)
                    else:
                        src = lambda t, b=b: (self.XB.t.ap()[b, t * 128:(t + 1) * 128, :], self.XB)
                    with s.phase():
                        self.phase_norm(src, self.AB[l][0], self.AB[l][1], b, 18)
                    if self.dbg.get("dump") and l == self.dbg.get("dump_l", 0) and b == 0 and self.dbg.get("dump_h") == "mix":
                        with s.phase():
                            s.dma("sp", self.hTd.t.ap(), self.hT[:], self.hTd, self.hT)
                    if ph is None or "na" in ph:
                        with s.phase():
                            self.phase_na(l, b)
                    if ph is None or "df" in ph:
                        with s.phase():
                            self.phase_df(l, b)
                    if ph is None or "ssm" in ph:
                        self.phase_ssm(l, b)
                    if ph is None or "out" in ph:
                        with s.phase():
                            self.phase_out(l, b, src, nt)
                    if ph is None or "ffn" in ph:
                        srcA = lambda t, b=b, l=l: (self.XA[l].t.ap()[b, t * 128:(t + 1) * 128, :], self.XA[l])
                        with s.phase():
                            self.phase_norm(srcA, self.AB[l][2], self.AB[l][3], b, nt)
                        with s.phase():
                            self.phase_ffn_up(l, b, nt)
                        with s.phase():
                            self.phase_ffn_down(l, b, srcA, nt)
        return nc

    def phase_adaln(self):
        s = self.s
        cT = s.sb("cT", [128, 8, 3], F32)
        siluT = s.sb("siluT", [128, 8, 3], F32)
        s.dma("sp", cT[:], self.cT.t.ap(), cT, self.cT)
        s.I("act", "activation", out=siluT[:], in_=cT[:], func=AF.Silu, rd=[cT], wr=[siluT])
        wa = [s.sb("wa%d" % i, [128, 8, 512], F32) for i in range(3)]
        for l in range(DEPTH):
            brow = s.sb("brow%d" % l, [3, 6 * D], F32)
            modrow = s.sb("modrow%d" % l, [3, 6 * D], F32)
            s.dma("sp", brow[:], dap(self.b_ada, l * 6 * D, [[0, 3], [1, 6 * D]]), brow, self.b_ada)
            for j in range(12):
                w = wa[j % 3]
                s.dma("sp", w[:], self.w_ada.t.ap()[l, :, j * 512:(j + 1) * 512].rearrange("(k p) n -> p k n", p=128), w, self.w_ada)
                pm = s.ps()
                for k in range(8):
                    s.I("pe", "matmul", pm[0:3, :], siluT[:, k, :], w[:, k, :], start=(k == 0), stop=(k == 7), rd=[siluT, w], wr=[pm])
                s.I("dve", "tensor_tensor", out=modrow[:, j * 512:(j + 1) * 512], in0=pm[0:3, :], in1=brow[:, j * 512:(j + 1) * 512], op=ALU.add,
                    rd=[pm, brow], wr=[modrow])
            s.dma("sp", self.modrow_d.t.ap()[l], modrow[:], self.modrow_d, modrow)
            pT = s.ps()
            for c in range(48):
                s.I("pe", "transpose", pT[:, c * 3:(c + 1) * 3], modrow[0:3, c * 128:(c + 1) * 128], self.ident[0:3, 0:3], rd=[modrow, self.ident], wr=[pT])
            modcol = s.sb("modcol%d" % l, [128, 48, 3], F32)
            s.I("dve", "tensor_copy", modcol[:].rearrange("p a b -> p (a b)"), pT[:, 0:144], rd=[pT], wr=[modcol])
            gm = s.sb("gm%d" % l, [128, 8, 1], F32)
            gf = s.sb("gf%d" % l, [128, 8, 1], F32)
            s.dma("sp", gm[:, :, 0], self.g_mixc.t.ap()[l], gm, self.g_mixc)
            s.dma("sp", gf[:, :, 0], self.g_ffnc.t.ap()[l], gf, self.g_ffnc)
            A1, B1, A2, B2 = self.AB[l]
            for (A, Bv, g, sc0, sh0) in ((A1, B1, gm, 8, 0), (A2, B2, gf, 32, 24)):
                s.I("dve", "scalar_tensor_tensor", out=A[:], in0=modcol[:, sc0:sc0 + 8, :], scalar=1.0, in1=g[:].to_broadcast([128, 8, 3]),
                    op0=ALU.add, op1=ALU.mult, rd=[modcol, g], wr=[A])
                s.I("dve", "tensor_copy", Bv[:], modcol[:, sh0:sh0 + 8, :], rd=[modcol], wr=[Bv])

    def phase_norm(self, src, A, Bv, b, ntiles):
        s = self.s
        xt = [s.sb("nxt%d" % i, [128, D], F32) for i in range(4)]
        xn = [s.sb("nxn%d" % i, [128, D], F32) for i in range(8)]
        junks = [s.sb("njunk%d" % i, [128, D], BF16) for i in range(2)]
        sts = [s.sb("nst%d" % i, [128, 1, 3], F32) for i in range(8)]
        ngroups = (ntiles + 3) // 4
        for g in range(ngroups):
            tiles = list(range(4 * g, min(4 * g + 4, ntiles)))
            j = b if tiles[0] < 16 else 2
            for ti, t in enumerate(tiles):
                ap, sbuf = src(t)
                x_ = xt[t % 4]
                n_ = xn[t % 8]
                s.dma("sp", x_[:], ap, x_, sbuf)
                st, junk = sts[t % 8], junks[t % 2]
                s.I("act", "activation", out=junk[:], in_=x_[:], func=AF.Square, accum_out=st[:, 0, 0:1], rd=[x_], wr=[junk, st])
                s.I("act", "activation", out=st[:, 0, 1:2], in_=st[:, 0, 0:1], func=AF.Sqrt, scale=1.0 / D, bias=EPS, rd=[st], wr=[st])
                s.I("dve", "reciprocal", st[:, 0, 2:3], st[:, 0, 1:2], rd=[st], wr=[st])
                s.I("dve", "tensor_scalar", out=n_[:], in0=x_[:], scalar1=st[:, 0, 2:3], scalar2=None, op0=ALU.mult, rd=[x_, st], wr=[n_])
            n = len(tiles) * 128
            for c in range(8):
                p = s.ps()
                for ti, t in enumerate(tiles):
                    n_ = xn[t % 8]
                    s.I("pe", "transpose", p[:, ti * 128:(ti + 1) * 128], n_[:, c * 128:(c + 1) * 128], self.ident[:], rd=[n_, self.ident], wr=[p])
                s.I("act", "activation", out=self.hT[:, c, tiles[0] * 128:tiles[0] * 128 + n], in_=p[:, 0:n], func=AF.Identity,
                    scale=A[:, c, j:j + 1], bias=Bv[:, c, j:j + 1], rd=[p, A, Bv], wr=[self.hT])

    def load_w(self, name, src_buf, src_ap, shape):
        s = self.s
        w = s.sb(name, shape, BF16)
        s.dma("pool", w[:], src_ap, w, src_buf)
        return w

    def bcast_row(self, name, src_buf, off, n, dt=F32, parts=128):
        s = self.s
        t = s.sb(name, [parts, n], dt)
        s.dma("sp", t[:], dap(src_buf, off, [[0, parts], [1, n]]), t, src_buf)
        return t

    def phase_out(self, l, b, src, ntiles):
        s = self.s
        w = self.load_w("wout", self.w_out, self.w_out.t.ap()[l].rearrange("(k p) n -> p k n", p=128), [128, 8, D])
        G = {}
        G[b] = self.bcast_row("g1b", self.modrow_d, (l * 3 + b) * 6 * D + 2 * D, D)
        if ntiles > 16:
            G[2] = self.bcast_row("g1c", self.modrow_d, (l * 3 + 2) * 6 * D + 2 * D, D)
        yt = [s.sb("oyt%d" % i, [128, D], F32) for i in range(8)]
        xr = [s.sb("oxr%d" % i, [128, D], F32) for i in range(4)]
        yT = [s.sb("oyT%d" % i, [128, 8, 512], BF16) for i in range(2)]
        tmp = [s.sb("otmp%d" % i, [128, D], F32) for i in range(2)]
        xo = [s.sb("oxo%d" % i, [128, D], F32) for i in range(2)]
        ngroups = (ntiles + 3) // 4
        for g in range(ngroups):
            tiles = list(range(4 * g, min(4 * g + 4, ntiles)))
            j = b if tiles[0] < 16 else 2
            for t in tiles:
                y_ = yt[t % 8]
                if self.Yin is not None:
                    s.dma("sp", y_[:], self.Yin.t.ap()[l, b, t * 128:(t + 1) * 128, :], y_, self.Yin)
                else:
                    s.dma("sp", y_[:], self.Yd.t.ap()[b, t * 128:(t + 1) * 128, :], y_, self.Yd)
                ap, sbuf = src(t)
                s.dma("sp", xr[t % 4][:], ap, xr[t % 4], sbuf)
            yT_ = yT[g % 2]
            n = len(tiles) * 128
            for c in range(8):
                p = s.ps()
                for ti, t in enumerate(tiles):
                    s.I("pe", "transpose", p[:, ti * 128:(ti + 1) * 128], yt[t % 8][:, c * 128:(c + 1) * 128], self.ident[:], rd=[yt[t % 8], self.ident], wr=[p])
                s.I("act", "activation", out=yT_[:, c, 0:n], in_=p[:, 0:n], func=AF.Copy, rd=[p], wr=[yT_])
            for ti, t in enumerate(tiles):
                tm = tmp[t % 2]
                xo_ = xo[t % 2]
                for hf in range(2):
                    p = s.ps()
                    for k in range(8):
                        s.I("pe", "matmul", p[:, :], yT_[:, k, ti * 128:(ti + 1) * 128], w[:, k, hf * 512:(hf + 1) * 512], start=(k == 0), stop=(k == 7),
                            rd=[yT_, w], wr=[p])
                    s.I("dve", "tensor_tensor", out=tm[:, hf * 512:(hf + 1) * 512], in0=p[:, :], in1=G[j][:, hf * 512:(hf + 1) * 512], op=ALU.mult,
                        rd=[p, G[j]], wr=[tm])
                s.I("pool", "tensor_tensor", out=xo_[:], in0=tm[:], in1=xr[t % 4][:], op=ALU.add, rd=[tm, xr[t % 4]], wr=[xo_])
                s.dma("sp", self.XA[l].t.ap()[b, t * 128:(t + 1) * 128, :], xo_[:], self.XA[l], xo_)

    def phase_ffn_up(self, l, b, ntiles):
        s = self.s
        groups = [(0, 512), (512, 512), (1024, 512), (1536, 512)]
        if ntiles > 16:
            groups.append((2048, 256))
        ntok = ntiles * 128
        wu = [s.sb("wu%d" % i, [128, 8, 256], BF16) for i in range(3)]
        Gl = [s.sb("fGl%d" % i, [128, L + 2], F32) for i in range(2)]
        Gc = [s.sb("fGc%d" % i, [128, LC + 2], F32) for i in range(2)]
        V = [s.sb("fV%d" % i, [128, T], F32) for i in range(2)]
        acc = [s.sb("facc%d" % i, [128, T], F32) for i in range(2)]
        at = [s.sb("fat%d" % i, [128, T], BF16) for i in range(2)]
        cw = s.sb("fcw", [128, NF, 3], F32)
        cb = s.sb("fcb", [128, NF], F32)
        s.dma("sp", cw[:], self.fconv_wc.t.ap()[l], cw, self.fconv_wc)
        s.dma("sp", cb[:], self.fconv_bc.t.ap()[l], cb, self.fconv_bc)
        for i in range(2):
            s.I("pool", "memset", Gl[i][:], 0.0, wr=[Gl[i]])
            s.I("pool", "memset", Gc[i][:], 0.0, wr=[Gc[i]])
        wup = self.w_up.t.ap()[l]

        def loadw(f):
            w = wu[f % 3]
            s.dma("pool", w[:, :, 0:128], wup[:, f * 128:(f + 1) * 128].rearrange("(k p) n -> p k n", p=128), w, self.w_up)
            s.dma("pool", w[:, :, 128:256], wup[:, DFF + f * 128:DFF + (f + 1) * 128].rearrange("(k p) n -> p k n", p=128), w, self.w_up)

        loadw(0)
        for f in range(NF):
            if f + 1 < NF:
                loadw(f + 1)
            w = wu[f % 3]
            gl, gc, v, a, o = Gl[f % 2], Gc[f % 2], V[f % 2], acc[f % 2], at[f % 2]
            for (t0, n) in groups:
                pa = s.ps()
                pb = s.ps()
                for k in range(8):
                    s.I("pe", "matmul", pa[:, 0:n], w[:, k, 0:128], self.hT[:, k, t0:t0 + n], start=(k == 0), stop=(k == 7), rd=[w, self.hT], wr=[pa])
                for k in range(8):
                    s.I("pe", "matmul", pb[:, 0:n], w[:, k, 128:256], self.hT[:, k, t0:t0 + n], start=(k == 0), stop=(k == 7), rd=[w, self.hT], wr=[pb])
                if t0 < L:
                    s.I("act", "activation", out=gl[:, 1 + t0:1 + t0 + n], in_=pa[:, 0:n], func=AF.Copy, rd=[pa], wr=[gl])
                else:
                    s.I("act", "activation", out=gc[:, 1:1 + n], in_=pa[:, 0:n], func=AF.Copy, rd=[pa], wr=[gc])
                s.I("act", "activation", out=v[:, t0:t0 + n], in_=pb[:, 0:n], func=AF.Copy, rd=[pb], wr=[v])
            segs = [(gl, 0, L)] + ([(gc, L, LC)] if ntiles > 16 else [])
            for (gb, o0, n) in segs:
                s.I("dve", "tensor_scalar", out=a[:, o0:o0 + n], in0=gb[:, 0:n], scalar1=cw[:, f, 0:1], scalar2=None, op0=ALU.mult, rd=[gb, cw], wr=[a])
                for k in (1, 2):
                    s.I("dve", "scalar_tensor_tensor", out=a[:, o0:o0 + n], in0=gb[:, k:k + n], scalar=cw[:, f, k:k + 1], in1=a[:, o0:o0 + n],
                        op0=ALU.mult, op1=ALU.add, rd=[gb, cw, a], wr=[a])
            s.I("act", "activation", out=a[:, 0:ntok], in_=a[:, 0:ntok], func=AF.Silu, bias=cb[:, f:f + 1], scale=1.0, rd=[a, cb], wr=[a])
            s.I("pool", "tensor_tensor", out=o[:, 0:ntok], in0=a[:, 0:ntok], in1=v[:, 0:ntok], op=ALU.mult, rd=[a, v], wr=[o])
            s.dma("sp", self.ATd.t.ap()[f, :, 0:ntok], o[:, 0:ntok], self.ATd, o)

    def phase_ffn_down(self, l, b, src, ntiles):
        s = self.s
        last = l == DEPTH - 1
        w = self.load_w("wdn", self.w_down, self.w_down.t.ap()[l].rearrange("(f p) n -> p f n", p=128), [128, NF, D])
        G = {}
        G[b] = self.bcast_row("g2b", self.modrow_d, (l * 3 + b) * 6 * D + 5 * D, D)
        if ntiles > 16:
            G[2] = self.bcast_row("g2c", self.modrow_d, (l * 3 + 2) * 6 * D + 5 * D, D)
        aT = [s.sb("daT%d" % i, [128, NF, 512], BF16) for i in range(2)]
        xr = [s.sb("dxr%d" % i, [128, D], F32) for i in range(4)]
        tmp = [s.sb("dtmp%d" % i, [128, D], F32) for i in range(2)]
        xo = [s.sb("dxo%d" % i, [128, D], F32) for i in range(2)]
        ngroups = (ntiles + 3) // 4
        for g in range(ngroups):
            tiles = list(range(4 * g, min(4 * g + 4, ntiles)))
            j = b if tiles[0] < 16 else 2
            n = len(tiles) * 128
            a_ = aT[g % 2]
            s.dma("sp", a_[:, :, 0:n], self.ATd.t.ap()[:, :, tiles[0] * 128:tiles[0] * 128 + n].rearrange("f p t -> p f t"), a_, self.ATd)
            for t in tiles:
                ap, sbuf = src(t)
                s.dma("sp", xr[t % 4][:], ap, xr[t % 4], sbuf)
            for ti, t in enumerate(tiles):
                tm = tmp[t % 2]
                xo_ = xo[t % 2]
                for hf in range(2):
                    p = s.ps()
                    for f in range(NF):
                        s.I("pe", "matmul", p[:, :], a_[:, f, ti * 128:(ti + 1) * 128], w[:, f, hf * 512:(hf + 1) * 512], start=(f == 0), stop=(f == NF - 1),
                            rd=[a_, w], wr=[p])
                    s.I("dve", "tensor_tensor", out=tm[:, hf * 512:(hf + 1) * 512], in0=p[:, :], in1=G[j][:, hf * 512:(hf + 1) * 512], op=ALU.mult,
                        rd=[p, G[j]], wr=[tm])
                s.I("pool", "tensor_tensor", out=xo_[:], in0=tm[:], in1=xr[t % 4][:], op=ALU.add, rd=[tm, xr[t % 4]], wr=[xo_])
                if last:
                    s.dma("sp", self.out.t.ap()[b, t * 128:(t + 1) * 128, :], xo_[:], self.out, xo_)
                else:
                    s.dma("sp", self.XB.t.ap()[b, t * 128:(t + 1) * 128, :], xo_[:], self.XB, xo_)

    def group_norm(self, p, ngrp, gd, gains, out, sq, st, view):
        s = self.s
        n = ngrp * gd
        s.I("act", "activation", out=sq[:, 0:n], in_=p[:, 0:n], func=AF.Square, rd=[p], wr=[sq])
        s.I("dve", "tensor_reduce", out=st[:, 0:ngrp, 0], in_=sq[:, 0:n].rearrange("p (g d) -> p g d", d=gd), axis=AX.X, op=ALU.add, rd=[sq], wr=[st])
        s.I("act", "activation", out=st[:, 0:ngrp, 1], in_=st[:, 0:ngrp, 0], func=AF.Sqrt, scale=1.0 / gd, bias=EPS, rd=[st], wr=[st])
        s.I("dve", "reciprocal", st[:, 0:ngrp, 2], st[:, 0:ngrp, 1], rd=[st], wr=[st])
        s.I("dve", "tensor_tensor", out=sq[:, 0:n].rearrange("p (g d) -> p g d", d=gd), in0=p[:, 0:n].rearrange("p (g d) -> p g d", d=gd),
            in1=st[:, 0:ngrp, 2:3].to_broadcast([128, ngrp, gd]), op=ALU.mult, rd=[p, st], wr=[sq])
        s.I("pool", "tensor_tensor", out=out, in0=view(sq[:, 0:n]), in1=gains, op=ALU.mult, rd=[sq] + self._gn_rd, wr=self._gn_wr)

    def phase_na_bias(self, l):
        s = self.s
        bt32 = s.sb("bt32", [128, 4, 14, 64], F32)
        mk = s.sb("namask", [128, 64], F32)
        s.dma("sp", mk[:], self.na_mask.t.ap(), mk, self.na_mask)
        for h in range(4):
            for half in range(2):
                s.dma("sp", bt32[64 * half:64 * half + 64, h, :, :], self.rpb_t.t.ap()[l, h, half:half + 14].rearrange("d k q -> k d q"), bt32, self.rpb_t)
        s.I("dve", "tensor_tensor", out=bt32[:].rearrange("p h d q -> p (h d) q"), in0=bt32[:].rearrange("p h d q -> p (h d) q"),
            in1=mk[:].rearrange("p (o q) -> p o q", o=1).to_broadcast([128, 56, 64]), op=ALU.add, rd=[bt32, mk], wr=[bt32])
        s.I("dve", "tensor_scalar", out=self.BT[:].rearrange("p h d q -> p (h d q)"), in0=bt32[:].rearrange("p h d q -> p (h d q)"), scalar1=8.0, scalar2=None,
            op0=ALU.mult, rd=[bt32], wr=[self.BT])

    def phase_na(self, l, b):
        s = self.s
        last = l == DEPTH - 1
        w = self.load_w("wna", self.w_in, self.w_in.t.ap()[l, :, 0:768].rearrange("(k p) n -> p k n", p=128), [128, 8, 768])
        gains = s.sb("nagain", [128, 2, 1, 64], F32)
        s.dma("sp", gains[:, 0, 0, :], dap(self.na_qg, l * 64, [[0, 128], [1, 64]]), gains, self.na_qg)
        s.dma("sp", gains[:, 1, 0, :], dap(self.na_kg, l * 64, [[0, 128], [1, 64]]), gains, self.na_kg)
        QKT = s.sb("naQKT", [128, 4, T], BF16)
        VE = s.sb("naVE", [128, 18, 4, 65], BF16)
        VO = s.sb("naVO", [128, 15, 4, 65], BF16)
        s.I("pool", "memset", VE[:], 1.0, wr=[VE])
        s.I("pool", "memset", VO[:], 1.0, wr=[VO])
        qn = [s.sb("naqn%d" % i, [128, 2, 4, 64], F32) for i in range(2)]
        gsq = [s.sb("nagsq%d" % i, [128, 512], F32) for i in range(2)]
        gst = [s.sb("nagst%d" % i, [128, 16, 3], F32) for i in range(2)]
        for t in range(18):
            pq = s.ps()
            pv = s.ps()
            for k in range(8):
                s.I("pe", "matmul", pq[:, :], self.hT[:, k, t * 128:(t + 1) * 128], w[:, k, 0:512], start=(k == 0), stop=(k == 7), rd=[self.hT, w], wr=[pq])
            for k in range(8):
                s.I("pe", "matmul", pv[:, 0:256], self.hT[:, k, t * 128:(t + 1) * 128], w[:, k, 512:768], start=(k == 0), stop=(k == 7), rd=[self.hT, w], wr=[pv])
            q_ = qn[t % 2]
            self._gn_rd = [gains]
            self._gn_wr = [q_]
            self.group_norm(pq, 8, 64, gains[:].to_broadcast([128, 2, 4, 64]), q_[:], gsq[t % 2], gst[t % 2],
                            lambda ap: ap.rearrange("p (a h d) -> p a h d", a=2, h=4))
            pt = s.ps()
            qf = q_[:].rearrange("p a h d -> p (a h d)")
            for cc in range(4):
                s.I("pe", "transpose", pt[:, cc * 128:(cc + 1) * 128], qf[:, cc * 128:(cc + 1) * 128], self.ident[:], rd=[q_, self.ident], wr=[pt])
            s.I("act", "activation", out=QKT[:, :, t * 128:(t + 1) * 128], in_=pt[:, :].rearrange("p (c t) -> p c t", c=4), func=AF.Copy, rd=[pt], wr=[QKT])
            s.I("dve", "tensor_copy", VE[:, t, :, 0:64], pv[:, 0:256].rearrange("p (h d) -> p h d", h=4), rd=[pv], wr=[VE])
        for i in range(15):
            pv = s.ps()
            for k in range(8):
                s.I("pe", "matmul", pv[:, 0:256], self.hT[:, k, 64 + i * 128:64 + (i + 1) * 128], w[:, k, 512:768], start=(k == 0), stop=(k == 7), rd=[self.hT, w], wr=[pv])
            s.I("dve", "tensor_copy", VO[:, i, :, 0:64], pv[:, 0:256].rearrange("p (h d) -> p h d", h=4), rd=[pv], wr=[VO])
        PT = [s.sb("naPT%d" % i, [128, 6, 64], BF16) for i in range(4)]
        rec = [s.sb("narec%d" % i, [128, 4, 1], F32) for i in range(2)]
        yo = [s.sb("nayo%d" % i, [128, 4, 64], F32) for i in range(2)]
        it = 0
        for rp in range(16):
            po = s.psacc(rp)
            for rr in range(2):
                r = 2 * rp + rr
                R0 = min(max(r - 4, 0), 24)
                for h in range(4):
                    pair, base = h // 2, 64 * (h % 2)
                    pS = s.ps()
                    q_ap = QKT[base:base + 64, pair, r * 64:(r + 1) * 64]
                    for ci in range(4):
                        kr = R0 + 2 * ci
                        d = kr - r + 7
                        s.I("pe", "matmul", pS[:, ci * 64:(ci + 1) * 64], QKT[base:base + 64, 2 + pair, kr * 64:kr * 64 + 128], q_ap, start=True, stop=False,
                            rd=[QKT], wr=[pS])
                        s.I("pe", "matmul", pS[:, ci * 64:(ci + 1) * 64], self.identb[:], self.BT[:, h, d, :], start=False, stop=True,
                            rd=[self.identb, self.BT], wr=[pS])
                    for cc in range(2):
                        s.I("pe", "matmul", pS[:, (4 + cc) * 64:(5 + cc) * 64], QKT[base:base + 64, 2 + pair, L + cc * 128:L + (cc + 1) * 128], q_ap,
                            start=True, stop=True, rd=[QKT], wr=[pS])
                    P_ = PT[it % 4]
                    it += 1
                    s.I("act", "activation", out=P_[:].rearrange("p c q -> p (c q)"), in_=pS[:, 0:384], func=AF.Exp, scale=0.125, rd=[pS], wr=[P_])
                    for c in range(6):
                        if c < 4:
                            kr = R0 + 2 * c
                            vb, v_ap = (VE, VE[:, kr // 2, h, :]) if kr % 2 == 0 else (VO, VO[:, (kr - 1) // 2, h, :])
                        else:
                            vb, v_ap = VE, VE[:, 16 + (c - 4), h, :]
                        s.I("pe", "matmul", po[64 * rr:64 * rr + 64, h * 65:(h + 1) * 65], P_[:, c, :], v_ap, start=(c == 0), stop=(c == 5), rd=[P_, vb], wr=[po])
            rc, y_ = rec[rp % 2], yo[rp % 2]
            pov = po[:, 0:260].rearrange("p (h e) -> p h e", e=65)
            s.I("dve", "reciprocal", rc[:], pov[:, :, 64:65], rd=[po], wr=[rc])
            s.I("dve", "tensor_tensor", out=y_[:], in0=pov[:, :, 0:64], in1=rc[:].to_broadcast([128, 4, 64]), op=ALU.mult, rd=[po, rc], wr=[y_])
            s.dma("sp", self.Yd.t.ap()[b, rp * 128:(rp + 1) * 128, 0:256], y_[:].rearrange("p h d -> p (h d)"), self.Yd, y_)
        if not last:
            PTc = [s.sb("naPTc%d" % i, [128, 2, 256], BF16) for i in range(2)]
            pos = [s.psacc(0), s.psacc(1)]
            for h in range(4):
                pair, base = h // 2, 64 * (h % 2)
                pS = s.ps()
                for cc in range(2):
                    s.I("pe", "matmul", pS[:, cc * 256:(cc + 1) * 256], QKT[base:base + 64, 2 + pair, L + cc * 128:L + (cc + 1) * 128],
                        QKT[base:base + 64, pair, L:L + 256], start=True, stop=True, rd=[QKT], wr=[pS])
                P_ = PTc[h % 2]
                s.I("act", "activation", out=P_[:].rearrange("p c q -> p (c q)"), in_=pS[:, :], func=AF.Exp, scale=0.125, rd=[pS], wr=[P_])
                for qt in range(2):
                    for cc in range(2):
                        s.I("pe", "matmul", pos[qt][:, h * 65:(h + 1) * 65], P_[:, cc, qt * 128:(qt + 1) * 128], VE[:, 16 + cc, h, :], start=(cc == 0), stop=(cc == 1),
                            rd=[P_, VE], wr=[pos[qt]])
            for qt in range(2):
                rc, y_ = rec[qt], yo[qt]
                pov = pos[qt][:, 0:260].rearrange("p (h e) -> p h e", e=65)
                s.I("dve", "reciprocal", rc[:], pov[:, :, 64:65], rd=[pos[qt]], wr=[rc])
                s.I("dve", "tensor_tensor", out=y_[:], in0=pov[:, :, 0:64], in1=rc[:].to_broadcast([128, 4, 64]), op=ALU.mult, rd=[pos[qt], rc], wr=[y_])
                s.dma("sp", self.Yd.t.ap()[b, L + qt * 128:L + (qt + 1) * 128, 0:256], y_[:].rearrange("p h d -> p (h d)"), self.Yd, y_)

    def phase_df(self, l, b):
        s = self.s
        last = l == DEPTH - 1
        lam_init = 0.8 - 0.6 * math.exp(-0.3 * l)
        w = self.load_w("wdf", self.w_in, self.w_in.t.ap()[l, :, 768:1536].rearrange("(k p) n -> p k n", p=128), [128, 8, 768])
        gains = s.sb("dfgain", [128, 2, 1, 32], F32)
        s.dma("sp", gains[:, 0, 0, :], dap(self.df_qg, l * 32, [[0, 128], [1, 32]]), gains, self.df_qg)
        s.dma("sp", gains[:, 1, 0, :], dap(self.df_kg, l * 32, [[0, 128], [1, 32]]), gains, self.df_kg)
        COS = s.sb("dfcos", [128, 16, 32], F32)
        SIN = s.sb("dfsin", [128, 16, 32], F32)
        s.dma("sp", COS[:], self.rope_cos.t.ap().rearrange("(t p) d -> p t d", p=128), COS, self.rope_cos)
        s.dma("sp", SIN[:], self.rope_sin.t.ap().rearrange("(t p) d -> p t d", p=128), SIN, self.rope_sin)
        lv = s.sb("dflv", [128, 4, 32], F32)
        s.dma("sp", lv[:].rearrange("p a d -> p (a d)"), dap(self.df_lam, l * 128, [[0, 128], [1, 128]]), lv, self.df_lam)
        lp = s.sb("dflp", [128, 2, 32], F32)
        ls = s.sb("dfls", [128, 8], F32)
        s.I("dve", "tensor_tensor", out=lp[:, 0, :], in0=lv[:, 0, :], in1=lv[:, 1, :], op=ALU.mult, rd=[lv], wr=[lp])
        s.I("dve", "tensor_tensor", out=lp[:, 1, :], in0=lv[:, 2, :], in1=lv[:, 3, :], op=ALU.mult, rd=[lv], wr=[lp])
        s.I("dve", "tensor_reduce", out=ls[:, 0:2], in_=lp[:], axis=AX.X, op=ALU.add, rd=[lp], wr=[ls])
        s.I("act", "activation", out=ls[:, 2:4], in_=ls[:, 0:2], func=AF.Exp, rd=[ls], wr=[ls])
        s.I("dve", "scalar_tensor_tensor", out=ls[:, 4:5], in0=ls[:, 3:4], scalar=-lam_init, in1=ls[:, 2:3], op0=ALU.add, op1=ALU.subtract, rd=[ls], wr=[ls])
        neglam = ls[:, 4:5]
        sub = s.sb("dfsub", [128, 1, 64], F32)
        s.dma("sp", sub[:, 0, :], dap(self.df_subln, l * 64, [[0, 128], [1, 64]]), sub, self.df_subln)
        s.I("dve", "tensor_scalar", out=sub[:], in0=sub[:], scalar1=1.0 - lam_init, scalar2=None, op0=ALU.mult, rd=[sub], wr=[sub])

        QZ = s.sb("dfQZ", [128, 2, 2, T], BF16)
        KT = s.sb("dfKT", [128, 2, T], BF16)
        VD = s.sb("dfVD", [128, 18, 4, 65], BF16)
        s.I("pool", "memset", VD[:], 1.0, wr=[VD])
        qn = [s.sb("dfqn%d" % i, [128, 16, 32], F32) for i in range(2)]
        gsq = [s.sb("dfgsq%d" % i, [128, 512], F32) for i in range(2)]
        gst = [s.sb("dfgst%d" % i, [128, 16, 3], F32) for i in range(2)]
        t1 = [s.sb("dft1%d" % i, [128, 16, 32], F32) for i in range(2)]
        t2 = [s.sb("dft2%d" % i, [128, 16, 32], F32) for i in range(2)]
        qz = [s.sb("dfqz%d" % i, [128, 3, 256], F32) for i in range(2)]
        for i in range(2):
            s.I("pool", "memset", qz[i][:], 0.0, wr=[qz[i]])
        for t in range(18):
            pq = s.ps()
            pv = s.ps()
            for k in range(8):
                s.I("pe", "matmul", pq[:, :], self.hT[:, k, t * 128:(t + 1) * 128], w[:, k, 0:512], start=(k == 0), stop=(k == 7), rd=[self.hT, w], wr=[pq])
            for k in range(8):
                s.I("pe", "matmul", pv[:, 0:256], self.hT[:, k, t * 128:(t + 1) * 128], w[:, k, 512:768], start=(k == 0), stop=(k == 7), rd=[self.hT, w], wr=[pv])
            q_ = qn[t % 2]
            self._gn_rd = [gains]
            self._gn_wr = [q_]
            self.group_norm(pq, 16, 32, gains[:].to_broadcast([128, 2, 8, 32]), q_[:].rearrange("p (a h) d -> p a h d", a=2), gsq[t % 2], gst[t % 2],
                            lambda ap: ap.rearrange("p (a h d) -> p a h d", a=2, h=8))
            z_ = qz[t % 2]
            if t < 16:
                a_, b_ = t1[t % 2], t2[t % 2]
                s.I("pool", "tensor_tensor", out=a_[:], in0=q_[:], in1=COS[:, t:t + 1, :].to_broadcast([128, 16, 32]), op=ALU.mult, rd=[q_, COS], wr=[a_])
                q5 = q_[:].rearrange("p g (a f e) -> p g a f e", a=2, f=2)
                b5 = b_[:].rearrange("p g (a f e) -> p g a f e", a=2, f=2)
                s5 = SIN[:, t:t + 1, :].to_broadcast([128, 16, 32]).rearrange("p g (a f e) -> p g a f e", a=2, f=2)
                s.I("dve", "tensor_tensor", out=b5[:, :, :, 0, :], in0=q5[:, :, :, 1, :], in1=s5[:, :, :, 0, :], op=ALU.mult, rd=[q_, SIN], wr=[b_])
                s.I("dve", "tensor_tensor", out=b5[:, :, :, 1, :], in0=q5[:, :, :, 0, :], in1=s5[:, :, :, 1, :], op=ALU.mult, rd=[q_, SIN], wr=[b_])
                srcs = (a_, b_)
            else:
                srcs = (q_,)

            def comb(out_ap, sel):
                if len(srcs) == 2:
                    s.I("pool", "tensor_tensor", out=out_ap, in0=sel(srcs[0]), in1=sel(srcs[1]), op=ALU.add, rd=list(srcs), wr=[z_])
                else:
                    s.I("pool", "tensor_copy", out_ap, sel(srcs[0]), rd=list(srcs), wr=[z_])
            for m in range(2):
                comb(z_[:, m, :].rearrange("p (h m d) -> p h m d", h=4, m=2)[:, :, m, :],
                     lambda bf: bf[:, 0:8, :].rearrange("p (h m) d -> p h m d", m=2)[:, :, m, :])
            comb(z_[:, 2, :].rearrange("p (g d) -> p g d", d=32), lambda bf: bf[:, 8:16, :])
            pt = s.ps()
            pt2 = s.ps()
            for m in range(2):
                for pr in range(2):
                    s.I("pe", "transpose", pt[:, (m * 2 + pr) * 128:(m * 2 + pr + 1) * 128], z_[:, m, pr * 128:(pr + 1) * 128], self.ident[:], rd=[z_, self.ident], wr=[pt])
            for pr in range(2):
                s.I("pe", "transpose", pt2[:, pr * 128:(pr + 1) * 128], z_[:, 2, pr * 128:(pr + 1) * 128], self.ident[:], rd=[z_, self.ident], wr=[pt2])
            s.I("act", "activation", out=QZ[:, :, :, t * 128:(t + 1) * 128], in_=pt[:, :].rearrange("p (m c t) -> p m c t", m=2, c=2), func=AF.Copy, rd=[pt], wr=[QZ])
            s.I("act", "activation", out=KT[:, :, t * 128:(t + 1) * 128], in_=pt2[:, 0:256].rearrange("p (c t) -> p c t", c=2), func=AF.Copy, rd=[pt2], wr=[KT])
            s.I("dve", "tensor_copy", VD[:, t, :, 0:64], pv[:, 0:256].rearrange("p (h d) -> p h d", h=4), rd=[pv], wr=[VD])
        PT = [[s.sb("dfPT%d%d" % (i, m), [128, 18, 512], BF16) for m in range(2)] for i in range(2)]
        rec = s.sb("dfrec", [128, 2, 4, 1], F32)
        o0 = s.sb("dfo0", [128, 4, 64], F32)
        o1 = s.sb("dfo1", [128, 4, 64], F32)
        yd = [s.sb("dfyd%d" % i, [128, 4, 64], F32) for i in range(2)]
        sq = s.sb("dfsq2", [128, 4, 64], F32)
        st = s.sb("dfst2", [128, 4, 3], F32)
        blocks = [(qb * 512, 512, list(range(18))) for qb in range(4)]
        if not last:
            blocks.append((L, 256, [16, 17]))
        it = 0
        for h in range(4):
            pair, base = h // 2, 64 * (h % 2)
            for (q0, nq, kcs) in blocks:
                P_ = PT[it % 2]
                it += 1
                for m in range(2):
                    for kc in kcs:
                        pS = s.ps()
                        s.I("pe", "matmul", pS[:, 0:nq], KT[base:base + 64, pair, kc * 128:(kc + 1) * 128], QZ[base:base + 64, m, pair, q0:q0 + nq], start=True, stop=True,
                            rd=[KT, QZ], wr=[pS])
                        s.I("act", "activation", out=P_[m][:, kc, 0:nq], in_=pS[:, 0:nq], func=AF.Exp, scale=32.0 ** -0.5, rd=[pS], wr=[P_[m]])
                po = [s.psacc(0), s.psacc(1)]
                nqs = nq // 128
                for m in range(2):
                    for qs in range(nqs):
                        for i, kc in enumerate(kcs):
                            s.I("pe", "matmul", po[m][:, qs * 65:(qs + 1) * 65], P_[m][:, kc, qs * 128:(qs + 1) * 128], VD[:, kc, h, :], start=(i == 0), stop=(i == len(kcs) - 1),
                                rd=[P_[m], VD], wr=[po[m]])
                pv0 = po[0][:, 0:nqs * 65].rearrange("p (q e) -> p q e", e=65)
                pv1 = po[1][:, 0:nqs * 65].rearrange("p (q e) -> p q e", e=65)
                y_ = yd[it % 2]
                s.I("dve", "reciprocal", rec[:, 0, 0:nqs, :], pv0[:, :, 64:65], rd=[po[0]], wr=[rec])
                s.I("dve", "reciprocal", rec[:, 1, 0:nqs, :], pv1[:, :, 64:65], rd=[po[1]], wr=[rec])
                s.I("dve", "tensor_tensor", out=o0[:, 0:nqs, :], in0=pv0[:, :, 0:64], in1=rec[:, 0, 0:nqs, :].to_broadcast([128, nqs, 64]), op=ALU.mult, rd=[po[0], rec], wr=[o0])
                s.I("dve", "tensor_tensor", out=o1[:, 0:nqs, :], in0=pv1[:, :, 0:64], in1=rec[:, 1, 0:nqs, :].to_broadcast([128, nqs, 64]), op=ALU.mult, rd=[po[1], rec], wr=[o1])
                s.I("dve", "scalar_tensor_tensor", out=o0[:, 0:nqs, :], in0=o1[:, 0:nqs, :], scalar=neglam, in1=o0[:, 0:nqs, :], op0=ALU.mult, op1=ALU.add, rd=[o1, o0, ls], wr=[o0])
                s.I("pool", "tensor_tensor", out=sq[:, 0:nqs, :], in0=o0[:, 0:nqs, :], in1=o0[:, 0:nqs, :], op=ALU.mult, rd=[o0], wr=[sq])
                s.I("dve", "tensor_reduce", out=st[:, 0:nqs, 0], in_=sq[:, 0:nqs, :], axis=AX.X, op=ALU.add, rd=[sq], wr=[st])
                s.I("act", "activation", out=st[:, 0:nqs, 1], in_=st[:, 0:nqs, 0], func=AF.Sqrt, scale=1.0 / 64, bias=EPS, rd=[st], wr=[st])
                s.I("dve", "reciprocal", st[:, 0:nqs, 2], st[:, 0:nqs, 1], rd=[st], wr=[st])
                s.I("dve", "tensor_tensor", out=sq[:, 0:nqs, :], in0=o0[:, 0:nqs, :], in1=st[:, 0:nqs, 2:3].to_broadcast([128, nqs, 64]), op=ALU.mult, rd=[o0, st], wr=[sq])
                s.I("pool", "tensor_tensor", out=y_[:, 0:nqs, :], in0=sq[:, 0:nqs, :], in1=sub[:].to_broadcast([128, nqs, 64]), op=ALU.mult, rd=[sq, sub], wr=[y_])
                s.dma("sp", self.Yd.t.ap()[b, q0:q0 + nq, 256 + h * 64:256 + (h + 1) * 64].rearrange("(q p) d -> p q d", p=128), y_[:, 0:nqs, :], self.Yd, y_)

    def phase_ssm(self, l, b):
        s = self.s
        last = l == DEPTH - 1
        ntl = 16 if last else 18
        with ExitStack() as mid:
            def msb(name, shape, dt):
                s.uid += 1
                t = mid.enter_context(self.nc.sbuf_tensor("%s_%d" % (name, s.uid), list(shape), dt))
                return Buf(name, t, True)
            XS = msb("ssXS", [128, 18, 512], F32)
            BMT = msb("ssBMT", [128, 2, T], BF16)
            CMT = msb("ssCMT", [128, 2, T], BF16)
            BM = msb("ssBM", [128, 18, 2, 128], BF16)
            DT = msb("ssDT", [128, 18, 16], F32)
            LA = msb("ssLA", [128, 18, 16], F32)
            with s.phase():
                self.ssm_prep(l, b, XS, BMT, CMT, BM, DT, LA)
            with s.phase():
                self.ssm_scan(l, b, XS, BMT, CMT, BM, DT, LA, ntl)
            for bb in (XS, BMT, CMT, BM, DT, LA):
                if bb.dsem is not None:
                    s.dfree.append(bb.dsem)
                    bb.dsem = None

    def ssm_prep(self, l, b, XS, BMT, CMT, BM, DT, LA):
        s = self.s
        w = self.load_w("wss", self.w_in, self.w_in.t.ap()[l, :, 1536:3088].rearrange("(k p) n -> p k n", p=128), [128, 8, 1552])
        dtb = self.bcast_row("ssdtb", self.dt_bias, l * 16, 16)
        alog = self.bcast_row("ssalog", self.a_log, l * 16, 16)
        A = s.sb("ssA", [128, 16], F32)
        s.I("act", "activation", out=A[:], in_=alog[:], func=AF.Exp, rd=[alog], wr=[A])
        s.I("dve", "tensor_scalar", out=A[:], in0=A[:], scalar1=-1.0, scalar2=None, op0=ALU.mult, rd=[A], wr=[A])
        cw = s.sb("sscw", [128, 8, 5], F32)
        cb = s.sb("sscb", [128, 8], F32)
        s.dma("sp", cw[:], self.conv_wc.t.ap()[l], cw, self.conv_wc)
        s.dma("sp", cb[:], self.conv_bc.t.ap()[l], cb, self.conv_bc)
        zs = [s.sb("sszs%d" % i, [128, 512], F32) for i in range(2)]
        tmp = s.sb("sstmp", [128, 18, 16], F32)
        for t in range(18):
            pz = s.ps()
            pd = s.ps()
            for k in range(8):
                s.I("pe", "matmul", pz[:, :], self.hT[:, k, t * 128:(t + 1) * 128], w[:, k, 0:512], start=(k == 0), stop=(k == 7), rd=[self.hT, w], wr=[pz])
            for k in range(8):
                s.I("pe", "matmul", pd[:, 0:16], self.hT[:, k, t * 128:(t + 1) * 128], w[:, k, 1536:1552], start=(k == 0), stop=(k == 7), rd=[self.hT, w], wr=[pd])
            z_ = zs[t % 2]
            s.I("act", "activation", out=z_[:], in_=pz[:, :], func=AF.Silu, rd=[pz], wr=[z_])
            s.dma("sp", self.Zd.t.ap()[t * 128:(t + 1) * 128, :], z_[:], self.Zd, z_)
            s.I("dve", "tensor_tensor", out=tmp[:, t, :], in0=pd[:, 0:16], in1=dtb[:], op=ALU.add, rd=[pd, dtb], wr=[tmp])
        s.I("act", "activation", out=tmp[:], in_=tmp[:], func=AF.Exp, rd=[tmp], wr=[tmp])
        s.I("act", "activation", out=DT[:], in_=tmp[:], func=AF.Ln, bias=1.0, scale=1.0, rd=[tmp], wr=[DT])
        s.I("dve", "tensor_tensor", out=LA[:], in0=DT[:], in1=A[:].rearrange("p (o d) -> p o d", o=1).to_broadcast([128, 18, 16]), op=ALU.mult, rd=[DT, A], wr=[LA])
        Gl = [s.sb("ssGl%d" % i, [128, L + 4], F32) for i in range(1)]
        Gc = [s.sb("ssGc%d" % i, [128, LC + 4], F32) for i in range(1)]
        acc = [s.sb("ssacc%d" % i, [128, T], F32) for i in range(1)]
        for i in range(1):
            s.I("pool", "memset", Gl[i][:], 0.0, wr=[Gl[i]])
            s.I("pool", "memset", Gc[i][:], 0.0, wr=[Gc[i]])
        groups = [(0, 512), (512, 512), (1024, 512), (1536, 512), (2048, 256)]
        for c in range(8):
            gl, gc, a = Gl[0], Gc[0], acc[0]
            for (t0, n) in groups:
                p = s.ps()
                for k in range(8):
                    s.I("pe", "matmul", p[:, 0:n], w[:, k, 512 + c * 128:512 + (c + 1) * 128], self.hT[:, k, t0:t0 + n], start=(k == 0), stop=(k == 7), rd=[w, self.hT], wr=[p])
                if t0 < L:
                    s.I("act", "activation", out=gl[:, 2 + t0:2 + t0 + n], in_=p[:, 0:n], func=AF.Copy, rd=[p], wr=[gl])
                else:
                    s.I("act", "activation", out=gc[:, 2:2 + n], in_=p[:, 0:n], func=AF.Copy, rd=[p], wr=[gc])
            for (gb, o0, n) in ((gl, 0, L), (gc, L, LC)):
                s.I("dve", "tensor_scalar", out=a[:, o0:o0 + n], in0=gb[:, 0:n], scalar1=cw[:, c, 0:1], scalar2=None, op0=ALU.mult, rd=[gb, cw], wr=[a])
                for k in range(1, 5):
                    s.I("dve", "scalar_tensor_tensor", out=a[:, o0:o0 + n], in0=gb[:, k:k + n], scalar=cw[:, c, k:k + 1], in1=a[:, o0:o0 + n],
                        op0=ALU.mult, op1=ALU.add, rd=[gb, cw, a], wr=[a])
            if c < 4 or c in (4, 5):
                s.I("act", "activation", out=a[:], in_=a[:], func=AF.Silu, bias=cb[:, c:c + 1], scale=1.0, rd=[a, cb], wr=[a])
            if c < 4:
                for g4 in range(5):
                    tiles = list(range(4 * g4, min(4 * g4 + 4, 18)))
                    p = s.ps()
                    for ti, t in enumerate(tiles):
                        s.I("pe", "transpose", p[:, ti * 128:(ti + 1) * 128], a[:, t * 128:(t + 1) * 128], self.ident[:], rd=[a, self.ident], wr=[p])
                    s.I("act", "activation", out=XS[:, tiles[0]:tiles[0] + len(tiles), c * 128:(c + 1) * 128], in_=p[:, 0:len(tiles) * 128].rearrange("p (t d) -> p t d", d=128),
                        func=AF.Copy, rd=[p], wr=[XS])
            elif c in (4, 5):
                g = c - 4
                s.I("pool", "tensor_copy", BMT[:, g, :], a[:], rd=[a], wr=[BMT])
                for g4 in range(5):
                    tiles = list(range(4 * g4, min(4 * g4 + 4, 18)))
                    p = s.ps()
                    for ti, t in enumerate(tiles):
                        s.I("pe", "transpose", p[:, ti * 128:(ti + 1) * 128], a[:, t * 128:(t + 1) * 128], self.ident[:], rd=[a, self.ident], wr=[p])
                    s.I("act", "activation", out=BM[:, tiles[0]:tiles[0] + len(tiles), g, :], in_=p[:, 0:len(tiles) * 128].rearrange("p (t d) -> p t d", d=128),
                        func=AF.Copy, rd=[p], wr=[BM])
            else:
                g = c - 6
                s.I("act", "activation", out=CMT[:, g, :], in_=a[:], func=AF.Silu, bias=cb[:, c:c + 1], scale=1.0, rd=[a, cb], wr=[CMT])

    def ssm_scan(self, l, b, XS, BMT, CMT, BM, DT, LA, ntl):
        s = self.s
        TRI = s.sb("ssTRI", [128, 5, 128], F32)
        s.dma("sp", TRI[:], self.tri_d.t.ap().rearrange("a p n -> p a n"), TRI, self.tri_d)
        SL, SU, LE, GE, ON = (TRI[:, i, :] for i in range(5))
        Y = s.sb("ssY", [128, 18, 512], F32)
        s.I("pool", "memset", Y[:], 0.0, wr=[Y])
        ST = [[s.sb("ssST%d%d" % (d, g), [128, 4, 64], F32) for g in range(2)] for d in range(2)]
        STb = [[s.sb("ssSTb%d%d" % (d, g), [128, 256], BF16) for g in range(2)] for d in range(2)]
        for d in range(2):
            for g in range(2):
                s.I("pool", "memset", ST[d][g][:], 0.0, wr=[ST[d][g]])
                s.I("pool", "memset", STb[d][g][:], 0.0, wr=[STb[d][g]])
        NEGM = s.sb("ssNEGM", [128, 2, 128], BF16)
        s.dma("pool", NEGM[:], self.negm_d.t.ap(), NEGM, self.negm_d)
        SEL = s.sb("ssSEL", [8, 8, 128], F32)
        s.dma("sp", SEL[:], self.sel_d.t.ap(), SEL, self.sel_d)
        EX = [s.sb("ssEX%d" % i, [128, 3, 8], F32) for i in range(3)]
        CST = [s.sb("ssCST%d" % i, [8, 2, 128], F32) for i in range(3)]
        XDT = [s.sb("ssXDT%d" % i, [128, 8, 64], BF16) for i in range(3)]
        XDD = [s.sb("ssXDD%d" % i, [128, 8, 64], BF16) for i in range(3)]
        cbs = [s.sb("sscbs%d" % i, [128, 1, 128], F32) for i in range(6)]
        Lt = [s.sb("ssLt%d" % i, [128, 4, 128], F32) for i in range(4)]
        Wm = [s.sb("ssW%d" % i, [128, 4, 128], BF16) for i in range(4)]
        tm1 = [s.sb("sstm1%d" % i, [128, 4, 64], F32) for i in range(4)]
        yt = [s.sb("ssyt%d" % i, [128, 4, 64], F32) for i in range(4)]
        tm2 = [s.sb("sstm2%d" % i, [128, 4, 64], F32) for i in range(4)]
        order = [(0, 16), (1, 17), (0, 17), (1, 16)]
        for i in range(16):
            order.append((0, i))
            order.append((1, 15 - i))
        ctxs = {}

        def stageA(k):
            dr, t = order[k]
            want_y = t < ntl
            ex, cst, xdt, xdd = EX[k % 3], CST[k % 3], XDT[k % 3], XDD[k % 3]
            la8 = LA[:, t, dr * 8:(dr + 1) * 8]
            pe_ = s.ps()
            m1, m2 = (SL, LE) if dr == 0 else (SU, GE)
            s.I("pe", "matmul", pe_[:, 0:8], m1, la8, start=True, stop=True, rd=[TRI, LA], wr=[pe_])
            s.I("pe", "matmul", pe_[:, 8:16], m2, la8, start=True, stop=True, rd=[TRI, LA], wr=[pe_])
            s.I("pe", "matmul", pe_[:, 16:24], ON, la8, start=True, stop=True, rd=[TRI, LA], wr=[pe_])
            s.I("act", "activation", out=ex[:].rearrange("p a h -> p (a h)"), in_=pe_[:, 0:24], func=AF.Exp, rd=[pe_], wr=[ex])
            s.I("dve", "tensor_tensor", out=xdt[:], in0=XS[:, t, :].rearrange("p (h d) -> p h d", d=64),
                in1=DT[:, t, dr * 8:(dr + 1) * 8].rearrange("p (h o) -> p h o", o=1).to_broadcast([128, 8, 64]), op=ALU.mult, rd=[XS, DT], wr=[xdt])
            s.I("dve", "tensor_tensor", out=xdd[:], in0=xdt[:], in1=ex[:, 0, :].rearrange("p (h o) -> p h o", o=1).to_broadcast([128, 8, 64]), op=ALU.mult,
                rd=[xdt, ex], wr=[xdd])
            cb2 = []
            if want_y:
                pcs = s.ps()
                s.I("pe", "matmul", pcs[0:8, 0:128], la8, m2, start=True, stop=True, rd=[LA, TRI], wr=[pcs])
                s.I("act", "activation", out=cst[:, 0, :], in_=pcs[0:8, 0:128], func=AF.Copy, rd=[pcs], wr=[cst])
                s.I("act", "activation", out=cst[:, 1, :], in_=pcs[0:8, 0:128], func=AF.Copy, scale=-1.0, rd=[pcs], wr=[cst])
                for g in range(2):
                    cb_ = cbs[(2 * k + g) % 6]
                    pc = s.ps()
                    s.I("pe", "matmul", pc[:, 0:128], BMT[:, g, t * 128:(t + 1) * 128], CMT[:, g, t * 128:(t + 1) * 128], start=True, stop=True, rd=[BMT, CMT], wr=[pc])
                    s.I("act", "activation", out=cb_[:, 0, :], in_=pc[:, 0:128], func=AF.Copy, rd=[pc], wr=[cb_])
                    cb2.append(cb_)
            ctxs[k] = (ex, cst, xdt, xdd, cb2)

        def stageB(k):
            dr, t = order[k]
            want_y = t < ntl
            ex, cst, xdt, xdd, cb2 = ctxs[k]
            ws = []
            if want_y:
                for g in range(2):
                    lt, wm = Lt[(2 * k + g) % 4], Wm[(2 * k + g) % 4]
                    pd_ = s.ps()
                    for e in range(4):
                        h8 = g * 4 + e
                        o_ = pd_[:, e * 128:(e + 1) * 128]
                        s.I("pe", "matmul", o_, SEL[0:8, h8, :], cst[0:8, 0, :], start=True, stop=False, rd=[SEL, cst], wr=[pd_])
                        s.I("pe", "matmul", o_, cst[0:8, 1, :], SEL[0:8, h8, :], start=False, stop=False, rd=[SEL, cst], wr=[pd_])
                        s.I("pe", "matmul", o_, self.identb[:], NEGM[:, dr, :], start=False, stop=True, rd=[self.identb, NEGM], wr=[pd_])
                    s.I("act", "activation", out=lt[:].rearrange("p e i -> p (e i)"), in_=pd_[:, :], func=AF.Exp, rd=[pd_], wr=[lt])
                    s.I("dve", "tensor_tensor", out=wm[:], in0=lt[:], in1=cb2[g][:].to_broadcast([128, 4, 128]), op=ALU.mult, rd=[lt, cb2[g]], wr=[wm])
                    ws.append(wm)
            ctxs[k] = (ex, cst, xdt, xdd, ws)

        def stageC(k):
            dr, t = order[k]
            want_y = t < ntl
            ex, cst, xdt, xdd, ws = ctxs.pop(k)
            for g in range(2):
                if want_y:
                    py = s.psacc(g)
                    for e in range(4):
                        s.I("pe", "matmul", py[:, e * 64:(e + 1) * 64], ws[g][:, e, :], xdt[:, g * 4 + e, :], start=True, stop=True, rd=[ws[g], xdt], wr=[py])
                    po = s.ps()
                    s.I("pe", "matmul", po[:, 0:256], CMT[:, g, t * 128:(t + 1) * 128], STb[dr][g][:], start=True, stop=True, rd=[CMT, STb[dr][g]], wr=[po])
                    a_, y_ = tm1[(2 * k + g) % 4], yt[(2 * k + g) % 4]
                    s.I("dve", "tensor_tensor", out=a_[:], in0=po[:, 0:256].rearrange("p (h d) -> p h d", d=64),
                        in1=ex[:, 1, g * 4:(g + 1) * 4].rearrange("p (h o) -> p h o", o=1).to_broadcast([128, 4, 64]), op=ALU.mult, rd=[po, ex], wr=[a_])
                    s.I("dve", "tensor_tensor", out=y_[:], in0=py[:, 0:256].rearrange("p (h d) -> p h d", d=64), in1=a_[:], op=ALU.add, rd=[py, a_], wr=[y_])
                    yv = Y[:, t, g * 256:(g + 1) * 256].rearrange("p (h d) -> p h d", d=64)
                    s.I("pool", "tensor_tensor", out=yv, in0=yv, in1=y_[:], op=ALU.add, rd=[Y, y_], wr=[Y])
                pst = s.ps()
                s.I("pe", "matmul", pst[:, 0:256], BM[:, t, g, :], xdd[:, g * 4:(g + 1) * 4, :], start=True, stop=True, rd=[BM, xdd], wr=[pst])
                c_ = tm2[(2 * k + g) % 4]
                s.I("pool", "tensor_tensor", out=c_[:], in0=ST[dr][g][:], in1=ex[:, 2, g * 4:(g + 1) * 4].rearrange("p (h o) -> p h o", o=1).to_broadcast([128, 4, 64]),
                    op=ALU.mult, rd=[ST[dr][g], ex], wr=[c_])
                s.I("dve", "tensor_tensor", out=ST[dr][g][:], in0=pst[:, 0:256].rearrange("p (h d) -> p h d", d=64), in1=c_[:], op=ALU.add, rd=[pst, c_], wr=[ST[dr][g]])
                s.I("act", "activation", out=STb[dr][g][:], in_=ST[dr][g][:].rearrange("p h d -> p (h d)"), func=AF.Copy, rd=[ST[dr][g]], wr=[STb[dr][g]])

        n = len(order)
        for k in range(n + 2):
            if k < n:
                stageA(k)
            if 1 <= k <= n:
                stageB(k - 1)
            if k >= 2:
                stageC(k - 2)
        dsk = self.bcast_row("ssdsk", self.ssm_d, l * 8, 8)
        ng = self.bcast_row("ssng", self.ssm_norm, l * 512, 512)
        zt = [s.sb("sszt%d" % i, [128, 512], F32) for i in range(2)]
        u = [s.sb("ssu%d" % i, [128, 512], F32) for i in range(2)]
        junks = [s.sb("ssjunk%d" % i, [128, 512], BF16) for i in range(2)]
        sts = [s.sb("ssfst%d" % i, [128, 1, 3], F32) for i in range(4)]
        for t in range(ntl):
            z_, u_ = zt[t % 2], u[t % 2]
            st, junk = sts[t % 4], junks[t % 2]
            s.dma("sp", z_[:], self.Zd.t.ap()[t * 128:(t + 1) * 128, :], z_, self.Zd)
            s.I("dve", "tensor_tensor", out=u_[:].rearrange("p (h d) -> p h d", d=64), in0=XS[:, t, :].rearrange("p (h d) -> p h d", d=64),
                in1=dsk[:].rearrange("p (h o) -> p h o", o=1).to_broadcast([128, 8, 64]), op=ALU.mult, rd=[XS, dsk], wr=[u_])
            s.I("pool", "tensor_tensor", out=u_[:], in0=u_[:], in1=Y[:, t, :], op=ALU.add, rd=[u_, Y], wr=[u_])
            s.I("dve", "tensor_tensor", out=u_[:], in0=u_[:], in1=z_[:], op=ALU.mult, rd=[u_, z_], wr=[u_])
            s.I("act", "activation", out=junk[:], in_=u_[:], func=AF.Square, accum_out=st[:, 0, 0:1], rd=[u_], wr=[junk, st])
            s.I("act", "activation", out=st[:, 0, 1:2], in_=st[:, 0, 0:1], func=AF.Sqrt, scale=1.0 / 512, bias=EPS, rd=[st], wr=[st])
            s.I("dve", "reciprocal", st[:, 0, 2:3], st[:, 0, 1:2], rd=[st], wr=[st])
            s.I("dve", "scalar_tensor_tensor", out=u_[:], in0=u_[:], scalar=st[:, 0, 2:3], in1=ng[:], op0=ALU.mult, op1=ALU.mult, rd=[u_, st, ng], wr=[u_])
            s.dma("sp", self.Yd.t.ap()[b, t * 128:(t + 1) * 128, 512:1024], u_[:], self.Yd, u_)


def _consts():
    ident = np.eye(128, dtype=np.float32)
    i = np.arange(128)
    SL = (i[:, None] > i[None, :]).astype(np.float32)
    SU = (i[:, None] < i[None, :]).astype(np.float32)
    LE = (i[:, None] <= i[None, :]).astype(np.float32)
    GE = (i[:, None] >= i[None, :]).astype(np.float32)
    ON = np.ones((128, 128), np.float32)
    tri = np.stack([SL, SU, LE, GE, ON]).astype(np.float32)
    qc = np.arange(64)
    ws = np.clip(qc - 8, 0, 48)
    kc = np.arange(64)
    ok = (kc[:, None] >= ws[None, :]) & (kc[:, None] < ws[None, :] + 16)
    m = np.where(ok, 0.0, NEG).astype(np.float32)
    na_mask = np.concatenate([m, m], axis=0)
    per_axis = 16
    inv_freq = (10000.0 ** (-np.arange(0, per_axis, 2, dtype=np.float32) / per_axis)).astype(np.float32)
    t = np.arange(L)
    pos = np.stack([t // 64, t % 64], axis=-1).astype(np.float32)
    ang = pos[:, :, None] * inv_freq
    ang = np.concatenate([ang, ang], axis=-1).reshape(L, 32)
    cos = np.cos(ang).astype(np.float32)
    sin = np.sin(ang).astype(np.float32).reshape(L, 2, 2, 8).copy()
    sin[:, :, 0, :] *= -1.0
    negm = np.stack([np.where(i[None, :] < i[:, None], NEG, 0.0), np.where(i[None, :] > i[:, None], NEG, 0.0)], axis=1).astype(np.float32)
    sel = np.zeros((8, 8, 128), np.float32)
    for h in range(8):
        sel[h, h, :] = 1.0
    return ident, tri, na_mask, cos, sin.reshape(L, 32), negm, sel


def make_in_maps(inp):
    f = lambda a: np.ascontiguousarray(np.asarray(a, dtype=np.float32))
    ident, tri, na_mask, cos, sin, negm, sel = _consts()
    colv = lambda v, n: f(np.asarray(v).reshape(DEPTH, n, 128).transpose(0, 2, 1))
    idx = np.clip(np.arange(64)[:, None] - np.arange(64)[None, :], -15, 15) + 15
    shared = {
        "w_ada": f(inp["w_ada"]), "b_ada": f(inp["b_ada"]),
        "g_mixc": colv(inp["g_mix"], 8), "g_ffnc": colv(inp["g_ffn"], 8),
        "w_in": f(inp["w_in"]), "w_out": f(inp["w_out"]), "w_up": f(inp["ffn_w_up"]), "w_down": f(inp["ffn_w_down"]),
        "na_qg": f(inp["na_q_gain"]), "na_kg": f(inp["na_k_gain"]),
        "rpb_t": f(np.asarray(inp["na_rpb"])[..., idx]), "na_mask": na_mask,
        "df_qg": f(inp["df_q_gain"]), "df_kg": f(inp["df_k_gain"]), "df_lam": f(inp["df_lambda"]), "df_subln": f(inp["df_subln"]),
        "rope_cos": cos, "rope_sin": sin,
        "conv_wc": f(np.asarray(inp["ssm_conv_w"]).reshape(DEPTH, 5, 8, 128).transpose(0, 3, 2, 1)),
        "conv_bc": colv(inp["ssm_conv_b"], 8),
        "dt_bias": f(np.asarray(inp["ssm_dt_bias"]).reshape(DEPTH, 16)), "a_log": f(np.asarray(inp["ssm_a_log"]).reshape(DEPTH, 16)),
        "ssm_d": f(inp["ssm_d"]), "ssm_norm": f(inp["ssm_norm"]),
        "fconv_wc": f(np.asarray(inp["ffn_conv_w"]).reshape(DEPTH, 3, NF, 128).transpose(0, 3, 2, 1)),
        "fconv_bc": colv(inp["ffn_conv_b"], NF),
        "ident": ident, "tri": tri, "negm": negm, "sel": sel,
    }
    maps = []
    x, c, ctx, c_ctx = (np.asarray(inp[k]) for k in ("x", "c", "ctx", "c_ctx"))
    for i in range(NCORES):
        sl = slice(i * NB, (i + 1) * NB)
        c3 = np.concatenate([c[sl], c_ctx[None, :]], axis=0)
        cT = f(c3.reshape(3, 8, 128).transpose(2, 1, 0))
        m = dict(shared)
        m.update({"x": f(x[sl]), "ctx": f(ctx[sl]), "cT": cT})
        maps.append(m)
    return maps


_NC_CACHE = {}


def kernel(**inputs):
    if "nc" not in _NC_CACHE:
        _NC_CACHE["nc"] = Builder().build()
    nc = _NC_CACHE["nc"]
    maps = make_in_maps(inputs)
    res = run_bass_kernel_spmd(nc, maps, core_ids=list(range(NCORES)))
    return np.concatenate([np.asarray(r["out"]) for r in res.results], axis=0).astype(np.float32)
```

```python
import math
from contextlib import ExitStack

import numpy as np
import concourse.bass as bass
import concourse.mybir as mybir
from concourse.bass_utils import run_bass_kernel_spmd

F32 = mybir.dt.float32
BF16 = mybir.dt.bfloat16
AF = mybir.ActivationFunctionType
ALU = mybir.AluOpType
AX = mybir.AxisListType

NCORES = 8
NB = 2
L = 2048
LC = 256
T = L + LC
D = 1024
DEPTH = 2
DFF = 2816
NF = DFF // 128
EPS = 1e-6
NEG = -30000.0


class Buf:
    __slots__ = ("name", "w", "r", "dsem", "t", "persist")

    def __init__(self, name, t=None, persist=False):
        self.name = name
        self.w = {}
        self.r = {}
        self.dsem = None
        self.t = t
        self.persist = persist

    def __getitem__(self, idx):
        return self.t[idx]


class Sched:
    ENG = ("pe", "act", "dve", "pool", "sp")

    def __init__(self, nc, es, ndsem=48):
        self.nc = nc
        self.ges = es
        self.es = es
        self.eng = {"pe": nc.tensor, "act": nc.scalar, "dve": nc.vector, "pool": nc.gpsimd, "sp": nc.sync}
        self.sem = {k: es.enter_context(nc.semaphore("c_" + k)) for k in self.ENG}
        self.cnt = {k: 0 for k in self.ENG}
        self.seen = {k: {} for k in self.ENG}
        self.prog = {k: [] for k in self.ENG}
        self.dsems = [es.enter_context(nc.semaphore("d%d" % i)) for i in range(ndsem)]
        self.dval = [0] * ndsem
        self.dfree = list(range(ndsem))
        self.phase_bufs = []
        self.uid = 0
        self.ps_rr = 0
        self.PS = []

    def sb(self, name, shape, dt, persist=False):
        self.uid += 1
        es = self.ges if persist else self.es
        t = es.enter_context(self.nc.sbuf_tensor("%s_%d" % (name, self.uid), list(shape), dt))
        b = Buf(name, t, persist)
        if not persist:
            self.phase_bufs.append(b)
        return b

    def dram(self, name, shape, dt, kind="Internal"):
        t = self.nc.dram_tensor(name, list(shape), dt, kind=kind)
        return Buf(name, t, True)

    def psum_init(self):
        for i in range(8):
            t = self.ges.enter_context(self.nc.psum_tensor("psb%d" % i, [128, 512], F32))
            self.PS.append(Buf("ps%d" % i, t, True))

    def ps(self):
        b = self.PS[self.ps_rr % 6]
        self.ps_rr += 1
        return b

    def psacc(self, i):
        return self.PS[6 + (i % 2)]

    def _deps(self, e, reads, writes):
        d = {}
        own = "c_" + e

        def add(tokdict, is_read_set):
            for key, (sem, val) in tokdict.items():
                if key == own and (e == "pe" or is_read_set):
                    continue
                if d.get(key, (None, 0))[1] < val:
                    d[key] = (sem, val)

        for b in reads:
            add(b.w, False)
        for b in writes:
            add(b.w, False)
            add(b.r, True)
        return d

    def _wait(self, e, d):
        seen = self.seen[e]
        for key, (sem, val) in d.items():
            if seen.get(key, 0) >= val:
                continue
            self.prog[e].append(("w", sem, val))
            seen[key] = val

    def I(self, e, name, *args, rd=(), wr=(), **kw):
        d = self._deps(e, rd, wr)
        self._wait(e, d)
        self.cnt[e] += 1
        self.prog[e].append(("i", name, args, kw, self.sem[e], 1))
        key = "c_" + e
        tok = (self.sem[e], self.cnt[e])
        for b in rd:
            b.r[key] = tok
        for b in wr:
            b.w[key] = tok

    def dma(self, e, out_ap, in_ap, dst, src, **kw):
        if dst.dsem is None:
            dst.dsem = self.dfree.pop()
        i = dst.dsem
        d = self._deps(e, [src], [dst])
        self._wait(e, d)
        self.dval[i] += 16
        self.prog[e].append(("i", "dma_start", (), dict(out=out_ap, in_=in_ap, **kw), self.dsems[i], 16))
        key = "d%d" % i
        tok = (self.dsems[i], self.dval[i])
        src.r[key] = tok
        dst.w[key] = tok

    def drain(self):
        d = {}
        for i, v in enumerate(self.dval):
            if v:
                d["d%d" % i] = (self.dsems[i], v)
        self._wait("sp", d)

    def emit(self):
        self.drain()
        prog = self.prog
        with self.nc.Block() as block:
            def mk(e):
                def body(g):
                    for it in prog[e]:
                        if it[0] == "w":
                            g.wait_ge(it[1], it[2])
                        else:
                            getattr(g, it[1])(*it[2], **it[3]).then_inc(it[4], it[5])
                return body
            block.tensor(mk("pe"))
            block.scalar(mk("act"))
            block.vector(mk("dve"))
            block.gpsimd(mk("pool"))
            block.sync(mk("sp"))
        self.prog = {k: [] for k in self.ENG}
        for b in self.phase_bufs:
            if b.dsem is not None:
                self.dfree.append(b.dsem)
                b.dsem = None
        self.phase_bufs = []

    class _Phase:
        def __init__(self, s):
            self.s = s

        def __enter__(self):
            self.es = ExitStack()
            self.es.__enter__()
            self.s.es = self.es
            return self

        def __exit__(self, *a):
            if a[0] is None:
                self.s.emit()
            self.s.es = self.s.ges
            return self.es.__exit__(*a)

    def phase(self):
        return Sched._Phase(self)


def dap(buf, off, dims):
    return bass.AP(buf.t, off, [list(d) for d in dims])


class Builder:
    def __init__(self, dbg=None):
        self.dbg = dbg or {}
        self.nc = bass.Bass("TRN2", target_bir_lowering=False)
        self.outs = []

    def declare(self, s):
        I = lambda n, sh, dt=F32: s.dram(n, sh, dt, kind="ExternalInput")
        self.x = I("x", [NB, L, D])
        self.ctx = I("ctx", [NB, LC, D])
        self.cT = I("cT", [128, 8, 3])
        self.w_ada = I("w_ada", [DEPTH, D, 6 * D])
        self.b_ada = I("b_ada", [DEPTH, 6 * D])
        self.g_mixc = I("g_mixc", [DEPTH, 128, 8])
        self.g_ffnc = I("g_ffnc", [DEPTH, 128, 8])
        self.w_in = I("w_in", [DEPTH, D, 3088])
        self.w_out = I("w_out", [DEPTH, D, D])
        self.w_up = I("w_up", [DEPTH, D, 2 * DFF])
        self.w_down = I("w_down", [DEPTH, DFF, D])
        self.na_qg = I("na_qg", [DEPTH, 64])
        self.na_kg = I("na_kg", [DEPTH, 64])
        self.rpb_t = I("rpb_t", [DEPTH, 4, 15, 64, 64])
        self.na_mask = I("na_mask", [128, 64])
        self.df_qg = I("df_qg", [DEPTH, 32])
        self.df_kg = I("df_kg", [DEPTH, 32])
        self.df_lam = I("df_lam", [DEPTH, 4, 32])
        self.df_subln = I("df_subln", [DEPTH, 64])
        self.rope_cos = I("rope_cos", [L, 32])
        self.rope_sin = I("rope_sin", [L, 32])
        self.conv_wc = I("conv_wc", [DEPTH, 128, 8, 5])
        self.conv_bc = I("conv_bc", [DEPTH, 128, 8])
        self.dt_bias = I("dt_bias", [DEPTH, 16])
        self.a_log = I("a_log", [DEPTH, 16])
        self.ssm_d = I("ssm_d", [DEPTH, 8])
        self.ssm_norm = I("ssm_norm", [DEPTH, 512])
        self.fconv_wc = I("fconv_wc", [DEPTH, 128, NF, 3])
        self.fconv_bc = I("fconv_bc", [DEPTH, 128, NF])
        self.ident_d = I("ident", [128, 128])
        self.tri_d = I("tri", [5, 128, 128])
        self.negm_d = I("negm", [128, 2, 128])
        self.sel_d = I("sel", [128, 2, 8, 128])
        self.out = s.dram("out", [NB, L, D], F32, kind="ExternalOutput")
        dk = "ExternalOutput" if self.dbg.get("dump") else "Internal"
        self.modrow_d = s.dram("modrow_d", [DEPTH, 3, 6 * D], F32, kind=dk)
        self.XA = [s.dram("XA%d" % l, [NB, T, D], F32, kind=dk) for l in range(DEPTH)]
        self.XB = s.dram("XB0", [NB, T, D], F32, kind=dk)
        self.Yd = s.dram("Yd", [NB, T, D], F32, kind=dk)
        self.Zd = s.dram("Zd", [T, 512], F32, kind=dk)
        self.ATd = s.dram("ATd", [NF, 128, T], BF16, kind=dk)
        self.hTd = s.dram("hTd", [128, 8, T], BF16, kind=dk) if self.dbg.get("dump") else None
        self.Yin = I("Yin", [DEPTH, NB, T, D]) if self.dbg.get("feed_y") else None

    def build(self):
        nc = self.nc
        with ExitStack() as es:
            s = Sched(nc, es)
            self.s = s
            self.declare(s)
            s.psum_init()
            self.ident = s.sb("ident", [128, 128], F32, persist=True)
            self.identb = s.sb("identb", [128, 128], BF16, persist=True)
            self.hT = s.sb("hT", [128, 8, T], BF16, persist=True)
            self.AB = [[s.sb("AB%d%d" % (l, i), [128, 8, 3], F32, persist=True) for i in range(4)] for l in range(DEPTH)]
            self.BT = s.sb("BT", [128, 4, 14, 64], BF16, persist=True)
            ph = self.dbg.get("phases")
            with s.phase():
                s.dma("sp", self.ident[:], self.ident_d.t.ap(), self.ident, self.ident_d)
                s.I("act", "activation", out=self.identb[:], in_=self.ident[:], func=AF.Copy, rd=[self.ident], wr=[self.identb])
                self.phase_adaln()
            for l in self.dbg.get("layers", range(DEPTH)):
                last = l == DEPTH - 1
                nt = 16 if last else 18
                if ph is None or "na" in ph:
                    with s.phase():
                        self.phase_na_bias(l)
                for b in range(NB):
                    if l == 0:
                        src = lambda t, b=b: (self.x.t.ap()[b, t * 128:(t + 1) * 128, :], self.x) if t < 16 else \
                            (self.ctx.t.ap()[b, (t - 16) * 128:(t - 15) * 128, :], self.ctx)
                    else:
                        src = lambda t, b=b: (self.XB.t.ap()[b, t * 128:(t + 1) * 128, :], self.XB)
                    with s.phase():
                        self.phase_norm(src, self.AB[l][0], self.AB[l][1], b, 18)
                    if self.dbg.get("dump") and l == self.dbg.get("dump_l", 0) and b == 0 and self.dbg.get("dump_h") == "mix":
                        with s.phase():
                            s.dma("sp", self.hTd.t.ap(), self.hT[:], self.hTd, self.hT)
                    if ph is None or "na" in ph:
                        with s.phase():
                            self.phase_na(l, b)
                    if ph is None or "df" in ph:
                        with s.phase():
                            self.phase_df(l, b)
                    if ph is None or "ssm" in ph:
                        self.phase_ssm(l, b)
                    if ph is None or "out" in ph:
                        with s.phase():
                            self.phase_out(l, b, src, nt)
                    if ph is None or "ffn" in ph:
                        srcA = lambda t, b=b, l=l: (self.XA[l].t.ap()[b, t * 128:(t + 1) * 128, :], self.XA[l])
                        with s.phase():
                            self.phase_norm(srcA, self.AB[l][2], self.AB[l][3], b, nt)
                        with s.phase():
                            self.phase_ffn_up(l, b, nt)
                        with s.phase():
                            self.phase_ffn_down(l, b, srcA, nt)
        return nc

    def phase_adaln(self):
        s = self.s
        cT = s.sb("cT", [128, 8, 3], F32)
        siluT = s.sb("siluT", [128, 8, 3], F32)
        s.dma("sp", cT[:], self.cT.t.ap(), cT, self.cT)
        s.I("act", "activation", out=siluT[:], in_=cT[:], func=AF.Silu, rd=[cT], wr=[siluT])
        wa = [s.sb("wa%d" % i, [128, 8, 512], F32) for i in range(3)]
        for l in range(DEPTH):
            brow = s.sb("brow%d" % l, [3, 6 * D], F32)
            modrow = s.sb("modrow%d" % l, [3, 6 * D], F32)
            s.dma("sp", brow[:], dap(self.b_ada, l * 6 * D, [[0, 3], [1, 6 * D]]), brow, self.b_ada)
            for j in range(12):
                w = wa[j % 3]
                s.dma("sp", w[:], self.w_ada.t.ap()[l, :, j * 512:(j + 1) * 512].rearrange("(k p) n -> p k n", p=128), w, self.w_ada)
                pm = s.ps()
                for k in range(8):
                    s.I("pe", "matmul", pm[0:3, :], siluT[:, k, :], w[:, k, :], start=(k == 0), stop=(k == 7), rd=[siluT, w], wr=[pm])
                s.I("dve", "tensor_tensor", out=modrow[:, j * 512:(j + 1) * 512], in0=pm[0:3, :], in1=brow[:, j * 512:(j + 1) * 512], op=ALU.add,
                    rd=[pm, brow], wr=[modrow])
            s.dma("sp", self.modrow_d.t.ap()[l], modrow[:], self.modrow_d, modrow)
            pT = s.ps()
            for c in range(48):
                s.I("pe", "transpose", pT[:, c * 3:(c + 1) * 3], modrow[0:3, c * 128:(c + 1) * 128], self.ident[0:3, 0:3], rd=[modrow, self.ident], wr=[pT])
            modcol = s.sb("modcol%d" % l, [128, 48, 3], F32)
            s.I("dve", "tensor_copy", modcol[:].rearrange("p a b -> p (a b)"), pT[:, 0:144], rd=[pT], wr=[modcol])
            gm = s.sb("gm%d" % l, [128, 8, 1], F32)
            gf = s.sb("gf%d" % l, [128, 8, 1], F32)
            s.dma("sp", gm[:, :, 0], self.g_mixc.t.ap()[l], gm, self.g_mixc)
            s.dma("sp", gf[:, :, 0], self.g_ffnc.t.ap()[l], gf, self.g_ffnc)
            A1, B1, A2, B2 = self.AB[l]
            for (A, Bv, g, sc0, sh0) in ((A1, B1, gm, 8, 0), (A2, B2, gf, 32, 24)):
                s.I("dve", "scalar_tensor_tensor", out=A[:], in0=modcol[:, sc0:sc0 + 8, :], scalar=1.0, in1=g[:].to_broadcast([128, 8, 3]),
                    op0=ALU.add, op1=ALU.mult, rd=[modcol, g], wr=[A])
                s.I("dve", "tensor_copy", Bv[:], modcol[:, sh0:sh0 + 8, :], rd=[modcol], wr=[Bv])

    def phase_norm(self, src, A, Bv, b, ntiles):
        s = self.s
        xt = [s.sb("nxt%d" % i, [128, D], F32) for i in range(4)]
        xn = [s.sb("nxn%d" % i, [128, D], F32) for i in range(8)]
        junks = [s.sb("njunk%d" % i, [128, D], BF16) for i in range(2)]
        sts = [s.sb("nst%d" % i, [128, 1, 3], F32) for i in range(8)]
        ngroups = (ntiles + 3) // 4
        for g in range(ngroups):
            tiles = list(range(4 * g, min(4 * g + 4, ntiles)))
            j = b if tiles[0] < 16 else 2
            for ti, t in enumerate(tiles):
                ap, sbuf = src(t)
                x_ = xt[t % 4]
                n_ = xn[t % 8]
                s.dma("sp", x_[:], ap, x_, sbuf)
                st, junk = sts[t % 8], junks[t % 2]
                s.I("act", "activation", out=junk[:], in_=x_[:], func=AF.Square, accum_out=st[:, 0, 0:1], rd=[x_], wr=[junk, st])
                s.I("act", "activation", out=st[:, 0, 1:2], in_=st[:, 0, 0:1], func=AF.Sqrt, scale=1.0 / D, bias=EPS, rd=[st], wr=[st])
                s.I("dve", "reciprocal", st[:, 0, 2:3], st[:, 0, 1:2], rd=[st], wr=[st])
                s.I("dve", "tensor_scalar", out=n_[:], in0=x_[:], scalar1=st[:, 0, 2:3], scalar2=None, op0=ALU.mult, rd=[x_, st], wr=[n_])
            n = len(tiles) * 128
            for c in range(8):
                p = s.ps()
                for ti, t in enumerate(tiles):
                    n_ = xn[t % 8]
                    s.I("pe", "transpose", p[:, ti * 128:(ti + 1) * 128], n_[:, c * 128:(c + 1) * 128], self.ident[:], rd=[n_, self.ident], wr=[p])
                s.I("act", "activation", out=self.hT[:, c, tiles[0] * 128:tiles[0] * 128 + n], in_=p[:, 0:n], func=AF.Identity,
                    scale=A[:, c, j:j + 1], bias=Bv[:, c, j:j + 1], rd=[p, A, Bv], wr=[self.hT])

    def load_w(self, name, src_buf, src_ap, shape):
        s = self.s
        w = s.sb(name, shape, BF16)
        s.dma("pool", w[:], src_ap, w, src_buf)
        return w

    def bcast_row(self, name, src_buf, off, n, dt=F32, parts=128):
        s = self.s
        t = s.sb(name, [parts, n], dt)
        s.dma("sp", t[:], dap(src_buf, off, [[0, parts], [1, n]]), t, src_buf)
        return t

    def phase_out(self, l, b, src, ntiles):
        s = self.s
        w = self.load_w("wout", self.w_out, self.w_out.t.ap()[l].rearrange("(k p) n -> p k n", p=128), [128, 8, D])
        G = {}
        G[b] = self.bcast_row("g1b", self.modrow_d, (l * 3 + b) * 6 * D + 2 * D, D)
        if ntiles > 16:
            G[2] = self.bcast_row("g1c", self.modrow_d, (l * 3 + 2) * 6 * D + 2 * D, D)
        yt = [s.sb("oyt%d" % i, [128, D], F32) for i in range(8)]
        xr = [s.sb("oxr%d" % i, [128, D], F32) for i in range(4)]
        yT = [s.sb("oyT%d" % i, [128, 8, 512], BF16) for i in range(2)]
        tmp = [s.sb("otmp%d" % i, [128, D], F32) for i in range(2)]
        xo = [s.sb("oxo%d" % i, [128, D], F32) for i in range(2)]
        ngroups = (ntiles + 3) // 4
        for g in range(ngroups):
            tiles = list(range(4 * g, min(4 * g + 4, ntiles)))
            j = b if tiles[0] < 16 else 2
            for t in tiles:
                y_ = yt[t % 8]
                if self.Yin is not None:
                    s.dma("sp", y_[:], self.Yin.t.ap()[l, b, t * 128:(t + 1) * 128, :], y_, self.Yin)
                else:
                    s.dma("sp", y_[:], self.Yd.t.ap()[b, t * 128:(t + 1) * 128, :], y_, self.Yd)
                ap, sbuf = src(t)
                s.dma("sp", xr[t % 4][:], ap, xr[t % 4], sbuf)
            yT_ = yT[g % 2]
            n = len(tiles) * 128
            for c in range(8):
                p = s.ps()
                for ti, t in enumerate(tiles):
                    s.I("pe", "transpose", p[:, ti * 128:(ti + 1) * 128], yt[t % 8][:, c * 128:(c + 1) * 128], self.ident[:], rd=[yt[t % 8], self.ident], wr=[p])
                s.I("act", "activation", out=yT_[:, c, 0:n], in_=p[:, 0:n], func=AF.Copy, rd=[p], wr=[yT_])
            for ti, t in enumerate(tiles):
                tm = tmp[t % 2]
                xo_ = xo[t % 2]
                for hf in range(2):
                    p = s.ps()
                    for k in range(8):
                        s.I("pe", "matmul", p[:, :], yT_[:, k, ti * 128:(ti + 1) * 128], w[:, k, hf * 512:(hf + 1) * 512], start=(k == 0), stop=(k == 7),
                            rd=[yT_, w], wr=[p])
                    s.I("dve", "tensor_tensor", out=tm[:, hf * 512:(hf + 1) * 512], in0=p[:, :], in1=G[j][:, hf * 512:(hf + 1) * 512], op=ALU.mult,
                        rd=[p, G[j]], wr=[tm])
                s.I("pool", "tensor_tensor", out=xo_[:], in0=tm[:], in1=xr[t % 4][:], op=ALU.add, rd=[tm, xr[t % 4]], wr=[xo_])
                s.dma("sp", self.XA[l].t.ap()[b, t * 128:(t + 1) * 128, :], xo_[:], self.XA[l], xo_)

    def phase_ffn_up(self, l, b, ntiles):
        s = self.s
        groups = [(0, 512), (512, 512), (1024, 512), (1536, 512)]
        if ntiles > 16:
            groups.append((2048, 256))
        ntok = ntiles * 128
        wu = [s.sb("wu%d" % i, [128, 8, 256], BF16) for i in range(3)]
        Gl = [s.sb("fGl%d" % i, [128, L + 2], F32) for i in range(2)]
        Gc = [s.sb("fGc%d" % i, [128, LC + 2], F32) for i in range(2)]
        V = [s.sb("fV%d" % i, [128, T], F32) for i in range(2)]
        acc = [s.sb("facc%d" % i, [128, T], F32) for i in range(2)]
        at = [s.sb("fat%d" % i, [128, T], BF16) for i in range(2)]
        cw = s.sb("fcw", [128, NF, 3], F32)
        cb = s.sb("fcb", [128, NF], F32)
        s.dma("sp", cw[:], self.fconv_wc.t.ap()[l], cw, self.fconv_wc)
        s.dma("sp", cb[:], self.fconv_bc.t.ap()[l], cb, self.fconv_bc)
        for i in range(2):
            s.I("pool", "memset", Gl[i][:], 0.0, wr=[Gl[i]])
            s.I("pool", "memset", Gc[i][:], 0.0, wr=[Gc[i]])
        wup = self.w_up.t.ap()[l]

        def loadw(f):
            w = wu[f % 3]
            s.dma("pool", w[:, :, 0:128], wup[:, f * 128:(f + 1) * 128].rearrange("(k p) n -> p k n", p=128), w, self.w_up)
            s.dma("pool", w[:, :, 128:256], wup[:, DFF + f * 128:DFF + (f + 1) * 128].rearrange("(k p) n -> p k n", p=128), w, self.w_up)

        loadw(0)
        for f in range(NF):
            if f + 1 < NF:
                loadw(f + 1)
            w = wu[f % 3]
            gl, gc, v, a, o = Gl[f % 2], Gc[f % 2], V[f % 2], acc[f % 2], at[f % 2]
            for (t0, n) in groups:
                pa = s.ps()
                pb = s.ps()
                for k in range(8):
                    s.I("pe", "matmul", pa[:, 0:n], w[:, k, 0:128], self.hT[:, k, t0:t0 + n], start=(k == 0), stop=(k == 7), rd=[w, self.hT], wr=[pa])
                for k in range(8):
                    s.I("pe", "matmul", pb[:, 0:n], w[:, k, 128:256], self.hT[:, k, t0:t0 + n], start=(k == 0), stop=(k == 7), rd=[w, self.hT], wr=[pb])
                if t0 < L:
                    s.I("act", "activation", out=gl[:, 1 + t0:1 + t0 + n], in_=pa[:, 0:n], func=AF.Copy, rd=[pa], wr=[gl])
                else:
                    s.I("act", "activation", out=gc[:, 1:1 + n], in_=pa[:, 0:n], func=AF.Copy, rd=[pa], wr=[gc])
                s.I("act", "activation", out=v[:, t0:t0 + n], in_=pb[:, 0:n], func=AF.Copy, rd=[pb], wr=[v])
            segs = [(gl, 0, L)] + ([(gc, L, LC)] if ntiles > 16 else [])
            for (gb, o0, n) in segs:
                s.I("dve", "tensor_scalar", out=a[:, o0:o0 + n], in0=gb[:, 0:n], scalar1=cw[:, f, 0:1], scalar2=None, op0=ALU.mult, rd=[gb, cw], wr=[a])
                for k in (1, 2):
                    s.I("dve", "scalar_tensor_tensor", out=a[:, o0:o0 + n], in0=gb[:, k:k + n], scalar=cw[:, f, k:k + 1], in1=a[:, o0:o0 + n],
                        op0=ALU.mult, op1=ALU.add, rd=[gb, cw, a], wr=[a])
            s.I("act", "activation", out=a[:, 0:ntok], in_=a[:, 0:ntok], func=AF.Silu, bias=cb[:, f:f + 1], scale=1.0, rd=[a, cb], wr=[a])
            s.I("pool", "tensor_tensor", out=o[:, 0:ntok], in0=a[:, 0:ntok], in1=v[:, 0:ntok], op=ALU.mult, rd=[a, v], wr=[o])
            s.dma("sp", self.ATd.t.ap()[f, :, 0:ntok], o[:, 0:ntok], self.ATd, o)

    def phase_ffn_down(self, l, b, src, ntiles):
        s = self.s
        last = l == DEPTH - 1
        w = self.load_w("wdn", self.w_down, self.w_down.t.ap()[l].rearrange("(f p) n -> p f n", p=128), [128, NF, D])
        G = {}
        G[b] = self.bcast_row("g2b", self.modrow_d, (l * 3 + b) * 6 * D + 5 * D, D)
        if ntiles > 16:
            G[2] = self.bcast_row("g2c", self.modrow_d, (l * 3 + 2) * 6 * D + 5 * D, D)
        aT = [s.sb("daT%d" % i, [128, NF, 512], BF16) for i in range(2)]
        xr = [s.sb("dxr%d" % i, [128, D], F32) for i in range(4)]
        tmp = [s.sb("dtmp%d" % i, [128, D], F32) for i in range(2)]
        xo = [s.sb("dxo%d" % i, [128, D], F32) for i in range(2)]
        ngroups = (ntiles + 3) // 4
        for g in range(ngroups):
            tiles = list(range(4 * g, min(4 * g + 4, ntiles)))
            j = b if tiles[0] < 16 else 2
            n = len(tiles) * 128
            a_ = aT[g % 2]
            s.dma("sp", a_[:, :, 0:n], self.ATd.t.ap()[:, :, tiles[0] * 128:tiles[0] * 128 + n].rearrange("f p t -> p f t"), a_, self.ATd)
            for t in tiles:
                ap, sbuf = src(t)
                s.dma("sp", xr[t % 4][:], ap, xr[t % 4], sbuf)
            for ti, t in enumerate(tiles):
                tm = tmp[t % 2]
                xo_ = xo[t % 2]
                for hf in range(2):
                    p = s.ps()
                    for f in range(NF):
                        s.I("pe", "matmul", p[:, :], a_[:, f, ti * 128:(ti + 1) * 128], w[:, f, hf * 512:(hf + 1) * 512], start=(f == 0), stop=(f == NF - 1),
                            rd=[a_, w], wr=[p])
                    s.I("dve", "tensor_tensor", out=tm[:, hf * 512:(hf + 1) * 512], in0=p[:, :], in1=G[j][:, hf * 512:(hf + 1) * 512], op=ALU.mult,
                        rd=[p, G[j]], wr=[tm])
                s.I("pool", "tensor_tensor", out=xo_[:], in0=tm[:], in1=xr[t % 4][:], op=ALU.add, rd=[tm, xr[t % 4]], wr=[xo_])
                if last:
                    s.dma("sp", self.out.t.ap()[b, t * 128:(t + 1) * 128, :], xo_[:], self.out, xo_)
                else:
                    s.dma("sp", self.XB.t.ap()[b, t * 128:(t + 1) * 128, :], xo_[:], self.XB, xo_)

    def group_norm(self, p, ngrp, gd, gains, out, sq, st, view):
        s = self.s
        n = ngrp * gd
        s.I("act", "activation", out=sq[:, 0:n], in_=p[:, 0:n], func=AF.Square, rd=[p], wr=[sq])
        s.I("dve", "tensor_reduce", out=st[:, 0:ngrp, 0], in_=sq[:, 0:n].rearrange("p (g d) -> p g d", d=gd), axis=AX.X, op=ALU.add, rd=[sq], wr=[st])
        s.I("act", "activation", out=st[:, 0:ngrp, 1], in_=st[:, 0:ngrp, 0], func=AF.Sqrt, scale=1.0 / gd, bias=EPS, rd=[st], wr=[st])
        s.I("dve", "reciprocal", st[:, 0:ngrp, 2], st[:, 0:ngrp, 1], rd=[st], wr=[st])
        s.I("dve", "tensor_tensor", out=sq[:, 0:n].rearrange("p (g d) -> p g d", d=gd), in0=p[:, 0:n].rearrange("p (g d) -> p g d", d=gd),
            in1=st[:, 0:ngrp, 2:3].to_broadcast([128, ngrp, gd]), op=ALU.mult, rd=[p, st], wr=[sq])
        s.I("pool", "tensor_tensor", out=out, in0=view(sq[:, 0:n]), in1=gains, op=ALU.mult, rd=[sq] + self._gn_rd, wr=self._gn_wr)

    def phase_na_bias(self, l):
        s = self.s
        bt32 = s.sb("bt32", [128, 4, 14, 64], F32)
        mk = s.sb("namask", [128, 64], F32)
        s.dma("sp", mk[:], self.na_mask.t.ap(), mk, self.na_mask)
        for h in range(4):
            for half in range(2):
                s.dma("sp", bt32[64 * half:64 * half + 64, h, :, :], self.rpb_t.t.ap()[l, h, half:half + 14].rearrange("d k q -> k d q"), bt32, self.rpb_t)
        s.I("dve", "tensor_tensor", out=bt32[:].rearrange("p h d q -> p (h d) q"), in0=bt32[:].rearrange("p h d q -> p (h d) q"),
            in1=mk[:].rearrange("p (o q) -> p o q", o=1).to_broadcast([128, 56, 64]), op=ALU.add, rd=[bt32, mk], wr=[bt32])
        s.I("dve", "tensor_scalar", out=self.BT[:].rearrange("p h d q -> p (h d q)"), in0=bt32[:].rearrange("p h d q -> p (h d q)"), scalar1=8.0, scalar2=None,
            op0=ALU.mult, rd=[bt32], wr=[self.BT])

    def phase_na(self, l, b):
        s = self.s
        last = l == DEPTH - 1
        w = self.load_w("wna", self.w_in, self.w_in.t.ap()[l, :, 0:768].rearrange("(k p) n -> p k n", p=128), [128, 8, 768])
        gains = s.sb("nagain", [128, 2, 1, 64], F32)
        s.dma("sp", gains[:, 0, 0, :], dap(self.na_qg, l * 64, [[0, 128], [1, 64]]), gains, self.na_qg)
        s.dma("sp", gains[:, 1, 0, :], dap(self.na_kg, l * 64, [[0, 128], [1, 64]]), gains, self.na_kg)
        QKT = s.sb("naQKT", [128, 6, T], BF16)
        kz = [s.sb("nakz%d" % i, [128, 2, 256], F32) for i in range(2)]
        for i in range(2):
            s.I("pool", "memset", kz[i][:], 0.0, wr=[kz[i]])
        VE = s.sb("naVE", [128, 18, 4, 65], BF16)
        VO = s.sb("naVO", [128, 15, 4, 65], BF16)
        s.I("pool", "memset", VE[:], 1.0, wr=[VE])
        s.I("pool", "memset", VO[:], 1.0, wr=[VO])
        qn = [s.sb("naqn%d" % i, [128, 2, 4, 64], F32) for i in range(2)]
        gsq = [s.sb("nagsq%d" % i, [128, 512], F32) for i in range(2)]
        gst = [s.sb("nagst%d" % i, [128, 16, 3], F32) for i in range(2)]
        for t in range(18):
            pq = s.ps()
            pv = s.ps()
            for k in range(8):
                s.I("pe", "matmul", pq[:, :], self.hT[:, k, t * 128:(t + 1) * 128], w[:, k, 0:512], start=(k == 0), stop=(k == 7), rd=[self.hT, w], wr=[pq])
            for k in range(8):
                s.I("pe", "matmul", pv[:, 0:256], self.hT[:, k, t * 128:(t + 1) * 128], w[:, k, 512:768], start=(k == 0), stop=(k == 7), rd=[self.hT, w], wr=[pv])
            q_ = qn[t % 2]
            self._gn_rd = [gains]
            self._gn_wr = [q_]
            self.group_norm(pq, 8, 64, gains[:].to_broadcast([128, 2, 4, 64]), q_[:], gsq[t % 2], gst[t % 2],
                            lambda ap: ap.rearrange("p (a h d) -> p a h d", a=2, h=4))
            kz_ = kz[t % 2]
            for hs in range(2):
                s.I("pool", "tensor_copy", kz_[:, hs, :].rearrange("p (pr h2 d) -> p pr h2 d", pr=2, h2=2)[:, :, hs, :],
                    q_[:, 1, :, :].rearrange("p (pr h2) d -> p pr h2 d", pr=2)[:, :, hs, :], rd=[q_], wr=[kz_])
            pt = s.ps()
            pt2 = s.ps()
            qf = q_[:].rearrange("p a h d -> p (a h d)")
            for cc in range(2):
                s.I("pe", "transpose", pt[:, cc * 128:(cc + 1) * 128], qf[:, cc * 128:(cc + 1) * 128], self.ident[:], rd=[q_, self.ident], wr=[pt])
            for hs in range(2):
                for pr in range(2):
                    s.I("pe", "transpose", pt2[:, (hs * 2 + pr) * 128:(hs * 2 + pr + 1) * 128], kz_[:, hs, pr * 128:(pr + 1) * 128], self.ident[:], rd=[kz_, self.ident], wr=[pt2])
            s.I("act", "activation", out=QKT[:, 0:2, t * 128:(t + 1) * 128], in_=pt[:, 0:256].rearrange("p (c t) -> p c t", c=2), func=AF.Copy, rd=[pt], wr=[QKT])
            s.I("act", "activation", out=QKT[:, 2:6, t * 128:(t + 1) * 128], in_=pt2[:, :].rearrange("p (c t) -> p c t", c=4), func=AF.Copy, rd=[pt2], wr=[QKT])
            s.I("dve", "tensor_copy", VE[:, t, :, 0:64], pv[:, 0:256].rearrange("p (h d) -> p h d", h=4), rd=[pv], wr=[VE])
        for i in range(15):
            pv = s.ps()
            for k in range(8):
                s.I("pe", "matmul", pv[:, 0:256], self.hT[:, k, 64 + i * 128:64 + (i + 1) * 128], w[:, k, 512:768], start=(k == 0), stop=(k == 7), rd=[self.hT, w], wr=[pv])
            s.I("dve", "tensor_copy", VO[:, i, :, 0:64], pv[:, 0:256].rearrange("p (h d) -> p h d", h=4), rd=[pv], wr=[VO])
        PT = [s.sb("naPT%d" % i, [128, 6, 64], BF16) for i in range(4)]
        rec = [s.sb("narec%d" % i, [128, 4, 1], F32) for i in range(2)]
        yo = [s.sb("nayo%d" % i, [128, 4, 64], F32) for i in range(2)]
        it = 0
        for rp in range(16):
            po = s.psacc(rp)
            for rr in range(2):
                r = 2 * rp + rr
                R0 = min(max(r - 4, 0), 24)
                for h in range(4):
                    pair, base = h // 2, 64 * (h % 2)
                    pS = s.ps()
                    q_ap = QKT[:, pair, r * 64:(r + 1) * 64]
                    kk = 2 + 2 * (h % 2) + pair
                    for ci in range(4):
                        kr = R0 + 2 * ci
                        d = kr - r + 7
                        s.I("pe", "matmul", pS[:, ci * 64:(ci + 1) * 64], QKT[:, kk, kr * 64:kr * 64 + 128], q_ap, start=True, stop=False,
                            rd=[QKT], wr=[pS])
                        s.I("pe", "matmul", pS[:, ci * 64:(ci + 1) * 64], self.identb[:], self.BT[:, h, d, :], start=False, stop=True,
                            rd=[self.identb, self.BT], wr=[pS])
                    for cc in range(2):
                        s.I("pe", "matmul", pS[:, (4 + cc) * 64:(5 + cc) * 64], QKT[:, kk, L + cc * 128:L + (cc + 1) * 128], q_ap,
                            start=True, stop=True, rd=[QKT], wr=[pS])
                    P_ = PT[it % 4]
                    it += 1
                    s.I("act", "activation", out=P_[:].rearrange("p c q -> p (c q)"), in_=pS[:, 0:384], func=AF.Exp, scale=0.125, rd=[pS], wr=[P_])
                    for c in range(6):
                        if c < 4:
                            kr = R0 + 2 * c
                            vb, v_ap = (VE, VE[:, kr // 2, h, :]) if kr % 2 == 0 else (VO, VO[:, (kr - 1) // 2, h, :])
                        else:
                            vb, v_ap = VE, VE[:, 16 + (c - 4), h, :]
                        s.I("pe", "matmul", po[64 * rr:64 * rr + 64, h * 65:(h + 1) * 65], P_[:, c, :], v_ap, start=(c == 0), stop=(c == 5), rd=[P_, vb], wr=[po])
            rc, y_ = rec[rp % 2], yo[rp % 2]
            pov = po[:, 0:260].rearrange("p (h e) -> p h e", e=65)
            s.I("dve", "reciprocal", rc[:], pov[:, :, 64:65], rd=[po], wr=[rc])
            s.I("dve", "tensor_tensor", out=y_[:], in0=pov[:, :, 0:64], in1=rc[:].to_broadcast([128, 4, 64]), op=ALU.mult, rd=[po, rc], wr=[y_])
            s.dma("sp", self.Yd.t.ap()[b, rp * 128:(rp + 1) * 128, 0:256], y_[:].rearrange("p h d -> p (h d)"), self.Yd, y_)
        if not last:
            PTc = [s.sb("naPTc%d" % i, [128, 2, 256], BF16) for i in range(2)]
            pos = [s.psacc(0), s.psacc(1)]
            for h in range(4):
                pair, base = h // 2, 64 * (h % 2)
                pS = s.ps()
                for cc in range(2):
                    s.I("pe", "matmul", pS[:, cc * 256:(cc + 1) * 256], QKT[:, 2 + 2 * (h % 2) + pair, L + cc * 128:L + (cc + 1) * 128],
                        QKT[:, pair, L:L + 256], start=True, stop=True, rd=[QKT], wr=[pS])
                P_ = PTc[h % 2]
                s.I("act", "activation", out=P_[:].rearrange("p c q -> p (c q)"), in_=pS[:, :], func=AF.Exp, scale=0.125, rd=[pS], wr=[P_])
                for qt in range(2):
                    for cc in range(2):
                        s.I("pe", "matmul", pos[qt][:, h * 65:(h + 1) * 65], P_[:, cc, qt * 128:(qt + 1) * 128], VE[:, 16 + cc, h, :], start=(cc == 0), stop=(cc == 1),
                            rd=[P_, VE], wr=[pos[qt]])
            for qt in range(2):
                rc, y_ = rec[qt], yo[qt]
                pov = pos[qt][:, 0:260].rearrange("p (h e) -> p h e", e=65)
                s.I("dve", "reciprocal", rc[:], pov[:, :, 64:65], rd=[pos[qt]], wr=[rc])
                s.I("dve", "tensor_tensor", out=y_[:], in0=pov[:, :, 0:64], in1=rc[:].to_broadcast([128, 4, 64]), op=ALU.mult, rd=[pos[qt], rc], wr=[y_])
                s.dma("sp", self.Yd.t.ap()[b, L + qt * 128:L + (qt + 1) * 128, 0:256], y_[:].rearrange("p h d -> p (h d)"), self.Yd, y_)

    def phase_df(self, l, b):
        s = self.s
        last = l == DEPTH - 1
        lam_init = 0.8 - 0.6 * math.exp(-0.3 * l)
        w = self.load_w("wdf", self.w_in, self.w_in.t.ap()[l, :, 768:1536].rearrange("(k p) n -> p k n", p=128), [128, 8, 768])
        gains = s.sb("dfgain", [128, 2, 1, 32], F32)
        s.dma("sp", gains[:, 0, 0, :], dap(self.df_qg, l * 32, [[0, 128], [1, 32]]), gains, self.df_qg)
        s.dma("sp", gains[:, 1, 0, :], dap(self.df_kg, l * 32, [[0, 128], [1, 32]]), gains, self.df_kg)
        COS = s.sb("dfcos", [128, 16, 32], F32)
        SIN = s.sb("dfsin", [128, 16, 32], F32)
        s.dma("sp", COS[:], self.rope_cos.t.ap().rearrange("(t p) d -> p t d", p=128), COS, self.rope_cos)
        s.dma("sp", SIN[:], self.rope_sin.t.ap().rearrange("(t p) d -> p t d", p=128), SIN, self.rope_sin)
        lv = s.sb("dflv", [128, 4, 32], F32)
        s.dma("sp", lv[:].rearrange("p a d -> p (a d)"), dap(self.df_lam, l * 128, [[0, 128], [1, 128]]), lv, self.df_lam)
        lp = s.sb("dflp", [128, 2, 32], F32)
        ls = s.sb("dfls", [128, 8], F32)
        s.I("dve", "tensor_tensor", out=lp[:, 0, :], in0=lv[:, 0, :], in1=lv[:, 1, :], op=ALU.mult, rd=[lv], wr=[lp])
        s.I("dve", "tensor_tensor", out=lp[:, 1, :], in0=lv[:, 2, :], in1=lv[:, 3, :], op=ALU.mult, rd=[lv], wr=[lp])
        s.I("dve", "tensor_reduce", out=ls[:, 0:2], in_=lp[:], axis=AX.X, op=ALU.add, rd=[lp], wr=[ls])
        s.I("act", "activation", out=ls[:, 2:4], in_=ls[:, 0:2], func=AF.Exp, rd=[ls], wr=[ls])
        s.I("dve", "scalar_tensor_tensor", out=ls[:, 4:5], in0=ls[:, 3:4], scalar=-lam_init, in1=ls[:, 2:3], op0=ALU.add, op1=ALU.subtract, rd=[ls], wr=[ls])
        neglam = ls[:, 4:5]
        sub = s.sb("dfsub", [128, 1, 64], F32)
        s.dma("sp", sub[:, 0, :], dap(self.df_subln, l * 64, [[0, 128], [1, 64]]), sub, self.df_subln)
        s.I("dve", "tensor_scalar", out=sub[:], in0=sub[:], scalar1=1.0 - lam_init, scalar2=None, op0=ALU.mult, rd=[sub], wr=[sub])

        QZ = s.sb("dfQZ", [128, 2, 2, T], BF16)
        KZ = s.sb("dfKZ", [128, 2, 2, T], BF16)
        VD = s.sb("dfVD", [128, 18, 4, 65], BF16)
        s.I("pool", "memset", VD[:], 1.0, wr=[VD])
        qn = [s.sb("dfqn%d" % i, [128, 16, 32], F32) for i in range(2)]
        gsq = [s.sb("dfgsq%d" % i, [128, 512], F32) for i in range(1)] * 2
        gst = [s.sb("dfgst%d" % i, [128, 16, 3], F32) for i in range(2)]
        t1 = [s.sb("dft1%d" % i, [128, 16, 32], F32) for i in range(1)] * 2
        t2 = [s.sb("dft2%d" % i, [128, 16, 32], F32) for i in range(1)] * 2
        qz = [s.sb("dfqz%d" % i, [128, 4, 256], F32) for i in range(2)]
        for i in range(2):
            s.I("pool", "memset", qz[i][:], 0.0, wr=[qz[i]])
        for t in range(18):
            pq = s.ps()
            pv = s.ps()
            for k in range(8):
                s.I("pe", "matmul", pq[:, :], self.hT[:, k, t * 128:(t + 1) * 128], w[:, k, 0:512], start=(k == 0), stop=(k == 7), rd=[self.hT, w], wr=[pq])
            for k in range(8):
                s.I("pe", "matmul", pv[:, 0:256], self.hT[:, k, t * 128:(t + 1) * 128], w[:, k, 512:768], start=(k == 0), stop=(k == 7), rd=[self.hT, w], wr=[pv])
            q_ = qn[t % 2]
            self._gn_rd = [gains]
            self._gn_wr = [q_]
            self.group_norm(pq, 16, 32, gains[:].to_broadcast([128, 2, 8, 32]), q_[:].rearrange("p (a h) d -> p a h d", a=2), gsq[t % 2], gst[t % 2],
                            lambda ap: ap.rearrange("p (a h d) -> p a h d", a=2, h=8))
            z_ = qz[t % 2]
            if t < 16:
                a_, b_ = t1[t % 2], t2[t % 2]
                s.I("pool", "tensor_tensor", out=a_[:], in0=q_[:], in1=COS[:, t:t + 1, :].to_broadcast([128, 16, 32]), op=ALU.mult, rd=[q_, COS], wr=[a_])
                q5 = q_[:].rearrange("p g (a f e) -> p g a f e", a=2, f=2)
                b5 = b_[:].rearrange("p g (a f e) -> p g a f e", a=2, f=2)
                s5 = SIN[:, t:t + 1, :].to_broadcast([128, 16, 32]).rearrange("p g (a f e) -> p g a f e", a=2, f=2)
                s.I("dve", "tensor_tensor", out=b5[:, :, :, 0, :], in0=q5[:, :, :, 1, :], in1=s5[:, :, :, 0, :], op=ALU.mult, rd=[q_, SIN], wr=[b_])
                s.I("dve", "tensor_tensor", out=b5[:, :, :, 1, :], in0=q5[:, :, :, 0, :], in1=s5[:, :, :, 1, :], op=ALU.mult, rd=[q_, SIN], wr=[b_])
                srcs = (a_, b_)
            else:
                srcs = (q_,)

            def comb(out_ap, sel):
                if len(srcs) == 2:
                    s.I("pool", "tensor_tensor", out=out_ap, in0=sel(srcs[0]), in1=sel(srcs[1]), op=ALU.add, rd=list(srcs), wr=[z_])
                else:
                    s.I("pool", "tensor_copy", out_ap, sel(srcs[0]), rd=list(srcs), wr=[z_])
            for m in range(2):
                comb(z_[:, m, :].rearrange("p (h m d) -> p h m d", h=4, m=2)[:, :, m, :],
                     lambda bf: bf[:, 0:8, :].rearrange("p (h m) d -> p h m d", m=2)[:, :, m, :])
            for hs in range(2):
                comb(z_[:, 2 + hs, :].rearrange("p (pr h2 e) -> p pr h2 e", pr=2, h2=2)[:, :, hs, :],
                     lambda bf: bf[:, 8:16, :].rearrange("p (pr h2 m) d -> p pr h2 (m d)", pr=2, h2=2)[:, :, hs, :])
            pt = s.ps()
            pt2 = s.ps()
            for m in range(2):
                for pr in range(2):
                    s.I("pe", "transpose", pt[:, (m * 2 + pr) * 128:(m * 2 + pr + 1) * 128], z_[:, m, pr * 128:(pr + 1) * 128], self.ident[:], rd=[z_, self.ident], wr=[pt])
            for hs in range(2):
                for pr in range(2):
                    s.I("pe", "transpose", pt2[:, (hs * 2 + pr) * 128:(hs * 2 + pr + 1) * 128], z_[:, 2 + hs, pr * 128:(pr + 1) * 128], self.ident[:], rd=[z_, self.ident], wr=[pt2])
            s.I("act", "activation", out=QZ[:, :, :, t * 128:(t + 1) * 128], in_=pt[:, :].rearrange("p (m c t) -> p m c t", m=2, c=2), func=AF.Copy, rd=[pt], wr=[QZ])
            s.I("act", "activation", out=KZ[:, :, :, t * 128:(t + 1) * 128], in_=pt2[:, :].rearrange("p (a c t) -> p a c t", a=2, c=2), func=AF.Copy, rd=[pt2], wr=[KZ])
            s.I("dve", "tensor_copy", VD[:, t, :, 0:64], pv[:, 0:256].rearrange("p (h d) -> p h d", h=4), rd=[pv], wr=[VD])
        PT = [[s.sb("dfPT%d%d" % (i, m), [128, 18, 512], BF16) for m in range(2)] for i in range(2)]
        rec = s.sb("dfrec", [128, 2, 4, 1], F32)
        o0 = s.sb("dfo0", [128, 4, 64], F32)
        o1 = s.sb("dfo1", [128, 4, 64], F32)
        yd = [s.sb("dfyd%d" % i, [128, 4, 64], F32) for i in range(2)]
        sq = s.sb("dfsq2", [128, 4, 64], F32)
        st = s.sb("dfst2", [128, 4, 3], F32)
        blocks = [(qb * 512, 512, list(range(18))) for qb in range(4)]
        if not last:
            blocks.append((L, 256, [16, 17]))
        it = 0
        for h in range(4):
            pair, base = h // 2, 64 * (h % 2)
            for (q0, nq, kcs) in blocks:
                P_ = PT[it % 2]
                it += 1
                for m in range(2):
                    for kc in kcs:
                        pS = s.ps()
                        s.I("pe", "matmul", pS[:, 0:nq], KZ[:, h % 2, pair, kc * 128:(kc + 1) * 128], QZ[:, m, pair, q0:q0 + nq], start=True, stop=True,
                            rd=[KZ, QZ], wr=[pS])
                        s.I("act", "activation", out=P_[m][:, kc, 0:nq], in_=pS[:, 0:nq], func=AF.Exp, scale=32.0 ** -0.5, rd=[pS], wr=[P_[m]])
                po = [s.psacc(0), s.psacc(1)]
                nqs = nq // 128
                for m in range(2):
                    for qs in range(nqs):
                        for i, kc in enumerate(kcs):
                            s.I("pe", "matmul", po[m][:, qs * 65:(qs + 1) * 65], P_[m][:, kc, qs * 128:(qs + 1) * 128], VD[:, kc, h, :], start=(i == 0), stop=(i == len(kcs) - 1),
                                rd=[P_[m], VD], wr=[po[m]])
                pv0 = po[0][:, 0:nqs * 65].rearrange("p (q e) -> p q e", e=65)
                pv1 = po[1][:, 0:nqs * 65].rearrange("p (q e) -> p q e", e=65)
                y_ = yd[it % 2]
                s.I("dve", "reciprocal", rec[:, 0, 0:nqs, :], pv0[:, :, 64:65], rd=[po[0]], wr=[rec])
                s.I("dve", "reciprocal", rec[:, 1, 0:nqs, :], pv1[:, :, 64:65], rd=[po[1]], wr=[rec])
                s.I("dve", "tensor_tensor", out=o0[:, 0:nqs, :], in0=pv0[:, :, 0:64], in1=rec[:, 0, 0:nqs, :].to_broadcast([128, nqs, 64]), op=ALU.mult, rd=[po[0], rec], wr=[o0])
                s.I("dve", "tensor_tensor", out=o1[:, 0:nqs, :], in0=pv1[:, :, 0:64], in1=rec[:, 1, 0:nqs, :].to_broadcast([128, nqs, 64]), op=ALU.mult, rd=[po[1], rec], wr=[o1])
                s.I("dve", "scalar_tensor_tensor", out=o0[:, 0:nqs, :], in0=o1[:, 0:nqs, :], scalar=neglam, in1=o0[:, 0:nqs, :], op0=ALU.mult, op1=ALU.add, rd=[o1, o0, ls], wr=[o0])
                s.I("pool", "tensor_tensor", out=sq[:, 0:nqs, :], in0=o0[:, 0:nqs, :], in1=o0[:, 0:nqs, :], op=ALU.mult, rd=[o0], wr=[sq])
                s.I("dve", "tensor_reduce", out=st[:, 0:nqs, 0], in_=sq[:, 0:nqs, :], axis=AX.X, op=ALU.add, rd=[sq], wr=[st])
                s.I("act", "activation", out=st[:, 0:nqs, 1], in_=st[:, 0:nqs, 0], func=AF.Sqrt, scale=1.0 / 64, bias=EPS, rd=[st], wr=[st])
                s.I("dve", "reciprocal", st[:, 0:nqs, 2], st[:, 0:nqs, 1], rd=[st], wr=[st])
                s.I("dve", "tensor_tensor", out=sq[:, 0:nqs, :], in0=o0[:, 0:nqs, :], in1=st[:, 0:nqs, 2:3].to_broadcast([128, nqs, 64]), op=ALU.mult, rd=[o0, st], wr=[sq])
                s.I("pool", "tensor_tensor", out=y_[:, 0:nqs, :], in0=sq[:, 0:nqs, :], in1=sub[:].to_broadcast([128, nqs, 64]), op=ALU.mult, rd=[sq, sub], wr=[y_])
                s.dma("sp", self.Yd.t.ap()[b, q0:q0 + nq, 256 + h * 64:256 + (h + 1) * 64].rearrange("(q p) d -> p q d", p=128), y_[:, 0:nqs, :], self.Yd, y_)

    def phase_ssm(self, l, b):
        s = self.s
        last = l == DEPTH - 1
        ntl = 16 if last else 18
        with ExitStack() as mid:
            def msb(name, shape, dt):
                s.uid += 1
                t = mid.enter_context(self.nc.sbuf_tensor("%s_%d" % (name, s.uid), list(shape), dt))
                return Buf(name, t, True)
            XS = msb("ssXS", [128, 18, 512], F32)
            BMT = msb("ssBMT", [128, 2, T], BF16)
            CMT = msb("ssCMT", [128, 2, T], BF16)
            BM = msb("ssBM", [128, 18, 2, 128], BF16)
            DT = msb("ssDT", [128, 18, 16], F32)
            LA = msb("ssLA", [128, 18, 16], F32)
            with s.phase():
                self.ssm_prep(l, b, XS, BMT, CMT, BM, DT, LA)
            with s.phase():
                self.ssm_scan(l, b, XS, BMT, CMT, BM, DT, LA, ntl)
            for bb in (XS, BMT, CMT, BM, DT, LA):
                if bb.dsem is not None:
                    s.dfree.append(bb.dsem)
                    bb.dsem = None

    def ssm_prep(self, l, b, XS, BMT, CMT, BM, DT, LA):
        s = self.s
        w = self.load_w("wss", self.w_in, self.w_in.t.ap()[l, :, 1536:3088].rearrange("(k p) n -> p k n", p=128), [128, 8, 1552])
        dtb = self.bcast_row("ssdtb", self.dt_bias, l * 16, 16)
        alog = self.bcast_row("ssalog", self.a_log, l * 16, 16)
        A = s.sb("ssA", [128, 16], F32)
        s.I("act", "activation", out=A[:], in_=alog[:], func=AF.Exp, rd=[alog], wr=[A])
        s.I("dve", "tensor_scalar", out=A[:], in0=A[:], scalar1=-1.0, scalar2=None, op0=ALU.mult, rd=[A], wr=[A])
        cw = s.sb("sscw", [128, 8, 5], F32)
        cb = s.sb("sscb", [128, 8], F32)
        s.dma("sp", cw[:], self.conv_wc.t.ap()[l], cw, self.conv_wc)
        s.dma("sp", cb[:], self.conv_bc.t.ap()[l], cb, self.conv_bc)
        zs = [s.sb("sszs%d" % i, [128, 512], F32) for i in range(2)]
        tmp = s.sb("sstmp", [128, 18, 16], F32)
        for t in range(18):
            pz = s.ps()
            pd = s.ps()
            for k in range(8):
                s.I("pe", "matmul", pz[:, :], self.hT[:, k, t * 128:(t + 1) * 128], w[:, k, 0:512], start=(k == 0), stop=(k == 7), rd=[self.hT, w], wr=[pz])
            for k in range(8):
                s.I("pe", "matmul", pd[:, 0:16], self.hT[:, k, t * 128:(t + 1) * 128], w[:, k, 1536:1552], start=(k == 0), stop=(k == 7), rd=[self.hT, w], wr=[pd])
            z_ = zs[t % 2]
            s.I("act", "activation", out=z_[:], in_=pz[:, :], func=AF.Silu, rd=[pz], wr=[z_])
            s.dma("sp", self.Zd.t.ap()[t * 128:(t + 1) * 128, :], z_[:], self.Zd, z_)
            s.I("dve", "tensor_tensor", out=tmp[:, t, :], in0=pd[:, 0:16], in1=dtb[:], op=ALU.add, rd=[pd, dtb], wr=[tmp])
        s.I("act", "activation", out=tmp[:], in_=tmp[:], func=AF.Exp, rd=[tmp], wr=[tmp])
        s.I("act", "activation", out=DT[:], in_=tmp[:], func=AF.Ln, bias=1.0, scale=1.0, rd=[tmp], wr=[DT])
        s.I("dve", "tensor_tensor", out=LA[:], in0=DT[:], in1=A[:].rearrange("p (o d) -> p o d", o=1).to_broadcast([128, 18, 16]), op=ALU.mult, rd=[DT, A], wr=[LA])
        Gl = [s.sb("ssGl%d" % i, [128, L + 4], F32) for i in range(1)]
        Gc = [s.sb("ssGc%d" % i, [128, LC + 4], F32) for i in range(1)]
        acc = [s.sb("ssacc%d" % i, [128, T], F32) for i in range(1)]
        for i in range(1):
            s.I("pool", "memset", Gl[i][:], 0.0, wr=[Gl[i]])
            s.I("pool", "memset", Gc[i][:], 0.0, wr=[Gc[i]])
        groups = [(0, 512), (512, 512), (1024, 512), (1536, 512), (2048, 256)]
        for c in range(8):
            gl, gc, a = Gl[0], Gc[0], acc[0]
            for (t0, n) in groups:
                p = s.ps()
                for k in range(8):
                    s.I("pe", "matmul", p[:, 0:n], w[:, k, 512 + c * 128:512 + (c + 1) * 128], self.hT[:, k, t0:t0 + n], start=(k == 0), stop=(k == 7), rd=[w, self.hT], wr=[p])
                if t0 < L:
                    s.I("act", "activation", out=gl[:, 2 + t0:2 + t0 + n], in_=p[:, 0:n], func=AF.Copy, rd=[p], wr=[gl])
                else:
                    s.I("act", "activation", out=gc[:, 2:2 + n], in_=p[:, 0:n], func=AF.Copy, rd=[p], wr=[gc])
            for (gb, o0, n) in ((gl, 0, L), (gc, L, LC)):
                s.I("dve", "tensor_scalar", out=a[:, o0:o0 + n], in0=gb[:, 0:n], scalar1=cw[:, c, 0:1], scalar2=None, op0=ALU.mult, rd=[gb, cw], wr=[a])
                for k in range(1, 5):
                    s.I("dve", "scalar_tensor_tensor", out=a[:, o0:o0 + n], in0=gb[:, k:k + n], scalar=cw[:, c, k:k + 1], in1=a[:, o0:o0 + n],
                        op0=ALU.mult, op1=ALU.add, rd=[gb, cw, a], wr=[a])
            if c < 4 or c in (4, 5):
                s.I("act", "activation", out=a[:], in_=a[:], func=AF.Silu, bias=cb[:, c:c + 1], scale=1.0, rd=[a, cb], wr=[a])
            if c < 4:
                for g4 in range(5):
                    tiles = list(range(4 * g4, min(4 * g4 + 4, 18)))
                    p = s.ps()
                    for ti, t in enumerate(tiles):
                        s.I("pe", "transpose", p[:, ti * 128:(ti + 1) * 128], a[:, t * 128:(t + 1) * 128], self.ident[:], rd=[a, self.ident], wr=[p])
                    s.I("act", "activation", out=XS[:, tiles[0]:tiles[0] + len(tiles), c * 128:(c + 1) * 128], in_=p[:, 0:len(tiles) * 128].rearrange("p (t d) -> p t d", d=128),
                        func=AF.Copy, rd=[p], wr=[XS])
            elif c in (4, 5):
                g = c - 4
                s.I("pool", "tensor_copy", BMT[:, g, :], a[:], rd=[a], wr=[BMT])
                for g4 in range(5):
                    tiles = list(range(4 * g4, min(4 * g4 + 4, 18)))
                    p = s.ps()
                    for ti, t in enumerate(tiles):
                        s.I("pe", "transpose", p[:, ti * 128:(ti + 1) * 128], a[:, t * 128:(t + 1) * 128], self.ident[:], rd=[a, self.ident], wr=[p])
                    s.I("act", "activation", out=BM[:, tiles[0]:tiles[0] + len(tiles), g, :], in_=p[:, 0:len(tiles) * 128].rearrange("p (t d) -> p t d", d=128),
                        func=AF.Copy, rd=[p], wr=[BM])
            else:
                g = c - 6
                s.I("act", "activation", out=CMT[:, g, :], in_=a[:], func=AF.Silu, bias=cb[:, c:c + 1], scale=1.0, rd=[a, cb], wr=[CMT])

    def ssm_scan(self, l, b, XS, BMT, CMT, BM, DT, LA, ntl):
        s = self.s
        TRI = s.sb("ssTRI", [128, 5, 128], F32)
        s.dma("sp", TRI[:], self.tri_d.t.ap().rearrange("a p n -> p a n"), TRI, self.tri_d)
        SL, SU, LE, GE, ON = (TRI[:, i, :] for i in range(5))
        Y = s.sb("ssY", [128, 18, 512], F32)
        s.I("pool", "memset", Y[:], 0.0, wr=[Y])
        ST = [[s.sb("ssST%d%d" % (d, g), [128, 4, 64], F32) for g in range(2)] for d in range(2)]
        STb = [[s.sb("ssSTb%d%d" % (d, g), [128, 256], BF16) for g in range(2)] for d in range(2)]
        for d in range(2):
            for g in range(2):
                s.I("pool", "memset", ST[d][g][:], 0.0, wr=[ST[d][g]])
                s.I("pool", "memset", STb[d][g][:], 0.0, wr=[STb[d][g]])
        NEGM = s.sb("ssNEGM", [128, 2, 128], BF16)
        s.dma("pool", NEGM[:], self.negm_d.t.ap(), NEGM, self.negm_d)
        E12 = s.sb("ssE12", [128, 2, 8, 128], BF16)
        s.dma("pool", E12[:], self.sel_d.t.ap(), E12, self.sel_d)
        EX = [s.sb("ssEX%d" % i, [128, 3, 8], F32) for i in range(3)]
        CST = [s.sb("ssCST%d" % i, [128, 128], BF16) for i in range(3)]
        R1 = [s.sb("ssR1%d" % i, [8, 128], F32) for i in range(3)]
        for i in range(3):
            s.I("pool", "memset", CST[i][:], 0.0, wr=[CST[i]])
        XDT = [s.sb("ssXDT%d" % i, [128, 8, 64], BF16) for i in range(3)]
        XDD = [s.sb("ssXDD%d" % i, [128, 8, 64], BF16) for i in range(3)]
        cbs = [s.sb("sscbs%d" % i, [128, 1, 128], F32) for i in range(6)]
        Lt = [s.sb("ssLt%d" % i, [128, 4, 128], F32) for i in range(2)] * 2
        Wm = [s.sb("ssW%d" % i, [128, 4, 128], BF16) for i in range(4)]
        tm1 = [s.sb("sstm1%d" % i, [128, 4, 64], F32) for i in range(4)]
        yt = [s.sb("ssyt%d" % i, [128, 4, 64], F32) for i in range(4)]
        tm2 = [s.sb("sstm2%d" % i, [128, 4, 64], F32) for i in range(4)]
        order = [(0, 16), (1, 17), (0, 17), (1, 16)]
        for i in range(16):
            order.append((0, i))
            order.append((1, 15 - i))
        ctxs = {}

        def stageA(k):
            dr, t = order[k]
            want_y = t < ntl
            ex, cst, xdt, xdd = EX[k % 3], CST[k % 3], XDT[k % 3], XDD[k % 3]
            la8 = LA[:, t, dr * 8:(dr + 1) * 8]
            pe_ = s.ps()
            m1, m2 = (SL, LE) if dr == 0 else (SU, GE)
            s.I("pe", "matmul", pe_[:, 0:8], m1, la8, start=True, stop=True, rd=[TRI, LA], wr=[pe_])
            s.I("pe", "matmul", pe_[:, 8:16], m2, la8, start=True, stop=True, rd=[TRI, LA], wr=[pe_])
            s.I("pe", "matmul", pe_[:, 16:24], ON, la8, start=True, stop=True, rd=[TRI, LA], wr=[pe_])
            s.I("act", "activation", out=ex[:].rearrange("p a h -> p (a h)"), in_=pe_[:, 0:24], func=AF.Exp, rd=[pe_], wr=[ex])
            s.I("dve", "tensor_tensor", out=xdt[:], in0=XS[:, t, :].rearrange("p (h d) -> p h d", d=64),
                in1=DT[:, t, dr * 8:(dr + 1) * 8].rearrange("p (h o) -> p h o", o=1).to_broadcast([128, 8, 64]), op=ALU.mult, rd=[XS, DT], wr=[xdt])
            s.I("dve", "tensor_tensor", out=xdd[:], in0=xdt[:], in1=ex[:, 0, :].rearrange("p (h o) -> p h o", o=1).to_broadcast([128, 8, 64]), op=ALU.mult,
                rd=[xdt, ex], wr=[xdd])
            cb2 = []
            if want_y:
                pcs = s.ps()
                s.I("pe", "matmul", pcs[0:8, 0:128], la8, m2, start=True, stop=True, rd=[LA, TRI], wr=[pcs])
                r1 = R1[k % 3]
                s.I("act", "activation", out=cst[0:8, :], in_=pcs[0:8, 0:128], func=AF.Copy, rd=[pcs], wr=[cst])
                s.I("dve", "tensor_tensor", out=r1[:], in0=pcs[0:8, 0:128], in1=cst[0:8, :], op=ALU.subtract, rd=[pcs, cst], wr=[r1])
                s.I("act", "activation", out=cst[32:40, :], in_=r1[:], func=AF.Copy, rd=[r1], wr=[cst])
                for g in range(2):
                    cb_ = cbs[(2 * k + g) % 6]
                    pc = s.ps()
                    s.I("pe", "matmul", pc[:, 0:128], BMT[:, g, t * 128:(t + 1) * 128], CMT[:, g, t * 128:(t + 1) * 128], start=True, stop=True, rd=[BMT, CMT], wr=[pc])
                    s.I("act", "activation", out=cb_[:, 0, :], in_=pc[:, 0:128], func=AF.Copy, rd=[pc], wr=[cb_])
                    cb2.append(cb_)
            ctxs[k] = (ex, cst, xdt, xdd, cb2)

        def stageB(k):
            dr, t = order[k]
            want_y = t < ntl
            ex, cst, xdt, xdd, cb2 = ctxs[k]
            ws = []
            if want_y:
                for g in range(2):
                    lt, wm = Lt[(2 * k + g) % 4], Wm[(2 * k + g) % 4]
                    pd_ = s.ps()
                    for e in range(4):
                        h8 = g * 4 + e
                        o_ = pd_[:, e * 128:(e + 1) * 128]
                        s.I("pe", "matmul", o_, E12[:, 0, h8, :], cst[:], start=True, stop=False, rd=[E12, cst], wr=[pd_])
                        s.I("pe", "matmul", o_, cst[:], E12[:, 1, h8, :], start=False, stop=False, rd=[E12, cst], wr=[pd_])
                        s.I("pe", "matmul", o_, self.identb[:], NEGM[:, dr, :], start=False, stop=True, rd=[self.identb, NEGM], wr=[pd_])
                    s.I("act", "activation", out=lt[:].rearrange("p e i -> p (e i)"), in_=pd_[:, :], func=AF.Exp, rd=[pd_], wr=[lt])
                    s.I("dve", "tensor_tensor", out=wm[:], in0=lt[:], in1=cb2[g][:].to_broadcast([128, 4, 128]), op=ALU.mult, rd=[lt, cb2[g]], wr=[wm])
                    ws.append(wm)
            ctxs[k] = (ex, cst, xdt, xdd, ws)

        def stageC(k):
            dr, t = order[k]
            want_y = t < ntl
            ex, cst, xdt, xdd, ws = ctxs.pop(k)
            for g in range(2):
                if want_y:
                    py = s.psacc(g)
                    for e in range(4):
                        s.I("pe", "matmul", py[:, e * 64:(e + 1) * 64], ws[g][:, e, :], xdt[:, g * 4 + e, :], start=True, stop=True, rd=[ws[g], xdt], wr=[py])
                    po = s.ps()
                    s.I("pe", "matmul", po[:, 0:256], CMT[:, g, t * 128:(t + 1) * 128], STb[dr][g][:], start=True, stop=True, rd=[CMT, STb[dr][g]], wr=[po])
                    a_, y_ = tm1[(2 * k + g) % 4], yt[(2 * k + g) % 4]
                    s.I("dve", "tensor_tensor", out=a_[:], in0=po[:, 0:256].rearrange("p (h d) -> p h d", d=64),
                        in1=ex[:, 1, g * 4:(g + 1) * 4].rearrange("p (h o) -> p h o", o=1).to_broadcast([128, 4, 64]), op=ALU.mult, rd=[po, ex], wr=[a_])
                    s.I("dve", "tensor_tensor", out=y_[:], in0=py[:, 0:256].rearrange("p (h d) -> p h d", d=64), in1=a_[:], op=ALU.add, rd=[py, a_], wr=[y_])
                    yv = Y[:, t, g * 256:(g + 1) * 256].rearrange("p (h d) -> p h d", d=64)
                    s.I("pool", "tensor_tensor", out=yv, in0=yv, in1=y_[:], op=ALU.add, rd=[Y, y_], wr=[Y])
                pst = s.ps()
                s.I("pe", "matmul", pst[:, 0:256], BM[:, t, g, :], xdd[:, g * 4:(g + 1) * 4, :], start=True, stop=True, rd=[BM, xdd], wr=[pst])
                c_ = tm2[(2 * k + g) % 4]
                s.I("pool", "tensor_tensor", out=c_[:], in0=ST[dr][g][:], in1=ex[:, 2, g * 4:(g + 1) * 4].rearrange("p (h o) -> p h o", o=1).to_broadcast([128, 4, 64]),
                    op=ALU.mult, rd=[ST[dr][g], ex], wr=[c_])
                s.I("dve", "tensor_tensor", out=ST[dr][g][:], in0=pst[:, 0:256].rearrange("p (h d) -> p h d", d=64), in1=c_[:], op=ALU.add, rd=[pst, c_], wr=[ST[dr][g]])
                s.I("act", "activation", out=STb[dr][g][:], in_=ST[dr][g][:].rearrange("p h d -> p (h d)"), func=AF.Copy, rd=[ST[dr][g]], wr=[STb[dr][g]])

        n = len(order)
        for k in range(n + 2):
            if k < n:
                stageA(k)
            if 1 <= k <= n:
                stageB(k - 1)
            if k >= 2:
                stageC(k - 2)
        dsk = self.bcast_row("ssdsk", self.ssm_d, l * 8, 8)
        ng = self.bcast_row("ssng", self.ssm_norm, l * 512, 512)
        zt = [s.sb("sszt%d" % i, [128, 512], F32) for i in range(3)]
        u = [s.sb("ssu%d" % i, [128, 512], F32) for i in range(3)]
        junks = [s.sb("ssjunk%d" % i, [128, 512], BF16) for i in range(2)]
        sts = [s.sb("ssfst%d" % i, [128, 1, 3], F32) for i in range(4)]
        for t in range(ntl):
            z_, u_ = zt[t % 3], u[t % 3]
            st, junk = sts[t % 4], junks[t % 2]
            s.dma("sp", z_[:], self.Zd.t.ap()[t * 128:(t + 1) * 128, :], z_, self.Zd)
            s.I("dve", "tensor_tensor", out=u_[:].rearrange("p (h d) -> p h d", d=64), in0=XS[:, t, :].rearrange("p (h d) -> p h d", d=64),
                in1=dsk[:].rearrange("p (h o) -> p h o", o=1).to_broadcast([128, 8, 64]), op=ALU.mult, rd=[XS, dsk], wr=[u_])
            s.I("pool", "tensor_tensor", out=u_[:], in0=u_[:], in1=Y[:, t, :], op=ALU.add, rd=[u_, Y], wr=[u_])
            s.I("dve", "tensor_tensor", out=u_[:], in0=u_[:], in1=z_[:], op=ALU.mult, rd=[u_, z_], wr=[u_])
            s.I("act", "activation", out=junk[:], in_=u_[:], func=AF.Square, accum_out=st[:, 0, 0:1], rd=[u_], wr=[junk, st])
            s.I("act", "activation", out=st[:, 0, 1:2], in_=st[:, 0, 0:1], func=AF.Sqrt, scale=1.0 / 512, bias=EPS, rd=[st], wr=[st])
            s.I("dve", "reciprocal", st[:, 0, 2:3], st[:, 0, 1:2], rd=[st], wr=[st])
            s.I("dve", "scalar_tensor_tensor", out=u_[:], in0=u_[:], scalar=st[:, 0, 2:3], in1=ng[:], op0=ALU.mult, op1=ALU.mult, rd=[u_, st, ng], wr=[u_])
            s.dma("sp", self.Yd.t.ap()[b, t * 128:(t + 1) * 128, 512:1024], u_[:], self.Yd, u_)


def _consts():
    ident = np.eye(128, dtype=np.float32)
    i = np.arange(128)
    SL = (i[:, None] > i[None, :]).astype(np.float32)
    SU = (i[:, None] < i[None, :]).astype(np.float32)
    LE = (i[:, None] <= i[None, :]).astype(np.float32)
    GE = (i[:, None] >= i[None, :]).astype(np.float32)
    ON = np.ones((128, 128), np.float32)
    tri = np.stack([SL, SU, LE, GE, ON]).astype(np.float32)
    qc = np.arange(64)
    ws = np.clip(qc - 8, 0, 48)
    kc = np.arange(64)
    ok = (kc[:, None] >= ws[None, :]) & (kc[:, None] < ws[None, :] + 16)
    m = np.where(ok, 0.0, NEG).astype(np.float32)
    na_mask = np.concatenate([m, m], axis=0)
    per_axis = 16
    inv_freq = (10000.0 ** (-np.arange(0, per_axis, 2, dtype=np.float32) / per_axis)).astype(np.float32)
    t = np.arange(L)
    pos = np.stack([t // 64, t % 64], axis=-1).astype(np.float32)
    ang = pos[:, :, None] * inv_freq
    ang = np.concatenate([ang, ang], axis=-1).reshape(L, 32)
    cos = np.cos(ang).astype(np.float32)
    sin = np.sin(ang).astype(np.float32).reshape(L, 2, 2, 8).copy()
    sin[:, :, 0, :] *= -1.0
    negm = np.stack([np.where(i[None, :] < i[:, None], NEG, 0.0), np.where(i[None, :] > i[:, None], NEG, 0.0)], axis=1).astype(np.float32)
    sel = np.zeros((128, 2, 8, 128), np.float32)
    for h in range(8):
        sel[h, 0, h, :] = 1.0
        sel[32 + h, 0, h, :] = 1.0
    sel[:, 1] = -sel[:, 0]
    return ident, tri, na_mask, cos, sin.reshape(L, 32), negm, sel


def make_in_maps(inp):
    f = lambda a: np.ascontiguousarray(np.asarray(a, dtype=np.float32))
    ident, tri, na_mask, cos, sin, negm, sel = _consts()
    colv = lambda v, n: f(np.asarray(v).reshape(DEPTH, n, 128).transpose(0, 2, 1))
    idx = np.clip(np.arange(64)[:, None] - np.arange(64)[None, :], -15, 15) + 15
    shared = {
        "w_ada": f(inp["w_ada"]), "b_ada": f(inp["b_ada"]),
        "g_mixc": colv(inp["g_mix"], 8), "g_ffnc": colv(inp["g_ffn"], 8),
        "w_in": f(inp["w_in"]), "w_out": f(inp["w_out"]), "w_up": f(inp["ffn_w_up"]), "w_down": f(inp["ffn_w_down"]),
        "na_qg": f(inp["na_q_gain"]), "na_kg": f(inp["na_k_gain"]),
        "rpb_t": f(np.asarray(inp["na_rpb"])[..., idx]), "na_mask": na_mask,
        "df_qg": f(inp["df_q_gain"]), "df_kg": f(inp["df_k_gain"]), "df_lam": f(inp["df_lambda"]), "df_subln": f(inp["df_subln"]),
        "rope_cos": cos, "rope_sin": sin,
        "conv_wc": f(np.asarray(inp["ssm_conv_w"]).reshape(DEPTH, 5, 8, 128).transpose(0, 3, 2, 1)),
        "conv_bc": colv(inp["ssm_conv_b"], 8),
        "dt_bias": f(np.asarray(inp["ssm_dt_bias"]).reshape(DEPTH, 16)), "a_log": f(np.asarray(inp["ssm_a_log"]).reshape(DEPTH, 16)),
        "ssm_d": f(inp["ssm_d"]), "ssm_norm": f(inp["ssm_norm"]),
        "fconv_wc": f(np.asarray(inp["ffn_conv_w"]).reshape(DEPTH, 3, NF, 128).transpose(0, 3, 2, 1)),
        "fconv_bc": colv(inp["ffn_conv_b"], NF),
        "ident": ident, "tri": tri, "negm": negm, "sel": sel,
    }
    maps = []
    x, c, ctx, c_ctx = (np.asarray(inp[k]) for k in ("x", "c", "ctx", "c_ctx"))
    for i in range(NCORES):
        sl = slice(i * NB, (i + 1) * NB)
        c3 = np.concatenate([c[sl], c_ctx[None, :]], axis=0)
        cT = f(c3.reshape(3, 8, 128).transpose(2, 1, 0))
        m = dict(shared)
        m.update({"x": f(x[sl]), "ctx": f(ctx[sl]), "cT": cT})
        maps.append(m)
    return maps


_NC_CACHE = {}


def kernel(**inputs):
    if "nc" not in _NC_CACHE:
        _NC_CACHE["nc"] = Builder().build()
    nc = _NC_CACHE["nc"]
    maps = make_in_maps(inputs)
    res = run_bass_kernel_spmd(nc, maps, core_ids=list(range(NCORES)))
    return np.concatenate([np.asarray(r["out"]) for r in res.results], axis=0).astype(np.float32)
```

```python
import math
from contextlib import ExitStack

import numpy as np
import concourse.bass as bass
import concourse.mybir as mybir
from concourse.bass_utils import run_bass_kernel_spmd

F32 = mybir.dt.float32
BF16 = mybir.dt.bfloat16
AF = mybir.ActivationFunctionType
ALU = mybir.AluOpType
AX = mybir.AxisListType

NCORES = 8
NB = 2
L = 2048
LC = 256
T = L + LC
D = 1024
DEPTH = 2
DFF = 2816
NF = DFF // 128
EPS = 1e-6
NEG = -30000.0


class Buf:
    __slots__ = ("name", "w", "r", "dsem", "t", "persist")

    def __init__(self, name, t=None, persist=False):
        self.name = name
        self.w = {}
        self.r = {}
        self.dsem = None
        self.t = t
        self.persist = persist

    def __getitem__(self, idx):
        return self.t[idx]


class Sched:
    ENG = ("pe", "act", "dve", "pool", "sp")

    def __init__(self, nc, es, ndsem=48):
        self.nc = nc
        self.ges = es
        self.es = es
        self.eng = {"pe": nc.tensor, "act": nc.scalar, "dve": nc.vector, "pool": nc.gpsimd, "sp": nc.sync}
        self.sem = {k: es.enter_context(nc.semaphore("c_" + k)) for k in self.ENG}
        self.cnt = {k: 0 for k in self.ENG}
        self.seen = {k: {} for k in self.ENG}
        self.prog = {k: [] for k in self.ENG}
        self.dsems = [es.enter_context(nc.semaphore("d%d" % i)) for i in range(ndsem)]
        self.dval = [0] * ndsem
        self.dfree = list(range(ndsem))
        self.phase_bufs = []
        self.uid = 0
        self.ps_rr = 0
        self.PS = []

    def sb(self, name, shape, dt, persist=False):
        self.uid += 1
        es = self.ges if persist else self.es
        t = es.enter_context(self.nc.sbuf_tensor("%s_%d" % (name, self.uid), list(shape), dt))
        b = Buf(name, t, persist)
        if not persist:
            self.phase_bufs.append(b)
        return b

    def dram(self, name, shape, dt, kind="Internal"):
        t = self.nc.dram_tensor(name, list(shape), dt, kind=kind)
        return Buf(name, t, True)

    def psum_init(self):
        for i in range(8):
            t = self.ges.enter_context(self.nc.psum_tensor("psb%d" % i, [128, 512], F32))
            self.PS.append(Buf("ps%d" % i, t, True))

    def ps(self):
        b = self.PS[self.ps_rr % 6]
        self.ps_rr += 1
        return b

    def psacc(self, i):
        return self.PS[6 + (i % 2)]

    def _deps(self, e, reads, writes):
        d = {}
        own = "c_" + e

        def add(tokdict, is_read_set):
            for key, (sem, val) in tokdict.items():
                if key == own and (e == "pe" or is_read_set):
                    continue
                if d.get(key, (None, 0))[1] < val:
                    d[key] = (sem, val)

        for b in reads:
            add(b.w, False)
        for b in writes:
            add(b.w, False)
            add(b.r, True)
        return d

    def _wait(self, e, d):
        seen = self.seen[e]
        for key, (sem, val) in d.items():
            if seen.get(key, 0) >= val:
                continue
            self.prog[e].append(("w", sem, val))
            seen[key] = val

    def I(self, e, name, *args, rd=(), wr=(), **kw):
        d = self._deps(e, rd, wr)
        self._wait(e, d)
        self.cnt[e] += 1
        self.prog[e].append(("i", name, args, kw, self.sem[e], 1))
        key = "c_" + e
        tok = (self.sem[e], self.cnt[e])
        for b in rd:
            b.r[key] = tok
        for b in wr:
            b.w[key] = tok

    def dma(self, e, out_ap, in_ap, dst, src, **kw):
        if dst.dsem is None:
            dst.dsem = self.dfree.pop()
        i = dst.dsem
        d = self._deps(e, [src], [dst])
        self._wait(e, d)
        self.dval[i] += 16
        self.prog[e].append(("i", "dma_start", (), dict(out=out_ap, in_=in_ap, **kw), self.dsems[i], 16))
        key = "d%d" % i
        tok = (self.dsems[i], self.dval[i])
        src.r[key] = tok
        dst.w[key] = tok

    def drain(self):
        d = {}
        for i, v in enumerate(self.dval):
            if v:
                d["d%d" % i] = (self.dsems[i], v)
        self._wait("sp", d)

    def emit(self):
        self.drain()
        prog = self.prog
        with self.nc.Block() as block:
            def mk(e):
                def body(g):
                    for it in prog[e]:
                        if it[0] == "w":
                            g.wait_ge(it[1], it[2])
                        else:
                            getattr(g, it[1])(*it[2], **it[3]).then_inc(it[4], it[5])
                return body
            block.tensor(mk("pe"))
            block.scalar(mk("act"))
            block.vector(mk("dve"))
            block.gpsimd(mk("pool"))
            block.sync(mk("sp"))
        self.prog = {k: [] for k in self.ENG}
        for b in self.phase_bufs:
            if b.dsem is not None:
                self.dfree.append(b.dsem)
                b.dsem = None
        self.phase_bufs = []

    class _Phase:
        def __init__(self, s):
            self.s = s

        def __enter__(self):
            self.es = ExitStack()
            self.es.__enter__()
            self.s.es = self.es
            return self

        def __exit__(self, *a):
            if a[0] is None:
                self.s.emit()
            self.s.es = self.s.ges
            return self.es.__exit__(*a)

    def phase(self):
        return Sched._Phase(self)


def dap(buf, off, dims):
    return bass.AP(buf.t, off, [list(d) for d in dims])


class Builder:
    def __init__(self, dbg=None):
        self.dbg = dbg or {}
        self.nc = bass.Bass("TRN2", target_bir_lowering=False)
        self.outs = []
        self.pre = {}
        self.scoped = []

    def declare(self, s):
        I = lambda n, sh, dt=F32: s.dram(n, sh, dt, kind="ExternalInput")
        self.x = I("x", [NB, L, D])
        self.ctx = I("ctx", [NB, LC, D])
        self.cT = I("cT", [128, 8, 3])
        self.w_ada = I("w_ada", [DEPTH, D, 6 * D])
        self.b_ada = I("b_ada", [DEPTH, 6 * D])
        self.g_mixc = I("g_mixc", [DEPTH, 128, 8])
        self.g_ffnc = I("g_ffnc", [DEPTH, 128, 8])
        self.w_in = I("w_in", [DEPTH, D, 3088])
        self.w_out = I("w_out", [DEPTH, D, D])
        self.w_up = I("w_up", [DEPTH, D, 2 * DFF])
        self.w_down = I("w_down", [DEPTH, DFF, D])
        self.na_qg = I("na_qg", [DEPTH, 64])
        self.na_kg = I("na_kg", [DEPTH, 64])
        self.rpb_t = I("rpb_t", [DEPTH, 4, 15, 64, 64])
        self.na_mask = I("na_mask", [128, 64])
        self.df_qg = I("df_qg", [DEPTH, 32])
        self.df_kg = I("df_kg", [DEPTH, 32])
        self.df_lam = I("df_lam", [DEPTH, 4, 32])
        self.df_subln = I("df_subln", [DEPTH, 64])
        self.rope_cos = I("rope_cos", [L, 32])
        self.rope_sin = I("rope_sin", [L, 32])
        self.conv_wc = I("conv_wc", [DEPTH, 128, 8, 5])
        self.conv_bc = I("conv_bc", [DEPTH, 128, 8])
        self.dt_bias = I("dt_bias", [DEPTH, 16])
        self.a_log = I("a_log", [DEPTH, 16])
        self.ssm_d = I("ssm_d", [DEPTH, 8])
        self.ssm_norm = I("ssm_norm", [DEPTH, 512])
        self.fconv_wc = I("fconv_wc", [DEPTH, 128, NF, 3])
        self.fconv_bc = I("fconv_bc", [DEPTH, 128, NF])
        self.ident_d = I("ident", [128, 128])
        self.tri_d = I("tri", [5, 128, 128])
        self.negm_d = I("negm", [128, 2, 128])
        self.sel_d = I("sel", [128, 2, 8, 128])
        self.out = s.dram("out", [NB, L, D], F32, kind="ExternalOutput")
        dk = "ExternalOutput" if self.dbg.get("dump") else "Internal"
        self.modrow_d = s.dram("modrow_d", [DEPTH, 3, 6 * D], F32, kind=dk)
        self.XA = [s.dram("XA%d" % l, [NB, T, D], F32, kind=dk) for l in range(DEPTH)]
        self.XB = s.dram("XB0", [NB, T, D], F32, kind=dk)
        self.Yd = s.dram("Yd", [NB, T, D], F32, kind=dk)
        self.Zd = s.dram("Zd", [T, 512], F32, kind=dk)
        self.ATd = s.dram("ATd", [NF, 128, T], BF16, kind=dk)
        self.hTd = s.dram("hTd", [128, 8, T], BF16, kind=dk) if self.dbg.get("dump") else None
        self.Yin = I("Yin", [DEPTH, NB, T, D]) if self.dbg.get("feed_y") else None

    def build(self):
        nc = self.nc
        with ExitStack() as es:
            s = Sched(nc, es)
            self.s = s
            self.declare(s)
            s.psum_init()
            self.ident = s.sb("ident", [128, 128], F32, persist=True)
            self.identb = s.sb("identb", [128, 128], BF16, persist=True)
            self.hT = s.sb("hT", [128, 8, T], BF16, persist=True)
            self.AB = [[s.sb("AB%d%d" % (l, i), [128, 8, 3], F32, persist=True) for i in range(4)] for l in range(DEPTH)]
            self.BT = s.sb("BT", [128, 4, 14, 64], BF16, persist=True)
            ph = self.dbg.get("phases")
            with s.phase():
                s.dma("sp", self.ident[:], self.ident_d.t.ap(), self.ident, self.ident_d)
                s.I("act", "activation", out=self.identb[:], in_=self.ident[:], func=AF.Copy, rd=[self.ident], wr=[self.identb])
                self.phase_adaln()
            for l in self.dbg.get("layers", range(DEPTH)):
                last = l == DEPTH - 1
                nt = 16 if last else 18
                if ph is None or "na" in ph:
                    with s.phase():
                        self.phase_na_bias(l)
                for b in range(NB):
                    if l == 0:
                        src = lambda t, b=b: (self.x.t.ap()[b, t * 128:(t + 1) * 128, :], self.x) if t < 16 else \
                            (self.ctx.t.ap()[b, (t - 16) * 128:(t - 15) * 128, :], self.ctx)
                    else:
                        src = lambda t, b=b: (self.XB.t.ap()[b, t * 128:(t + 1) * 128, :], self.XB)
                    full = ph is None
                    with ExitStack() as sc1:
                        wdf_buf = self.alloc_scoped(sc1, "wdf", [128, 8, 768]) if full else None
                        with s.phase():
                            if full:
                                self.pre["wna"] = self.w_na(l)
                            self.phase_norm(src, self.AB[l][0], self.AB[l][1], b, 18)
                            if self.dbg.get("dump") and l == self.dbg.get("dump_l", 0) and b == 0 and self.dbg.get("dump_h") == "mix":
                                s.dma("sp", self.hTd.t.ap(), self.hT[:], self.hTd, self.hT)
                            if ph is None or "na" in ph:
                                if full:
                                    self.pre["wdf"] = self.w_df(l, wdf_buf)
                                self.phase_na(l, b)
                        if ph is None or "df" in ph:
                            with s.phase():
                                self.phase_df(l, b)
                        self.release_scoped()
                    if ph is None or "ssm" in ph:
                        self.phase_ssm(l, b)
                    if ph is None or "out" in ph:
                        with s.phase():
                            self.phase_out(l, b, src, nt)
                    if ph is None or "ffn" in ph:
                        srcA = lambda t, b=b, l=l: (self.XA[l].t.ap()[b, t * 128:(t + 1) * 128, :], self.XA[l])
                        with s.phase():
                            self.phase_norm(srcA, self.AB[l][2], self.AB[l][3], b, nt)
                        with ExitStack() as sc2:
                            wdn_buf = self.alloc_scoped(sc2, "wdn", [128, NF, D])
                            with s.phase():
                                self.pre["wdn"] = self.w_dn(l, wdn_buf)
                                self.phase_ffn_up(l, b, nt)
                            with s.phase():
                                self.phase_ffn_down(l, b, srcA, nt)
                            self.release_scoped()
        return nc

    def phase_adaln(self):
        s = self.s
        cT = s.sb("cT", [128, 8, 3], F32)
        siluT = s.sb("siluT", [128, 8, 3], F32)
        s.dma("sp", cT[:], self.cT.t.ap(), cT, self.cT)
        s.I("act", "activation", out=siluT[:], in_=cT[:], func=AF.Silu, rd=[cT], wr=[siluT])
        wa = [s.sb("wa%d" % i, [128, 8, 512], F32) for i in range(3)]
        for l in range(DEPTH):
            brow = s.sb("brow%d" % l, [3, 6 * D], F32)
            modrow = s.sb("modrow%d" % l, [3, 6 * D], F32)
            s.dma("sp", brow[:], dap(self.b_ada, l * 6 * D, [[0, 3], [1, 6 * D]]), brow, self.b_ada)
            for j in range(12):
                w = wa[j % 3]
                s.dma("sp", w[:], self.w_ada.t.ap()[l, :, j * 512:(j + 1) * 512].rearrange("(k p) n -> p k n", p=128), w, self.w_ada)
                pm = s.ps()
                for k in range(8):
                    s.I("pe", "matmul", pm[0:3, :], siluT[:, k, :], w[:, k, :], start=(k == 0), stop=(k == 7), rd=[siluT, w], wr=[pm])
                s.I("dve", "tensor_tensor", out=modrow[:, j * 512:(j + 1) * 512], in0=pm[0:3, :], in1=brow[:, j * 512:(j + 1) * 512], op=ALU.add,
                    rd=[pm, brow], wr=[modrow])
            s.dma("sp", self.modrow_d.t.ap()[l], modrow[:], self.modrow_d, modrow)
            pT = s.ps()
            for c in range(48):
                s.I("pe", "transpose", pT[:, c * 3:(c + 1) * 3], modrow[0:3, c * 128:(c + 1) * 128], self.ident[0:3, 0:3], rd=[modrow, self.ident], wr=[pT])
            modcol = s.sb("modcol%d" % l, [128, 48, 3], F32)
            s.I("dve", "tensor_copy", modcol[:].rearrange("p a b -> p (a b)"), pT[:, 0:144], rd=[pT], wr=[modcol])
            gm = s.sb("gm%d" % l, [128, 8, 1], F32)
            gf = s.sb("gf%d" % l, [128, 8, 1], F32)
            s.dma("sp", gm[:, :, 0], self.g_mixc.t.ap()[l], gm, self.g_mixc)
            s.dma("sp", gf[:, :, 0], self.g_ffnc.t.ap()[l], gf, self.g_ffnc)
            A1, B1, A2, B2 = self.AB[l]
            for (A, Bv, g, sc0, sh0) in ((A1, B1, gm, 8, 0), (A2, B2, gf, 32, 24)):
                s.I("dve", "scalar_tensor_tensor", out=A[:], in0=modcol[:, sc0:sc0 + 8, :], scalar=1.0, in1=g[:].to_broadcast([128, 8, 3]),
                    op0=ALU.add, op1=ALU.mult, rd=[modcol, g], wr=[A])
                s.I("dve", "tensor_copy", Bv[:], modcol[:, sh0:sh0 + 8, :], rd=[modcol], wr=[Bv])

    def phase_norm(self, src, A, Bv, b, ntiles):
        s = self.s
        xt = [s.sb("nxt%d" % i, [128, D], F32) for i in range(4)]
        xn = [s.sb("nxn%d" % i, [128, D], F32) for i in range(8)]
        junks = [s.sb("njunk%d" % i, [128, D], BF16) for i in range(2)]
        sts = [s.sb("nst%d" % i, [128, 1, 3], F32) for i in range(8)]
        ngroups = (ntiles + 3) // 4
        for g in range(ngroups):
            tiles = list(range(4 * g, min(4 * g + 4, ntiles)))
            j = b if tiles[0] < 16 else 2
            for ti, t in enumerate(tiles):
                ap, sbuf = src(t)
                x_ = xt[t % 4]
                n_ = xn[t % 8]
                s.dma("sp", x_[:], ap, x_, sbuf)
                st, junk = sts[t % 8], junks[t % 2]
                s.I("act", "activation", out=junk[:], in_=x_[:], func=AF.Square, accum_out=st[:, 0, 0:1], rd=[x_], wr=[junk, st])
                s.I("act", "activation", out=st[:, 0, 1:2], in_=st[:, 0, 0:1], func=AF.Sqrt, scale=1.0 / D, bias=EPS, rd=[st], wr=[st])
                s.I("dve", "reciprocal", st[:, 0, 2:3], st[:, 0, 1:2], rd=[st], wr=[st])
                s.I("dve", "tensor_scalar", out=n_[:], in0=x_[:], scalar1=st[:, 0, 2:3], scalar2=None, op0=ALU.mult, rd=[x_, st], wr=[n_])
            n = len(tiles) * 128
            for c in range(8):
                p = s.ps()
                for ti, t in enumerate(tiles):
                    n_ = xn[t % 8]
                    s.I("pe", "transpose", p[:, ti * 128:(ti + 1) * 128], n_[:, c * 128:(c + 1) * 128], self.ident[:], rd=[n_, self.ident], wr=[p])
                s.I("act", "activation", out=self.hT[:, c, tiles[0] * 128:tiles[0] * 128 + n], in_=p[:, 0:n], func=AF.Identity,
                    scale=A[:, c, j:j + 1], bias=Bv[:, c, j:j + 1], rd=[p, A, Bv], wr=[self.hT])

    def load_w(self, name, src_buf, src_ap, shape, scope=None):
        s = self.s
        if name in self.pre:
            return self.pre.pop(name)
        if scope is None:
            w = s.sb(name, shape, BF16)
        else:
            w = scope
        s.dma("pool", w[:], src_ap, w, src_buf)
        return w

    def alloc_scoped(self, es, name, shape):
        s = self.s
        s.uid += 1
        t = es.enter_context(self.nc.sbuf_tensor("%s_%d" % (name, s.uid), list(shape), BF16))
        w = Buf(name, t, True)
        self.scoped.append(w)
        return w

    def release_scoped(self):
        for b in self.scoped:
            if b.dsem is not None:
                self.s.dfree.append(b.dsem)
                b.dsem = None
        self.scoped = []

    def w_na(self, l, scope=None):
        return self.load_w("wna", self.w_in, self.w_in.t.ap()[l, :, 0:768].rearrange("(k p) n -> p k n", p=128), [128, 8, 768], scope)

    def w_df(self, l, scope=None):
        return self.load_w("wdf", self.w_in, self.w_in.t.ap()[l, :, 768:1536].rearrange("(k p) n -> p k n", p=128), [128, 8, 768], scope)

    def w_dn(self, l, scope=None):
        return self.load_w("wdn", self.w_down, self.w_down.t.ap()[l].rearrange("(f p) n -> p f n", p=128), [128, NF, D], scope)

    def bcast_row(self, name, src_buf, off, n, dt=F32, parts=128):
        s = self.s
        t = s.sb(name, [parts, n], dt)
        s.dma("sp", t[:], dap(src_buf, off, [[0, parts], [1, n]]), t, src_buf)
        return t

    def phase_out(self, l, b, src, ntiles):
        s = self.s
        w = self.load_w("wout", self.w_out, self.w_out.t.ap()[l].rearrange("(k p) n -> p k n", p=128), [128, 8, D])
        G = {}
        G[b] = self.bcast_row("g1b", self.modrow_d, (l * 3 + b) * 6 * D + 2 * D, D)
        if ntiles > 16:
            G[2] = self.bcast_row("g1c", self.modrow_d, (l * 3 + 2) * 6 * D + 2 * D, D)
        yt = [s.sb("oyt%d" % i, [128, D], F32) for i in range(8)]
        xr = [s.sb("oxr%d" % i, [128, D], F32) for i in range(8)]
        yT = [s.sb("oyT%d" % i, [128, 8, 512], BF16) for i in range(2)]
        tmp = [s.sb("otmp%d" % i, [128, D], F32) for i in range(2)]
        xo = [s.sb("oxo%d" % i, [128, D], F32) for i in range(2)]
        ngroups = (ntiles + 3) // 4

        def loads(g):
            for t in range(4 * g, min(4 * g + 4, ntiles)):
                y_ = yt[t % 8]
                if self.Yin is not None:
                    s.dma("sp", y_[:], self.Yin.t.ap()[l, b, t * 128:(t + 1) * 128, :], y_, self.Yin)
                else:
                    s.dma("sp", y_[:], self.Yd.t.ap()[b, t * 128:(t + 1) * 128, :], y_, self.Yd)
                ap, sbuf = src(t)
                s.dma("sp", xr[t % 8][:], ap, xr[t % 8], sbuf)

        loads(0)
        for g in range(ngroups):
            if g + 1 < ngroups:
                loads(g + 1)
            tiles = list(range(4 * g, min(4 * g + 4, ntiles)))
            j = b if tiles[0] < 16 else 2
            yT_ = yT[g % 2]
            n = len(tiles) * 128
            for c in range(8):
                p = s.ps()
                for ti, t in enumerate(tiles):
                    s.I("pe", "transpose", p[:, ti * 128:(ti + 1) * 128], yt[t % 8][:, c * 128:(c + 1) * 128], self.ident[:], rd=[yt[t % 8], self.ident], wr=[p])
                s.I("act", "activation", out=yT_[:, c, 0:n], in_=p[:, 0:n], func=AF.Copy, rd=[p], wr=[yT_])
            for ti, t in enumerate(tiles):
                tm = tmp[t % 2]
                xo_ = xo[t % 2]
                for hf in range(2):
                    p = s.ps()
                    for k in range(8):
                        s.I("pe", "matmul", p[:, :], yT_[:, k, ti * 128:(ti + 1) * 128], w[:, k, hf * 512:(hf + 1) * 512], start=(k == 0), stop=(k == 7),
                            rd=[yT_, w], wr=[p])
                    s.I("dve", "tensor_tensor", out=tm[:, hf * 512:(hf + 1) * 512], in0=p[:, :], in1=G[j][:, hf * 512:(hf + 1) * 512], op=ALU.mult,
                        rd=[p, G[j]], wr=[tm])
                s.I("pool", "tensor_tensor", out=xo_[:], in0=tm[:], in1=xr[t % 8][:], op=ALU.add, rd=[tm, xr[t % 8]], wr=[xo_])
                s.dma("sp", self.XA[l].t.ap()[b, t * 128:(t + 1) * 128, :], xo_[:], self.XA[l], xo_)

    def phase_ffn_up(self, l, b, ntiles):
        s = self.s
        groups = [(0, 512), (512, 512), (1024, 512), (1536, 512)]
        if ntiles > 16:
            groups.append((2048, 256))
        ntok = ntiles * 128
        wu = [s.sb("wu%d" % i, [128, 8, 256], BF16) for i in range(3)]
        Gl = [s.sb("fGl%d" % i, [128, L + 2], F32) for i in range(2)]
        Gc = [s.sb("fGc%d" % i, [128, LC + 2], F32) for i in range(2)]
        V = [s.sb("fV%d" % i, [128, T], F32) for i in range(2)]
        acc = [s.sb("facc%d" % i, [128, T], F32) for i in range(2)]
        at = [s.sb("fat%d" % i, [128, T], BF16) for i in range(2)]
        cw = s.sb("fcw", [128, NF, 3], F32)
        cb = s.sb("fcb", [128, NF], F32)
        s.dma("sp", cw[:], self.fconv_wc.t.ap()[l], cw, self.fconv_wc)
        s.dma("sp", cb[:], self.fconv_bc.t.ap()[l], cb, self.fconv_bc)
        for i in range(2):
            s.I("pool", "memset", Gl[i][:], 0.0, wr=[Gl[i]])
            s.I("pool", "memset", Gc[i][:], 0.0, wr=[Gc[i]])
        wup = self.w_up.t.ap()[l]

        def loadw(f):
            w = wu[f % 3]
            s.dma("pool", w[:, :, 0:128], wup[:, f * 128:(f + 1) * 128].rearrange("(k p) n -> p k n", p=128), w, self.w_up)
            s.dma("pool", w[:, :, 128:256], wup[:, DFF + f * 128:DFF + (f + 1) * 128].rearrange("(k p) n -> p k n", p=128), w, self.w_up)

        loadw(0)
        pend = []
        for f in range(NF):
            if f + 1 < NF:
                loadw(f + 1)
            w = wu[f % 3]
            gl, gc, v, a, o = Gl[f % 2], Gc[f % 2], V[f % 2], acc[f % 2], at[f % 2]
            for gi, (t0, n) in enumerate(groups):
                if gi == 2 and pend:
                    pend.pop(0)()
                pa = s.ps()
                pb = s.ps()
                for k in range(8):
                    s.I("pe", "matmul", pa[:, 0:n], w[:, k, 0:128], self.hT[:, k, t0:t0 + n], start=(k == 0), stop=(k == 7), rd=[w, self.hT], wr=[pa])
                for k in range(8):
                    s.I("pe", "matmul", pb[:, 0:n], w[:, k, 128:256], self.hT[:, k, t0:t0 + n], start=(k == 0), stop=(k == 7), rd=[w, self.hT], wr=[pb])
                if t0 < L:
                    s.I("act", "activation", out=gl[:, 1 + t0:1 + t0 + n], in_=pa[:, 0:n], func=AF.Copy, rd=[pa], wr=[gl])
                else:
                    s.I("act", "activation", out=gc[:, 1:1 + n], in_=pa[:, 0:n], func=AF.Copy, rd=[pa], wr=[gc])
                s.I("act", "activation", out=v[:, t0:t0 + n], in_=pb[:, 0:n], func=AF.Copy, rd=[pb], wr=[v])
            segs = [(gl, 0, L)] + ([(gc, L, LC)] if ntiles > 16 else [])
            for (gb, o0, n) in segs:
                s.I("dve", "tensor_scalar", out=a[:, o0:o0 + n], in0=gb[:, 0:n], scalar1=cw[:, f, 0:1], scalar2=None, op0=ALU.mult, rd=[gb, cw], wr=[a])
                for k in (1, 2):
                    s.I("dve", "scalar_tensor_tensor", out=a[:, o0:o0 + n], in0=gb[:, k:k + n], scalar=cw[:, f, k:k + 1], in1=a[:, o0:o0 + n],
                        op0=ALU.mult, op1=ALU.add, rd=[gb, cw, a], wr=[a])

            def tail(f=f, a=a, v=v, o=o):
                s.I("act", "activation", out=a[:, 0:ntok], in_=a[:, 0:ntok], func=AF.Silu, bias=cb[:, f:f + 1], scale=1.0, rd=[a, cb], wr=[a])
                s.I("pool", "tensor_tensor", out=o[:, 0:ntok], in0=a[:, 0:ntok], in1=v[:, 0:ntok], op=ALU.mult, rd=[a, v], wr=[o])
                s.dma("sp", self.ATd.t.ap()[f, :, 0:ntok], o[:, 0:ntok], self.ATd, o)
            pend.append(tail)
        while pend:
            pend.pop(0)()

    def phase_ffn_down(self, l, b, src, ntiles):
        s = self.s
        last = l == DEPTH - 1
        w = self.w_dn(l)
        G = {}
        G[b] = self.bcast_row("g2b", self.modrow_d, (l * 3 + b) * 6 * D + 5 * D, D)
        if ntiles > 16:
            G[2] = self.bcast_row("g2c", self.modrow_d, (l * 3 + 2) * 6 * D + 5 * D, D)
        aT = [s.sb("daT%d" % i, [128, NF, 512], BF16) for i in range(2)]
        xr = [s.sb("dxr%d" % i, [128, D], F32) for i in range(8)]
        tmp = [s.sb("dtmp%d" % i, [128, D], F32) for i in range(2)]
        xo = [s.sb("dxo%d" % i, [128, D], F32) for i in range(2)]
        ngroups = (ntiles + 3) // 4

        def loads(g):
            tiles = list(range(4 * g, min(4 * g + 4, ntiles)))
            n = len(tiles) * 128
            a_ = aT[g % 2]
            s.dma("sp", a_[:, :, 0:n], self.ATd.t.ap()[:, :, tiles[0] * 128:tiles[0] * 128 + n].rearrange("f p t -> p f t"), a_, self.ATd)
            for t in tiles:
                ap, sbuf = src(t)
                s.dma("sp", xr[t % 8][:], ap, xr[t % 8], sbuf)

        loads(0)
        for g in range(ngroups):
            if g + 1 < ngroups:
                loads(g + 1)
            tiles = list(range(4 * g, min(4 * g + 4, ntiles)))
            j = b if tiles[0] < 16 else 2
            a_ = aT[g % 2]
            for ti, t in enumerate(tiles):
                tm = tmp[t % 2]
                xo_ = xo[t % 2]
                for hf in range(2):
                    p = s.ps()
                    for f in range(NF):
                        s.I("pe", "matmul", p[:, :], a_[:, f, ti * 128:(ti + 1) * 128], w[:, f, hf * 512:(hf + 1) * 512], start=(f == 0), stop=(f == NF - 1),
                            rd=[a_, w], wr=[p])
                    s.I("dve", "tensor_tensor", out=tm[:, hf * 512:(hf + 1) * 512], in0=p[:, :], in1=G[j][:, hf * 512:(hf + 1) * 512], op=ALU.mult,
                        rd=[p, G[j]], wr=[tm])
                s.I("pool", "tensor_tensor", out=xo_[:], in0=tm[:], in1=xr[t % 8][:], op=ALU.add, rd=[tm, xr[t % 8]], wr=[xo_])
                if last:
                    s.dma("sp", self.out.t.ap()[b, t * 128:(t + 1) * 128, :], xo_[:], self.out, xo_)
                else:
                    s.dma("sp", self.XB.t.ap()[b, t * 128:(t + 1) * 128, :], xo_[:], self.XB, xo_)

    def group_norm(self, p, ngrp, gd, gains, out, sq, st, view):
        s = self.s
        n = ngrp * gd
        s.I("act", "activation", out=sq[:, 0:n], in_=p[:, 0:n], func=AF.Square, rd=[p], wr=[sq])
        s.I("dve", "tensor_reduce", out=st[:, 0:ngrp, 0], in_=sq[:, 0:n].rearrange("p (g d) -> p g d", d=gd), axis=AX.X, op=ALU.add, rd=[sq], wr=[st])
        s.I("act", "activation", out=st[:, 0:ngrp, 1], in_=st[:, 0:ngrp, 0], func=AF.Sqrt, scale=1.0 / gd, bias=EPS, rd=[st], wr=[st])
        s.I("dve", "reciprocal", st[:, 0:ngrp, 2], st[:, 0:ngrp, 1], rd=[st], wr=[st])
        s.I("dve", "tensor_tensor", out=sq[:, 0:n].rearrange("p (g d) -> p g d", d=gd), in0=p[:, 0:n].rearrange("p (g d) -> p g d", d=gd),
            in1=st[:, 0:ngrp, 2:3].to_broadcast([128, ngrp, gd]), op=ALU.mult, rd=[p, st], wr=[sq])
        s.I("pool", "tensor_tensor", out=out, in0=view(sq[:, 0:n]), in1=gains, op=ALU.mult, rd=[sq] + self._gn_rd, wr=self._gn_wr)

    def phase_na_bias(self, l):
        s = self.s
        bt32 = s.sb("bt32", [128, 4, 14, 64], F32)
        mk = s.sb("namask", [128, 64], F32)
        s.dma("sp", mk[:], self.na_mask.t.ap(), mk, self.na_mask)
        for h in range(4):
            for half in range(2):
                s.dma("sp", bt32[64 * half:64 * half + 64, h, :, :], self.rpb_t.t.ap()[l, h, half:half + 14].rearrange("d k q -> k d q"), bt32, self.rpb_t)
        s.I("dve", "tensor_tensor", out=bt32[:].rearrange("p h d q -> p (h d) q"), in0=bt32[:].rearrange("p h d q -> p (h d) q"),
            in1=mk[:].rearrange("p (o q) -> p o q", o=1).to_broadcast([128, 56, 64]), op=ALU.add, rd=[bt32, mk], wr=[bt32])
        s.I("dve", "tensor_scalar", out=self.BT[:].rearrange("p h d q -> p (h d q)"), in0=bt32[:].rearrange("p h d q -> p (h d q)"), scalar1=8.0, scalar2=None,
            op0=ALU.mult, rd=[bt32], wr=[self.BT])

    def phase_na(self, l, b):
        s = self.s
        last = l == DEPTH - 1
        w = self.w_na(l)
        gains = s.sb("nagain", [128, 2, 1, 64], F32)
        s.dma("sp", gains[:, 0, 0, :], dap(self.na_qg, l * 64, [[0, 128], [1, 64]]), gains, self.na_qg)
        s.dma("sp", gains[:, 1, 0, :], dap(self.na_kg, l * 64, [[0, 128], [1, 64]]), gains, self.na_kg)
        QKT = s.sb("naQKT", [128, 6, T], BF16)
        kz = [s.sb("nakz%d" % i, [128, 2, 256], F32) for i in range(2)]
        for i in range(2):
            s.I("pool", "memset", kz[i][:], 0.0, wr=[kz[i]])
        VE = s.sb("naVE", [128, 18, 4, 65], BF16)
        VO = s.sb("naVO", [128, 15, 4, 65], BF16)
        s.I("pool", "memset", VE[:], 1.0, wr=[VE])
        s.I("pool", "memset", VO[:], 1.0, wr=[VO])
        qn = [s.sb("naqn%d" % i, [128, 2, 4, 64], F32) for i in range(2)]
        gsq = [s.sb("nagsq%d" % i, [128, 512], F32) for i in range(2)]
        gst = [s.sb("nagst%d" % i, [128, 16, 3], F32) for i in range(2)]
        pending = None
        for t in range(18):
            pq = s.ps()
            pv = s.ps()
            for k in range(8):
                s.I("pe", "matmul", pq[:, :], self.hT[:, k, t * 128:(t + 1) * 128], w[:, k, 0:512], start=(k == 0), stop=(k == 7), rd=[self.hT, w], wr=[pq])
            for k in range(8):
                s.I("pe", "matmul", pv[:, 0:256], self.hT[:, k, t * 128:(t + 1) * 128], w[:, k, 512:768], start=(k == 0), stop=(k == 7), rd=[self.hT, w], wr=[pv])
            q_ = qn[t % 2]
            self._gn_rd = [gains]
            self._gn_wr = [q_]
            self.group_norm(pq, 8, 64, gains[:].to_broadcast([128, 2, 4, 64]), q_[:], gsq[t % 2], gst[t % 2],
                            lambda ap: ap.rearrange("p (a h d) -> p a h d", a=2, h=4))
            kz_ = kz[t % 2]
            for hs in range(2):
                s.I("pool", "tensor_copy", kz_[:, hs, :].rearrange("p (pr h2 d) -> p pr h2 d", pr=2, h2=2)[:, :, hs, :],
                    q_[:, 1, :, :].rearrange("p (pr h2) d -> p pr h2 d", pr=2)[:, :, hs, :], rd=[q_], wr=[kz_])
            s.I("dve", "tensor_copy", VE[:, t, :, 0:64], pv[:, 0:256].rearrange("p (h d) -> p h d", h=4), rd=[pv], wr=[VE])

            def tail(t=t, q_=q_, kz_=kz_):
                pt = s.ps()
                pt2 = s.ps()
                qf = q_[:].rearrange("p a h d -> p (a h d)")
                for cc in range(2):
                    s.I("pe", "transpose", pt[:, cc * 128:(cc + 1) * 128], qf[:, cc * 128:(cc + 1) * 128], self.ident[:], rd=[q_, self.ident], wr=[pt])
                for hs in range(2):
                    for pr in range(2):
                        s.I("pe", "transpose", pt2[:, (hs * 2 + pr) * 128:(hs * 2 + pr + 1) * 128], kz_[:, hs, pr * 128:(pr + 1) * 128], self.ident[:], rd=[kz_, self.ident], wr=[pt2])
                s.I("act", "activation", out=QKT[:, 0:2, t * 128:(t + 1) * 128], in_=pt[:, 0:256].rearrange("p (c t) -> p c t", c=2), func=AF.Copy, rd=[pt], wr=[QKT])
                s.I("act", "activation", out=QKT[:, 2:6, t * 128:(t + 1) * 128], in_=pt2[:, :].rearrange("p (c t) -> p c t", c=4), func=AF.Copy, rd=[pt2], wr=[QKT])
            if pending is not None:
                pending()
            pending = tail
        pending()
        for i in range(15):
            pv = s.ps()
            for k in range(8):
                s.I("pe", "matmul", pv[:, 0:256], self.hT[:, k, 64 + i * 128:64 + (i + 1) * 128], w[:, k, 512:768], start=(k == 0), stop=(k == 7), rd=[self.hT, w], wr=[pv])
            s.I("dve", "tensor_copy", VO[:, i, :, 0:64], pv[:, 0:256].rearrange("p (h d) -> p h d", h=4), rd=[pv], wr=[VO])
        PT = [s.sb("naPT%d" % i, [128, 6, 64], BF16) for i in range(4)]
        rec = [s.sb("narec%d" % i, [128, 4, 1], F32) for i in range(2)]
        yo = [s.sb("nayo%d" % i, [128, 4, 64], F32) for i in range(2)]
        def s_part(r, h, P_):
            R0 = min(max(r - 4, 0), 24)
            pair = h // 2
            pS = s.ps()
            q_ap = QKT[:, pair, r * 64:(r + 1) * 64]
            kk = 2 + 2 * (h % 2) + pair
            for ci in range(4):
                kr = R0 + 2 * ci
                d = kr - r + 7
                s.I("pe", "matmul", pS[:, ci * 64:(ci + 1) * 64], QKT[:, kk, kr * 64:kr * 64 + 128], q_ap, start=True, stop=False,
                    rd=[QKT], wr=[pS])
                s.I("pe", "matmul", pS[:, ci * 64:(ci + 1) * 64], self.identb[:], self.BT[:, h, d, :], start=False, stop=True,
                    rd=[self.identb, self.BT], wr=[pS])
            for cc in range(2):
                s.I("pe", "matmul", pS[:, (4 + cc) * 64:(5 + cc) * 64], QKT[:, kk, L + cc * 128:L + (cc + 1) * 128], q_ap,
                    start=True, stop=True, rd=[QKT], wr=[pS])
            s.I("act", "activation", out=P_[:].rearrange("p c q -> p (c q)"), in_=pS[:, 0:384], func=AF.Exp, scale=0.125, rd=[pS], wr=[P_])

        def pv_part(r, h, P_):
            R0 = min(max(r - 4, 0), 24)
            rp, rr = r // 2, r % 2
            po = s.psacc(rp)
            for c in range(6):
                if c < 4:
                    kr = R0 + 2 * c
                    vb, v_ap = (VE, VE[:, kr // 2, h, :]) if kr % 2 == 0 else (VO, VO[:, (kr - 1) // 2, h, :])
                else:
                    vb, v_ap = VE, VE[:, 16 + (c - 4), h, :]
                s.I("pe", "matmul", po[64 * rr:64 * rr + 64, h * 65:(h + 1) * 65], P_[:, c, :], v_ap, start=(c == 0), stop=(c == 5), rd=[P_, vb], wr=[po])

        def row_finish(rp):
            po = s.psacc(rp)
            rc, y_ = rec[rp % 2], yo[rp % 2]
            pov = po[:, 0:260].rearrange("p (h e) -> p h e", e=65)
            s.I("dve", "reciprocal", rc[:], pov[:, :, 64:65], rd=[po], wr=[rc])
            s.I("dve", "tensor_tensor", out=y_[:], in0=pov[:, :, 0:64], in1=rc[:].to_broadcast([128, 4, 64]), op=ALU.mult, rd=[po, rc], wr=[y_])
            s.dma("sp", self.Yd.t.ap()[b, rp * 128:(rp + 1) * 128, 0:256], y_[:].rearrange("p h d -> p (h d)"), self.Yd, y_)

        seq = [(r, h) for r in range(32) for h in range(4)]
        SK = 2
        for i in range(len(seq) + SK):
            if i < len(seq):
                s_part(seq[i][0], seq[i][1], PT[i % 4])
            j = i - SK
            if j >= 0:
                pv_part(seq[j][0], seq[j][1], PT[j % 4])
                if seq[j][0] % 2 == 1 and seq[j][1] == 3:
                    row_finish(seq[j][0] // 2)
        if not last:
            PTc = [s.sb("naPTc%d" % i, [128, 2, 256], BF16) for i in range(2)]
            pos = [s.psacc(0), s.psacc(1)]
            for h in range(4):
                pair, base = h // 2, 64 * (h % 2)
                pS = s.ps()
                for cc in range(2):
                    s.I("pe", "matmul", pS[:, cc * 256:(cc + 1) * 256], QKT[:, 2 + 2 * (h % 2) + pair, L + cc * 128:L + (cc + 1) * 128],
                        QKT[:, pair, L:L + 256], start=True, stop=True, rd=[QKT], wr=[pS])
                P_ = PTc[h % 2]
                s.I("act", "activation", out=P_[:].rearrange("p c q -> p (c q)"), in_=pS[:, :], func=AF.Exp, scale=0.125, rd=[pS], wr=[P_])
                for qt in range(2):
                    for cc in range(2):
                        s.I("pe", "matmul", pos[qt][:, h * 65:(h + 1) * 65], P_[:, cc, qt * 128:(qt + 1) * 128], VE[:, 16 + cc, h, :], start=(cc == 0), stop=(cc == 1),
                            rd=[P_, VE], wr=[pos[qt]])
            for qt in range(2):
                rc, y_ = rec[qt], yo[qt]
                pov = pos[qt][:, 0:260].rearrange("p (h e) -> p h e", e=65)
                s.I("dve", "reciprocal", rc[:], pov[:, :, 64:65], rd=[pos[qt]], wr=[rc])
                s.I("dve", "tensor_tensor", out=y_[:], in0=pov[:, :, 0:64], in1=rc[:].to_broadcast([128, 4, 64]), op=ALU.mult, rd=[pos[qt], rc], wr=[y_])
                s.dma("sp", self.Yd.t.ap()[b, L + qt * 128:L + (qt + 1) * 128, 0:256], y_[:].rearrange("p h d -> p (h d)"), self.Yd, y_)

    def phase_df(self, l, b):
        s = self.s
        last = l == DEPTH - 1
        lam_init = 0.8 - 0.6 * math.exp(-0.3 * l)
        w = self.w_df(l)
        gains = s.sb("dfgain", [128, 2, 1, 32], F32)
        s.dma("sp", gains[:, 0, 0, :], dap(self.df_qg, l * 32, [[0, 128], [1, 32]]), gains, self.df_qg)
        s.dma("sp", gains[:, 1, 0, :], dap(self.df_kg, l * 32, [[0, 128], [1, 32]]), gains, self.df_kg)
        COS = s.sb("dfcos", [128, 16, 32], F32)
        SIN = s.sb("dfsin", [128, 16, 32], F32)
        s.dma("sp", COS[:], self.rope_cos.t.ap().rearrange("(t p) d -> p t d", p=128), COS, self.rope_cos)
        s.dma("sp", SIN[:], self.rope_sin.t.ap().rearrange("(t p) d -> p t d", p=128), SIN, self.rope_sin)
        lv = s.sb("dflv", [128, 4, 32], F32)
        s.dma("sp", lv[:].rearrange("p a d -> p (a d)"), dap(self.df_lam, l * 128, [[0, 128], [1, 128]]), lv, self.df_lam)
        lp = s.sb("dflp", [128, 2, 32], F32)
        ls = s.sb("dfls", [128, 8], F32)
        s.I("dve", "tensor_tensor", out=lp[:, 0, :], in0=lv[:, 0, :], in1=lv[:, 1, :], op=ALU.mult, rd=[lv], wr=[lp])
        s.I("dve", "tensor_tensor", out=lp[:, 1, :], in0=lv[:, 2, :], in1=lv[:, 3, :], op=ALU.mult, rd=[lv], wr=[lp])
        s.I("dve", "tensor_reduce", out=ls[:, 0:2], in_=lp[:], axis=AX.X, op=ALU.add, rd=[lp], wr=[ls])
        s.I("act", "activation", out=ls[:, 2:4], in_=ls[:, 0:2], func=AF.Exp, rd=[ls], wr=[ls])
        s.I("dve", "scalar_tensor_tensor", out=ls[:, 4:5], in0=ls[:, 3:4], scalar=-lam_init, in1=ls[:, 2:3], op0=ALU.add, op1=ALU.subtract, rd=[ls], wr=[ls])
        neglam = ls[:, 4:5]
        sub = s.sb("dfsub", [128, 1, 64], F32)
        s.dma("sp", sub[:, 0, :], dap(self.df_subln, l * 64, [[0, 128], [1, 64]]), sub, self.df_subln)
        s.I("dve", "tensor_scalar", out=sub[:], in0=sub[:], scalar1=1.0 - lam_init, scalar2=None, op0=ALU.mult, rd=[sub], wr=[sub])

        QZ = s.sb("dfQZ", [128, 2, 2, T], BF16)
        KZ = s.sb("dfKZ", [128, 2, 2, T], BF16)
        VD = s.sb("dfVD", [128, 18, 4, 65], BF16)
        s.I("pool", "memset", VD[:], 1.0, wr=[VD])
        qn = [s.sb("dfqn%d" % i, [128, 16, 32], F32) for i in range(2)]
        gsq = [s.sb("dfgsq%d" % i, [128, 512], F32) for i in range(1)] * 2
        gst = [s.sb("dfgst%d" % i, [128, 16, 3], F32) for i in range(2)]
        t1 = [s.sb("dft1%d" % i, [128, 16, 32], F32) for i in range(1)] * 2
        t2 = [s.sb("dft2%d" % i, [128, 16, 32], F32) for i in range(1)] * 2
        qz = [s.sb("dfqz%d" % i, [128, 4, 256], F32) for i in range(2)]
        for i in range(2):
            s.I("pool", "memset", qz[i][:], 0.0, wr=[qz[i]])
        pending = None
        for t in range(18):
            pq = s.ps()
            pv = s.ps()
            for k in range(8):
                s.I("pe", "matmul", pq[:, :], self.hT[:, k, t * 128:(t + 1) * 128], w[:, k, 0:512], start=(k == 0), stop=(k == 7), rd=[self.hT, w], wr=[pq])
            for k in range(8):
                s.I("pe", "matmul", pv[:, 0:256], self.hT[:, k, t * 128:(t + 1) * 128], w[:, k, 512:768], start=(k == 0), stop=(k == 7), rd=[self.hT, w], wr=[pv])
            q_ = qn[t % 2]
            self._gn_rd = [gains]
            self._gn_wr = [q_]
            self.group_norm(pq, 16, 32, gains[:].to_broadcast([128, 2, 8, 32]), q_[:].rearrange("p (a h) d -> p a h d", a=2), gsq[t % 2], gst[t % 2],
                            lambda ap: ap.rearrange("p (a h d) -> p a h d", a=2, h=8))
            z_ = qz[t % 2]
            if t < 16:
                a_, b_ = t1[t % 2], t2[t % 2]
                s.I("pool", "tensor_tensor", out=a_[:], in0=q_[:], in1=COS[:, t:t + 1, :].to_broadcast([128, 16, 32]), op=ALU.mult, rd=[q_, COS], wr=[a_])
                q5 = q_[:].rearrange("p g (a f e) -> p g a f e", a=2, f=2)
                b5 = b_[:].rearrange("p g (a f e) -> p g a f e", a=2, f=2)
                s5 = SIN[:, t:t + 1, :].to_broadcast([128, 16, 32]).rearrange("p g (a f e) -> p g a f e", a=2, f=2)
                s.I("dve", "tensor_tensor", out=b5[:, :, :, 0, :], in0=q5[:, :, :, 1, :], in1=s5[:, :, :, 0, :], op=ALU.mult, rd=[q_, SIN], wr=[b_])
                s.I("dve", "tensor_tensor", out=b5[:, :, :, 1, :], in0=q5[:, :, :, 0, :], in1=s5[:, :, :, 1, :], op=ALU.mult, rd=[q_, SIN], wr=[b_])
                srcs = (a_, b_)
            else:
                srcs = (q_,)

            def comb(out_ap, sel):
                if len(srcs) == 2:
                    s.I("pool", "tensor_tensor", out=out_ap, in0=sel(srcs[0]), in1=sel(srcs[1]), op=ALU.add, rd=list(srcs), wr=[z_])
                else:
                    s.I("pool", "tensor_copy", out_ap, sel(srcs[0]), rd=list(srcs), wr=[z_])
            for m in range(2):
                comb(z_[:, m, :].rearrange("p (h m d) -> p h m d", h=4, m=2)[:, :, m, :],
                     lambda bf: bf[:, 0:8, :].rearrange("p (h m) d -> p h m d", m=2)[:, :, m, :])
            for hs in range(2):
                comb(z_[:, 2 + hs, :].rearrange("p (pr h2 e) -> p pr h2 e", pr=2, h2=2)[:, :, hs, :],
                     lambda bf: bf[:, 8:16, :].rearrange("p (pr h2 m) d -> p pr h2 (m d)", pr=2, h2=2)[:, :, hs, :])
            s.I("dve", "tensor_copy", VD[:, t, :, 0:64], pv[:, 0:256].rearrange("p (h d) -> p h d", h=4), rd=[pv], wr=[VD])

            def tail(t=t, z_=z_):
                pt = s.ps()
                pt2 = s.ps()
                for m in range(2):
                    for pr in range(2):
                        s.I("pe", "transpose", pt[:, (m * 2 + pr) * 128:(m * 2 + pr + 1) * 128], z_[:, m, pr * 128:(pr + 1) * 128], self.ident[:], rd=[z_, self.ident], wr=[pt])
                for hs in range(2):
                    for pr in range(2):
                        s.I("pe", "transpose", pt2[:, (hs * 2 + pr) * 128:(hs * 2 + pr + 1) * 128], z_[:, 2 + hs, pr * 128:(pr + 1) * 128], self.ident[:], rd=[z_, self.ident], wr=[pt2])
                s.I("act", "activation", out=QZ[:, :, :, t * 128:(t + 1) * 128], in_=pt[:, :].rearrange("p (m c t) -> p m c t", m=2, c=2), func=AF.Copy, rd=[pt], wr=[QZ])
                s.I("act", "activation", out=KZ[:, :, :, t * 128:(t + 1) * 128], in_=pt2[:, :].rearrange("p (a c t) -> p a c t", a=2, c=2), func=AF.Copy, rd=[pt2], wr=[KZ])
            if pending is not None:
                pending()
            pending = tail
        pending()
        PT = [[s.sb("dfPT%d%d" % (i, m), [128, 18, 512], BF16) for m in range(2)] for i in range(2)]
        rec = s.sb("dfrec", [128, 2, 4, 1], F32)
        o0 = s.sb("dfo0", [128, 4, 64], F32)
        o1 = s.sb("dfo1", [128, 4, 64], F32)
        yd = [s.sb("dfyd%d" % i, [128, 4, 64], F32) for i in range(2)]
        sq = s.sb("dfsq2", [128, 4, 64], F32)
        st = s.sb("dfst2", [128, 4, 3], F32)
        epsb = s.sb("dfeps", [128, 1], F32)
        s.I("pool", "memset", epsb[:], EPS, wr=[epsb])
        blocks = [(qb * 512, 512, list(range(18))) for qb in range(4)]
        if not last:
            blocks.append((L, 256, [16, 17]))
        items = [(h, q0, nq, kcs) for h in range(4) for (q0, nq, kcs) in blocks]

        def s_steps(idx):
            h, q0, nq, kcs = items[idx]
            pair = h // 2
            P_ = PT[idx % 2]
            out = []
            for m in range(2):
                for kc in kcs:
                    def f(m=m, kc=kc):
                        pS = s.ps()
                        s.I("pe", "matmul", pS[:, 0:nq], KZ[:, h % 2, pair, kc * 128:(kc + 1) * 128], QZ[:, m, pair, q0:q0 + nq], start=True, stop=True,
                            rd=[KZ, QZ], wr=[pS])
                        s.I("act", "activation", out=P_[m][:, kc, 0:nq], in_=pS[:, 0:nq], func=AF.Exp, scale=32.0 ** -0.5, rd=[pS], wr=[P_[m]])
                    out.append(f)
            return out

        def pv_steps(idx):
            h, q0, nq, kcs = items[idx]
            P_ = PT[idx % 2]
            po = [s.psacc(0), s.psacc(1)]
            out = []
            for m in range(2):
                for qs in range(nq // 128):
                    for i, kc in enumerate(kcs):
                        def f(m=m, qs=qs, i=i, kc=kc):
                            s.I("pe", "matmul", po[m][:, qs * 65:(qs + 1) * 65], P_[m][:, kc, qs * 128:(qs + 1) * 128], VD[:, kc, h, :], start=(i == 0), stop=(i == len(kcs) - 1),
                                rd=[P_[m], VD], wr=[po[m]])
                        out.append(f)
            return out

        def finish(idx):
            h, q0, nq, kcs = items[idx]
            po = [s.psacc(0), s.psacc(1)]
            nqs = nq // 128
            y_ = yd[idx % 2]
            pv0 = po[0][:, 0:nqs * 65].rearrange("p (q e) -> p q e", e=65)
            pv1 = po[1][:, 0:nqs * 65].rearrange("p (q e) -> p q e", e=65)
            s.I("dve", "reciprocal", rec[:, 0, 0:nqs, :], pv0[:, :, 64:65], rd=[po[0]], wr=[rec])
            s.I("dve", "reciprocal", rec[:, 1, 0:nqs, :], pv1[:, :, 64:65], rd=[po[1]], wr=[rec])
            s.I("dve", "tensor_tensor", out=o0[:, 0:nqs, :], in0=pv0[:, :, 0:64], in1=rec[:, 0, 0:nqs, :].to_broadcast([128, nqs, 64]), op=ALU.mult, rd=[po[0], rec], wr=[o0])
            s.I("dve", "tensor_tensor", out=o1[:, 0:nqs, :], in0=pv1[:, :, 0:64], in1=rec[:, 1, 0:nqs, :].to_broadcast([128, nqs, 64]), op=ALU.mult, rd=[po[1], rec], wr=[o1])
            s.I("dve", "scalar_tensor_tensor", out=o0[:, 0:nqs, :], in0=o1[:, 0:nqs, :], scalar=neglam, in1=o0[:, 0:nqs, :], op0=ALU.mult, op1=ALU.add, rd=[o1, o0, ls], wr=[o0])
            s.I("pool", "tensor_tensor", out=sq[:, 0:nqs, :], in0=o0[:, 0:nqs, :], in1=o0[:, 0:nqs, :], op=ALU.mult, rd=[o0], wr=[sq])
            s.I("dve", "tensor_reduce", out=st[:, 0:nqs, 0], in_=sq[:, 0:nqs, :], axis=AX.X, op=ALU.add, rd=[sq], wr=[st])
            s.I("act", "activation", out=st[:, 0:nqs, 1], in_=st[:, 0:nqs, 0], func=AF.Ln, scale=1.0 / 64, bias=epsb[:, 0:1], rd=[st, epsb], wr=[st])
            s.I("act", "activation", out=st[:, 0:nqs, 2], in_=st[:, 0:nqs, 1], func=AF.Exp, scale=-0.5, rd=[st], wr=[st])
            s.I("dve", "tensor_tensor", out=sq[:, 0:nqs, :], in0=o0[:, 0:nqs, :], in1=st[:, 0:nqs, 2:3].to_broadcast([128, nqs, 64]), op=ALU.mult, rd=[o0, st], wr=[sq])
            s.I("pool", "tensor_tensor", out=y_[:, 0:nqs, :], in0=sq[:, 0:nqs, :], in1=sub[:].to_broadcast([128, nqs, 64]), op=ALU.mult, rd=[sq, sub], wr=[y_])
            s.dma("sp", self.Yd.t.ap()[b, q0:q0 + nq, 256 + h * 64:256 + (h + 1) * 64].rearrange("(q p) d -> p q d", p=128), y_[:, 0:nqs, :], self.Yd, y_)


        for f in s_steps(0):
            f()
        for idx in range(len(items)):
            A = s_steps(idx + 1) if idx + 1 < len(items) else []
            B = pv_steps(idx)
            ratio = max(1, len(B) // max(1, len(A)))
            ai = 0
            for bi, fb in enumerate(B):
                if bi % ratio == 0 and ai < len(A):
                    A[ai]()
                    ai += 1
                fb()
            while ai < len(A):
                A[ai]()
                ai += 1
            finish(idx)

    def phase_ssm(self, l, b):
        s = self.s
        last = l == DEPTH - 1
        ntl = 16 if last else 18
        with ExitStack() as mid:
            def msb(name, shape, dt):
                s.uid += 1
                t = mid.enter_context(self.nc.sbuf_tensor("%s_%d" % (name, s.uid), list(shape), dt))
                return Buf(name, t, True)
            XS = msb("ssXS", [128, 18, 512], F32)
            BMT = msb("ssBMT", [128, 2, T], BF16)
            CMT = msb("ssCMT", [128, 2, T], BF16)
            BM = msb("ssBM", [128, 18, 2, 128], BF16)
            DT = msb("ssDT", [128, 18, 16], F32)
            LA = msb("ssLA", [128, 18, 16], F32)
            with s.phase():
                self.ssm_prep(l, b, XS, BMT, CMT, BM, DT, LA)
            with s.phase():
                self.ssm_scan(l, b, XS, BMT, CMT, BM, DT, LA, ntl)
            for bb in (XS, BMT, CMT, BM, DT, LA):
                if bb.dsem is not None:
                    s.dfree.append(bb.dsem)
                    bb.dsem = None

    def ssm_prep(self, l, b, XS, BMT, CMT, BM, DT, LA):
        s = self.s
        w = self.load_w("wss", self.w_in, self.w_in.t.ap()[l, :, 1536:3088].rearrange("(k p) n -> p k n", p=128), [128, 8, 1552])
        dtb = self.bcast_row("ssdtb", self.dt_bias, l * 16, 16)
        alog = self.bcast_row("ssalog", self.a_log, l * 16, 16)
        A = s.sb("ssA", [128, 16], F32)
        s.I("act", "activation", out=A[:], in_=alog[:], func=AF.Exp, rd=[alog], wr=[A])
        s.I("dve", "tensor_scalar", out=A[:], in0=A[:], scalar1=-1.0, scalar2=None, op0=ALU.mult, rd=[A], wr=[A])
        cw = s.sb("sscw", [128, 8, 5], F32)
        cb = s.sb("sscb", [128, 8], F32)
        s.dma("sp", cw[:], self.conv_wc.t.ap()[l], cw, self.conv_wc)
        s.dma("sp", cb[:], self.conv_bc.t.ap()[l], cb, self.conv_bc)
        zs = [s.sb("sszs%d" % i, [128, 512], F32) for i in range(2)]
        tmp = s.sb("sstmp", [128, 18, 16], F32)
        for t in range(18):
            pz = s.ps()
            pd = s.ps()
            for k in range(8):
                s.I("pe", "matmul", pz[:, :], self.hT[:, k, t * 128:(t + 1) * 128], w[:, k, 0:512], start=(k == 0), stop=(k == 7), rd=[self.hT, w], wr=[pz])
            for k in range(8):
                s.I("pe", "matmul", pd[:, 0:16], self.hT[:, k, t * 128:(t + 1) * 128], w[:, k, 1536:1552], start=(k == 0), stop=(k == 7), rd=[self.hT, w], wr=[pd])
            z_ = zs[t % 2]
            s.I("act", "activation", out=z_[:], in_=pz[:, :], func=AF.Silu, rd=[pz], wr=[z_])
            s.dma("sp", self.Zd.t.ap()[t * 128:(t + 1) * 128, :], z_[:], self.Zd, z_)
            s.I("dve", "tensor_tensor", out=tmp[:, t, :], in0=pd[:, 0:16], in1=dtb[:], op=ALU.add, rd=[pd, dtb], wr=[tmp])
        s.I("act", "activation", out=tmp[:], in_=tmp[:], func=AF.Exp, rd=[tmp], wr=[tmp])
        s.I("act", "activation", out=DT[:], in_=tmp[:], func=AF.Ln, bias=1.0, scale=1.0, rd=[tmp], wr=[DT])
        s.I("dve", "tensor_tensor", out=LA[:], in0=DT[:], in1=A[:].rearrange("p (o d) -> p o d", o=1).to_broadcast([128, 18, 16]), op=ALU.mult, rd=[DT, A], wr=[LA])
        Gl = [s.sb("ssGl%d" % i, [128, L + 4], F32) for i in range(1)]
        Gc = [s.sb("ssGc%d" % i, [128, LC + 4], F32) for i in range(1)]
        acc = [s.sb("ssacc%d" % i, [128, T], F32) for i in range(1)]
        for i in range(1):
            s.I("pool", "memset", Gl[i][:], 0.0, wr=[Gl[i]])
            s.I("pool", "memset", Gc[i][:], 0.0, wr=[Gc[i]])
        groups = [(0, 512), (512, 512), (1024, 512), (1536, 512), (2048, 256)]
        for c in range(8):
            gl, gc, a = Gl[0], Gc[0], acc[0]
            for (t0, n) in groups:
                p = s.ps()
                for k in range(8):
                    s.I("pe", "matmul", p[:, 0:n], w[:, k, 512 + c * 128:512 + (c + 1) * 128], self.hT[:, k, t0:t0 + n], start=(k == 0), stop=(k == 7), rd=[w, self.hT], wr=[p])
                if t0 < L:
                    s.I("act", "activation", out=gl[:, 2 + t0:2 + t0 + n], in_=p[:, 0:n], func=AF.Copy, rd=[p], wr=[gl])
                else:
                    s.I("act", "activation", out=gc[:, 2:2 + n], in_=p[:, 0:n], func=AF.Copy, rd=[p], wr=[gc])
            for (gb, o0, n) in ((gl, 0, L), (gc, L, LC)):
                s.I("dve", "tensor_scalar", out=a[:, o0:o0 + n], in0=gb[:, 0:n], scalar1=cw[:, c, 0:1], scalar2=None, op0=ALU.mult, rd=[gb, cw], wr=[a])
                for k in range(1, 5):
                    s.I("dve", "scalar_tensor_tensor", out=a[:, o0:o0 + n], in0=gb[:, k:k + n], scalar=cw[:, c, k:k + 1], in1=a[:, o0:o0 + n],
                        op0=ALU.mult, op1=ALU.add, rd=[gb, cw, a], wr=[a])
            if c < 4 or c in (4, 5):
                s.I("act", "activation", out=a[:], in_=a[:], func=AF.Silu, bias=cb[:, c:c + 1], scale=1.0, rd=[a, cb], wr=[a])
            if c < 4:
                for g4 in range(5):
                    tiles = list(range(4 * g4, min(4 * g4 + 4, 18)))
                    p = s.ps()
                    for ti, t in enumerate(tiles):
                        s.I("pe", "transpose", p[:, ti * 128:(ti + 1) * 128], a[:, t * 128:(t + 1) * 128], self.ident[:], rd=[a, self.ident], wr=[p])
                    s.I("act", "activation", out=XS[:, tiles[0]:tiles[0] + len(tiles), c * 128:(c + 1) * 128], in_=p[:, 0:len(tiles) * 128].rearrange("p (t d) -> p t d", d=128),
                        func=AF.Copy, rd=[p], wr=[XS])
            elif c in (4, 5):
                g = c - 4
                s.I("pool", "tensor_copy", BMT[:, g, :], a[:], rd=[a], wr=[BMT])
                for g4 in range(5):
                    tiles = list(range(4 * g4, min(4 * g4 + 4, 18)))
                    p = s.ps()
                    for ti, t in enumerate(tiles):
                        s.I("pe", "transpose", p[:, ti * 128:(ti + 1) * 128], a[:, t * 128:(t + 1) * 128], self.ident[:], rd=[a, self.ident], wr=[p])
                    s.I("act", "activation", out=BM[:, tiles[0]:tiles[0] + len(tiles), g, :], in_=p[:, 0:len(tiles) * 128].rearrange("p (t d) -> p t d", d=128),
                        func=AF.Copy, rd=[p], wr=[BM])
            else:
                g = c - 6
                s.I("act", "activation", out=CMT[:, g, :], in_=a[:], func=AF.Silu, bias=cb[:, c:c + 1], scale=1.0, rd=[a, cb], wr=[CMT])

    def ssm_scan(self, l, b, XS, BMT, CMT, BM, DT, LA, ntl):
        s = self.s
        TRI = s.sb("ssTRI", [128, 5, 128], F32)
        s.dma("sp", TRI[:], self.tri_d.t.ap().rearrange("a p n -> p a n"), TRI, self.tri_d)
        SL, SU, LE, GE, ON = (TRI[:, i, :] for i in range(5))
        Y = s.sb("ssY", [128, 18, 512], F32)
        s.I("pool", "memset", Y[:], 0.0, wr=[Y])
        ST = [[s.sb("ssST%d%d" % (d, g), [128, 4, 64], F32) for g in range(2)] for d in range(2)]
        STb = [[s.sb("ssSTb%d%d" % (d, g), [128, 256], BF16) for g in range(2)] for d in range(2)]
        for d in range(2):
            for g in range(2):
                s.I("pool", "memset", ST[d][g][:], 0.0, wr=[ST[d][g]])
                s.I("pool", "memset", STb[d][g][:], 0.0, wr=[STb[d][g]])
        NEGM = s.sb("ssNEGM", [128, 2, 128], BF16)
        s.dma("pool", NEGM[:], self.negm_d.t.ap(), NEGM, self.negm_d)
        E12 = s.sb("ssE12", [128, 2, 8, 128], BF16)
        s.dma("pool", E12[:], self.sel_d.t.ap(), E12, self.sel_d)
        EX = [s.sb("ssEX%d" % i, [128, 3, 8], F32) for i in range(3)]
        CST = [s.sb("ssCST%d" % i, [128, 128], BF16) for i in range(3)]
        R1 = [s.sb("ssR1%d" % i, [8, 128], F32) for i in range(3)]
        for i in range(3):
            s.I("pool", "memset", CST[i][:], 0.0, wr=[CST[i]])
        XDT = [s.sb("ssXDT%d" % i, [128, 8, 64], BF16) for i in range(3)]
        XDD = [s.sb("ssXDD%d" % i, [128, 8, 64], BF16) for i in range(3)]
        cbs = [s.sb("sscbs%d" % i, [128, 1, 128], F32) for i in range(6)]
        Lt = [s.sb("ssLt%d" % i, [128, 4, 128], F32) for i in range(2)] * 2
        Wm = [s.sb("ssW%d" % i, [128, 4, 128], BF16) for i in range(4)]
        tm1 = [s.sb("sstm1%d" % i, [128, 4, 64], F32) for i in range(4)]
        yt = [s.sb("ssyt%d" % i, [128, 4, 64], F32) for i in range(4)]
        tm2 = [s.sb("sstm2%d" % i, [128, 4, 64], F32) for i in range(4)]
        order = [(0, 16), (1, 17), (0, 17), (1, 16)]
        for i in range(16):
            order.append((0, i))
            order.append((1, 15 - i))
        ctxs = {}

        def stageA(k):
            dr, t = order[k]
            want_y = t < ntl
            ex, cst, xdt, xdd = EX[k % 3], CST[k % 3], XDT[k % 3], XDD[k % 3]
            la8 = LA[:, t, dr * 8:(dr + 1) * 8]
            pe_ = s.ps()
            m1, m2 = (SL, LE) if dr == 0 else (SU, GE)
            s.I("pe", "matmul", pe_[:, 0:8], m1, la8, start=True, stop=True, rd=[TRI, LA], wr=[pe_])
            s.I("pe", "matmul", pe_[:, 8:16], m2, la8, start=True, stop=True, rd=[TRI, LA], wr=[pe_])
            s.I("pe", "matmul", pe_[:, 16:24], ON, la8, start=True, stop=True, rd=[TRI, LA], wr=[pe_])
            s.I("act", "activation", out=ex[:].rearrange("p a h -> p (a h)"), in_=pe_[:, 0:24], func=AF.Exp, rd=[pe_], wr=[ex])
            s.I("dve", "tensor_tensor", out=xdt[:], in0=XS[:, t, :].rearrange("p (h d) -> p h d", d=64),
                in1=DT[:, t, dr * 8:(dr + 1) * 8].rearrange("p (h o) -> p h o", o=1).to_broadcast([128, 8, 64]), op=ALU.mult, rd=[XS, DT], wr=[xdt])
            s.I("dve", "tensor_tensor", out=xdd[:], in0=xdt[:], in1=ex[:, 0, :].rearrange("p (h o) -> p h o", o=1).to_broadcast([128, 8, 64]), op=ALU.mult,
                rd=[xdt, ex], wr=[xdd])
            cb2 = []
            if want_y:
                pcs = s.ps()
                s.I("pe", "matmul", pcs[0:8, 0:128], la8, m2, start=True, stop=True, rd=[LA, TRI], wr=[pcs])
                r1 = R1[k % 3]
                s.I("act", "activation", out=cst[0:8, :], in_=pcs[0:8, 0:128], func=AF.Copy, rd=[pcs], wr=[cst])
                s.I("dve", "tensor_tensor", out=r1[:], in0=pcs[0:8, 0:128], in1=cst[0:8, :], op=ALU.subtract, rd=[pcs, cst], wr=[r1])
                s.I("act", "activation", out=cst[32:40, :], in_=r1[:], func=AF.Copy, rd=[r1], wr=[cst])
                for g in range(2):
                    cb_ = cbs[(2 * k + g) % 6]
                    pc = s.ps()
                    s.I("pe", "matmul", pc[:, 0:128], BMT[:, g, t * 128:(t + 1) * 128], CMT[:, g, t * 128:(t + 1) * 128], start=True, stop=True, rd=[BMT, CMT], wr=[pc])
                    s.I("act", "activation", out=cb_[:, 0, :], in_=pc[:, 0:128], func=AF.Copy, rd=[pc], wr=[cb_])
                    cb2.append(cb_)
            ctxs[k] = (ex, cst, xdt, xdd, cb2)

        def stageB(k):
            dr, t = order[k]
            want_y = t < ntl
            ex, cst, xdt, xdd, cb2 = ctxs[k]
            ws = []
            if want_y:
                for g in range(2):
                    lt, wm = Lt[(2 * k + g) % 4], Wm[(2 * k + g) % 4]
                    pd_ = s.ps()
                    for e in range(4):
                        h8 = g * 4 + e
                        o_ = pd_[:, e * 128:(e + 1) * 128]
                        s.I("pe", "matmul", o_, E12[:, 0, h8, :], cst[:], start=True, stop=False, rd=[E12, cst], wr=[pd_])
                        s.I("pe", "matmul", o_, cst[:], E12[:, 1, h8, :], start=False, stop=False, rd=[E12, cst], wr=[pd_])
                        s.I("pe", "matmul", o_, self.identb[:], NEGM[:, dr, :], start=False, stop=True, rd=[self.identb, NEGM], wr=[pd_])
                    s.I("act", "activation", out=lt[:].rearrange("p e i -> p (e i)"), in_=pd_[:, :], func=AF.Exp, rd=[pd_], wr=[lt])
                    s.I("dve", "tensor_tensor", out=wm[:], in0=lt[:], in1=cb2[g][:].to_broadcast([128, 4, 128]), op=ALU.mult, rd=[lt, cb2[g]], wr=[wm])
                    ws.append(wm)
            ctxs[k] = (ex, cst, xdt, xdd, ws)

        def stageC(k):
            dr, t = order[k]
            want_y = t < ntl
            ex, cst, xdt, xdd, ws = ctxs.pop(k)
            for g in range(2):
                if want_y:
                    py = s.psacc(g)
                    for e in range(4):
                        s.I("pe", "matmul", py[:, e * 64:(e + 1) * 64], ws[g][:, e, :], xdt[:, g * 4 + e, :], start=True, stop=True, rd=[ws[g], xdt], wr=[py])
                    po = s.ps()
                    s.I("pe", "matmul", po[:, 0:256], CMT[:, g, t * 128:(t + 1) * 128], STb[dr][g][:], start=True, stop=True, rd=[CMT, STb[dr][g]], wr=[po])
                    a_, y_ = tm1[(2 * k + g) % 4], yt[(2 * k + g) % 4]
                    s.I("dve", "tensor_tensor", out=a_[:], in0=po[:, 0:256].rearrange("p (h d) -> p h d", d=64),
                        in1=ex[:, 1, g * 4:(g + 1) * 4].rearrange("p (h o) -> p h o", o=1).to_broadcast([128, 4, 64]), op=ALU.mult, rd=[po, ex], wr=[a_])
                    s.I("dve", "tensor_tensor", out=y_[:], in0=py[:, 0:256].rearrange("p (h d) -> p h d", d=64), in1=a_[:], op=ALU.add, rd=[py, a_], wr=[y_])
                    yv = Y[:, t, g * 256:(g + 1) * 256].rearrange("p (h d) -> p h d", d=64)
                    s.I("pool", "tensor_tensor", out=yv, in0=yv, in1=y_[:], op=ALU.add, rd=[Y, y_], wr=[Y])
                pst = s.ps()
                s.I("pe", "matmul", pst[:, 0:256], BM[:, t, g, :], xdd[:, g * 4:(g + 1) * 4, :], start=True, stop=True, rd=[BM, xdd], wr=[pst])
                c_ = tm2[(2 * k + g) % 4]
                s.I("pool", "tensor_tensor", out=c_[:], in0=ST[dr][g][:], in1=ex[:, 2, g * 4:(g + 1) * 4].rearrange("p (h o) -> p h o", o=1).to_broadcast([128, 4, 64]),
                    op=ALU.mult, rd=[ST[dr][g], ex], wr=[c_])
                s.I("dve", "tensor_tensor", out=ST[dr][g][:], in0=pst[:, 0:256].rearrange("p (h d) -> p h d", d=64), in1=c_[:], op=ALU.add, rd=[pst, c_], wr=[ST[dr][g]])
                s.I("act", "activation", out=STb[dr][g][:], in_=ST[dr][g][:].rearrange("p h d -> p (h d)"), func=AF.Copy, rd=[ST[dr][g]], wr=[STb[dr][g]])

        n = len(order)
        for k in range(n + 2):
            if k < n:
                stageA(k)
            if 1 <= k <= n:
                stageB(k - 1)
            if k >= 2:
                stageC(k - 2)
        dsk = self.bcast_row("ssdsk", self.ssm_d, l * 8, 8)
        ng = self.bcast_row("ssng", self.ssm_norm, l * 512, 512)
        zt = [s.sb("sszt%d" % i, [128, 512], F32) for i in range(3)]
        u = [s.sb("ssu%d" % i, [128, 512], F32) for i in range(3)]
        junks = [s.sb("ssjunk%d" % i, [128, 512], BF16) for i in range(2)]
        sts = [s.sb("ssfst%d" % i, [128, 1, 3], F32) for i in range(4)]
        def zload(t):
            s.dma("sp", zt[t % 3][:], self.Zd.t.ap()[t * 128:(t + 1) * 128, :], zt[t % 3], self.Zd)
        zload(0)
        zload(1)
        for t in range(ntl):
            z_, u_ = zt[t % 3], u[t % 3]
            st, junk = sts[t % 4], junks[t % 2]
            if t + 2 < ntl:
                zload(t + 2)
            s.I("dve", "tensor_tensor", out=u_[:].rearrange("p (h d) -> p h d", d=64), in0=XS[:, t, :].rearrange("p (h d) -> p h d", d=64),
                in1=dsk[:].rearrange("p (h o) -> p h o", o=1).to_broadcast([128, 8, 64]), op=ALU.mult, rd=[XS, dsk], wr=[u_])
            s.I("pool", "tensor_tensor", out=u_[:], in0=u_[:], in1=Y[:, t, :], op=ALU.add, rd=[u_, Y], wr=[u_])
            s.I("dve", "tensor_tensor", out=u_[:], in0=u_[:], in1=z_[:], op=ALU.mult, rd=[u_, z_], wr=[u_])
            s.I("act", "activation", out=junk[:], in_=u_[:], func=AF.Square, accum_out=st[:, 0, 0:1], rd=[u_], wr=[junk, st])
            s.I("act", "activation", out=st[:, 0, 1:2], in_=st[:, 0, 0:1], func=AF.Sqrt, scale=1.0 / 512, bias=EPS, rd=[st], wr=[st])
            s.I("dve", "reciprocal", st[:, 0, 2:3], st[:, 0, 1:2], rd=[st], wr=[st])
            s.I("dve", "scalar_tensor_tensor", out=u_[:], in0=u_[:], scalar=st[:, 0, 2:3], in1=ng[:], op0=ALU.mult, op1=ALU.mult, rd=[u_, st, ng], wr=[u_])
            s.dma("sp", self.Yd.t.ap()[b, t * 128:(t + 1) * 128, 512:1024], u_[:], self.Yd, u_)


def _consts():
    ident = np.eye(128, dtype=np.float32)
    i = np.arange(128)
    SL = (i[:, None] > i[None, :]).astype(np.float32)
    SU = (i[:, None] < i[None, :]).astype(np.float32)
    LE = (i[:, None] <= i[None, :]).astype(np.float32)
    GE = (i[:, None] >= i[None, :]).astype(np.float32)
    ON = np.ones((128, 128), np.float32)
    tri = np.stack([SL, SU, LE, GE, ON]).astype(np.float32)
    qc = np.arange(64)
    ws = np.clip(qc - 8, 0, 48)
    kc = np.arange(64)
    ok = (kc[:, None] >= ws[None, :]) & (kc[:, None] < ws[None, :] + 16)
    m = np.where(ok, 0.0, NEG).astype(np.float32)
    na_mask = np.concatenate([m, m], axis=0)
    per_axis = 16
    inv_freq = (10000.0 ** (-np.arange(0, per_axis, 2, dtype=np.float32) / per_axis)).astype(np.float32)
    t = np.arange(L)
    pos = np.stack([t // 64, t % 64], axis=-1).astype(np.float32)
    ang = pos[:, :, None] * inv_freq
    ang = np.concatenate([ang, ang], axis=-1).reshape(L, 32)
    cos = np.cos(ang).astype(np.float32)
    sin = np.sin(ang).astype(np.float32).reshape(L, 2, 2, 8).copy()
    sin[:, :, 0, :] *= -1.0
    negm = np.stack([np.where(i[None, :] < i[:, None], NEG, 0.0), np.where(i[None, :] > i[:, None], NEG, 0.0)], axis=1).astype(np.float32)
    sel = np.zeros((128, 2, 8, 128), np.float32)
    for h in range(8):
        sel[h, 0, h, :] = 1.0
        sel[32 + h, 0, h, :] = 1.0
    sel[:, 1] = -sel[:, 0]
    return ident, tri, na_mask, cos, sin.reshape(L, 32), negm, sel


def make_in_maps(inp):
    f = lambda a: np.ascontiguousarray(np.asarray(a, dtype=np.float32))
    ident, tri, na_mask, cos, sin, negm, sel = _consts()
    colv = lambda v, n: f(np.asarray(v).reshape(DEPTH, n, 128).transpose(0, 2, 1))
    idx = np.clip(np.arange(64)[:, None] - np.arange(64)[None, :], -15, 15) + 15
    shared = {
        "w_ada": f(inp["w_ada"]), "b_ada": f(inp["b_ada"]),
        "g_mixc": colv(inp["g_mix"], 8), "g_ffnc": colv(inp["g_ffn"], 8),
        "w_in": f(inp["w_in"]), "w_out": f(inp["w_out"]), "w_up": f(inp["ffn_w_up"]), "w_down": f(inp["ffn_w_down"]),
        "na_qg": f(inp["na_q_gain"]), "na_kg": f(inp["na_k_gain"]),
        "rpb_t": f(np.asarray(inp["na_rpb"])[..., idx]), "na_mask": na_mask,
        "df_qg": f(inp["df_q_gain"]), "df_kg": f(inp["df_k_gain"]), "df_lam": f(inp["df_lambda"]), "df_subln": f(inp["df_subln"]),
        "rope_cos": cos, "rope_sin": sin,
        "conv_wc": f(np.asarray(inp["ssm_conv_w"]).reshape(DEPTH, 5, 8, 128).transpose(0, 3, 2, 1)),
        "conv_bc": colv(inp["ssm_conv_b"], 8),
        "dt_bias": f(np.asarray(inp["ssm_dt_bias"]).reshape(DEPTH, 16)), "a_log": f(np.asarray(inp["ssm_a_log"]).reshape(DEPTH, 16)),
        "ssm_d": f(inp["ssm_d"]), "ssm_norm": f(inp["ssm_norm"]),
        "fconv_wc": f(np.asarray(inp["ffn_conv_w"]).reshape(DEPTH, 3, NF, 128).transpose(0, 3, 2, 1)),
        "fconv_bc": colv(inp["ffn_conv_b"], NF),
        "ident": ident, "tri": tri, "negm": negm, "sel": sel,
    }
    maps = []
    x, c, ctx, c_ctx = (np.asarray(inp[k]) for k in ("x", "c", "ctx", "c_ctx"))
    for i in range(NCORES):
        sl = slice(i * NB, (i + 1) * NB)
        c3 = np.concatenate([c[sl], c_ctx[None, :]], axis=0)
        cT = f(c3.reshape(3, 8, 128).transpose(2, 1, 0))
        m = dict(shared)
        m.update({"x": f(x[sl]), "ctx": f(ctx[sl]), "cT": cT})
        maps.append(m)
    return maps


_NC_CACHE = {}


def kernel(**inputs):
    if "nc" not in _NC_CACHE:
        _NC_CACHE["nc"] = Builder().build()
    nc = _NC_CACHE["nc"]
    maps = make_in_maps(inputs)
    res = run_bass_kernel_spmd(nc, maps, core_ids=list(range(NCORES)))
    return np.concatenate([np.asarray(r["out"]) for r in res.results], axis=0).astype(np.float32)
```

```python
import math
from contextlib import ExitStack

import numpy as np
import concourse.bass as bass
import concourse.mybir as mybir
from concourse.bass_utils import run_bass_kernel_spmd

F32 = mybir.dt.float32
BF16 = mybir.dt.bfloat16
AF = mybir.ActivationFunctionType
ALU = mybir.AluOpType
AX = mybir.AxisListType

NCORES = 8
NB = 2
L = 2048
LC = 256
T = L + LC
D = 1024
DEPTH = 2
DFF = 2816
NF = DFF // 128
EPS = 1e-6
NEG = -30000.0


class Buf:
    __slots__ = ("name", "w", "r", "dsem", "t", "persist")

    def __init__(self, name, t=None, persist=False):
        self.name = name
        self.w = {}
        self.r = {}
        self.dsem = None
        self.t = t
        self.persist = persist

    def __getitem__(self, idx):
        return self.t[idx]


class Sched:
    ENG = ("pe", "act", "dve", "pool", "sp")

    def __init__(self, nc, es, ndsem=48):
        self.nc = nc
        self.ges = es
        self.es = es
        self.eng = {"pe": nc.tensor, "act": nc.scalar, "dve": nc.vector, "pool": nc.gpsimd, "sp": nc.sync}
        self.sem = {k: es.enter_context(nc.semaphore("c_" + k)) for k in self.ENG}
        self.cnt = {k: 0 for k in self.ENG}
        self.seen = {k: {} for k in self.ENG}
        self.prog = {k: [] for k in self.ENG}
        self.dsems = [es.enter_context(nc.semaphore("d%d" % i)) for i in range(ndsem)]
        self.dval = [0] * ndsem
        self.dfree = list(range(ndsem))
        self.phase_bufs = []
        self.uid = 0
        self.ps_rr = 0
        self.PS = []

    def sb(self, name, shape, dt, persist=False):
        self.uid += 1
        es = self.ges if persist else self.es
        t = es.enter_context(self.nc.sbuf_tensor("%s_%d" % (name, self.uid), list(shape), dt))
        b = Buf(name, t, persist)
        if not persist:
            self.phase_bufs.append(b)
        return b

    def dram(self, name, shape, dt, kind="Internal"):
        t = self.nc.dram_tensor(name, list(shape), dt, kind=kind)
        return Buf(name, t, True)

    def psum_init(self):
        for i in range(8):
            t = self.ges.enter_context(self.nc.psum_tensor("psb%d" % i, [128, 512], F32))
            self.PS.append(Buf("ps%d" % i, t, True))

    def ps(self):
        b = self.PS[self.ps_rr % 6]
        self.ps_rr += 1
        return b

    def psacc(self, i):
        return self.PS[6 + (i % 2)]

    def _deps(self, e, reads, writes):
        d = {}
        own = "c_" + e

        def add(tokdict, is_read_set):
            for key, (sem, val) in tokdict.items():
                if key == own and (e == "pe" or is_read_set):
                    continue
                if d.get(key, (None, 0))[1] < val:
                    d[key] = (sem, val)

        for b in reads:
            add(b.w, False)
        for b in writes:
            add(b.w, False)
            add(b.r, True)
        return d

    def _wait(self, e, d):
        seen = self.seen[e]
        for key, (sem, val) in d.items():
            if seen.get(key, 0) >= val:
                continue
            self.prog[e].append(("w", sem, val))
            seen[key] = val

    def I(self, e, name, *args, rd=(), wr=(), **kw):
        d = self._deps(e, rd, wr)
        self._wait(e, d)
        self.cnt[e] += 1
        self.prog[e].append(("i", name, args, kw, self.sem[e], 1))
        key = "c_" + e
        tok = (self.sem[e], self.cnt[e])
        for b in rd:
            b.r[key] = tok
        for b in wr:
            b.w[key] = tok

    def dma(self, e, out_ap, in_ap, dst, src, **kw):
        if dst.dsem is None:
            dst.dsem = self.dfree.pop()
        i = dst.dsem
        d = self._deps(e, [src], [dst])
        self._wait(e, d)
        self.dval[i] += 16
        self.prog[e].append(("i", "dma_start", (), dict(out=out_ap, in_=in_ap, **kw), self.dsems[i], 16))
        key = "d%d" % i
        tok = (self.dsems[i], self.dval[i])
        src.r[key] = tok
        dst.w[key] = tok

    def drain(self):
        d = {}
        for i, v in enumerate(self.dval):
            if v:
                d["d%d" % i] = (self.dsems[i], v)
        self._wait("sp", d)

    def emit(self):
        self.drain()
        prog = self.prog
        with self.nc.Block() as block:
            def mk(e):
                def body(g):
                    for it in prog[e]:
                        if it[0] == "w":
                            g.wait_ge(it[1], it[2])
                        else:
                            getattr(g, it[1])(*it[2], **it[3]).then_inc(it[4], it[5])
                return body
            block.tensor(mk("pe"))
            block.scalar(mk("act"))
            block.vector(mk("dve"))
            block.gpsimd(mk("pool"))
            block.sync(mk("sp"))
        self.prog = {k: [] for k in self.ENG}
        for b in self.phase_bufs:
            if b.dsem is not None:
                self.dfree.append(b.dsem)
                b.dsem = None
        self.phase_bufs = []

    class _Phase:
        def __init__(self, s):
            self.s = s

        def __enter__(self):
            self.es = ExitStack()
            self.es.__enter__()
            self.s.es = self.es
            return self

        def __exit__(self, *a):
            if a[0] is None:
                self.s.emit()
            self.s.es = self.s.ges
            return self.es.__exit__(*a)

    def phase(self):
        return Sched._Phase(self)


def dap(buf, off, dims):
    return bass.AP(buf.t, off, [list(d) for d in dims])


class Builder:
    def __init__(self, dbg=None):
        self.dbg = dbg or {}
        self.nc = bass.Bass("TRN2", target_bir_lowering=False)
        self.outs = []
        self.pre = {}
        self.scoped = []

    def declare(self, s):
        I = lambda n, sh, dt=F32: s.dram(n, sh, dt, kind="ExternalInput")
        self.x = I("x", [NB, L, D])
        self.ctx = I("ctx", [NB, LC, D])
        self.cT = I("cT", [128, 8, 3])
        self.w_ada = I("w_ada", [DEPTH, D, 6 * D])
        self.b_ada = I("b_ada", [DEPTH, 6 * D])
        self.g_mixc = I("g_mixc", [DEPTH, 128, 8])
        self.g_ffnc = I("g_ffnc", [DEPTH, 128, 8])
        self.w_in = I("w_in", [DEPTH, D, 3088])
        self.w_out = I("w_out", [DEPTH, D, D])
        self.w_up = I("w_up", [DEPTH, D, 2 * DFF])
        self.w_down = I("w_down", [DEPTH, DFF, D])
        self.na_qg = I("na_qg", [DEPTH, 64])
        self.na_kg = I("na_kg", [DEPTH, 64])
        self.rpb_t = I("rpb_t", [DEPTH, 4, 15, 64, 64])
        self.na_mask = I("na_mask", [128, 64])
        self.df_qg = I("df_qg", [DEPTH, 32])
        self.df_kg = I("df_kg", [DEPTH, 32])
        self.df_lam = I("df_lam", [DEPTH, 4, 32])
        self.df_subln = I("df_subln", [DEPTH, 64])
        self.rope_cos = I("rope_cos", [L, 32])
        self.rope_sin = I("rope_sin", [L, 32])
        self.conv_wc = I("conv_wc", [DEPTH, 128, 8, 5])
        self.conv_bc = I("conv_bc", [DEPTH, 128, 8])
        self.dt_bias = I("dt_bias", [DEPTH, 16])
        self.a_log = I("a_log", [DEPTH, 16])
        self.ssm_d = I("ssm_d", [DEPTH, 8])
        self.ssm_norm = I("ssm_norm", [DEPTH, 512])
        self.fconv_wc = I("fconv_wc", [DEPTH, 128, NF, 3])
        self.fconv_bc = I("fconv_bc", [DEPTH, 128, NF])
        self.ident_d = I("ident", [128, 128])
        self.tri_d = I("tri", [5, 128, 128])
        self.negm_d = I("negm", [128, 2, 4, 128])
        self.sel_d = I("sel", [128, 2, 8, 128])
        self.out = s.dram("out", [NB, L, D], F32, kind="ExternalOutput")
        dk = "ExternalOutput" if self.dbg.get("dump") else "Internal"
        self.modrow_d = s.dram("modrow_d", [DEPTH, 3, 6 * D], F32, kind=dk)
        self.XA = [s.dram("XA%d" % l, [NB, T, D], F32, kind=dk) for l in range(DEPTH)]
        self.XB = s.dram("XB0", [NB, T, D], F32, kind=dk)
        self.Yd = s.dram("Yd", [NB, T, D], F32, kind=dk)
        self.Zd = s.dram("Zd", [T, 512], F32, kind=dk)
        self.ATd = s.dram("ATd", [NF, 128, T], BF16, kind=dk)
        self.hTd = s.dram("hTd", [128, 8, T], BF16, kind=dk) if self.dbg.get("dump") else None
        self.Yin = I("Yin", [DEPTH, NB, T, D]) if self.dbg.get("feed_y") else None

    def build(self):
        nc = self.nc
        with ExitStack() as es:
            s = Sched(nc, es)
            self.s = s
            self.declare(s)
            s.psum_init()
            self.ident = s.sb("ident", [128, 128], F32, persist=True)
            self.identb = s.sb("identb", [128, 128], BF16, persist=True)
            self.hT = s.sb("hT", [128, 8, T], BF16, persist=True)
            self.AB = [[s.sb("AB%d%d" % (l, i), [128, 8, 3], F32, persist=True) for i in range(4)] for l in range(DEPTH)]
            self.BT = s.sb("BT", [128, 4, 14, 64], BF16, persist=True)
            ph = self.dbg.get("phases")
            with s.phase():
                s.dma("sp", self.ident[:], self.ident_d.t.ap(), self.ident, self.ident_d)
                s.I("act", "activation", out=self.identb[:], in_=self.ident[:], func=AF.Copy, rd=[self.ident], wr=[self.identb])
                self.phase_adaln()
            for l in self.dbg.get("layers", range(DEPTH)):
                last = l == DEPTH - 1
                nt = 16 if last else 18
                if ph is None or "na" in ph:
                    with s.phase():
                        self.phase_na_bias(l)
                for b in range(NB):
                    if l == 0:
                        src = lambda t, b=b: (self.x.t.ap()[b, t * 128:(t + 1) * 128, :], self.x) if t < 16 else \
                            (self.ctx.t.ap()[b, (t - 16) * 128:(t - 15) * 128, :], self.ctx)
                    else:
                        src = lambda t, b=b: (self.XB.t.ap()[b, t * 128:(t + 1) * 128, :], self.XB)
                    full = ph is None
                    with ExitStack() as sc1:
                        wdf_buf = self.alloc_scoped(sc1, "wdf", [128, 8, 768]) if full else None
                        with s.phase():
                            if full:
                                self.pre["wna"] = self.w_na(l)
                            self.phase_norm(src, self.AB[l][0], self.AB[l][1], b, 18)
                            if self.dbg.get("dump") and l == self.dbg.get("dump_l", 0) and b == 0 and self.dbg.get("dump_h") == "mix":
                                s.dma("sp", self.hTd.t.ap(), self.hT[:], self.hTd, self.hT)
                            if ph is None or "na" in ph:
                                if full:
                                    self.pre["wdf"] = self.w_df(l, wdf_buf)
                                self.phase_na(l, b)
                        if ph is None or "df" in ph:
                            with s.phase():
                                self.phase_df(l, b)
                        self.release_scoped()
                    if ph is None or "ssm" in ph:
                        self.phase_ssm(l, b)
                    if ph is None or "out" in ph:
                        with s.phase():
                            self.phase_out(l, b, src, nt)
                    if ph is None or "ffn" in ph:
                        srcA = lambda t, b=b, l=l: (self.XA[l].t.ap()[b, t * 128:(t + 1) * 128, :], self.XA[l])
                        with s.phase():
                            self.phase_norm(srcA, self.AB[l][2], self.AB[l][3], b, nt)
                        with ExitStack() as sc2:
                            wdn_buf = self.alloc_scoped(sc2, "wdn", [128, NF, D])
                            with s.phase():
                                self.pre["wdn"] = self.w_dn(l, wdn_buf)
                                self.phase_ffn_up(l, b, nt)
                            with s.phase():
                                self.phase_ffn_down(l, b, srcA, nt)
                            self.release_scoped()
        return nc

    def phase_adaln(self):
        s = self.s
        cT = s.sb("cT", [128, 8, 3], F32)
        siluT = s.sb("siluT", [128, 8, 3], F32)
        s.dma("sp", cT[:], self.cT.t.ap(), cT, self.cT)
        s.I("act", "activation", out=siluT[:], in_=cT[:], func=AF.Silu, rd=[cT], wr=[siluT])
        wa = [s.sb("wa%d" % i, [128, 8, 512], F32) for i in range(3)]
        for l in range(DEPTH):
            brow = s.sb("brow%d" % l, [3, 6 * D], F32)
            modrow = s.sb("modrow%d" % l, [3, 6 * D], F32)
            s.dma("sp", brow[:], dap(self.b_ada, l * 6 * D, [[0, 3], [1, 6 * D]]), brow, self.b_ada)
            for j in range(12):
                w = wa[j % 3]
                s.dma("sp", w[:], self.w_ada.t.ap()[l, :, j * 512:(j + 1) * 512].rearrange("(k p) n -> p k n", p=128), w, self.w_ada)
                pm = s.ps()
                for k in range(8):
                    s.I("pe", "matmul", pm[0:3, :], siluT[:, k, :], w[:, k, :], start=(k == 0), stop=(k == 7), rd=[siluT, w], wr=[pm])
                s.I("dve", "tensor_tensor", out=modrow[:, j * 512:(j + 1) * 512], in0=pm[0:3, :], in1=brow[:, j * 512:(j + 1) * 512], op=ALU.add,
                    rd=[pm, brow], wr=[modrow])
            s.dma("sp", self.modrow_d.t.ap()[l], modrow[:], self.modrow_d, modrow)
            pT = s.ps()
            for c in range(48):
                s.I("pe", "transpose", pT[:, c * 3:(c + 1) * 3], modrow[0:3, c * 128:(c + 1) * 128], self.ident[0:3, 0:3], rd=[modrow, self.ident], wr=[pT])
            modcol = s.sb("modcol%d" % l, [128, 48, 3], F32)
            s.I("dve", "tensor_copy", modcol[:].rearrange("p a b -> p (a b)"), pT[:, 0:144], rd=[pT], wr=[modcol])
            gm = s.sb("gm%d" % l, [128, 8, 1], F32)
            gf = s.sb("gf%d" % l, [128, 8, 1], F32)
            s.dma("sp", gm[:, :, 0], self.g_mixc.t.ap()[l], gm, self.g_mixc)
            s.dma("sp", gf[:, :, 0], self.g_ffnc.t.ap()[l], gf, self.g_ffnc)
            A1, B1, A2, B2 = self.AB[l]
            for (A, Bv, g, sc0, sh0) in ((A1, B1, gm, 8, 0), (A2, B2, gf, 32, 24)):
                s.I("dve", "scalar_tensor_tensor", out=A[:], in0=modcol[:, sc0:sc0 + 8, :], scalar=1.0, in1=g[:].to_broadcast([128, 8, 3]),
                    op0=ALU.add, op1=ALU.mult, rd=[modcol, g], wr=[A])
                s.I("dve", "tensor_copy", Bv[:], modcol[:, sh0:sh0 + 8, :], rd=[modcol], wr=[Bv])

    def phase_norm(self, src, A, Bv, b, ntiles):
        s = self.s
        xt = [s.sb("nxt%d" % i, [128, D], F32) for i in range(4)]
        xn = [s.sb("nxn%d" % i, [128, D], F32) for i in range(8)]
        junks = [s.sb("njunk%d" % i, [128, D], BF16) for i in range(2)]
        sts = [s.sb("nst%d" % i, [128, 1, 3], F32) for i in range(8)]
        ngroups = (ntiles + 3) // 4
        for g in range(ngroups):
            tiles = list(range(4 * g, min(4 * g + 4, ntiles)))
            j = b if tiles[0] < 16 else 2
            for ti, t in enumerate(tiles):
                ap, sbuf = src(t)
                x_ = xt[t % 4]
                n_ = xn[t % 8]
                s.dma("sp", x_[:], ap, x_, sbuf)
                st, junk = sts[t % 8], junks[t % 2]
                s.I("act", "activation", out=junk[:], in_=x_[:], func=AF.Square, accum_out=st[:, 0, 0:1], rd=[x_], wr=[junk, st])
                s.I("act", "activation", out=st[:, 0, 1:2], in_=st[:, 0, 0:1], func=AF.Sqrt, scale=1.0 / D, bias=EPS, rd=[st], wr=[st])
                s.I("dve", "reciprocal", st[:, 0, 2:3], st[:, 0, 1:2], rd=[st], wr=[st])
                s.I("dve", "tensor_scalar", out=n_[:], in0=x_[:], scalar1=st[:, 0, 2:3], scalar2=None, op0=ALU.mult, rd=[x_, st], wr=[n_])
            n = len(tiles) * 128
            for c in range(8):
                p = s.ps()
                for ti, t in enumerate(tiles):
                    n_ = xn[t % 8]
                    s.I("pe", "transpose", p[:, ti * 128:(ti + 1) * 128], n_[:, c * 128:(c + 1) * 128], self.ident[:], rd=[n_, self.ident], wr=[p])
                s.I("act", "activation", out=self.hT[:, c, tiles[0] * 128:tiles[0] * 128 + n], in_=p[:, 0:n], func=AF.Identity,
                    scale=A[:, c, j:j + 1], bias=Bv[:, c, j:j + 1], rd=[p, A, Bv], wr=[self.hT])

    def load_w(self, name, src_buf, src_ap, shape, scope=None):
        s = self.s
        if name in self.pre:
            return self.pre.pop(name)
        if scope is None:
            w = s.sb(name, shape, BF16)
        else:
            w = scope
        s.dma("pool", w[:], src_ap, w, src_buf)
        return w

    def alloc_scoped(self, es, name, shape):
        s = self.s
        s.uid += 1
        t = es.enter_context(self.nc.sbuf_tensor("%s_%d" % (name, s.uid), list(shape), BF16))
        w = Buf(name, t, True)
        self.scoped.append(w)
        return w

    def release_scoped(self):
        for b in self.scoped:
            if b.dsem is not None:
                self.s.dfree.append(b.dsem)
                b.dsem = None
        self.scoped = []

    def w_na(self, l, scope=None):
        return self.load_w("wna", self.w_in, self.w_in.t.ap()[l, :, 0:768].rearrange("(k p) n -> p k n", p=128), [128, 8, 768], scope)

    def w_df(self, l, scope=None):
        return self.load_w("wdf", self.w_in, self.w_in.t.ap()[l, :, 768:1536].rearrange("(k p) n -> p k n", p=128), [128, 8, 768], scope)

    def w_dn(self, l, scope=None):
        return self.load_w("wdn", self.w_down, self.w_down.t.ap()[l].rearrange("(f p) n -> p f n", p=128), [128, NF, D], scope)

    def bcast_row(self, name, src_buf, off, n, dt=F32, parts=128):
        s = self.s
        t = s.sb(name, [parts, n], dt)
        s.dma("sp", t[:], dap(src_buf, off, [[0, parts], [1, n]]), t, src_buf)
        return t

    def phase_out(self, l, b, src, ntiles):
        s = self.s
        w = self.load_w("wout", self.w_out, self.w_out.t.ap()[l].rearrange("(k p) n -> p k n", p=128), [128, 8, D])
        G = {}
        G[b] = self.bcast_row("g1b", self.modrow_d, (l * 3 + b) * 6 * D + 2 * D, D)
        if ntiles > 16:
            G[2] = self.bcast_row("g1c", self.modrow_d, (l * 3 + 2) * 6 * D + 2 * D, D)
        yt = [s.sb("oyt%d" % i, [128, D], F32) for i in range(8)]
        xr = [s.sb("oxr%d" % i, [128, D], F32) for i in range(8)]
        yT = [s.sb("oyT%d" % i, [128, 8, 512], BF16) for i in range(2)]
        tmp = [s.sb("otmp%d" % i, [128, D], F32) for i in range(2)]
        xo = [s.sb("oxo%d" % i, [128, D], F32) for i in range(2)]
        ngroups = (ntiles + 3) // 4

        def loads(g):
            for t in range(4 * g, min(4 * g + 4, ntiles)):
                y_ = yt[t % 8]
                if self.Yin is not None:
                    s.dma("sp", y_[:], self.Yin.t.ap()[l, b, t * 128:(t + 1) * 128, :], y_, self.Yin)
                else:
                    s.dma("sp", y_[:], self.Yd.t.ap()[b, t * 128:(t + 1) * 128, :], y_, self.Yd)
                ap, sbuf = src(t)
                s.dma("sp", xr[t % 8][:], ap, xr[t % 8], sbuf)

        loads(0)
        for g in range(ngroups):
            if g + 1 < ngroups:
                loads(g + 1)
            tiles = list(range(4 * g, min(4 * g + 4, ntiles)))
            j = b if tiles[0] < 16 else 2
            yT_ = yT[g % 2]
            n = len(tiles) * 128
            for c in range(8):
                p = s.ps()
                for ti, t in enumerate(tiles):
                    s.I("pe", "transpose", p[:, ti * 128:(ti + 1) * 128], yt[t % 8][:, c * 128:(c + 1) * 128], self.ident[:], rd=[yt[t % 8], self.ident], wr=[p])
                s.I("act", "activation", out=yT_[:, c, 0:n], in_=p[:, 0:n], func=AF.Copy, rd=[p], wr=[yT_])
            for ti, t in enumerate(tiles):
                tm = tmp[t % 2]
                xo_ = xo[t % 2]
                for hf in range(2):
                    p = s.ps()
                    for k in range(8):
                        s.I("pe", "matmul", p[:, :], yT_[:, k, ti * 128:(ti + 1) * 128], w[:, k, hf * 512:(hf + 1) * 512], start=(k == 0), stop=(k == 7),
                            rd=[yT_, w], wr=[p])
                    s.I("dve", "tensor_tensor", out=tm[:, hf * 512:(hf + 1) * 512], in0=p[:, :], in1=G[j][:, hf * 512:(hf + 1) * 512], op=ALU.mult,
                        rd=[p, G[j]], wr=[tm])
                s.I("pool", "tensor_tensor", out=xo_[:], in0=tm[:], in1=xr[t % 8][:], op=ALU.add, rd=[tm, xr[t % 8]], wr=[xo_])
                s.dma("sp", self.XA[l].t.ap()[b, t * 128:(t + 1) * 128, :], xo_[:], self.XA[l], xo_)

    def phase_ffn_up(self, l, b, ntiles):
        s = self.s
        groups = [(0, 512), (512, 512), (1024, 512), (1536, 512)]
        if ntiles > 16:
            groups.append((2048, 256))
        ntok = ntiles * 128
        wu = [s.sb("wu%d" % i, [128, 8, 256], BF16) for i in range(3)]
        Gl = [s.sb("fGl%d" % i, [128, L + 2], F32) for i in range(2)]
        Gc = [s.sb("fGc%d" % i, [128, LC + 2], F32) for i in range(2)]
        V = [s.sb("fV%d" % i, [128, T], F32) for i in range(2)]
        acc = [s.sb("facc%d" % i, [128, T], F32) for i in range(2)]
        at = [s.sb("fat%d" % i, [128, T], BF16) for i in range(2)]
        cw = s.sb("fcw", [128, NF, 3], F32)
        cb = s.sb("fcb", [128, NF], F32)
        s.dma("sp", cw[:], self.fconv_wc.t.ap()[l], cw, self.fconv_wc)
        s.dma("sp", cb[:], self.fconv_bc.t.ap()[l], cb, self.fconv_bc)
        for i in range(2):
            s.I("pool", "memset", Gl[i][:], 0.0, wr=[Gl[i]])
            s.I("pool", "memset", Gc[i][:], 0.0, wr=[Gc[i]])
        wup = self.w_up.t.ap()[l]

        def loadw(f):
            w = wu[f % 3]
            s.dma("pool", w[:, :, 0:128], wup[:, f * 128:(f + 1) * 128].rearrange("(k p) n -> p k n", p=128), w, self.w_up)
            s.dma("pool", w[:, :, 128:256], wup[:, DFF + f * 128:DFF + (f + 1) * 128].rearrange("(k p) n -> p k n", p=128), w, self.w_up)

        loadw(0)
        pend = []
        for f in range(NF):
            if f + 1 < NF:
                loadw(f + 1)
            w = wu[f % 3]
            gl, gc, v, a, o = Gl[f % 2], Gc[f % 2], V[f % 2], acc[f % 2], at[f % 2]
            for gi, (t0, n) in enumerate(groups):
                if gi == 2 and pend:
                    pend.pop(0)()
                pa = s.ps()
                pb = s.ps()
                for k in range(8):
                    s.I("pe", "matmul", pa[:, 0:n], w[:, k, 0:128], self.hT[:, k, t0:t0 + n], start=(k == 0), stop=(k == 7), rd=[w, self.hT], wr=[pa])
                for k in range(8):
                    s.I("pe", "matmul", pb[:, 0:n], w[:, k, 128:256], self.hT[:, k, t0:t0 + n], start=(k == 0), stop=(k == 7), rd=[w, self.hT], wr=[pb])
                if t0 < L:
                    s.I("act", "activation", out=gl[:, 1 + t0:1 + t0 + n], in_=pa[:, 0:n], func=AF.Copy, rd=[pa], wr=[gl])
                else:
                    s.I("act", "activation", out=gc[:, 1:1 + n], in_=pa[:, 0:n], func=AF.Copy, rd=[pa], wr=[gc])
                s.I("act", "activation", out=v[:, t0:t0 + n], in_=pb[:, 0:n], func=AF.Copy, rd=[pb], wr=[v])
            segs = [(gl, 0, L)] + ([(gc, L, LC)] if ntiles > 16 else [])
            for (gb, o0, n) in segs:
                s.I("dve", "tensor_scalar", out=a[:, o0:o0 + n], in0=gb[:, 0:n], scalar1=cw[:, f, 0:1], scalar2=None, op0=ALU.mult, rd=[gb, cw], wr=[a])
                for k in (1, 2):
                    s.I("dve", "scalar_tensor_tensor", out=a[:, o0:o0 + n], in0=gb[:, k:k + n], scalar=cw[:, f, k:k + 1], in1=a[:, o0:o0 + n],
                        op0=ALU.mult, op1=ALU.add, rd=[gb, cw, a], wr=[a])

            def tail(f=f, a=a, v=v, o=o):
                s.I("act", "activation", out=a[:, 0:ntok], in_=a[:, 0:ntok], func=AF.Silu, bias=cb[:, f:f + 1], scale=1.0, rd=[a, cb], wr=[a])
                s.I("pool", "tensor_tensor", out=o[:, 0:ntok], in0=a[:, 0:ntok], in1=v[:, 0:ntok], op=ALU.mult, rd=[a, v], wr=[o])
                s.dma("sp", self.ATd.t.ap()[f, :, 0:ntok], o[:, 0:ntok], self.ATd, o)
            pend.append(tail)
        while pend:
            pend.pop(0)()

    def phase_ffn_down(self, l, b, src, ntiles):
        s = self.s
        last = l == DEPTH - 1
        w = self.w_dn(l)
        G = {}
        G[b] = self.bcast_row("g2b", self.modrow_d, (l * 3 + b) * 6 * D + 5 * D, D)
        if ntiles > 16:
            G[2] = self.bcast_row("g2c", self.modrow_d, (l * 3 + 2) * 6 * D + 5 * D, D)
        aT = [s.sb("daT%d" % i, [128, NF, 512], BF16) for i in range(2)]
        xr = [s.sb("dxr%d" % i, [128, D], F32) for i in range(8)]
        tmp = [s.sb("dtmp%d" % i, [128, D], F32) for i in range(2)]
        xo = [s.sb("dxo%d" % i, [128, D], F32) for i in range(2)]
        ngroups = (ntiles + 3) // 4

        def loads(g):
            tiles = list(range(4 * g, min(4 * g + 4, ntiles)))
            n = len(tiles) * 128
            a_ = aT[g % 2]
            s.dma("sp", a_[:, :, 0:n], self.ATd.t.ap()[:, :, tiles[0] * 128:tiles[0] * 128 + n].rearrange("f p t -> p f t"), a_, self.ATd)
            for t in tiles:
                ap, sbuf = src(t)
                s.dma("sp", xr[t % 8][:], ap, xr[t % 8], sbuf)

        loads(0)
        for g in range(ngroups):
            if g + 1 < ngroups:
                loads(g + 1)
            tiles = list(range(4 * g, min(4 * g + 4, ntiles)))
            j = b if tiles[0] < 16 else 2
            a_ = aT[g % 2]
            for ti, t in enumerate(tiles):
                tm = tmp[t % 2]
                xo_ = xo[t % 2]
                for hf in range(2):
                    p = s.ps()
                    for f in range(NF):
                        s.I("pe", "matmul", p[:, :], a_[:, f, ti * 128:(ti + 1) * 128], w[:, f, hf * 512:(hf + 1) * 512], start=(f == 0), stop=(f == NF - 1),
                            rd=[a_, w], wr=[p])
                    s.I("dve", "tensor_tensor", out=tm[:, hf * 512:(hf + 1) * 512], in0=p[:, :], in1=G[j][:, hf * 512:(hf + 1) * 512], op=ALU.mult,
                        rd=[p, G[j]], wr=[tm])
                s.I("pool", "tensor_tensor", out=xo_[:], in0=tm[:], in1=xr[t % 8][:], op=ALU.add, rd=[tm, xr[t % 8]], wr=[xo_])
                if last:
                    s.dma("sp", self.out.t.ap()[b, t * 128:(t + 1) * 128, :], xo_[:], self.out, xo_)
                else:
                    s.dma("sp", self.XB.t.ap()[b, t * 128:(t + 1) * 128, :], xo_[:], self.XB, xo_)

    def group_norm(self, p, ngrp, gd, gains, out, sq, st, view):
        s = self.s
        n = ngrp * gd
        s.I("act", "activation", out=sq[:, 0:n], in_=p[:, 0:n], func=AF.Square, rd=[p], wr=[sq])
        s.I("dve", "tensor_reduce", out=st[:, 0:ngrp, 0], in_=sq[:, 0:n].rearrange("p (g d) -> p g d", d=gd), axis=AX.X, op=ALU.add, rd=[sq], wr=[st])
        s.I("act", "activation", out=st[:, 0:ngrp, 1], in_=st[:, 0:ngrp, 0], func=AF.Sqrt, scale=1.0 / gd, bias=EPS, rd=[st], wr=[st])
        s.I("dve", "reciprocal", st[:, 0:ngrp, 2], st[:, 0:ngrp, 1], rd=[st], wr=[st])
        s.I("dve", "tensor_tensor", out=sq[:, 0:n].rearrange("p (g d) -> p g d", d=gd), in0=p[:, 0:n].rearrange("p (g d) -> p g d", d=gd),
            in1=st[:, 0:ngrp, 2:3].to_broadcast([128, ngrp, gd]), op=ALU.mult, rd=[p, st], wr=[sq])
        s.I("pool", "tensor_tensor", out=out, in0=view(sq[:, 0:n]), in1=gains, op=ALU.mult, rd=[sq] + self._gn_rd, wr=self._gn_wr)

    def phase_na_bias(self, l):
        s = self.s
        bt32 = s.sb("bt32", [128, 4, 14, 64], F32)
        mk = s.sb("namask", [128, 64], F32)
        s.dma("sp", mk[:], self.na_mask.t.ap(), mk, self.na_mask)
        for h in range(4):
            for half in range(2):
                s.dma("sp", bt32[64 * half:64 * half + 64, h, :, :], self.rpb_t.t.ap()[l, h, half:half + 14].rearrange("d k q -> k d q"), bt32, self.rpb_t)
        s.I("dve", "tensor_tensor", out=bt32[:].rearrange("p h d q -> p (h d) q"), in0=bt32[:].rearrange("p h d q -> p (h d) q"),
            in1=mk[:].rearrange("p (o q) -> p o q", o=1).to_broadcast([128, 56, 64]), op=ALU.add, rd=[bt32, mk], wr=[bt32])
        s.I("dve", "tensor_scalar", out=self.BT[:].rearrange("p h d q -> p (h d q)"), in0=bt32[:].rearrange("p h d q -> p (h d q)"), scalar1=8.0, scalar2=None,
            op0=ALU.mult, rd=[bt32], wr=[self.BT])

    def phase_na(self, l, b):
        s = self.s
        last = l == DEPTH - 1
        w = self.w_na(l)
        gains = s.sb("nagain", [128, 2, 1, 64], F32)
        s.dma("sp", gains[:, 0, 0, :], dap(self.na_qg, l * 64, [[0, 128], [1, 64]]), gains, self.na_qg)
        s.dma("sp", gains[:, 1, 0, :], dap(self.na_kg, l * 64, [[0, 128], [1, 64]]), gains, self.na_kg)
        QKT = s.sb("naQKT", [128, 6, T], BF16)
        kz = [s.sb("nakz%d" % i, [128, 2, 256], F32) for i in range(2)]
        for i in range(2):
            s.I("pool", "memset", kz[i][:], 0.0, wr=[kz[i]])
        VE = s.sb("naVE", [128, 18, 4, 65], BF16)
        VO = s.sb("naVO", [128, 15, 4, 65], BF16)
        s.I("pool", "memset", VE[:], 1.0, wr=[VE])
        s.I("pool", "memset", VO[:], 1.0, wr=[VO])
        qn = [s.sb("naqn%d" % i, [128, 2, 4, 64], F32) for i in range(2)]
        gsq = [s.sb("nagsq%d" % i, [128, 512], F32) for i in range(2)]
        gst = [s.sb("nagst%d" % i, [128, 16, 3], F32) for i in range(2)]
        pending = None
        for t in range(18):
            pq = s.ps()
            pv = s.ps()
            for k in range(8):
                s.I("pe", "matmul", pq[:, :], self.hT[:, k, t * 128:(t + 1) * 128], w[:, k, 0:512], start=(k == 0), stop=(k == 7), rd=[self.hT, w], wr=[pq])
            for k in range(8):
                s.I("pe", "matmul", pv[:, 0:256], self.hT[:, k, t * 128:(t + 1) * 128], w[:, k, 512:768], start=(k == 0), stop=(k == 7), rd=[self.hT, w], wr=[pv])
            q_ = qn[t % 2]
            self._gn_rd = [gains]
            self._gn_wr = [q_]
            self.group_norm(pq, 8, 64, gains[:].to_broadcast([128, 2, 4, 64]), q_[:], gsq[t % 2], gst[t % 2],
                            lambda ap: ap.rearrange("p (a h d) -> p a h d", a=2, h=4))
            kz_ = kz[t % 2]
            for hs in range(2):
                s.I("pool", "tensor_copy", kz_[:, hs, :].rearrange("p (pr h2 d) -> p pr h2 d", pr=2, h2=2)[:, :, hs, :],
                    q_[:, 1, :, :].rearrange("p (pr h2) d -> p pr h2 d", pr=2)[:, :, hs, :], rd=[q_], wr=[kz_])
            s.I("dve", "tensor_copy", VE[:, t, :, 0:64], pv[:, 0:256].rearrange("p (h d) -> p h d", h=4), rd=[pv], wr=[VE])

            def tail(t=t, q_=q_, kz_=kz_):
                pt = s.ps()
                pt2 = s.ps()
                qf = q_[:].rearrange("p a h d -> p (a h d)")
                for cc in range(2):
                    s.I("pe", "transpose", pt[:, cc * 128:(cc + 1) * 128], qf[:, cc * 128:(cc + 1) * 128], self.ident[:], rd=[q_, self.ident], wr=[pt])
                for hs in range(2):
                    for pr in range(2):
                        s.I("pe", "transpose", pt2[:, (hs * 2 + pr) * 128:(hs * 2 + pr + 1) * 128], kz_[:, hs, pr * 128:(pr + 1) * 128], self.ident[:], rd=[kz_, self.ident], wr=[pt2])
                s.I("act", "activation", out=QKT[:, 0:2, t * 128:(t + 1) * 128], in_=pt[:, 0:256].rearrange("p (c t) -> p c t", c=2), func=AF.Copy, rd=[pt], wr=[QKT])
                s.I("act", "activation", out=QKT[:, 2:6, t * 128:(t + 1) * 128], in_=pt2[:, :].rearrange("p (c t) -> p c t", c=4), func=AF.Copy, rd=[pt2], wr=[QKT])
            if pending is not None:
                pending()
            pending = tail
        pending()
        for i in range(15):
            pv = s.ps()
            for k in range(8):
                s.I("pe", "matmul", pv[:, 0:256], self.hT[:, k, 64 + i * 128:64 + (i + 1) * 128], w[:, k, 512:768], start=(k == 0), stop=(k == 7), rd=[self.hT, w], wr=[pv])
            s.I("dve", "tensor_copy", VO[:, i, :, 0:64], pv[:, 0:256].rearrange("p (h d) -> p h d", h=4), rd=[pv], wr=[VO])
        PT = [s.sb("naPT%d" % i, [128, 6, 64], BF16) for i in range(4)]
        rec = [s.sb("narec%d" % i, [128, 4, 1], F32) for i in range(2)]
        yo = [s.sb("nayo%d" % i, [128, 4, 64], F32) for i in range(2)]
        def s_part(r, h, P_):
            R0 = min(max(r - 4, 0), 24)
            pair = h // 2
            pS = s.ps()
            q_ap = QKT[:, pair, r * 64:(r + 1) * 64]
            kk = 2 + 2 * (h % 2) + pair
            for ci in range(4):
                kr = R0 + 2 * ci
                d = kr - r + 7
                s.I("pe", "matmul", pS[:, ci * 64:(ci + 1) * 64], QKT[:, kk, kr * 64:kr * 64 + 128], q_ap, start=True, stop=False,
                    rd=[QKT], wr=[pS])
                s.I("pe", "matmul", pS[:, ci * 64:(ci + 1) * 64], self.identb[:], self.BT[:, h, d, :], start=False, stop=True,
                    rd=[self.identb, self.BT], wr=[pS])
            for cc in range(2):
                s.I("pe", "matmul", pS[:, (4 + cc) * 64:(5 + cc) * 64], QKT[:, kk, L + cc * 128:L + (cc + 1) * 128], q_ap,
                    start=True, stop=True, rd=[QKT], wr=[pS])
            s.I("act", "activation", out=P_[:].rearrange("p c q -> p (c q)"), in_=pS[:, 0:384], func=AF.Exp, scale=0.125, rd=[pS], wr=[P_])

        def pv_part(r, h, P_):
            R0 = min(max(r - 4, 0), 24)
            rp, rr = r // 2, r % 2
            po = s.psacc(rp)
            for c in range(6):
                if c < 4:
                    kr = R0 + 2 * c
                    vb, v_ap = (VE, VE[:, kr // 2, h, :]) if kr % 2 == 0 else (VO, VO[:, (kr - 1) // 2, h, :])
                else:
                    vb, v_ap = VE, VE[:, 16 + (c - 4), h, :]
                s.I("pe", "matmul", po[64 * rr:64 * rr + 64, h * 65:(h + 1) * 65], P_[:, c, :], v_ap, start=(c == 0), stop=(c == 5), rd=[P_, vb], wr=[po])

        def row_finish(rp):
            po = s.psacc(rp)
            rc, y_ = rec[rp % 2], yo[rp % 2]
            pov = po[:, 0:260].rearrange("p (h e) -> p h e", e=65)
            s.I("dve", "reciprocal", rc[:], pov[:, :, 64:65], rd=[po], wr=[rc])
            s.I("dve", "tensor_tensor", out=y_[:], in0=pov[:, :, 0:64], in1=rc[:].to_broadcast([128, 4, 64]), op=ALU.mult, rd=[po, rc], wr=[y_])
            s.dma("sp", self.Yd.t.ap()[b, rp * 128:(rp + 1) * 128, 0:256], y_[:].rearrange("p h d -> p (h d)"), self.Yd, y_)

        seq = [(r, h) for r in range(32) for h in range(4)]
        SK = 2
        for i in range(len(seq) + SK):
            if i < len(seq):
                s_part(seq[i][0], seq[i][1], PT[i % 4])
            j = i - SK
            if j >= 0:
                pv_part(seq[j][0], seq[j][1], PT[j % 4])
                if seq[j][0] % 2 == 1 and seq[j][1] == 3:
                    row_finish(seq[j][0] // 2)
        if not last:
            PTc = [s.sb("naPTc%d" % i, [128, 2, 256], BF16) for i in range(2)]
            pos = [s.psacc(0), s.psacc(1)]
            for h in range(4):
                pair, base = h // 2, 64 * (h % 2)
                pS = s.ps()
                for cc in range(2):
                    s.I("pe", "matmul", pS[:, cc * 256:(cc + 1) * 256], QKT[:, 2 + 2 * (h % 2) + pair, L + cc * 128:L + (cc + 1) * 128],
                        QKT[:, pair, L:L + 256], start=True, stop=True, rd=[QKT], wr=[pS])
                P_ = PTc[h % 2]
                s.I("act", "activation", out=P_[:].rearrange("p c q -> p (c q)"), in_=pS[:, :], func=AF.Exp, scale=0.125, rd=[pS], wr=[P_])
                for qt in range(2):
                    for cc in range(2):
                        s.I("pe", "matmul", pos[qt][:, h * 65:(h + 1) * 65], P_[:, cc, qt * 128:(qt + 1) * 128], VE[:, 16 + cc, h, :], start=(cc == 0), stop=(cc == 1),
                            rd=[P_, VE], wr=[pos[qt]])
            for qt in range(2):
                rc, y_ = rec[qt], yo[qt]
                pov = pos[qt][:, 0:260].rearrange("p (h e) -> p h e", e=65)
                s.I("dve", "reciprocal", rc[:], pov[:, :, 64:65], rd=[pos[qt]], wr=[rc])
                s.I("dve", "tensor_tensor", out=y_[:], in0=pov[:, :, 0:64], in1=rc[:].to_broadcast([128, 4, 64]), op=ALU.mult, rd=[pos[qt], rc], wr=[y_])
                s.dma("sp", self.Yd.t.ap()[b, L + qt * 128:L + (qt + 1) * 128, 0:256], y_[:].rearrange("p h d -> p (h d)"), self.Yd, y_)

    def phase_df(self, l, b):
        s = self.s
        last = l == DEPTH - 1
        lam_init = 0.8 - 0.6 * math.exp(-0.3 * l)
        w = self.w_df(l)
        gains = s.sb("dfgain", [128, 2, 1, 32], F32)
        s.dma("sp", gains[:, 0, 0, :], dap(self.df_qg, l * 32, [[0, 128], [1, 32]]), gains, self.df_qg)
        s.dma("sp", gains[:, 1, 0, :], dap(self.df_kg, l * 32, [[0, 128], [1, 32]]), gains, self.df_kg)
        COS = s.sb("dfcos", [128, 16, 32], F32)
        SIN = s.sb("dfsin", [128, 16, 32], F32)
        s.dma("sp", COS[:], self.rope_cos.t.ap().rearrange("(t p) d -> p t d", p=128), COS, self.rope_cos)
        s.dma("sp", SIN[:], self.rope_sin.t.ap().rearrange("(t p) d -> p t d", p=128), SIN, self.rope_sin)
        lv = s.sb("dflv", [128, 4, 32], F32)
        s.dma("sp", lv[:].rearrange("p a d -> p (a d)"), dap(self.df_lam, l * 128, [[0, 128], [1, 128]]), lv, self.df_lam)
        lp = s.sb("dflp", [128, 2, 32], F32)
        ls = s.sb("dfls", [128, 8], F32)
        s.I("dve", "tensor_tensor", out=lp[:, 0, :], in0=lv[:, 0, :], in1=lv[:, 1, :], op=ALU.mult, rd=[lv], wr=[lp])
        s.I("dve", "tensor_tensor", out=lp[:, 1, :], in0=lv[:, 2, :], in1=lv[:, 3, :], op=ALU.mult, rd=[lv], wr=[lp])
        s.I("dve", "tensor_reduce", out=ls[:, 0:2], in_=lp[:], axis=AX.X, op=ALU.add, rd=[lp], wr=[ls])
        s.I("act", "activation", out=ls[:, 2:4], in_=ls[:, 0:2], func=AF.Exp, rd=[ls], wr=[ls])
        s.I("dve", "scalar_tensor_tensor", out=ls[:, 4:5], in0=ls[:, 3:4], scalar=-lam_init, in1=ls[:, 2:3], op0=ALU.add, op1=ALU.subtract, rd=[ls], wr=[ls])
        neglam = ls[:, 4:5]
        sub = s.sb("dfsub", [128, 1, 64], F32)
        s.dma("sp", sub[:, 0, :], dap(self.df_subln, l * 64, [[0, 128], [1, 64]]), sub, self.df_subln)
        s.I("dve", "tensor_scalar", out=sub[:], in0=sub[:], scalar1=1.0 - lam_init, scalar2=None, op0=ALU.mult, rd=[sub], wr=[sub])

        QZ = s.sb("dfQZ", [128, 2, 2, T], BF16)
        KZ = s.sb("dfKZ", [128, 2, 2, T], BF16)
        VD = s.sb("dfVD", [128, 18, 4, 65], BF16)
        s.I("pool", "memset", VD[:], 1.0, wr=[VD])
        qn = [s.sb("dfqn%d" % i, [128, 16, 32], F32) for i in range(2)]
        gsq = [s.sb("dfgsq%d" % i, [128, 512], F32) for i in range(1)] * 2
        gst = [s.sb("dfgst%d" % i, [128, 16, 3], F32) for i in range(2)]
        t1 = [s.sb("dft1%d" % i, [128, 16, 32], F32) for i in range(1)] * 2
        t2 = [s.sb("dft2%d" % i, [128, 16, 32], F32) for i in range(1)] * 2
        qz = [s.sb("dfqz%d" % i, [128, 4, 256], F32) for i in range(2)]
        for i in range(2):
            s.I("pool", "memset", qz[i][:], 0.0, wr=[qz[i]])
        pending = None
        for t in range(18):
            pq = s.ps()
            pv = s.ps()
            for k in range(8):
                s.I("pe", "matmul", pq[:, :], self.hT[:, k, t * 128:(t + 1) * 128], w[:, k, 0:512], start=(k == 0), stop=(k == 7), rd=[self.hT, w], wr=[pq])
            for k in range(8):
                s.I("pe", "matmul", pv[:, 0:256], self.hT[:, k, t * 128:(t + 1) * 128], w[:, k, 512:768], start=(k == 0), stop=(k == 7), rd=[self.hT, w], wr=[pv])
            q_ = qn[t % 2]
            self._gn_rd = [gains]
            self._gn_wr = [q_]
            self.group_norm(pq, 16, 32, gains[:].to_broadcast([128, 2, 8, 32]), q_[:].rearrange("p (a h) d -> p a h d", a=2), gsq[t % 2], gst[t % 2],
                            lambda ap: ap.rearrange("p (a h d) -> p a h d", a=2, h=8))
            z_ = qz[t % 2]
            if t < 16:
                a_, b_ = t1[t % 2], t2[t % 2]
                s.I("pool", "tensor_tensor", out=a_[:], in0=q_[:], in1=COS[:, t:t + 1, :].to_broadcast([128, 16, 32]), op=ALU.mult, rd=[q_, COS], wr=[a_])
                q5 = q_[:].rearrange("p g (a f e) -> p g a f e", a=2, f=2)
                b5 = b_[:].rearrange("p g (a f e) -> p g a f e", a=2, f=2)
                s5 = SIN[:, t:t + 1, :].to_broadcast([128, 16, 32]).rearrange("p g (a f e) -> p g a f e", a=2, f=2)
                s.I("dve", "tensor_tensor", out=b5[:, :, :, 0, :], in0=q5[:, :, :, 1, :], in1=s5[:, :, :, 0, :], op=ALU.mult, rd=[q_, SIN], wr=[b_])
                s.I("dve", "tensor_tensor", out=b5[:, :, :, 1, :], in0=q5[:, :, :, 0, :], in1=s5[:, :, :, 1, :], op=ALU.mult, rd=[q_, SIN], wr=[b_])
                srcs = (a_, b_)
            else:
                srcs = (q_,)

            def comb(out_ap, sel):
                if len(srcs) == 2:
                    s.I("pool", "tensor_tensor", out=out_ap, in0=sel(srcs[0]), in1=sel(srcs[1]), op=ALU.add, rd=list(srcs), wr=[z_])
                else:
                    s.I("pool", "tensor_copy", out_ap, sel(srcs[0]), rd=list(srcs), wr=[z_])
            for m in range(2):
                comb(z_[:, m, :].rearrange("p (h m d) -> p h m d", h=4, m=2)[:, :, m, :],
                     lambda bf: bf[:, 0:8, :].rearrange("p (h m) d -> p h m d", m=2)[:, :, m, :])
            for hs in range(2):
                comb(z_[:, 2 + hs, :].rearrange("p (pr h2 e) -> p pr h2 e", pr=2, h2=2)[:, :, hs, :],
                     lambda bf: bf[:, 8:16, :].rearrange("p (pr h2 m) d -> p pr h2 (m d)", pr=2, h2=2)[:, :, hs, :])
            s.I("dve", "tensor_copy", VD[:, t, :, 0:64], pv[:, 0:256].rearrange("p (h d) -> p h d", h=4), rd=[pv], wr=[VD])

            def tail(t=t, z_=z_):
                pt = s.ps()
                pt2 = s.ps()
                for m in range(2):
                    for pr in range(2):
                        s.I("pe", "transpose", pt[:, (m * 2 + pr) * 128:(m * 2 + pr + 1) * 128], z_[:, m, pr * 128:(pr + 1) * 128], self.ident[:], rd=[z_, self.ident], wr=[pt])
                for hs in range(2):
                    for pr in range(2):
                        s.I("pe", "transpose", pt2[:, (hs * 2 + pr) * 128:(hs * 2 + pr + 1) * 128], z_[:, 2 + hs, pr * 128:(pr + 1) * 128], self.ident[:], rd=[z_, self.ident], wr=[pt2])
                s.I("act", "activation", out=QZ[:, :, :, t * 128:(t + 1) * 128], in_=pt[:, :].rearrange("p (m c t) -> p m c t", m=2, c=2), func=AF.Copy, rd=[pt], wr=[QZ])
                s.I("act", "activation", out=KZ[:, :, :, t * 128:(t + 1) * 128], in_=pt2[:, :].rearrange("p (a c t) -> p a c t", a=2, c=2), func=AF.Copy, rd=[pt2], wr=[KZ])
            if pending is not None:
                pending()
            pending = tail
        pending()
        PT = [[s.sb("dfPT%d%d" % (i, m), [128, 18, 512], BF16) for m in range(2)] for i in range(2)]
        rec = s.sb("dfrec", [128, 2, 4, 1], F32)
        o0 = s.sb("dfo0", [128, 4, 64], F32)
        o1 = s.sb("dfo1", [128, 4, 64], F32)
        yd = [s.sb("dfyd%d" % i, [128, 4, 64], F32) for i in range(2)]
        sq = s.sb("dfsq2", [128, 4, 64], F32)
        st = s.sb("dfst2", [128, 4, 3], F32)
        epsb = s.sb("dfeps", [128, 1], F32)
        s.I("pool", "memset", epsb[:], EPS, wr=[epsb])
        blocks = [(qb * 512, 512, list(range(18))) for qb in range(4)]
        if not last:
            blocks.append((L, 256, [16, 17]))
        items = [(h, q0, nq, kcs) for h in range(4) for (q0, nq, kcs) in blocks]

        def s_steps(idx):
            h, q0, nq, kcs = items[idx]
            pair = h // 2
            P_ = PT[idx % 2]
            out = []
            for m in range(2):
                for kc in kcs:
                    def f(m=m, kc=kc):
                        pS = s.ps()
                        s.I("pe", "matmul", pS[:, 0:nq], KZ[:, h % 2, pair, kc * 128:(kc + 1) * 128], QZ[:, m, pair, q0:q0 + nq], start=True, stop=True,
                            rd=[KZ, QZ], wr=[pS])
                        s.I("act", "activation", out=P_[m][:, kc, 0:nq], in_=pS[:, 0:nq], func=AF.Exp, scale=32.0 ** -0.5, rd=[pS], wr=[P_[m]])
                    out.append(f)
            return out

        def pv_steps(idx):
            h, q0, nq, kcs = items[idx]
            P_ = PT[idx % 2]
            po = [s.psacc(0), s.psacc(1)]
            out = []
            for m in range(2):
                for qs in range(nq // 128):
                    for i, kc in enumerate(kcs):
                        def f(m=m, qs=qs, i=i, kc=kc):
                            s.I("pe", "matmul", po[m][:, qs * 65:(qs + 1) * 65], P_[m][:, kc, qs * 128:(qs + 1) * 128], VD[:, kc, h, :], start=(i == 0), stop=(i == len(kcs) - 1),
                                rd=[P_[m], VD], wr=[po[m]])
                        out.append(f)
            return out

        def finish(idx):
            h, q0, nq, kcs = items[idx]
            po = [s.psacc(0), s.psacc(1)]
            nqs = nq // 128
            y_ = yd[idx % 2]
            pv0 = po[0][:, 0:nqs * 65].rearrange("p (q e) -> p q e", e=65)
            pv1 = po[1][:, 0:nqs * 65].rearrange("p (q e) -> p q e", e=65)
            s.I("dve", "reciprocal", rec[:, 0, 0:nqs, :], pv0[:, :, 64:65], rd=[po[0]], wr=[rec])
            s.I("dve", "reciprocal", rec[:, 1, 0:nqs, :], pv1[:, :, 64:65], rd=[po[1]], wr=[rec])
            s.I("dve", "tensor_tensor", out=o0[:, 0:nqs, :], in0=pv0[:, :, 0:64], in1=rec[:, 0, 0:nqs, :].to_broadcast([128, nqs, 64]), op=ALU.mult, rd=[po[0], rec], wr=[o0])
            s.I("dve", "tensor_tensor", out=o1[:, 0:nqs, :], in0=pv1[:, :, 0:64], in1=rec[:, 1, 0:nqs, :].to_broadcast([128, nqs, 64]), op=ALU.mult, rd=[po[1], rec], wr=[o1])
            s.I("dve", "scalar_tensor_tensor", out=o0[:, 0:nqs, :], in0=o1[:, 0:nqs, :], scalar=neglam, in1=o0[:, 0:nqs, :], op0=ALU.mult, op1=ALU.add, rd=[o1, o0, ls], wr=[o0])
            s.I("pool", "tensor_tensor", out=sq[:, 0:nqs, :], in0=o0[:, 0:nqs, :], in1=o0[:, 0:nqs, :], op=ALU.mult, rd=[o0], wr=[sq])
            s.I("dve", "tensor_reduce", out=st[:, 0:nqs, 0], in_=sq[:, 0:nqs, :], axis=AX.X, op=ALU.add, rd=[sq], wr=[st])
            s.I("act", "activation", out=st[:, 0:nqs, 1], in_=st[:, 0:nqs, 0], func=AF.Ln, scale=1.0 / 64, bias=epsb[:, 0:1], rd=[st, epsb], wr=[st])
            s.I("act", "activation", out=st[:, 0:nqs, 2], in_=st[:, 0:nqs, 1], func=AF.Exp, scale=-0.5, rd=[st], wr=[st])
            s.I("dve", "tensor_tensor", out=sq[:, 0:nqs, :], in0=o0[:, 0:nqs, :], in1=st[:, 0:nqs, 2:3].to_broadcast([128, nqs, 64]), op=ALU.mult, rd=[o0, st], wr=[sq])
            s.I("pool", "tensor_tensor", out=y_[:, 0:nqs, :], in0=sq[:, 0:nqs, :], in1=sub[:].to_broadcast([128, nqs, 64]), op=ALU.mult, rd=[sq, sub], wr=[y_])
            s.dma("sp", self.Yd.t.ap()[b, q0:q0 + nq, 256 + h * 64:256 + (h + 1) * 64].rearrange("(q p) d -> p q d", p=128), y_[:, 0:nqs, :], self.Yd, y_)


        for f in s_steps(0):
            f()
        for idx in range(len(items)):
            A = s_steps(idx + 1) if idx + 1 < len(items) else []
            B = pv_steps(idx)
            ratio = max(1, len(B) // max(1, len(A)))
            ai = 0
            for bi, fb in enumerate(B):
                if bi % ratio == 0 and ai < len(A):
                    A[ai]()
                    ai += 1
                fb()
            while ai < len(A):
                A[ai]()
                ai += 1
            finish(idx)

    def phase_ssm(self, l, b):
        s = self.s
        last = l == DEPTH - 1
        ntl = 16 if last else 18
        with ExitStack() as mid:
            def msb(name, shape, dt):
                s.uid += 1
                t = mid.enter_context(self.nc.sbuf_tensor("%s_%d" % (name, s.uid), list(shape), dt))
                return Buf(name, t, True)
            XS = msb("ssXS", [128, 18, 512], F32)
            BMT = msb("ssBMT", [128, 2, T], BF16)
            CMT = msb("ssCMT", [128, 2, T], BF16)
            BM = msb("ssBM", [128, 18, 2, 128], BF16)
            DT = msb("ssDT", [128, 18, 16], F32)
            LA = msb("ssLA", [128, 18, 16], F32)
            with s.phase():
                self.ssm_prep(l, b, XS, BMT, CMT, BM, DT, LA)
            with s.phase():
                self.ssm_scan(l, b, XS, BMT, CMT, BM, DT, LA, ntl)
            for bb in (XS, BMT, CMT, BM, DT, LA):
                if bb.dsem is not None:
                    s.dfree.append(bb.dsem)
                    bb.dsem = None

    def ssm_prep(self, l, b, XS, BMT, CMT, BM, DT, LA):
        s = self.s
        w = self.load_w("wss", self.w_in, self.w_in.t.ap()[l, :, 1536:3088].rearrange("(k p) n -> p k n", p=128), [128, 8, 1552])
        dtb = self.bcast_row("ssdtb", self.dt_bias, l * 16, 16)
        alog = self.bcast_row("ssalog", self.a_log, l * 16, 16)
        A = s.sb("ssA", [128, 16], F32)
        s.I("act", "activation", out=A[:], in_=alog[:], func=AF.Exp, rd=[alog], wr=[A])
        s.I("dve", "tensor_scalar", out=A[:], in0=A[:], scalar1=-1.0, scalar2=None, op0=ALU.mult, rd=[A], wr=[A])
        cw = s.sb("sscw", [128, 8, 5], F32)
        cb = s.sb("sscb", [128, 8], F32)
        s.dma("sp", cw[:], self.conv_wc.t.ap()[l], cw, self.conv_wc)
        s.dma("sp", cb[:], self.conv_bc.t.ap()[l], cb, self.conv_bc)
        zs = [s.sb("sszs%d" % i, [128, 512], F32) for i in range(2)]
        tmp = s.sb("sstmp", [128, 18, 16], F32)
        for t in range(18):
            pz = s.ps()
            pd = s.ps()
            for k in range(8):
                s.I("pe", "matmul", pz[:, :], self.hT[:, k, t * 128:(t + 1) * 128], w[:, k, 0:512], start=(k == 0), stop=(k == 7), rd=[self.hT, w], wr=[pz])
            for k in range(8):
                s.I("pe", "matmul", pd[:, 0:16], self.hT[:, k, t * 128:(t + 1) * 128], w[:, k, 1536:1552], start=(k == 0), stop=(k == 7), rd=[self.hT, w], wr=[pd])
            z_ = zs[t % 2]
            s.I("act", "activation", out=z_[:], in_=pz[:, :], func=AF.Silu, rd=[pz], wr=[z_])
            s.dma("sp", self.Zd.t.ap()[t * 128:(t + 1) * 128, :], z_[:], self.Zd, z_)
            s.I("dve", "tensor_tensor", out=tmp[:, t, :], in0=pd[:, 0:16], in1=dtb[:], op=ALU.add, rd=[pd, dtb], wr=[tmp])
        s.I("act", "activation", out=tmp[:], in_=tmp[:], func=AF.Exp, rd=[tmp], wr=[tmp])
        s.I("act", "activation", out=DT[:], in_=tmp[:], func=AF.Ln, bias=1.0, scale=1.0, rd=[tmp], wr=[DT])
        s.I("dve", "tensor_tensor", out=LA[:], in0=DT[:], in1=A[:].rearrange("p (o d) -> p o d", o=1).to_broadcast([128, 18, 16]), op=ALU.mult, rd=[DT, A], wr=[LA])
        Gl = [s.sb("ssGl%d" % i, [128, L + 4], F32) for i in range(2)]
        Gc = [s.sb("ssGc%d" % i, [128, LC + 4], F32) for i in range(2)]
        acc = [s.sb("ssacc%d" % i, [128, T], F32) for i in range(2)]
        for i in range(2):
            s.I("pool", "memset", Gl[i][:], 0.0, wr=[Gl[i]])
            s.I("pool", "memset", Gc[i][:], 0.0, wr=[Gc[i]])
        groups = [(0, 512), (512, 512), (1024, 512), (1536, 512), (2048, 256)]

        def A1(c):
            gl, gc = Gl[c % 2], Gc[c % 2]
            for (t0, n) in groups:
                p = s.ps()
                for k in range(8):
                    s.I("pe", "matmul", p[:, 0:n], w[:, k, 512 + c * 128:512 + (c + 1) * 128], self.hT[:, k, t0:t0 + n], start=(k == 0), stop=(k == 7), rd=[w, self.hT], wr=[p])
                if t0 < L:
                    s.I("act", "activation", out=gl[:, 2 + t0:2 + t0 + n], in_=p[:, 0:n], func=AF.Copy, rd=[p], wr=[gl])
                else:
                    s.I("act", "activation", out=gc[:, 2:2 + n], in_=p[:, 0:n], func=AF.Copy, rd=[p], wr=[gc])

        def A2(c):
            gl, gc, a = Gl[c % 2], Gc[c % 2], acc[c % 2]
            for (gb, o0, n) in ((gl, 0, L), (gc, L, LC)):
                s.I("dve", "tensor_scalar", out=a[:, o0:o0 + n], in0=gb[:, 0:n], scalar1=cw[:, c, 0:1], scalar2=None, op0=ALU.mult, rd=[gb, cw], wr=[a])
                for k in range(1, 5):
                    s.I("dve", "scalar_tensor_tensor", out=a[:, o0:o0 + n], in0=gb[:, k:k + n], scalar=cw[:, c, k:k + 1], in1=a[:, o0:o0 + n],
                        op0=ALU.mult, op1=ALU.add, rd=[gb, cw, a], wr=[a])
            if c < 6:
                s.I("act", "activation", out=a[:], in_=a[:], func=AF.Silu, bias=cb[:, c:c + 1], scale=1.0, rd=[a, cb], wr=[a])
            else:
                s.I("act", "activation", out=CMT[:, c - 6, :], in_=a[:], func=AF.Silu, bias=cb[:, c:c + 1], scale=1.0, rd=[a, cb], wr=[CMT])

        def Bst(c):
            a = acc[c % 2]
            if c >= 6:
                return
            if c in (4, 5):
                s.I("pool", "tensor_copy", BMT[:, c - 4, :], a[:], rd=[a], wr=[BMT])
            for g4 in range(5):
                tiles = list(range(4 * g4, min(4 * g4 + 4, 18)))
                p = s.ps()
                for ti, t in enumerate(tiles):
                    s.I("pe", "transpose", p[:, ti * 128:(ti + 1) * 128], a[:, t * 128:(t + 1) * 128], self.ident[:], rd=[a, self.ident], wr=[p])
                if c < 4:
                    dst, dbuf = XS[:, tiles[0]:tiles[0] + len(tiles), c * 128:(c + 1) * 128], XS
                else:
                    dst, dbuf = BM[:, tiles[0]:tiles[0] + len(tiles), c - 4, :], BM
                s.I("act", "activation", out=dst, in_=p[:, 0:len(tiles) * 128].rearrange("p (t d) -> p t d", d=128), func=AF.Copy, rd=[p], wr=[dbuf])

        A1(0)
        A1(1)
        A2(0)
        for c in range(1, 8):
            if c + 1 < 8:
                A1(c + 1)
            Bst(c - 1)
            A2(c)
        Bst(7)

    def ssm_scan(self, l, b, XS, BMT, CMT, BM, DT, LA, ntl):
        s = self.s
        TRI = s.sb("ssTRI", [128, 5, 128], F32)
        s.dma("sp", TRI[:], self.tri_d.t.ap().rearrange("a p n -> p a n"), TRI, self.tri_d)
        SL, SU, LE, GE, ON = (TRI[:, i, :] for i in range(5))
        Y = s.sb("ssY", [128, 18, 512], F32)
        s.I("pool", "memset", Y[:], 0.0, wr=[Y])
        ST = [[s.sb("ssST%d%d" % (d, g), [128, 4, 64], F32) for g in range(2)] for d in range(2)]
        STb = [[s.sb("ssSTb%d%d" % (d, g), [128, 256], BF16) for g in range(2)] for d in range(2)]
        for d in range(2):
            for g in range(2):
                s.I("pool", "memset", ST[d][g][:], 0.0, wr=[ST[d][g]])
                s.I("pool", "memset", STb[d][g][:], 0.0, wr=[STb[d][g]])
        NEGM = s.sb("ssNEGM", [128, 2, 4, 128], BF16)
        s.dma("pool", NEGM[:], self.negm_d.t.ap(), NEGM, self.negm_d)
        E12 = s.sb("ssE12", [128, 2, 8, 128], BF16)
        s.dma("pool", E12[:], self.sel_d.t.ap(), E12, self.sel_d)
        EX = [s.sb("ssEX%d" % i, [128, 3, 8], F32) for i in range(3)]
        CST = [s.sb("ssCST%d" % i, [128, 128], BF16) for i in range(3)]
        R1 = [s.sb("ssR1%d" % i, [8, 128], F32) for i in range(3)]
        for i in range(3):
            s.I("pool", "memset", CST[i][:], 0.0, wr=[CST[i]])
        XDT = [s.sb("ssXDT%d" % i, [128, 8, 64], BF16) for i in range(3)]
        XDD = [s.sb("ssXDD%d" % i, [128, 8, 64], BF16) for i in range(3)]
        cbs = [s.sb("sscbs%d" % i, [128, 1, 128], F32) for i in range(4)]
        Lt = [s.sb("ssLt%d" % i, [128, 4, 128], F32) for i in range(2)] * 2
        Wm = [s.sb("ssW%d" % i, [128, 4, 128], BF16) for i in range(4)]
        tm1 = [s.sb("sstm1%d" % i, [128, 4, 64], F32) for i in range(4)]
        yt = [s.sb("ssyt%d" % i, [128, 4, 64], F32) for i in range(4)]
        tm2 = [s.sb("sstm2%d" % i, [128, 4, 64], F32) for i in range(4)]
        order = [(0, 16), (1, 17), (0, 17), (1, 16)]
        for i in range(16):
            order.append((0, i))
            order.append((1, 15 - i))
        ctxs = {}

        def stageA(k):
            dr, t = order[k]
            want_y = t < ntl
            ex, cst, xdt, xdd = EX[k % 3], CST[k % 3], XDT[k % 3], XDD[k % 3]
            la8 = LA[:, t, dr * 8:(dr + 1) * 8]
            pe_ = s.ps()
            m1, m2 = (SL, LE) if dr == 0 else (SU, GE)
            s.I("pe", "matmul", pe_[:, 0:8], m1, la8, start=True, stop=True, rd=[TRI, LA], wr=[pe_])
            s.I("pe", "matmul", pe_[:, 8:16], m2, la8, start=True, stop=True, rd=[TRI, LA], wr=[pe_])
            s.I("pe", "matmul", pe_[:, 16:24], ON, la8, start=True, stop=True, rd=[TRI, LA], wr=[pe_])
            s.I("act", "activation", out=ex[:].rearrange("p a h -> p (a h)"), in_=pe_[:, 0:24], func=AF.Exp, rd=[pe_], wr=[ex])
            s.I("dve", "tensor_tensor", out=xdt[:], in0=XS[:, t, :].rearrange("p (h d) -> p h d", d=64),
                in1=DT[:, t, dr * 8:(dr + 1) * 8].rearrange("p (h o) -> p h o", o=1).to_broadcast([128, 8, 64]), op=ALU.mult, rd=[XS, DT], wr=[xdt])
            s.I("dve", "tensor_tensor", out=xdd[:], in0=xdt[:], in1=ex[:, 0, :].rearrange("p (h o) -> p h o", o=1).to_broadcast([128, 8, 64]), op=ALU.mult,
                rd=[xdt, ex], wr=[xdd])
            cb2 = []
            if want_y:
                pcs = s.ps()
                s.I("pe", "matmul", pcs[0:8, 0:128], la8, m2, start=True, stop=True, rd=[LA, TRI], wr=[pcs])
                r1 = R1[k % 3]
                s.I("act", "activation", out=cst[0:8, :], in_=pcs[0:8, 0:128], func=AF.Copy, rd=[pcs], wr=[cst])
                s.I("dve", "tensor_tensor", out=r1[:], in0=pcs[0:8, 0:128], in1=cst[0:8, :], op=ALU.subtract, rd=[pcs, cst], wr=[r1])
                s.I("act", "activation", out=cst[32:40, :], in_=r1[:], func=AF.Copy, rd=[r1], wr=[cst])
                for g in range(2):
                    cb_ = cbs[(2 * k + g) % 4]
                    pc = s.ps()
                    s.I("pe", "matmul", pc[:, 0:128], BMT[:, g, t * 128:(t + 1) * 128], CMT[:, g, t * 128:(t + 1) * 128], start=True, stop=True, rd=[BMT, CMT], wr=[pc])
                    s.I("act", "activation", out=cb_[:, 0, :], in_=pc[:, 0:128], func=AF.Copy, rd=[pc], wr=[cb_])
                    cb2.append(cb_)
            ctxs[k] = (ex, cst, xdt, xdd, cb2)

        def stageB(k):
            dr, t = order[k]
            want_y = t < ntl
            ex, cst, xdt, xdd, cb2 = ctxs[k]
            ws = []
            if want_y:
                for g in range(2):
                    lt, wm = Lt[(2 * k + g) % 4], Wm[(2 * k + g) % 4]
                    pd_ = s.ps()
                    s.I("pe", "matmul", pd_[:, :], cst[:], E12[:, 1, g * 4:(g + 1) * 4, :].rearrange("p e i -> p (e i)"), start=True, stop=False, rd=[E12, cst], wr=[pd_])
                    s.I("pe", "matmul", pd_[:, :], self.identb[:], NEGM[:, dr, :, :].rearrange("p e i -> p (e i)"), start=False, stop=False, rd=[self.identb, NEGM], wr=[pd_])
                    for e in range(4):
                        h8 = g * 4 + e
                        s.I("pe", "matmul", pd_[:, e * 128:(e + 1) * 128], E12[:, 0, h8, :], cst[:], start=False, stop=(e == 3), rd=[E12, cst], wr=[pd_])
                    s.I("act", "activation", out=lt[:].rearrange("p e i -> p (e i)"), in_=pd_[:, :], func=AF.Exp, rd=[pd_], wr=[lt])
                    s.I("dve", "tensor_tensor", out=wm[:], in0=lt[:], in1=cb2[g][:].to_broadcast([128, 4, 128]), op=ALU.mult, rd=[lt, cb2[g]], wr=[wm])
                    ws.append(wm)
            ctxs[k] = (ex, cst, xdt, xdd, ws)

        def stageC(k):
            dr, t = order[k]
            want_y = t < ntl
            ex, cst, xdt, xdd, ws = ctxs.pop(k)
            for g in range(2):
                if want_y:
                    py = s.psacc(g)
                    for e in range(4):
                        s.I("pe", "matmul", py[:, e * 64:(e + 1) * 64], ws[g][:, e, :], xdt[:, g * 4 + e, :], start=True, stop=True, rd=[ws[g], xdt], wr=[py])
                    po = s.ps()
                    s.I("pe", "matmul", po[:, 0:256], CMT[:, g, t * 128:(t + 1) * 128], STb[dr][g][:], start=True, stop=True, rd=[CMT, STb[dr][g]], wr=[po])
                    a_, y_ = tm1[(2 * k + g) % 4], yt[(2 * k + g) % 4]
                    s.I("dve", "tensor_tensor", out=a_[:], in0=po[:, 0:256].rearrange("p (h d) -> p h d", d=64),
                        in1=ex[:, 1, g * 4:(g + 1) * 4].rearrange("p (h o) -> p h o", o=1).to_broadcast([128, 4, 64]), op=ALU.mult, rd=[po, ex], wr=[a_])
                    s.I("dve", "tensor_tensor", out=y_[:], in0=py[:, 0:256].rearrange("p (h d) -> p h d", d=64), in1=a_[:], op=ALU.add, rd=[py, a_], wr=[y_])
                    yv = Y[:, t, g * 256:(g + 1) * 256].rearrange("p (h d) -> p h d", d=64)
                    s.I("pool", "tensor_tensor", out=yv, in0=yv, in1=y_[:], op=ALU.add, rd=[Y, y_], wr=[Y])
                pst = s.ps()
                s.I("pe", "matmul", pst[:, 0:256], BM[:, t, g, :], xdd[:, g * 4:(g + 1) * 4, :], start=True, stop=True, rd=[BM, xdd], wr=[pst])
                c_ = tm2[(2 * k + g) % 4]
                s.I("pool", "tensor_tensor", out=c_[:], in0=ST[dr][g][:], in1=ex[:, 2, g * 4:(g + 1) * 4].rearrange("p (h o) -> p h o", o=1).to_broadcast([128, 4, 64]),
                    op=ALU.mult, rd=[ST[dr][g], ex], wr=[c_])
                s.I("dve", "tensor_tensor", out=ST[dr][g][:], in0=pst[:, 0:256].rearrange("p (h d) -> p h d", d=64), in1=c_[:], op=ALU.add, rd=[pst, c_], wr=[ST[dr][g]])
                s.I("act", "activation", out=STb[dr][g][:], in_=ST[dr][g][:].rearrange("p h d -> p (h d)"), func=AF.Copy, rd=[ST[dr][g]], wr=[STb[dr][g]])

        n = len(order)
        for k in range(n + 2):
            if k < n:
                stageA(k)
            if 1 <= k <= n:
                stageB(k - 1)
            if k >= 2:
                stageC(k - 2)
        dsk = self.bcast_row("ssdsk", self.ssm_d, l * 8, 8)
        ng = self.bcast_row("ssng", self.ssm_norm, l * 512, 512)
        zt = [s.sb("sszt%d" % i, [128, 512], F32) for i in range(3)]
        u = [s.sb("ssu%d" % i, [128, 512], F32) for i in range(3)]
        junks = [s.sb("ssjunk%d" % i, [128, 512], BF16) for i in range(1)] * 2
        sts = [s.sb("ssfst%d" % i, [128, 1, 3], F32) for i in range(4)]
        def zload(t):
            s.dma("sp", zt[t % 3][:], self.Zd.t.ap()[t * 128:(t + 1) * 128, :], zt[t % 3], self.Zd)
        zload(0)
        zload(1)
        for t in range(ntl):
            z_, u_ = zt[t % 3], u[t % 3]
            st, junk = sts[t % 4], junks[t % 2]
            if t + 2 < ntl:
                zload(t + 2)
            s.I("dve", "tensor_tensor", out=u_[:].rearrange("p (h d) -> p h d", d=64), in0=XS[:, t, :].rearrange("p (h d) -> p h d", d=64),
                in1=dsk[:].rearrange("p (h o) -> p h o", o=1).to_broadcast([128, 8, 64]), op=ALU.mult, rd=[XS, dsk], wr=[u_])
            s.I("pool", "tensor_tensor", out=u_[:], in0=u_[:], in1=Y[:, t, :], op=ALU.add, rd=[u_, Y], wr=[u_])
            s.I("dve", "tensor_tensor", out=u_[:], in0=u_[:], in1=z_[:], op=ALU.mult, rd=[u_, z_], wr=[u_])
            s.I("act", "activation", out=junk[:], in_=u_[:], func=AF.Square, accum_out=st[:, 0, 0:1], rd=[u_], wr=[junk, st])
            s.I("act", "activation", out=st[:, 0, 1:2], in_=st[:, 0, 0:1], func=AF.Sqrt, scale=1.0 / 512, bias=EPS, rd=[st], wr=[st])
            s.I("dve", "reciprocal", st[:, 0, 2:3], st[:, 0, 1:2], rd=[st], wr=[st])
            s.I("dve", "scalar_tensor_tensor", out=u_[:], in0=u_[:], scalar=st[:, 0, 2:3], in1=ng[:], op0=ALU.mult, op1=ALU.mult, rd=[u_, st, ng], wr=[u_])
            s.dma("sp", self.Yd.t.ap()[b, t * 128:(t + 1) * 128, 512:1024], u_[:], self.Yd, u_)


def _consts():
    ident = np.eye(128, dtype=np.float32)
    i = np.arange(128)
    SL = (i[:, None] > i[None, :]).astype(np.float32)
    SU = (i[:, None] < i[None, :]).astype(np.float32)
    LE = (i[:, None] <= i[None, :]).astype(np.float32)
    GE = (i[:, None] >= i[None, :]).astype(np.float32)
    ON = np.ones((128, 128), np.float32)
    tri = np.stack([SL, SU, LE, GE, ON]).astype(np.float32)
    qc = np.arange(64)
    ws = np.clip(qc - 8, 0, 48)
    kc = np.arange(64)
    ok = (kc[:, None] >= ws[None, :]) & (kc[:, None] < ws[None, :] + 16)
    m = np.where(ok, 0.0, NEG).astype(np.float32)
    na_mask = np.concatenate([m, m], axis=0)
    per_axis = 16
    inv_freq = (10000.0 ** (-np.arange(0, per_axis, 2, dtype=np.float32) / per_axis)).astype(np.float32)
    t = np.arange(L)
    pos = np.stack([t // 64, t % 64], axis=-1).astype(np.float32)
    ang = pos[:, :, None] * inv_freq
    ang = np.concatenate([ang, ang], axis=-1).reshape(L, 32)
    cos = np.cos(ang).astype(np.float32)
    sin = np.sin(ang).astype(np.float32).reshape(L, 2, 2, 8).copy()
    sin[:, :, 0, :] *= -1.0
    negm = np.stack([np.where(i[None, :] < i[:, None], NEG, 0.0), np.where(i[None, :] > i[:, None], NEG, 0.0)], axis=1).astype(np.float32)
    negm = np.ascontiguousarray(np.broadcast_to(negm[:, :, None, :], (128, 2, 4, 128))).astype(np.float32)
    sel = np.zeros((128, 2, 8, 128), np.float32)
    for h in range(8):
        sel[h, 0, h, :] = 1.0
        sel[32 + h, 0, h, :] = 1.0
    sel[:, 1] = -sel[:, 0]
    return ident, tri, na_mask, cos, sin.reshape(L, 32), negm, sel


def make_in_maps(inp):
    f = lambda a: np.ascontiguousarray(np.asarray(a, dtype=np.float32))
    ident, tri, na_mask, cos, sin, negm, sel = _consts()
    colv = lambda v, n: f(np.asarray(v).reshape(DEPTH, n, 128).transpose(0, 2, 1))
    idx = np.clip(np.arange(64)[:, None] - np.arange(64)[None, :], -15, 15) + 15
    shared = {
        "w_ada": f(inp["w_ada"]), "b_ada": f(inp["b_ada"]),
        "g_mixc": colv(inp["g_mix"], 8), "g_ffnc": colv(inp["g_ffn"], 8),
        "w_in": f(inp["w_in"]), "w_out": f(inp["w_out"]), "w_up": f(inp["ffn_w_up"]), "w_down": f(inp["ffn_w_down"]),
        "na_qg": f(inp["na_q_gain"]), "na_kg": f(inp["na_k_gain"]),
        "rpb_t": f(np.asarray(inp["na_rpb"])[..., idx]), "na_mask": na_mask,
        "df_qg": f(inp["df_q_gain"]), "df_kg": f(inp["df_k_gain"]), "df_lam": f(inp["df_lambda"]), "df_subln": f(inp["df_subln"]),
        "rope_cos": cos, "rope_sin": sin,
        "conv_wc": f(np.asarray(inp["ssm_conv_w"]).reshape(DEPTH, 5, 8, 128).transpose(0, 3, 2, 1)),
        "conv_bc": colv(inp["ssm_conv_b"], 8),
        "dt_bias": f(np.asarray(inp["ssm_dt_bias"]).reshape(DEPTH, 16)), "a_log": f(np.asarray(inp["ssm_a_log"]).reshape(DEPTH, 16)),
        "ssm_d": f(inp["ssm_d"]), "ssm_norm": f(inp["ssm_norm"]),
        "fconv_wc": f(np.asarray(inp["ffn_conv_w"]).reshape(DEPTH, 3, NF, 128).transpose(0, 3, 2, 1)),
        "fconv_bc": colv(inp["ffn_conv_b"], NF),
        "ident": ident, "tri": tri, "negm": negm, "sel": sel,
    }
    maps = []
    x, c, ctx, c_ctx = (np.asarray(inp[k]) for k in ("x", "c", "ctx", "c_ctx"))
    for i in range(NCORES):
        sl = slice(i * NB, (i + 1) * NB)
        c3 = np.concatenate([c[sl], c_ctx[None, :]], axis=0)
        cT = f(c3.reshape(3, 8, 128).transpose(2, 1, 0))
        m = dict(shared)
        m.update({"x": f(x[sl]), "ctx": f(ctx[sl]), "cT": cT})
        maps.append(m)
    return maps


_NC_CACHE = {}


def kernel(**inputs):
    if "nc" not in _NC_CACHE:
        _NC_CACHE["nc"] = Builder().build()
    nc = _NC_CACHE["nc"]
    maps = make_in_maps(inputs)
    res = run_bass_kernel_spmd(nc, maps, core_ids=list(range(NCORES)))
    return np.concatenate([np.asarray(r["out"]) for r in res.results], axis=0).astype(np.float32)
```

```python
import math
from contextlib import ExitStack

import numpy as np
import concourse.bass as bass
import concourse.mybir as mybir
from concourse.bass_utils import run_bass_kernel_spmd

F32 = mybir.dt.float32
BF16 = mybir.dt.bfloat16
AF = mybir.ActivationFunctionType
ALU = mybir.AluOpType
AX = mybir.AxisListType

NCORES = 8
NB = 2
L = 2048
LC = 256
T = L + LC
D = 1024
DEPTH = 2
DFF = 2816
NF = DFF // 128
EPS = 1e-6
NEG = -30000.0


class Buf:
    __slots__ = ("name", "w", "r", "dsem", "t", "persist")

    def __init__(self, name, t=None, persist=False):
        self.name = name
        self.w = {}
        self.r = {}
        self.dsem = None
        self.t = t
        self.persist = persist

    def __getitem__(self, idx):
        return self.t[idx]


class Sched:
    ENG = ("pe", "act", "dve", "pool", "sp")

    def __init__(self, nc, es, ndsem=48):
        self.nc = nc
        self.ges = es
        self.es = es
        self.eng = {"pe": nc.tensor, "act": nc.scalar, "dve": nc.vector, "pool": nc.gpsimd, "sp": nc.sync}
        self.sem = {k: es.enter_context(nc.semaphore("c_" + k)) for k in self.ENG}
        self.cnt = {k: 0 for k in self.ENG}
        self.seen = {k: {} for k in self.ENG}
        self.prog = {k: [] for k in self.ENG}
        self.dsems = [es.enter_context(nc.semaphore("d%d" % i)) for i in range(ndsem)]
        self.dval = [0] * ndsem
        self.dfree = list(range(ndsem))
        self.phase_bufs = []
        self.uid = 0
        self.ps_rr = 0
        self.PS = []

    def sb(self, name, shape, dt, persist=False):
        self.uid += 1
        es = self.ges if persist else self.es
        t = es.enter_context(self.nc.sbuf_tensor("%s_%d" % (name, self.uid), list(shape), dt))
        b = Buf(name, t, persist)
        if not persist:
            self.phase_bufs.append(b)
        return b

    def dram(self, name, shape, dt, kind="Internal"):
        t = self.nc.dram_tensor(name, list(shape), dt, kind=kind)
        return Buf(name, t, True)

    def psum_init(self):
        self.PSP = []
        for j in range(4):
            t = self.ges.enter_context(self.nc.psum_tensor("psp%d" % j, [128, 1024], F32))
            self.PSP.append(t)
            self.PS.append(Buf("ps%d" % (2 * j), t[:, 0:512], True))
            self.PS.append(Buf("ps%d" % (2 * j + 1), t[:, 512:1024], True))

    def pspair(self):
        if self.ps_rr % 2:
            self.ps_rr += 1
        j = (self.ps_rr % 6) // 2
        self.ps_rr += 2
        return self.PS[2 * j], self.PS[2 * j + 1], self.PSP[j]

    def ps(self):
        b = self.PS[self.ps_rr % 6]
        self.ps_rr += 1
        return b

    def psacc(self, i):
        return self.PS[6 + (i % 2)]

    def _deps(self, e, reads, writes):
        d = {}
        own = "c_" + e

        def add(tokdict, is_read_set):
            for key, (sem, val) in tokdict.items():
                if key == own and (e == "pe" or is_read_set):
                    continue
                if d.get(key, (None, 0))[1] < val:
                    d[key] = (sem, val)

        for b in reads:
            add(b.w, False)
        for b in writes:
            add(b.w, False)
            add(b.r, True)
        return d

    def _wait(self, e, d):
        seen = self.seen[e]
        for key, (sem, val) in d.items():
            if seen.get(key, 0) >= val:
                continue
            self.prog[e].append(("w", sem, val))
            seen[key] = val

    def I(self, e, name, *args, rd=(), wr=(), **kw):
        d = self._deps(e, rd, wr)
        self._wait(e, d)
        self.cnt[e] += 1
        self.prog[e].append(("i", name, args, kw, self.sem[e], 1))
        key = "c_" + e
        tok = (self.sem[e], self.cnt[e])
        for b in rd:
            b.r[key] = tok
        for b in wr:
            b.w[key] = tok

    def dma(self, e, out_ap, in_ap, dst, src, **kw):
        if dst.dsem is None:
            dst.dsem = self.dfree.pop()
        i = dst.dsem
        d = self._deps(e, [src], [dst])
        self._wait(e, d)
        self.dval[i] += 16
        self.prog[e].append(("i", "dma_start", (), dict(out=out_ap, in_=in_ap, **kw), self.dsems[i], 16))
        key = "d%d" % i
        tok = (self.dsems[i], self.dval[i])
        src.r[key] = tok
        dst.w[key] = tok

    def drain(self):
        d = {}
        for i, v in enumerate(self.dval):
            if v:
                d["d%d" % i] = (self.dsems[i], v)
        self._wait("sp", d)

    def emit(self):
        self.drain()
        prog = self.prog
        with self.nc.Block() as block:
            def mk(e):
                def body(g):
                    for it in prog[e]:
                        if it[0] == "w":
                            g.wait_ge(it[1], it[2])
                        else:
                            getattr(g, it[1])(*it[2], **it[3]).then_inc(it[4], it[5])
                return body
            block.tensor(mk("pe"))
            block.scalar(mk("act"))
            block.vector(mk("dve"))
            block.gpsimd(mk("pool"))
            block.sync(mk("sp"))
        self.prog = {k: [] for k in self.ENG}
        for b in self.phase_bufs:
            if b.dsem is not None:
                self.dfree.append(b.dsem)
                b.dsem = None
        self.phase_bufs = []

    class _Phase:
        def __init__(self, s):
            self.s = s

        def __enter__(self):
            self.es = ExitStack()
            self.es.__enter__()
            self.s.es = self.es
            return self

        def __exit__(self, *a):
            if a[0] is None:
                self.s.emit()
            self.s.es = self.s.ges
            return self.es.__exit__(*a)

    def phase(self):
        return Sched._Phase(self)


def dap(buf, off, dims):
    return bass.AP(buf.t, off, [list(d) for d in dims])


class Builder:
    def __init__(self, dbg=None):
        self.dbg = dbg or {}
        self.nc = bass.Bass("TRN2", target_bir_lowering=False)
        self.outs = []
        self.pre = {}
        self.scoped = []

    def declare(self, s):
        I = lambda n, sh, dt=F32: s.dram(n, sh, dt, kind="ExternalInput")
        self.x = I("x", [NB, L, D])
        self.ctx = I("ctx", [NB, LC, D])
        self.cT = I("cT", [128, 8, 3])
        self.w_ada = I("w_ada", [DEPTH, D, 6 * D])
        self.b_ada = I("b_ada", [DEPTH, 6 * D])
        self.g_mixc = I("g_mixc", [DEPTH, 128, 8])
        self.g_ffnc = I("g_ffnc", [DEPTH, 128, 8])
        self.w_in = I("w_in", [DEPTH, D, 3088])
        self.w_out = I("w_out", [DEPTH, D, D])
        self.w_up = I("w_up", [DEPTH, D, 2 * DFF])
        self.w_down = I("w_down", [DEPTH, DFF, D])
        self.na_qg = I("na_qg", [DEPTH, 64])
        self.na_kg = I("na_kg", [DEPTH, 64])
        self.rpb_t = I("rpb_t", [DEPTH, 4, 15, 64, 64])
        self.na_mask = I("na_mask", [128, 64])
        self.df_qg = I("df_qg", [DEPTH, 32])
        self.df_kg = I("df_kg", [DEPTH, 32])
        self.df_lam = I("df_lam", [DEPTH, 4, 32])
        self.df_subln = I("df_subln", [DEPTH, 64])
        self.rope_cos = I("rope_cos", [L, 32])
        self.rope_sin = I("rope_sin", [L, 32])
        self.conv_wc = I("conv_wc", [DEPTH, 128, 8, 5])
        self.conv_bc = I("conv_bc", [DEPTH, 128, 8])
        self.dt_bias = I("dt_bias", [DEPTH, 16])
        self.a_log = I("a_log", [DEPTH, 16])
        self.ssm_d = I("ssm_d", [DEPTH, 8])
        self.ssm_norm = I("ssm_norm", [DEPTH, 512])
        self.fconv_wc = I("fconv_wc", [DEPTH, 128, NF, 3])
        self.fconv_bc = I("fconv_bc", [DEPTH, 128, NF])
        self.ident_d = I("ident", [128, 128])
        self.tri_d = I("tri", [5, 128, 128])
        self.negm_d = I("negm", [128, 2, 4, 128])
        self.sel_d = I("sel", [128, 2, 8, 128])
        self.out = s.dram("out", [NB, L, D], F32, kind="ExternalOutput")
        dk = "ExternalOutput" if self.dbg.get("dump") else "Internal"
        self.modrow_d = s.dram("modrow_d", [DEPTH, 3, 6 * D], F32, kind=dk)
        self.XA = [s.dram("XA%d" % l, [NB, T, D], F32, kind=dk) for l in range(DEPTH)]
        self.XB = s.dram("XB0", [NB, T, D], F32, kind=dk)
        self.Yd = s.dram("Yd", [NB, T, D], F32, kind=dk)
        self.Zd = s.dram("Zd", [T, 512], F32, kind=dk)
        self.ATd = s.dram("ATd", [NF, 128, T], BF16, kind=dk)
        self.hTd = s.dram("hTd", [128, 8, T], BF16, kind=dk) if self.dbg.get("dump") else None
        self.Yin = I("Yin", [DEPTH, NB, T, D]) if self.dbg.get("feed_y") else None

    def build(self):
        nc = self.nc
        with ExitStack() as es:
            s = Sched(nc, es)
            self.s = s
            self.declare(s)
            s.psum_init()
            self.ident = s.sb("ident", [128, 128], F32, persist=True)
            self.identb = s.sb("identb", [128, 128], BF16, persist=True)
            self.hT = s.sb("hT", [128, 8, T], BF16, persist=True)
            self.AB = [[s.sb("AB%d%d" % (l, i), [128, 8, 3], F32, persist=True) for i in range(4)] for l in range(DEPTH)]
            self.BT = s.sb("BT", [128, 4, 14, 64], BF16, persist=True)
            ph = self.dbg.get("phases")
            with s.phase():
                s.dma("sp", self.ident[:], self.ident_d.t.ap(), self.ident, self.ident_d)
                s.I("act", "activation", out=self.identb[:], in_=self.ident[:], func=AF.Copy, rd=[self.ident], wr=[self.identb])
                self.phase_adaln()
            for l in self.dbg.get("layers", range(DEPTH)):
                last = l == DEPTH - 1
                nt = 16 if last else 18
                if ph is None or "na" in ph:
                    with s.phase():
                        self.phase_na_bias(l)
                for b in range(NB):
                    if l == 0:
                        src = lambda t, b=b: (self.x.t.ap()[b, t * 128:(t + 1) * 128, :], self.x) if t < 16 else \
                            (self.ctx.t.ap()[b, (t - 16) * 128:(t - 15) * 128, :], self.ctx)
                    else:
                        src = lambda t, b=b: (self.XB.t.ap()[b, t * 128:(t + 1) * 128, :], self.XB)
                    full = ph is None
                    with ExitStack() as sc1:
                        wdf_buf = self.alloc_scoped(sc1, "wdf", [128, 8, 768]) if full else None
                        with s.phase():
                            if full:
                                self.pre["wna"] = self.w_na(l)
                            self.phase_norm(src, self.AB[l][0], self.AB[l][1], b, 18)
                            if self.dbg.get("dump") and l == self.dbg.get("dump_l", 0) and b == 0 and self.dbg.get("dump_h") == "mix":
                                s.dma("sp", self.hTd.t.ap(), self.hT[:], self.hTd, self.hT)
                            if ph is None or "na" in ph:
                                if full:
                                    self.pre["wdf"] = self.w_df(l, wdf_buf)
                                self.phase_na(l, b)
                        if ph is None or "df" in ph:
                            with s.phase():
                                self.phase_df(l, b)
                        self.release_scoped()
                    if ph is None or "ssm" in ph:
                        self.phase_ssm(l, b)
                    if ph is None or "out" in ph:
                        with s.phase():
                            self.phase_out(l, b, src, nt)
                    if ph is None or "ffn" in ph:
                        srcA = lambda t, b=b, l=l: (self.XA[l].t.ap()[b, t * 128:(t + 1) * 128, :], self.XA[l])
                        with s.phase():
                            self.phase_norm(srcA, self.AB[l][2], self.AB[l][3], b, nt)
                        with ExitStack() as sc2:
                            wdn_buf = self.alloc_scoped(sc2, "wdn", [128, NF, D])
                            with s.phase():
                                self.pre["wdn"] = self.w_dn(l, wdn_buf)
                                self.phase_ffn_up(l, b, nt)
                            with s.phase():
                                self.phase_ffn_down(l, b, srcA, nt)
                            self.release_scoped()
        return nc

    def phase_adaln(self):
        s = self.s
        cT = s.sb("cT", [128, 8, 3], F32)
        siluT = s.sb("siluT", [128, 8, 3], F32)
        s.dma("sp", cT[:], self.cT.t.ap(), cT, self.cT)
        s.I("act", "activation", out=siluT[:], in_=cT[:], func=AF.Silu, rd=[cT], wr=[siluT])
        wa = [s.sb("wa%d" % i, [128, 8, 512], F32) for i in range(3)]
        for l in range(DEPTH):
            brow = s.sb("brow%d" % l, [3, 6 * D], F32)
            modrow = s.sb("modrow%d" % l, [3, 6 * D], F32)
            s.dma("sp", brow[:], dap(self.b_ada, l * 6 * D, [[0, 3], [1, 6 * D]]), brow, self.b_ada)
            for j in range(12):
                w = wa[j % 3]
                s.dma("sp", w[:], self.w_ada.t.ap()[l, :, j * 512:(j + 1) * 512].rearrange("(k p) n -> p k n", p=128), w, self.w_ada)
                pm = s.ps()
                for k in range(8):
                    s.I("pe", "matmul", pm[0:3, :], siluT[:, k, :], w[:, k, :], start=(k == 0), stop=(k == 7), rd=[siluT, w], wr=[pm])
                s.I("dve", "tensor_tensor", out=modrow[:, j * 512:(j + 1) * 512], in0=pm[0:3, :], in1=brow[:, j * 512:(j + 1) * 512], op=ALU.add,
                    rd=[pm, brow], wr=[modrow])
            s.dma("sp", self.modrow_d.t.ap()[l], modrow[:], self.modrow_d, modrow)
            pT = s.ps()
            for c in range(48):
                s.I("pe", "transpose", pT[:, c * 3:(c + 1) * 3], modrow[0:3, c * 128:(c + 1) * 128], self.ident[0:3, 0:3], rd=[modrow, self.ident], wr=[pT])
            modcol = s.sb("modcol%d" % l, [128, 48, 3], F32)
            s.I("dve", "tensor_copy", modcol[:].rearrange("p a b -> p (a b)"), pT[:, 0:144], rd=[pT], wr=[modcol])
            gm = s.sb("gm%d" % l, [128, 8, 1], F32)
            gf = s.sb("gf%d" % l, [128, 8, 1], F32)
            s.dma("sp", gm[:, :, 0], self.g_mixc.t.ap()[l], gm, self.g_mixc)
            s.dma("sp", gf[:, :, 0], self.g_ffnc.t.ap()[l], gf, self.g_ffnc)
            A1, B1, A2, B2 = self.AB[l]
            for (A, Bv, g, sc0, sh0) in ((A1, B1, gm, 8, 0), (A2, B2, gf, 32, 24)):
                s.I("dve", "scalar_tensor_tensor", out=A[:], in0=modcol[:, sc0:sc0 + 8, :], scalar=1.0, in1=g[:].to_broadcast([128, 8, 3]),
                    op0=ALU.add, op1=ALU.mult, rd=[modcol, g], wr=[A])
                s.I("dve", "tensor_copy", Bv[:], modcol[:, sh0:sh0 + 8, :], rd=[modcol], wr=[Bv])

    def phase_norm(self, src, A, Bv, b, ntiles):
        s = self.s
        xt = [s.sb("nxt%d" % i, [128, D], F32) for i in range(4)]
        xn = [s.sb("nxn%d" % i, [128, D], F32) for i in range(8)]
        junks = [s.sb("njunk%d" % i, [128, D], BF16) for i in range(2)]
        sts = [s.sb("nst%d" % i, [128, 1, 3], F32) for i in range(8)]
        ngroups = (ntiles + 3) // 4
        for g in range(ngroups):
            tiles = list(range(4 * g, min(4 * g + 4, ntiles)))
            j = b if tiles[0] < 16 else 2
            for ti, t in enumerate(tiles):
                ap, sbuf = src(t)
                x_ = xt[t % 4]
                n_ = xn[t % 8]
                s.dma("sp", x_[:], ap, x_, sbuf)
                st, junk = sts[t % 8], junks[t % 2]
                s.I("act", "activation", out=junk[:], in_=x_[:], func=AF.Square, accum_out=st[:, 0, 0:1], rd=[x_], wr=[junk, st])
                s.I("act", "activation", out=st[:, 0, 1:2], in_=st[:, 0, 0:1], func=AF.Sqrt, scale=1.0 / D, bias=EPS, rd=[st], wr=[st])
                s.I("dve", "reciprocal", st[:, 0, 2:3], st[:, 0, 1:2], rd=[st], wr=[st])
                s.I("dve", "tensor_scalar", out=n_[:], in0=x_[:], scalar1=st[:, 0, 2:3], scalar2=None, op0=ALU.mult, rd=[x_, st], wr=[n_])
            n = len(tiles) * 128
            for c in range(8):
                p = s.ps()
                for ti, t in enumerate(tiles):
                    n_ = xn[t % 8]
                    s.I("pe", "transpose", p[:, ti * 128:(ti + 1) * 128], n_[:, c * 128:(c + 1) * 128], self.ident[:], rd=[n_, self.ident], wr=[p])
                s.I("act", "activation", out=self.hT[:, c, tiles[0] * 128:tiles[0] * 128 + n], in_=p[:, 0:n], func=AF.Identity,
                    scale=A[:, c, j:j + 1], bias=Bv[:, c, j:j + 1], rd=[p, A, Bv], wr=[self.hT])

    def load_w(self, name, src_buf, src_ap, shape, scope=None):
        s = self.s
        if name in self.pre:
            return self.pre.pop(name)
        if scope is None:
            w = s.sb(name, shape, BF16)
        else:
            w = scope
        s.dma("pool", w[:], src_ap, w, src_buf)
        return w

    def alloc_scoped(self, es, name, shape):
        s = self.s
        s.uid += 1
        t = es.enter_context(self.nc.sbuf_tensor("%s_%d" % (name, s.uid), list(shape), BF16))
        w = Buf(name, t, True)
        self.scoped.append(w)
        return w

    def release_scoped(self):
        for b in self.scoped:
            if b.dsem is not None:
                self.s.dfree.append(b.dsem)
                b.dsem = None
        self.scoped = []

    def w_na(self, l, scope=None):
        return self.load_w("wna", self.w_in, self.w_in.t.ap()[l, :, 0:768].rearrange("(k p) n -> p k n", p=128), [128, 8, 768], scope)

    def w_df(self, l, scope=None):
        return self.load_w("wdf", self.w_in, self.w_in.t.ap()[l, :, 768:1536].rearrange("(k p) n -> p k n", p=128), [128, 8, 768], scope)

    def w_dn(self, l, scope=None):
        return self.load_w("wdn", self.w_down, self.w_down.t.ap()[l].rearrange("(f p) n -> p f n", p=128), [128, NF, D], scope)

    def bcast_row(self, name, src_buf, off, n, dt=F32, parts=128):
        s = self.s
        t = s.sb(name, [parts, n], dt)
        s.dma("sp", t[:], dap(src_buf, off, [[0, parts], [1, n]]), t, src_buf)
        return t

    def phase_out(self, l, b, src, ntiles):
        s = self.s
        w = self.load_w("wout", self.w_out, self.w_out.t.ap()[l].rearrange("(k p) n -> p k n", p=128), [128, 8, D])
        G = {}
        G[b] = self.bcast_row("g1b", self.modrow_d, (l * 3 + b) * 6 * D + 2 * D, D)
        if ntiles > 16:
            G[2] = self.bcast_row("g1c", self.modrow_d, (l * 3 + 2) * 6 * D + 2 * D, D)
        yt = [s.sb("oyt%d" % i, [128, D], F32) for i in range(8)]
        xr = [s.sb("oxr%d" % i, [128, D], F32) for i in range(8)]
        yT = [s.sb("oyT%d" % i, [128, 8, 512], BF16) for i in range(2)]
        tmp = [s.sb("otmp%d" % i, [128, D], F32) for i in range(2)]
        xo = [s.sb("oxo%d" % i, [128, D], F32) for i in range(2)]
        ngroups = (ntiles + 3) // 4

        def loads(g):
            for t in range(4 * g, min(4 * g + 4, ntiles)):
                y_ = yt[t % 8]
                if self.Yin is not None:
                    s.dma("sp", y_[:], self.Yin.t.ap()[l, b, t * 128:(t + 1) * 128, :], y_, self.Yin)
                else:
                    s.dma("sp", y_[:], self.Yd.t.ap()[b, t * 128:(t + 1) * 128, :], y_, self.Yd)
                ap, sbuf = src(t)
                s.dma("sp", xr[t % 8][:], ap, xr[t % 8], sbuf)

        loads(0)
        for g in range(ngroups):
            if g + 1 < ngroups:
                loads(g + 1)
            tiles = list(range(4 * g, min(4 * g + 4, ntiles)))
            j = b if tiles[0] < 16 else 2
            yT_ = yT[g % 2]
            n = len(tiles) * 128
            for c in range(8):
                p = s.ps()
                for ti, t in enumerate(tiles):
                    s.I("pe", "transpose", p[:, ti * 128:(ti + 1) * 128], yt[t % 8][:, c * 128:(c + 1) * 128], self.ident[:], rd=[yt[t % 8], self.ident], wr=[p])
                s.I("act", "activation", out=yT_[:, c, 0:n], in_=p[:, 0:n], func=AF.Copy, rd=[p], wr=[yT_])
            for ti, t in enumerate(tiles):
                tm = tmp[t % 2]
                xo_ = xo[t % 2]
                for hf in range(2):
                    p = s.ps()
                    for k in range(8):
                        s.I("pe", "matmul", p[:, :], yT_[:, k, ti * 128:(ti + 1) * 128], w[:, k, hf * 512:(hf + 1) * 512], start=(k == 0), stop=(k == 7),
                            rd=[yT_, w], wr=[p])
                    s.I("dve", "tensor_tensor", out=tm[:, hf * 512:(hf + 1) * 512], in0=p[:, :], in1=G[j][:, hf * 512:(hf + 1) * 512], op=ALU.mult,
                        rd=[p, G[j]], wr=[tm])
                s.I("pool", "tensor_tensor", out=xo_[:], in0=tm[:], in1=xr[t % 8][:], op=ALU.add, rd=[tm, xr[t % 8]], wr=[xo_])
                s.dma("sp", self.XA[l].t.ap()[b, t * 128:(t + 1) * 128, :], xo_[:], self.XA[l], xo_)

    def phase_ffn_up(self, l, b, ntiles):
        s = self.s
        groups = [(0, 512), (512, 512), (1024, 512), (1536, 512)]
        if ntiles > 16:
            groups.append((2048, 256))
        ntok = ntiles * 128
        wu = [s.sb("wu%d" % i, [128, 8, 256], BF16) for i in range(3)]
        Gl = [s.sb("fGl%d" % i, [128, L + 2], F32) for i in range(2)]
        Gc = [s.sb("fGc%d" % i, [128, LC + 2], F32) for i in range(2)]
        V = [s.sb("fV%d" % i, [128, T], F32) for i in range(2)]
        acc = [s.sb("facc%d" % i, [128, T], F32) for i in range(2)]
        at = [s.sb("fat%d" % i, [128, T], BF16) for i in range(2)]
        cw = s.sb("fcw", [128, NF, 3], F32)
        cb = s.sb("fcb", [128, NF], F32)
        s.dma("sp", cw[:], self.fconv_wc.t.ap()[l], cw, self.fconv_wc)
        s.dma("sp", cb[:], self.fconv_bc.t.ap()[l], cb, self.fconv_bc)
        for i in range(2):
            s.I("pool", "memset", Gl[i][:], 0.0, wr=[Gl[i]])
            s.I("pool", "memset", Gc[i][:], 0.0, wr=[Gc[i]])
        wup = self.w_up.t.ap()[l]

        def loadw(f):
            w = wu[f % 3]
            s.dma("pool", w[:, :, 0:128], wup[:, f * 128:(f + 1) * 128].rearrange("(k p) n -> p k n", p=128), w, self.w_up)
            s.dma("pool", w[:, :, 128:256], wup[:, DFF + f * 128:DFF + (f + 1) * 128].rearrange("(k p) n -> p k n", p=128), w, self.w_up)

        loadw(0)
        pend = []
        for f in range(NF):
            if f + 1 < NF:
                loadw(f + 1)
            w = wu[f % 3]
            gl, gc, v, a, o = Gl[f % 2], Gc[f % 2], V[f % 2], acc[f % 2], at[f % 2]
            for gi, (t0, n) in enumerate(groups):
                if gi == 2 and pend:
                    pend.pop(0)()
                pa = s.ps()
                pb = s.ps()
                for k in range(8):
                    s.I("pe", "matmul", pa[:, 0:n], w[:, k, 0:128], self.hT[:, k, t0:t0 + n], start=(k == 0), stop=(k == 7), rd=[w, self.hT], wr=[pa])
                for k in range(8):
                    s.I("pe", "matmul", pb[:, 0:n], w[:, k, 128:256], self.hT[:, k, t0:t0 + n], start=(k == 0), stop=(k == 7), rd=[w, self.hT], wr=[pb])
                if t0 < L:
                    s.I("act", "activation", out=gl[:, 1 + t0:1 + t0 + n], in_=pa[:, 0:n], func=AF.Copy, rd=[pa], wr=[gl])
                else:
                    s.I("act", "activation", out=gc[:, 1:1 + n], in_=pa[:, 0:n], func=AF.Copy, rd=[pa], wr=[gc])
                s.I("act", "activation", out=v[:, t0:t0 + n], in_=pb[:, 0:n], func=AF.Copy, rd=[pb], wr=[v])
            segs = [(gl, 0, L)] + ([(gc, L, LC)] if ntiles > 16 else [])
            for (gb, o0, n) in segs:
                s.I("dve", "tensor_scalar", out=a[:, o0:o0 + n], in0=gb[:, 0:n], scalar1=cw[:, f, 0:1], scalar2=None, op0=ALU.mult, rd=[gb, cw], wr=[a])
                for k in (1, 2):
                    s.I("dve", "scalar_tensor_tensor", out=a[:, o0:o0 + n], in0=gb[:, k:k + n], scalar=cw[:, f, k:k + 1], in1=a[:, o0:o0 + n],
                        op0=ALU.mult, op1=ALU.add, rd=[gb, cw, a], wr=[a])

            def tail(f=f, a=a, v=v, o=o):
                s.I("act", "activation", out=a[:, 0:ntok], in_=a[:, 0:ntok], func=AF.Silu, bias=cb[:, f:f + 1], scale=1.0, rd=[a, cb], wr=[a])
                s.I("pool", "tensor_tensor", out=o[:, 0:ntok], in0=a[:, 0:ntok], in1=v[:, 0:ntok], op=ALU.mult, rd=[a, v], wr=[o])
                s.dma("sp", self.ATd.t.ap()[f, :, 0:ntok], o[:, 0:ntok], self.ATd, o)
            pend.append(tail)
        while pend:
            pend.pop(0)()

    def phase_ffn_down(self, l, b, src, ntiles):
        s = self.s
        last = l == DEPTH - 1
        w = self.w_dn(l)
        G = {}
        G[b] = self.bcast_row("g2b", self.modrow_d, (l * 3 + b) * 6 * D + 5 * D, D)
        if ntiles > 16:
            G[2] = self.bcast_row("g2c", self.modrow_d, (l * 3 + 2) * 6 * D + 5 * D, D)
        aT = [s.sb("daT%d" % i, [128, NF, 512], BF16) for i in range(2)]
        xr = [s.sb("dxr%d" % i, [128, D], F32) for i in range(8)]
        tmp = [s.sb("dtmp%d" % i, [128, D], F32) for i in range(2)]
        xo = [s.sb("dxo%d" % i, [128, D], F32) for i in range(2)]
        ngroups = (ntiles + 3) // 4

        def loads(g):
            tiles = list(range(4 * g, min(4 * g + 4, ntiles)))
            n = len(tiles) * 128
            a_ = aT[g % 2]
            s.dma("sp", a_[:, :, 0:n], self.ATd.t.ap()[:, :, tiles[0] * 128:tiles[0] * 128 + n].rearrange("f p t -> p f t"), a_, self.ATd)
            for t in tiles:
                ap, sbuf = src(t)
                s.dma("sp", xr[t % 8][:], ap, xr[t % 8], sbuf)

        loads(0)
        for g in range(ngroups):
            if g + 1 < ngroups:
                loads(g + 1)
            tiles = list(range(4 * g, min(4 * g + 4, ntiles)))
            j = b if tiles[0] < 16 else 2
            a_ = aT[g % 2]
            for ti, t in enumerate(tiles):
                tm = tmp[t % 2]
                xo_ = xo[t % 2]
                for hf in range(2):
                    p = s.ps()
                    for f in range(NF):
                        s.I("pe", "matmul", p[:, :], a_[:, f, ti * 128:(ti + 1) * 128], w[:, f, hf * 512:(hf + 1) * 512], start=(f == 0), stop=(f == NF - 1),
                            rd=[a_, w], wr=[p])
                    s.I("dve", "tensor_tensor", out=tm[:, hf * 512:(hf + 1) * 512], in0=p[:, :], in1=G[j][:, hf * 512:(hf + 1) * 512], op=ALU.mult,
                        rd=[p, G[j]], wr=[tm])
                s.I("pool", "tensor_tensor", out=xo_[:], in0=tm[:], in1=xr[t % 8][:], op=ALU.add, rd=[tm, xr[t % 8]], wr=[xo_])
                if last:
                    s.dma("sp", self.out.t.ap()[b, t * 128:(t + 1) * 128, :], xo_[:], self.out, xo_)
                else:
                    s.dma("sp", self.XB.t.ap()[b, t * 128:(t + 1) * 128, :], xo_[:], self.XB, xo_)

    def group_norm(self, p, ngrp, gd, gains, out, sq, st, view):
        s = self.s
        n = ngrp * gd
        s.I("act", "activation", out=sq[:, 0:n], in_=p[:, 0:n], func=AF.Square, rd=[p], wr=[sq])
        s.I("dve", "tensor_reduce", out=st[:, 0:ngrp, 0], in_=sq[:, 0:n].rearrange("p (g d) -> p g d", d=gd), axis=AX.X, op=ALU.add, rd=[sq], wr=[st])
        s.I("act", "activation", out=st[:, 0:ngrp, 1], in_=st[:, 0:ngrp, 0], func=AF.Sqrt, scale=1.0 / gd, bias=EPS, rd=[st], wr=[st])
        s.I("dve", "reciprocal", st[:, 0:ngrp, 2], st[:, 0:ngrp, 1], rd=[st], wr=[st])
        s.I("dve", "tensor_tensor", out=sq[:, 0:n].rearrange("p (g d) -> p g d", d=gd), in0=p[:, 0:n].rearrange("p (g d) -> p g d", d=gd),
            in1=st[:, 0:ngrp, 2:3].to_broadcast([128, ngrp, gd]), op=ALU.mult, rd=[p, st], wr=[sq])
        s.I("pool", "tensor_tensor", out=out, in0=view(sq[:, 0:n]), in1=gains, op=ALU.mult, rd=[sq] + self._gn_rd, wr=self._gn_wr)

    def phase_na_bias(self, l):
        s = self.s
        bt32 = s.sb("bt32", [128, 4, 14, 64], F32)
        mk = s.sb("namask", [128, 64], F32)
        s.dma("sp", mk[:], self.na_mask.t.ap(), mk, self.na_mask)
        for h in range(4):
            for half in range(2):
                s.dma("sp", bt32[64 * half:64 * half + 64, h, :, :], self.rpb_t.t.ap()[l, h, half:half + 14].rearrange("d k q -> k d q"), bt32, self.rpb_t)
        s.I("dve", "tensor_tensor", out=bt32[:].rearrange("p h d q -> p (h d) q"), in0=bt32[:].rearrange("p h d q -> p (h d) q"),
            in1=mk[:].rearrange("p (o q) -> p o q", o=1).to_broadcast([128, 56, 64]), op=ALU.add, rd=[bt32, mk], wr=[bt32])
        s.I("dve", "tensor_scalar", out=self.BT[:].rearrange("p h d q -> p (h d q)"), in0=bt32[:].rearrange("p h d q -> p (h d q)"), scalar1=8.0, scalar2=None,
            op0=ALU.mult, rd=[bt32], wr=[self.BT])

    def phase_na(self, l, b):
        s = self.s
        last = l == DEPTH - 1
        w = self.w_na(l)
        gains = s.sb("nagain", [128, 2, 1, 64], F32)
        s.dma("sp", gains[:, 0, 0, :], dap(self.na_qg, l * 64, [[0, 128], [1, 64]]), gains, self.na_qg)
        s.dma("sp", gains[:, 1, 0, :], dap(self.na_kg, l * 64, [[0, 128], [1, 64]]), gains, self.na_kg)
        QKT = s.sb("naQKT", [128, 6, T], BF16)
        kz = [s.sb("nakz%d" % i, [128, 2, 256], F32) for i in range(2)]
        for i in range(2):
            s.I("pool", "memset", kz[i][:], 0.0, wr=[kz[i]])
        VE = s.sb("naVE", [128, 18, 4, 65], BF16)
        VO = s.sb("naVO", [128, 15, 4, 65], BF16)
        s.I("pool", "memset", VE[:], 1.0, wr=[VE])
        s.I("pool", "memset", VO[:], 1.0, wr=[VO])
        qn = [s.sb("naqn%d" % i, [128, 2, 4, 64], F32) for i in range(2)]
        gsq = [s.sb("nagsq%d" % i, [128, 512], F32) for i in range(2)]
        gst = [s.sb("nagst%d" % i, [128, 16, 3], F32) for i in range(2)]
        pending = None
        for t in range(18):
            pq = s.ps()
            pv = s.ps()
            for k in range(8):
                s.I("pe", "matmul", pq[:, :], self.hT[:, k, t * 128:(t + 1) * 128], w[:, k, 0:512], start=(k == 0), stop=(k == 7), rd=[self.hT, w], wr=[pq])
            for k in range(8):
                s.I("pe", "matmul", pv[:, 0:256], self.hT[:, k, t * 128:(t + 1) * 128], w[:, k, 512:768], start=(k == 0), stop=(k == 7), rd=[self.hT, w], wr=[pv])
            q_ = qn[t % 2]
            self._gn_rd = [gains]
            self._gn_wr = [q_]
            self.group_norm(pq, 8, 64, gains[:].to_broadcast([128, 2, 4, 64]), q_[:], gsq[t % 2], gst[t % 2],
                            lambda ap: ap.rearrange("p (a h d) -> p a h d", a=2, h=4))
            kz_ = kz[t % 2]
            for hs in range(2):
                s.I("pool", "tensor_copy", kz_[:, hs, :].rearrange("p (pr h2 d) -> p pr h2 d", pr=2, h2=2)[:, :, hs, :],
                    q_[:, 1, :, :].rearrange("p (pr h2) d -> p pr h2 d", pr=2)[:, :, hs, :], rd=[q_], wr=[kz_])
            s.I("dve", "tensor_copy", VE[:, t, :, 0:64], pv[:, 0:256].rearrange("p (h d) -> p h d", h=4), rd=[pv], wr=[VE])

            def tail(t=t, q_=q_, kz_=kz_):
                pt = s.ps()
                pt2 = s.ps()
                qf = q_[:].rearrange("p a h d -> p (a h d)")
                for cc in range(2):
                    s.I("pe", "transpose", pt[:, cc * 128:(cc + 1) * 128], qf[:, cc * 128:(cc + 1) * 128], self.ident[:], rd=[q_, self.ident], wr=[pt])
                for hs in range(2):
                    for pr in range(2):
                        s.I("pe", "transpose", pt2[:, (hs * 2 + pr) * 128:(hs * 2 + pr + 1) * 128], kz_[:, hs, pr * 128:(pr + 1) * 128], self.ident[:], rd=[kz_, self.ident], wr=[pt2])
                s.I("act", "activation", out=QKT[:, 0:2, t * 128:(t + 1) * 128], in_=pt[:, 0:256].rearrange("p (c t) -> p c t", c=2), func=AF.Copy, rd=[pt], wr=[QKT])
                s.I("act", "activation", out=QKT[:, 2:6, t * 128:(t + 1) * 128], in_=pt2[:, :].rearrange("p (c t) -> p c t", c=4), func=AF.Copy, rd=[pt2], wr=[QKT])
            if pending is not None:
                pending()
            pending = tail
        pending()
        for i in range(15):
            pv = s.ps()
            for k in range(8):
                s.I("pe", "matmul", pv[:, 0:256], self.hT[:, k, 64 + i * 128:64 + (i + 1) * 128], w[:, k, 512:768], start=(k == 0), stop=(k == 7), rd=[self.hT, w], wr=[pv])
            s.I("dve", "tensor_copy", VO[:, i, :, 0:64], pv[:, 0:256].rearrange("p (h d) -> p h d", h=4), rd=[pv], wr=[VO])
        PT = [s.sb("naPT%d" % i, [128, 6, 64], BF16) for i in range(4)]
        rec = [s.sb("narec%d" % i, [128, 4, 1], F32) for i in range(2)]
        yo = [s.sb("nayo%d" % i, [128, 4, 64], F32) for i in range(2)]
        def s_part(r, h, P_):
            R0 = min(max(r - 4, 0), 24)
            pair = h // 2
            pS = s.ps()
            q_ap = QKT[:, pair, r * 64:(r + 1) * 64]
            kk = 2 + 2 * (h % 2) + pair
            for ci in range(4):
                kr = R0 + 2 * ci
                d = kr - r + 7
                s.I("pe", "matmul", pS[:, ci * 64:(ci + 1) * 64], QKT[:, kk, kr * 64:kr * 64 + 128], q_ap, start=True, stop=False,
                    rd=[QKT], wr=[pS])
                s.I("pe", "matmul", pS[:, ci * 64:(ci + 1) * 64], self.identb[:], self.BT[:, h, d, :], start=False, stop=True,
                    rd=[self.identb, self.BT], wr=[pS])
            for cc in range(2):
                s.I("pe", "matmul", pS[:, (4 + cc) * 64:(5 + cc) * 64], QKT[:, kk, L + cc * 128:L + (cc + 1) * 128], q_ap,
                    start=True, stop=True, rd=[QKT], wr=[pS])
            s.I("act", "activation", out=P_[:].rearrange("p c q -> p (c q)"), in_=pS[:, 0:384], func=AF.Exp, scale=0.125, rd=[pS], wr=[P_])

        def pv_part(r, h, P_):
            R0 = min(max(r - 4, 0), 24)
            rp, rr = r // 2, r % 2
            po = s.psacc(rp)
            for c in range(6):
                if c < 4:
                    kr = R0 + 2 * c
                    vb, v_ap = (VE, VE[:, kr // 2, h, :]) if kr % 2 == 0 else (VO, VO[:, (kr - 1) // 2, h, :])
                else:
                    vb, v_ap = VE, VE[:, 16 + (c - 4), h, :]
                s.I("pe", "matmul", po[64 * rr:64 * rr + 64, h * 65:(h + 1) * 65], P_[:, c, :], v_ap, start=(c == 0), stop=(c == 5), rd=[P_, vb], wr=[po])

        def row_finish(rp):
            po = s.psacc(rp)
            rc, y_ = rec[rp % 2], yo[rp % 2]
            pov = po[:, 0:260].rearrange("p (h e) -> p h e", e=65)
            s.I("dve", "reciprocal", rc[:], pov[:, :, 64:65], rd=[po], wr=[rc])
            s.I("dve", "tensor_tensor", out=y_[:], in0=pov[:, :, 0:64], in1=rc[:].to_broadcast([128, 4, 64]), op=ALU.mult, rd=[po, rc], wr=[y_])
            s.dma("sp", self.Yd.t.ap()[b, rp * 128:(rp + 1) * 128, 0:256], y_[:].rearrange("p h d -> p (h d)"), self.Yd, y_)

        seq = [(r, h) for r in range(32) for h in range(4)]
        SK = 2
        for i in range(len(seq) + SK):
            if i < len(seq):
                s_part(seq[i][0], seq[i][1], PT[i % 4])
            j = i - SK
            if j >= 0:
                pv_part(seq[j][0], seq[j][1], PT[j % 4])
                if seq[j][0] % 2 == 1 and seq[j][1] == 3:
                    row_finish(seq[j][0] // 2)
        if not last:
            PTc = [s.sb("naPTc%d" % i, [128, 2, 256], BF16) for i in range(2)]
            pos = [s.psacc(0), s.psacc(1)]
            for h in range(4):
                pair, base = h // 2, 64 * (h % 2)
                pS = s.ps()
                for cc in range(2):
                    s.I("pe", "matmul", pS[:, cc * 256:(cc + 1) * 256], QKT[:, 2 + 2 * (h % 2) + pair, L + cc * 128:L + (cc + 1) * 128],
                        QKT[:, pair, L:L + 256], start=True, stop=True, rd=[QKT], wr=[pS])
                P_ = PTc[h % 2]
                s.I("act", "activation", out=P_[:].rearrange("p c q -> p (c q)"), in_=pS[:, :], func=AF.Exp, scale=0.125, rd=[pS], wr=[P_])
                for qt in range(2):
                    for cc in range(2):
                        s.I("pe", "matmul", pos[qt][:, h * 65:(h + 1) * 65], P_[:, cc, qt * 128:(qt + 1) * 128], VE[:, 16 + cc, h, :], start=(cc == 0), stop=(cc == 1),
                            rd=[P_, VE], wr=[pos[qt]])
            for qt in range(2):
                rc, y_ = rec[qt], yo[qt]
                pov = pos[qt][:, 0:260].rearrange("p (h e) -> p h e", e=65)
                s.I("dve", "reciprocal", rc[:], pov[:, :, 64:65], rd=[pos[qt]], wr=[rc])
                s.I("dve", "tensor_tensor", out=y_[:], in0=pov[:, :, 0:64], in1=rc[:].to_broadcast([128, 4, 64]), op=ALU.mult, rd=[pos[qt], rc], wr=[y_])
                s.dma("sp", self.Yd.t.ap()[b, L + qt * 128:L + (qt + 1) * 128, 0:256], y_[:].rearrange("p h d -> p (h d)"), self.Yd, y_)

    def phase_df(self, l, b):
        s = self.s
        last = l == DEPTH - 1
        lam_init = 0.8 - 0.6 * math.exp(-0.3 * l)
        w = self.w_df(l)
        gains = s.sb("dfgain", [128, 2, 1, 32], F32)
        s.dma("sp", gains[:, 0, 0, :], dap(self.df_qg, l * 32, [[0, 128], [1, 32]]), gains, self.df_qg)
        s.dma("sp", gains[:, 1, 0, :], dap(self.df_kg, l * 32, [[0, 128], [1, 32]]), gains, self.df_kg)
        COS = s.sb("dfcos", [128, 16, 32], F32)
        SIN = s.sb("dfsin", [128, 16, 32], F32)
        s.dma("sp", COS[:], self.rope_cos.t.ap().rearrange("(t p) d -> p t d", p=128), COS, self.rope_cos)
        s.dma("sp", SIN[:], self.rope_sin.t.ap().rearrange("(t p) d -> p t d", p=128), SIN, self.rope_sin)
        lv = s.sb("dflv", [128, 4, 32], F32)
        s.dma("sp", lv[:].rearrange("p a d -> p (a d)"), dap(self.df_lam, l * 128, [[0, 128], [1, 128]]), lv, self.df_lam)
        lp = s.sb("dflp", [128, 2, 32], F32)
        ls = s.sb("dfls", [128, 8], F32)
        s.I("dve", "tensor_tensor", out=lp[:, 0, :], in0=lv[:, 0, :], in1=lv[:, 1, :], op=ALU.mult, rd=[lv], wr=[lp])
        s.I("dve", "tensor_tensor", out=lp[:, 1, :], in0=lv[:, 2, :], in1=lv[:, 3, :], op=ALU.mult, rd=[lv], wr=[lp])
        s.I("dve", "tensor_reduce", out=ls[:, 0:2], in_=lp[:], axis=AX.X, op=ALU.add, rd=[lp], wr=[ls])
        s.I("act", "activation", out=ls[:, 2:4], in_=ls[:, 0:2], func=AF.Exp, rd=[ls], wr=[ls])
        s.I("dve", "scalar_tensor_tensor", out=ls[:, 4:5], in0=ls[:, 3:4], scalar=-lam_init, in1=ls[:, 2:3], op0=ALU.add, op1=ALU.subtract, rd=[ls], wr=[ls])
        neglam = ls[:, 4:5]
        sub = s.sb("dfsub", [128, 1, 64], F32)
        s.dma("sp", sub[:, 0, :], dap(self.df_subln, l * 64, [[0, 128], [1, 64]]), sub, self.df_subln)
        s.I("dve", "tensor_scalar", out=sub[:], in0=sub[:], scalar1=1.0 - lam_init, scalar2=None, op0=ALU.mult, rd=[sub], wr=[sub])

        QZ = s.sb("dfQZ", [128, 2, 2, T], BF16)
        KZ = s.sb("dfKZ", [128, 2, 2, T], BF16)
        VD = s.sb("dfVD", [128, 18, 4, 65], BF16)
        s.I("pool", "memset", VD[:], 1.0, wr=[VD])
        qn = [s.sb("dfqn%d" % i, [128, 16, 32], F32) for i in range(2)]
        gsq = [s.sb("dfgsq%d" % i, [128, 512], F32) for i in range(1)] * 2
        gst = [s.sb("dfgst%d" % i, [128, 16, 3], F32) for i in range(2)]
        t1 = [s.sb("dft1%d" % i, [128, 16, 32], F32) for i in range(1)] * 2
        t2 = [s.sb("dft2%d" % i, [128, 16, 32], F32) for i in range(1)] * 2
        qz = [s.sb("dfqz%d" % i, [128, 4, 256], F32) for i in range(2)]
        for i in range(2):
            s.I("pool", "memset", qz[i][:], 0.0, wr=[qz[i]])
        pending = None
        for t in range(18):
            pq = s.ps()
            pv = s.ps()
            for k in range(8):
                s.I("pe", "matmul", pq[:, :], self.hT[:, k, t * 128:(t + 1) * 128], w[:, k, 0:512], start=(k == 0), stop=(k == 7), rd=[self.hT, w], wr=[pq])
            for k in range(8):
                s.I("pe", "matmul", pv[:, 0:256], self.hT[:, k, t * 128:(t + 1) * 128], w[:, k, 512:768], start=(k == 0), stop=(k == 7), rd=[self.hT, w], wr=[pv])
            q_ = qn[t % 2]
            self._gn_rd = [gains]
            self._gn_wr = [q_]
            self.group_norm(pq, 16, 32, gains[:].to_broadcast([128, 2, 8, 32]), q_[:].rearrange("p (a h) d -> p a h d", a=2), gsq[t % 2], gst[t % 2],
                            lambda ap: ap.rearrange("p (a h d) -> p a h d", a=2, h=8))
            z_ = qz[t % 2]
            if t < 16:
                a_, b_ = t1[t % 2], t2[t % 2]
                s.I("pool", "tensor_tensor", out=a_[:], in0=q_[:], in1=COS[:, t:t + 1, :].to_broadcast([128, 16, 32]), op=ALU.mult, rd=[q_, COS], wr=[a_])
                q5 = q_[:].rearrange("p g (a f e) -> p g a f e", a=2, f=2)
                b5 = b_[:].rearrange("p g (a f e) -> p g a f e", a=2, f=2)
                s5 = SIN[:, t:t + 1, :].to_broadcast([128, 16, 32]).rearrange("p g (a f e) -> p g a f e", a=2, f=2)
                s.I("dve", "tensor_tensor", out=b5[:, :, :, 0, :], in0=q5[:, :, :, 1, :], in1=s5[:, :, :, 0, :], op=ALU.mult, rd=[q_, SIN], wr=[b_])
                s.I("dve", "tensor_tensor", out=b5[:, :, :, 1, :], in0=q5[:, :, :, 0, :], in1=s5[:, :, :, 1, :], op=ALU.mult, rd=[q_, SIN], wr=[b_])
                srcs = (a_, b_)
            else:
                srcs = (q_,)

            def comb(out_ap, sel):
                if len(srcs) == 2:
                    s.I("pool", "tensor_tensor", out=out_ap, in0=sel(srcs[0]), in1=sel(srcs[1]), op=ALU.add, rd=list(srcs), wr=[z_])
                else:
                    s.I("pool", "tensor_copy", out_ap, sel(srcs[0]), rd=list(srcs), wr=[z_])
            for m in range(2):
                comb(z_[:, m, :].rearrange("p (h m d) -> p h m d", h=4, m=2)[:, :, m, :],
                     lambda bf: bf[:, 0:8, :].rearrange("p (h m) d -> p h m d", m=2)[:, :, m, :])
            for hs in range(2):
                comb(z_[:, 2 + hs, :].rearrange("p (pr h2 e) -> p pr h2 e", pr=2, h2=2)[:, :, hs, :],
                     lambda bf: bf[:, 8:16, :].rearrange("p (pr h2 m) d -> p pr h2 (m d)", pr=2, h2=2)[:, :, hs, :])
            s.I("dve", "tensor_copy", VD[:, t, :, 0:64], pv[:, 0:256].rearrange("p (h d) -> p h d", h=4), rd=[pv], wr=[VD])

            def tail(t=t, z_=z_):
                pt = s.ps()
                pt2 = s.ps()
                for m in range(2):
                    for pr in range(2):
                        s.I("pe", "transpose", pt[:, (m * 2 + pr) * 128:(m * 2 + pr + 1) * 128], z_[:, m, pr * 128:(pr + 1) * 128], self.ident[:], rd=[z_, self.ident], wr=[pt])
                for hs in range(2):
                    for pr in range(2):
                        s.I("pe", "transpose", pt2[:, (hs * 2 + pr) * 128:(hs * 2 + pr + 1) * 128], z_[:, 2 + hs, pr * 128:(pr + 1) * 128], self.ident[:], rd=[z_, self.ident], wr=[pt2])
                s.I("act", "activation", out=QZ[:, :, :, t * 128:(t + 1) * 128], in_=pt[:, :].rearrange("p (m c t) -> p m c t", m=2, c=2), func=AF.Copy, rd=[pt], wr=[QZ])
                s.I("act", "activation", out=KZ[:, :, :, t * 128:(t + 1) * 128], in_=pt2[:, :].rearrange("p (a c t) -> p a c t", a=2, c=2), func=AF.Copy, rd=[pt2], wr=[KZ])
            if pending is not None:
                pending()
            pending = tail
        pending()
        PT = [[s.sb("dfPT%d%d" % (i, m), [128, 18, 512], BF16) for m in range(2)] for i in range(2)]
        rec = s.sb("dfrec", [128, 2, 4, 1], F32)
        o0 = s.sb("dfo0", [128, 4, 64], F32)
        o1 = s.sb("dfo1", [128, 4, 64], F32)
        yd = [s.sb("dfyd%d" % i, [128, 4, 64], F32) for i in range(2)]
        sq = s.sb("dfsq2", [128, 4, 64], F32)
        st = s.sb("dfst2", [128, 4, 3], F32)
        epsb = s.sb("dfeps", [128, 1], F32)
        s.I("pool", "memset", epsb[:], EPS, wr=[epsb])
        blocks = [(qb * 512, 512, list(range(18))) for qb in range(4)]
        if not last:
            blocks.append((L, 256, [16, 17]))
        items = [(h, q0, nq, kcs) for h in range(4) for (q0, nq, kcs) in blocks]

        def s_steps(idx):
            h, q0, nq, kcs = items[idx]
            pair = h // 2
            P_ = PT[idx % 2]
            out = []
            for m in range(2):
                for ki in range(0, len(kcs), 2):
                    def f(m=m, kc=kcs[ki]):
                        pa, pb, pp = s.pspair()
                        for o, pbuf in ((0, pa), (1, pb)):
                            s.I("pe", "matmul", pbuf[:, 0:nq], KZ[:, h % 2, pair, (kc + o) * 128:(kc + o + 1) * 128], QZ[:, m, pair, q0:q0 + nq], start=True, stop=True,
                                rd=[KZ, QZ], wr=[pbuf])
                        s.I("act", "activation", out=P_[m][:, kc:kc + 2, 0:nq], in_=pp[:, :].rearrange("p (c q) -> p c q", c=2)[:, :, 0:nq], func=AF.Exp, scale=32.0 ** -0.5,
                            rd=[pa, pb], wr=[P_[m]])
                    out.append(f)
            return out

        def pv_steps(idx):
            h, q0, nq, kcs = items[idx]
            P_ = PT[idx % 2]
            po = [s.psacc(0), s.psacc(1)]
            out = []
            for m in range(2):
                for qs in range(nq // 128):
                    for i, kc in enumerate(kcs):
                        def f(m=m, qs=qs, i=i, kc=kc):
                            s.I("pe", "matmul", po[m][:, qs * 65:(qs + 1) * 65], P_[m][:, kc, qs * 128:(qs + 1) * 128], VD[:, kc, h, :], start=(i == 0), stop=(i == len(kcs) - 1),
                                rd=[P_[m], VD], wr=[po[m]])
                        out.append(f)
            return out

        def finish(idx):
            h, q0, nq, kcs = items[idx]
            po = [s.psacc(0), s.psacc(1)]
            nqs = nq // 128
            y_ = yd[idx % 2]
            pv0 = po[0][:, 0:nqs * 65].rearrange("p (q e) -> p q e", e=65)
            pv1 = po[1][:, 0:nqs * 65].rearrange("p (q e) -> p q e", e=65)
            s.I("dve", "reciprocal", rec[:, 0, 0:nqs, :], pv0[:, :, 64:65], rd=[po[0]], wr=[rec])
            s.I("dve", "reciprocal", rec[:, 1, 0:nqs, :], pv1[:, :, 64:65], rd=[po[1]], wr=[rec])
            s.I("dve", "tensor_tensor", out=o0[:, 0:nqs, :], in0=pv0[:, :, 0:64], in1=rec[:, 0, 0:nqs, :].to_broadcast([128, nqs, 64]), op=ALU.mult, rd=[po[0], rec], wr=[o0])
            s.I("dve", "tensor_tensor", out=o1[:, 0:nqs, :], in0=pv1[:, :, 0:64], in1=rec[:, 1, 0:nqs, :].to_broadcast([128, nqs, 64]), op=ALU.mult, rd=[po[1], rec], wr=[o1])
            s.I("dve", "scalar_tensor_tensor", out=o0[:, 0:nqs, :], in0=o1[:, 0:nqs, :], scalar=neglam, in1=o0[:, 0:nqs, :], op0=ALU.mult, op1=ALU.add, rd=[o1, o0, ls], wr=[o0])
            s.I("pool", "tensor_tensor", out=sq[:, 0:nqs, :], in0=o0[:, 0:nqs, :], in1=o0[:, 0:nqs, :], op=ALU.mult, rd=[o0], wr=[sq])
            s.I("dve", "tensor_reduce", out=st[:, 0:nqs, 0], in_=sq[:, 0:nqs, :], axis=AX.X, op=ALU.add, rd=[sq], wr=[st])
            s.I("act", "activation", out=st[:, 0:nqs, 1], in_=st[:, 0:nqs, 0], func=AF.Ln, scale=1.0 / 64, bias=epsb[:, 0:1], rd=[st, epsb], wr=[st])
            s.I("act", "activation", out=st[:, 0:nqs, 2], in_=st[:, 0:nqs, 1], func=AF.Exp, scale=-0.5, rd=[st], wr=[st])
            s.I("dve", "tensor_tensor", out=sq[:, 0:nqs, :], in0=o0[:, 0:nqs, :], in1=st[:, 0:nqs, 2:3].to_broadcast([128, nqs, 64]), op=ALU.mult, rd=[o0, st], wr=[sq])
            s.I("pool", "tensor_tensor", out=y_[:, 0:nqs, :], in0=sq[:, 0:nqs, :], in1=sub[:].to_broadcast([128, nqs, 64]), op=ALU.mult, rd=[sq, sub], wr=[y_])
            s.dma("sp", self.Yd.t.ap()[b, q0:q0 + nq, 256 + h * 64:256 + (h + 1) * 64].rearrange("(q p) d -> p q d", p=128), y_[:, 0:nqs, :], self.Yd, y_)


        for f in s_steps(0):
            f()
        for idx in range(len(items)):
            A = s_steps(idx + 1) if idx + 1 < len(items) else []
            B = pv_steps(idx)
            ratio = max(1, len(B) // max(1, len(A)))
            ai = 0
            for bi, fb in enumerate(B):
                if bi % ratio == 0 and ai < len(A):
                    A[ai]()
                    ai += 1
                fb()
            while ai < len(A):
                A[ai]()
                ai += 1
            finish(idx)

    def phase_ssm(self, l, b):
        s = self.s
        last = l == DEPTH - 1
        ntl = 16 if last else 18
        with ExitStack() as mid:
            def msb(name, shape, dt):
                s.uid += 1
                t = mid.enter_context(self.nc.sbuf_tensor("%s_%d" % (name, s.uid), list(shape), dt))
                return Buf(name, t, True)
            XS = msb("ssXS", [128, 18, 512], F32)
            BMT = msb("ssBMT", [128, 2, T], BF16)
            CMT = msb("ssCMT", [128, 2, T], BF16)
            BM = msb("ssBM", [128, 18, 2, 128], BF16)
            DT = msb("ssDT", [128, 18, 16], F32)
            LA = msb("ssLA", [128, 18, 16], F32)
            with s.phase():
                self.ssm_prep(l, b, XS, BMT, CMT, BM, DT, LA)
            with s.phase():
                self.ssm_scan(l, b, XS, BMT, CMT, BM, DT, LA, ntl)
            for bb in (XS, BMT, CMT, BM, DT, LA):
                if bb.dsem is not None:
                    s.dfree.append(bb.dsem)
                    bb.dsem = None

    def ssm_prep(self, l, b, XS, BMT, CMT, BM, DT, LA):
        s = self.s
        w = self.load_w("wss", self.w_in, self.w_in.t.ap()[l, :, 1536:3088].rearrange("(k p) n -> p k n", p=128), [128, 8, 1552])
        dtb = self.bcast_row("ssdtb", self.dt_bias, l * 16, 16)
        alog = self.bcast_row("ssalog", self.a_log, l * 16, 16)
        A = s.sb("ssA", [128, 16], F32)
        s.I("act", "activation", out=A[:], in_=alog[:], func=AF.Exp, rd=[alog], wr=[A])
        s.I("dve", "tensor_scalar", out=A[:], in0=A[:], scalar1=-1.0, scalar2=None, op0=ALU.mult, rd=[A], wr=[A])
        cw = s.sb("sscw", [128, 8, 5], F32)
        cb = s.sb("sscb", [128, 8], F32)
        s.dma("sp", cw[:], self.conv_wc.t.ap()[l], cw, self.conv_wc)
        s.dma("sp", cb[:], self.conv_bc.t.ap()[l], cb, self.conv_bc)
        zs = [s.sb("sszs%d" % i, [128, 512], F32) for i in range(2)]
        tmp = s.sb("sstmp", [128, 18, 16], F32)
        for t in range(18):
            pz = s.ps()
            pd = s.ps()
            for k in range(8):
                s.I("pe", "matmul", pz[:, :], self.hT[:, k, t * 128:(t + 1) * 128], w[:, k, 0:512], start=(k == 0), stop=(k == 7), rd=[self.hT, w], wr=[pz])
            for k in range(8):
                s.I("pe", "matmul", pd[:, 0:16], self.hT[:, k, t * 128:(t + 1) * 128], w[:, k, 1536:1552], start=(k == 0), stop=(k == 7), rd=[self.hT, w], wr=[pd])
            z_ = zs[t % 2]
            s.I("act", "activation", out=z_[:], in_=pz[:, :], func=AF.Silu, rd=[pz], wr=[z_])
            s.dma("sp", self.Zd.t.ap()[t * 128:(t + 1) * 128, :], z_[:], self.Zd, z_)
            s.I("dve", "tensor_tensor", out=tmp[:, t, :], in0=pd[:, 0:16], in1=dtb[:], op=ALU.add, rd=[pd, dtb], wr=[tmp])
        s.I("act", "activation", out=tmp[:], in_=tmp[:], func=AF.Exp, rd=[tmp], wr=[tmp])
        s.I("act", "activation", out=DT[:], in_=tmp[:], func=AF.Ln, bias=1.0, scale=1.0, rd=[tmp], wr=[DT])
        s.I("dve", "tensor_tensor", out=LA[:], in0=DT[:], in1=A[:].rearrange("p (o d) -> p o d", o=1).to_broadcast([128, 18, 16]), op=ALU.mult, rd=[DT, A], wr=[LA])
        Gl = [s.sb("ssGl%d" % i, [128, L + 4], F32) for i in range(2)]
        Gc = [s.sb("ssGc%d" % i, [128, LC + 4], F32) for i in range(2)]
        acc = [s.sb("ssacc%d" % i, [128, T], F32) for i in range(2)]
        for i in range(2):
            s.I("pool", "memset", Gl[i][:], 0.0, wr=[Gl[i]])
            s.I("pool", "memset", Gc[i][:], 0.0, wr=[Gc[i]])
        groups = [(0, 512), (512, 512), (1024, 512), (1536, 512), (2048, 256)]

        def A1(c):
            gl, gc = Gl[c % 2], Gc[c % 2]
            for (t0, n) in groups:
                p = s.ps()
                for k in range(8):
                    s.I("pe", "matmul", p[:, 0:n], w[:, k, 512 + c * 128:512 + (c + 1) * 128], self.hT[:, k, t0:t0 + n], start=(k == 0), stop=(k == 7), rd=[w, self.hT], wr=[p])
                if t0 < L:
                    s.I("act", "activation", out=gl[:, 2 + t0:2 + t0 + n], in_=p[:, 0:n], func=AF.Copy, rd=[p], wr=[gl])
                else:
                    s.I("act", "activation", out=gc[:, 2:2 + n], in_=p[:, 0:n], func=AF.Copy, rd=[p], wr=[gc])

        def A2(c):
            gl, gc, a = Gl[c % 2], Gc[c % 2], acc[c % 2]
            for (gb, o0, n) in ((gl, 0, L), (gc, L, LC)):
                s.I("dve", "tensor_scalar", out=a[:, o0:o0 + n], in0=gb[:, 0:n], scalar1=cw[:, c, 0:1], scalar2=None, op0=ALU.mult, rd=[gb, cw], wr=[a])
                for k in range(1, 5):
                    s.I("dve", "scalar_tensor_tensor", out=a[:, o0:o0 + n], in0=gb[:, k:k + n], scalar=cw[:, c, k:k + 1], in1=a[:, o0:o0 + n],
                        op0=ALU.mult, op1=ALU.add, rd=[gb, cw, a], wr=[a])
            if c < 6:
                s.I("act", "activation", out=a[:], in_=a[:], func=AF.Silu, bias=cb[:, c:c + 1], scale=1.0, rd=[a, cb], wr=[a])
            else:
                s.I("act", "activation", out=CMT[:, c - 6, :], in_=a[:], func=AF.Silu, bias=cb[:, c:c + 1], scale=1.0, rd=[a, cb], wr=[CMT])

        def Bst(c):
            a = acc[c % 2]
            if c >= 6:
                return
            if c in (4, 5):
                s.I("pool", "tensor_copy", BMT[:, c - 4, :], a[:], rd=[a], wr=[BMT])
            for g4 in range(5):
                tiles = list(range(4 * g4, min(4 * g4 + 4, 18)))
                p = s.ps()
                for ti, t in enumerate(tiles):
                    s.I("pe", "transpose", p[:, ti * 128:(ti + 1) * 128], a[:, t * 128:(t + 1) * 128], self.ident[:], rd=[a, self.ident], wr=[p])
                if c < 4:
                    dst, dbuf = XS[:, tiles[0]:tiles[0] + len(tiles), c * 128:(c + 1) * 128], XS
                else:
                    dst, dbuf = BM[:, tiles[0]:tiles[0] + len(tiles), c - 4, :], BM
                s.I("act", "activation", out=dst, in_=p[:, 0:len(tiles) * 128].rearrange("p (t d) -> p t d", d=128), func=AF.Copy, rd=[p], wr=[dbuf])

        A1(0)
        A1(1)
        A2(0)
        for c in range(1, 8):
            if c + 1 < 8:
                A1(c + 1)
            Bst(c - 1)
            A2(c)
        Bst(7)

    def ssm_scan(self, l, b, XS, BMT, CMT, BM, DT, LA, ntl):
        s = self.s
        TRI = s.sb("ssTRI", [128, 5, 128], F32)
        s.dma("sp", TRI[:], self.tri_d.t.ap().rearrange("a p n -> p a n"), TRI, self.tri_d)
        SL, SU, LE, GE, ON = (TRI[:, i, :] for i in range(5))
        Y = s.sb("ssY", [128, 18, 512], F32)
        s.I("pool", "memset", Y[:], 0.0, wr=[Y])
        ST = [[s.sb("ssST%d%d" % (d, g), [128, 4, 64], F32) for g in range(2)] for d in range(2)]
        STb = [[s.sb("ssSTb%d%d" % (d, g), [128, 256], BF16) for g in range(2)] for d in range(2)]
        for d in range(2):
            for g in range(2):
                s.I("pool", "memset", ST[d][g][:], 0.0, wr=[ST[d][g]])
                s.I("pool", "memset", STb[d][g][:], 0.0, wr=[STb[d][g]])
        NEGM = s.sb("ssNEGM", [128, 2, 4, 128], BF16)
        s.dma("pool", NEGM[:], self.negm_d.t.ap(), NEGM, self.negm_d)
        E12 = s.sb("ssE12", [128, 2, 8, 128], BF16)
        s.dma("pool", E12[:], self.sel_d.t.ap(), E12, self.sel_d)
        EX = [s.sb("ssEX%d" % i, [128, 3, 8], F32) for i in range(3)]
        CST = [s.sb("ssCST%d" % i, [128, 128], BF16) for i in range(3)]
        R1 = [s.sb("ssR1%d" % i, [8, 128], F32) for i in range(3)]
        for i in range(3):
            s.I("pool", "memset", CST[i][:], 0.0, wr=[CST[i]])
        XDT = [s.sb("ssXDT%d" % i, [128, 8, 64], BF16) for i in range(3)]
        XDD = [s.sb("ssXDD%d" % i, [128, 8, 64], BF16) for i in range(3)]
        cbs = [s.sb("sscbs%d" % i, [128, 1, 128], F32) for i in range(4)]
        Lt = [s.sb("ssLt%d" % i, [128, 4, 128], F32) for i in range(2)] * 2
        Wm = [s.sb("ssW%d" % i, [128, 4, 128], BF16) for i in range(4)]
        tm1 = [s.sb("sstm1%d" % i, [128, 4, 64], F32) for i in range(4)]
        yt = [s.sb("ssyt%d" % i, [128, 4, 64], F32) for i in range(4)]
        tm2 = [s.sb("sstm2%d" % i, [128, 4, 64], F32) for i in range(4)]
        order = [(0, 16), (1, 17), (0, 17), (1, 16)]
        for i in range(16):
            order.append((0, i))
            order.append((1, 15 - i))
        ctxs = {}

        def stageA(k):
            dr, t = order[k]
            want_y = t < ntl
            ex, cst, xdt, xdd = EX[k % 3], CST[k % 3], XDT[k % 3], XDD[k % 3]
            la8 = LA[:, t, dr * 8:(dr + 1) * 8]
            pe_ = s.ps()
            m1, m2 = (SL, LE) if dr == 0 else (SU, GE)
            s.I("pe", "matmul", pe_[:, 0:8], m1, la8, start=True, stop=True, rd=[TRI, LA], wr=[pe_])
            s.I("pe", "matmul", pe_[:, 8:16], m2, la8, start=True, stop=True, rd=[TRI, LA], wr=[pe_])
            s.I("pe", "matmul", pe_[:, 16:24], ON, la8, start=True, stop=True, rd=[TRI, LA], wr=[pe_])
            s.I("act", "activation", out=ex[:].rearrange("p a h -> p (a h)"), in_=pe_[:, 0:24], func=AF.Exp, rd=[pe_], wr=[ex])
            s.I("dve", "tensor_tensor", out=xdt[:], in0=XS[:, t, :].rearrange("p (h d) -> p h d", d=64),
                in1=DT[:, t, dr * 8:(dr + 1) * 8].rearrange("p (h o) -> p h o", o=1).to_broadcast([128, 8, 64]), op=ALU.mult, rd=[XS, DT], wr=[xdt])
            s.I("dve", "tensor_tensor", out=xdd[:], in0=xdt[:], in1=ex[:, 0, :].rearrange("p (h o) -> p h o", o=1).to_broadcast([128, 8, 64]), op=ALU.mult,
                rd=[xdt, ex], wr=[xdd])
            cb2 = []
            if want_y:
                pcs = s.ps()
                s.I("pe", "matmul", pcs[0:8, 0:128], la8, m2, start=True, stop=True, rd=[LA, TRI], wr=[pcs])
                r1 = R1[k % 3]
                s.I("act", "activation", out=cst[0:8, :], in_=pcs[0:8, 0:128], func=AF.Copy, rd=[pcs], wr=[cst])
                s.I("dve", "tensor_tensor", out=r1[:], in0=pcs[0:8, 0:128], in1=cst[0:8, :], op=ALU.subtract, rd=[pcs, cst], wr=[r1])
                s.I("act", "activation", out=cst[32:40, :], in_=r1[:], func=AF.Copy, rd=[r1], wr=[cst])
                for g in range(2):
                    cb_ = cbs[(2 * k + g) % 4]
                    pc = s.ps()
                    s.I("pe", "matmul", pc[:, 0:128], BMT[:, g, t * 128:(t + 1) * 128], CMT[:, g, t * 128:(t + 1) * 128], start=True, stop=True, rd=[BMT, CMT], wr=[pc])
                    s.I("act", "activation", out=cb_[:, 0, :], in_=pc[:, 0:128], func=AF.Copy, rd=[pc], wr=[cb_])
                    cb2.append(cb_)
            ctxs[k] = (ex, cst, xdt, xdd, cb2)

        def stageB(k):
            dr, t = order[k]
            want_y = t < ntl
            ex, cst, xdt, xdd, cb2 = ctxs[k]
            ws = []
            if want_y:
                for g in range(2):
                    lt, wm = Lt[(2 * k + g) % 4], Wm[(2 * k + g) % 4]
                    pd_ = s.ps()
                    s.I("pe", "matmul", pd_[:, :], cst[:], E12[:, 1, g * 4:(g + 1) * 4, :].rearrange("p e i -> p (e i)"), start=True, stop=False, rd=[E12, cst], wr=[pd_])
                    s.I("pe", "matmul", pd_[:, :], self.identb[:], NEGM[:, dr, :, :].rearrange("p e i -> p (e i)"), start=False, stop=False, rd=[self.identb, NEGM], wr=[pd_])
                    for e in range(4):
                        h8 = g * 4 + e
                        s.I("pe", "matmul", pd_[:, e * 128:(e + 1) * 128], E12[:, 0, h8, :], cst[:], start=False, stop=(e == 3), rd=[E12, cst], wr=[pd_])
                    s.I("act", "activation", out=lt[:].rearrange("p e i -> p (e i)"), in_=pd_[:, :], func=AF.Exp, rd=[pd_], wr=[lt])
                    s.I("dve", "tensor_tensor", out=wm[:], in0=lt[:], in1=cb2[g][:].to_broadcast([128, 4, 128]), op=ALU.mult, rd=[lt, cb2[g]], wr=[wm])
                    ws.append(wm)
            ctxs[k] = (ex, cst, xdt, xdd, ws)

        def stageC(k):
            dr, t = order[k]
            want_y = t < ntl
            ex, cst, xdt, xdd, ws = ctxs.pop(k)
            for g in range(2):
                if want_y:
                    py = s.psacc(g)
                    for e in range(4):
                        s.I("pe", "matmul", py[:, e * 64:(e + 1) * 64], ws[g][:, e, :], xdt[:, g * 4 + e, :], start=True, stop=True, rd=[ws[g], xdt], wr=[py])
                    po = s.ps()
                    s.I("pe", "matmul", po[:, 0:256], CMT[:, g, t * 128:(t + 1) * 128], STb[dr][g][:], start=True, stop=True, rd=[CMT, STb[dr][g]], wr=[po])
                    a_, y_ = tm1[(2 * k + g) % 4], yt[(2 * k + g) % 4]
                    s.I("dve", "tensor_tensor", out=a_[:], in0=po[:, 0:256].rearrange("p (h d) -> p h d", d=64),
                        in1=ex[:, 1, g * 4:(g + 1) * 4].rearrange("p (h o) -> p h o", o=1).to_broadcast([128, 4, 64]), op=ALU.mult, rd=[po, ex], wr=[a_])
                    s.I("dve", "tensor_tensor", out=y_[:], in0=py[:, 0:256].rearrange("p (h d) -> p h d", d=64), in1=a_[:], op=ALU.add, rd=[py, a_], wr=[y_])
                    yv = Y[:, t, g * 256:(g + 1) * 256].rearrange("p (h d) -> p h d", d=64)
                    s.I("pool", "tensor_tensor", out=yv, in0=yv, in1=y_[:], op=ALU.add, rd=[Y, y_], wr=[Y])
                pst = s.ps()
                s.I("pe", "matmul", pst[:, 0:256], BM[:, t, g, :], xdd[:, g * 4:(g + 1) * 4, :], start=True, stop=True, rd=[BM, xdd], wr=[pst])
                c_ = tm2[(2 * k + g) % 4]
                s.I("pool", "tensor_tensor", out=c_[:], in0=ST[dr][g][:], in1=ex[:, 2, g * 4:(g + 1) * 4].rearrange("p (h o) -> p h o", o=1).to_broadcast([128, 4, 64]),
                    op=ALU.mult, rd=[ST[dr][g], ex], wr=[c_])
                s.I("dve", "tensor_tensor", out=ST[dr][g][:], in0=pst[:, 0:256].rearrange("p (h d) -> p h d", d=64), in1=c_[:], op=ALU.add, rd=[pst, c_], wr=[ST[dr][g]])
                s.I("act", "activation", out=STb[dr][g][:], in_=ST[dr][g][:].rearrange("p h d -> p (h d)"), func=AF.Copy, rd=[ST[dr][g]], wr=[STb[dr][g]])

        n = len(order)
        for k in range(n + 2):
            if k < n:
                stageA(k)
            if 1 <= k <= n:
                stageB(k - 1)
            if k >= 2:
                stageC(k - 2)
        dsk = self.bcast_row("ssdsk", self.ssm_d, l * 8, 8)
        ng = self.bcast_row("ssng", self.ssm_norm, l * 512, 512)
        zt = [s.sb("sszt%d" % i, [128, 512], F32) for i in range(3)]
        u = [s.sb("ssu%d" % i, [128, 512], F32) for i in range(3)]
        junks = [s.sb("ssjunk%d" % i, [128, 512], BF16) for i in range(1)] * 2
        sts = [s.sb("ssfst%d" % i, [128, 1, 3], F32) for i in range(4)]
        def zload(t):
            s.dma("sp", zt[t % 3][:], self.Zd.t.ap()[t * 128:(t + 1) * 128, :], zt[t % 3], self.Zd)
        zload(0)
        zload(1)
        for t in range(ntl):
            z_, u_ = zt[t % 3], u[t % 3]
            st, junk = sts[t % 4], junks[t % 2]
            if t + 2 < ntl:
                zload(t + 2)
            s.I("dve", "tensor_tensor", out=u_[:].rearrange("p (h d) -> p h d", d=64), in0=XS[:, t, :].rearrange("p (h d) -> p h d", d=64),
                in1=dsk[:].rearrange("p (h o) -> p h o", o=1).to_broadcast([128, 8, 64]), op=ALU.mult, rd=[XS, dsk], wr=[u_])
            s.I("pool", "tensor_tensor", out=u_[:], in0=u_[:], in1=Y[:, t, :], op=ALU.add, rd=[u_, Y], wr=[u_])
            s.I("dve", "tensor_tensor", out=u_[:], in0=u_[:], in1=z_[:], op=ALU.mult, rd=[u_, z_], wr=[u_])
            s.I("act", "activation", out=junk[:], in_=u_[:], func=AF.Square, accum_out=st[:, 0, 0:1], rd=[u_], wr=[junk, st])
            s.I("act", "activation", out=st[:, 0, 1:2], in_=st[:, 0, 0:1], func=AF.Sqrt, scale=1.0 / 512, bias=EPS, rd=[st], wr=[st])
            s.I("dve", "reciprocal", st[:, 0, 2:3], st[:, 0, 1:2], rd=[st], wr=[st])
            s.I("dve", "scalar_tensor_tensor", out=u_[:], in0=u_[:], scalar=st[:, 0, 2:3], in1=ng[:], op0=ALU.mult, op1=ALU.mult, rd=[u_, st, ng], wr=[u_])
            s.dma("sp", self.Yd.t.ap()[b, t * 128:(t + 1) * 128, 512:1024], u_[:], self.Yd, u_)


def _consts():
    ident = np.eye(128, dtype=np.float32)
    i = np.arange(128)
    SL = (i[:, None] > i[None, :]).astype(np.float32)
    SU = (i[:, None] < i[None, :]).astype(np.float32)
    LE = (i[:, None] <= i[None, :]).astype(np.float32)
    GE = (i[:, None] >= i[None, :]).astype(np.float32)
    ON = np.ones((128, 128), np.float32)
    tri = np.stack([SL, SU, LE, GE, ON]).astype(np.float32)
    qc = np.arange(64)
    ws = np.clip(qc - 8, 0, 48)
    kc = np.arange(64)
    ok = (kc[:, None] >= ws[None, :]) & (kc[:, None] < ws[None, :] + 16)
    m = np.where(ok, 0.0, NEG).astype(np.float32)
    na_mask = np.concatenate([m, m], axis=0)
    per_axis = 16
    inv_freq = (10000.0 ** (-np.arange(0, per_axis, 2, dtype=np.float32) / per_axis)).astype(np.float32)
    t = np.arange(L)
    pos = np.stack([t // 64, t % 64], axis=-1).astype(np.float32)
    ang = pos[:, :, None] * inv_freq
    ang = np.concatenate([ang, ang], axis=-1).reshape(L, 32)
    cos = np.cos(ang).astype(np.float32)
    sin = np.sin(ang).astype(np.float32).reshape(L, 2, 2, 8).copy()
    sin[:, :, 0, :] *= -1.0
    negm = np.stack([np.where(i[None, :] < i[:, None], NEG, 0.0), np.where(i[None, :] > i[:, None], NEG, 0.0)], axis=1).astype(np.float32)
    negm = np.ascontiguousarray(np.broadcast_to(negm[:, :, None, :], (128, 2, 4, 128))).astype(np.float32)
    sel = np.zeros((128, 2, 8, 128), np.float32)
    for h in range(8):
        sel[h, 0, h, :] = 1.0
        sel[32 + h, 0, h, :] = 1.0
    sel[:, 1] = -sel[:, 0]
    return ident, tri, na_mask, cos, sin.reshape(L, 32), negm, sel


def make_in_maps(inp):
    f = lambda a: np.ascontiguousarray(np.asarray(a, dtype=np.float32))
    ident, tri, na_mask, cos, sin, negm, sel = _consts()
    colv = lambda v, n: f(np.asarray(v).reshape(DEPTH, n, 128).transpose(0, 2, 1))
    idx = np.clip(np.arange(64)[:, None] - np.arange(64)[None, :], -15, 15) + 15
    shared = {
        "w_ada": f(inp["w_ada"]), "b_ada": f(inp["b_ada"]),
        "g_mixc": colv(inp["g_mix"], 8), "g_ffnc": colv(inp["g_ffn"], 8),
        "w_in": f(inp["w_in"]), "w_out": f(inp["w_out"]), "w_up": f(inp["ffn_w_up"]), "w_down": f(inp["ffn_w_down"]),
        "na_qg": f(inp["na_q_gain"]), "na_kg": f(inp["na_k_gain"]),
        "rpb_t": f(np.asarray(inp["na_rpb"])[..., idx]), "na_mask": na_mask,
        "df_qg": f(inp["df_q_gain"]), "df_kg": f(inp["df_k_gain"]), "df_lam": f(inp["df_lambda"]), "df_subln": f(inp["df_subln"]),
        "rope_cos": cos, "rope_sin": sin,
        "conv_wc": f(np.asarray(inp["ssm_conv_w"]).reshape(DEPTH, 5, 8, 128).transpose(0, 3, 2, 1)),
        "conv_bc": colv(inp["ssm_conv_b"], 8),
        "dt_bias": f(np.asarray(inp["ssm_dt_bias"]).reshape(DEPTH, 16)), "a_log": f(np.asarray(inp["ssm_a_log"]).reshape(DEPTH, 16)),
        "ssm_d": f(inp["ssm_d"]), "ssm_norm": f(inp["ssm_norm"]),
        "fconv_wc": f(np.asarray(inp["ffn_conv_w"]).reshape(DEPTH, 3, NF, 128).transpose(0, 3, 2, 1)),
        "fconv_bc": colv(inp["ffn_conv_b"], NF),
        "ident": ident, "tri": tri, "negm": negm, "sel": sel,
    }
    maps = []
    x, c, ctx, c_ctx = (np.asarray(inp[k]) for k in ("x", "c", "ctx", "c_ctx"))
    for i in range(NCORES):
        sl = slice(i * NB, (i + 1) * NB)
        c3 = np.concatenate([c[sl], c_ctx[None, :]], axis=0)
        cT = f(c3.reshape(3, 8, 128).transpose(2, 1, 0))
        m = dict(shared)
        m.update({"x": f(x[sl]), "ctx": f(ctx[sl]), "cT": cT})
        maps.append(m)
    return maps


_NC_CACHE = {}


def kernel(**inputs):
    if "nc" not in _NC_CACHE:
        _NC_CACHE["nc"] = Builder().build()
    nc = _NC_CACHE["nc"]
    maps = make_in_maps(inputs)
    res = run_bass_kernel_spmd(nc, maps, core_ids=list(range(NCORES)))
    return np.concatenate([np.asarray(r["out"]) for r in res.results], axis=0).astype(np.float32)
```

```python
import math
from contextlib import ExitStack

import numpy as np
import concourse.bass as bass
import concourse.mybir as mybir
from concourse.bass_utils import run_bass_kernel_spmd

F32 = mybir.dt.float32
BF16 = mybir.dt.bfloat16
AF = mybir.ActivationFunctionType
ALU = mybir.AluOpType
AX = mybir.AxisListType

NCORES = 8
NB = 2
L = 2048
LC = 256
T = L + LC
D = 1024
DEPTH = 2
DFF = 2816
NF = DFF // 128
EPS = 1e-6
NEG = -30000.0


class Buf:
    __slots__ = ("name", "w", "r", "dsem", "t", "persist")

    def __init__(self, name, t=None, persist=False):
        self.name = name
        self.w = {}
        self.r = {}
        self.dsem = None
        self.t = t
        self.persist = persist

    def __getitem__(self, idx):
        return self.t[idx]


class Sched:
    ENG = ("pe", "act", "dve", "pool", "sp")

    def __init__(self, nc, es, ndsem=48):
        self.nc = nc
        self.ges = es
        self.es = es
        self.eng = {"pe": nc.tensor, "act": nc.scalar, "dve": nc.vector, "pool": nc.gpsimd, "sp": nc.sync}
        self.sem = {k: es.enter_context(nc.semaphore("c_" + k)) for k in self.ENG}
        self.cnt = {k: 0 for k in self.ENG}
        self.seen = {k: {} for k in self.ENG}
        self.prog = {k: [] for k in self.ENG}
        self.dsems = [es.enter_context(nc.semaphore("d%d" % i)) for i in range(ndsem)]
        self.dval = [0] * ndsem
        self.dfree = list(range(ndsem))
        self.phase_bufs = []
        self.uid = 0
        self.ps_rr = 0
        self.PS = []

    def sb(self, name, shape, dt, persist=False):
        self.uid += 1
        es = self.ges if persist else self.es
        t = es.enter_context(self.nc.sbuf_tensor("%s_%d" % (name, self.uid), list(shape), dt))
        b = Buf(name, t, persist)
        if not persist:
            self.phase_bufs.append(b)
        return b

    def dram(self, name, shape, dt, kind="Internal"):
        t = self.nc.dram_tensor(name, list(shape), dt, kind=kind)
        return Buf(name, t, True)

    def psum_init(self):
        self.PSP = []
        for j in range(4):
            t = self.ges.enter_context(self.nc.psum_tensor("psp%d" % j, [128, 1024], F32))
            self.PSP.append(t)
            self.PS.append(Buf("ps%d" % (2 * j), t[:, 0:512], True))
            self.PS.append(Buf("ps%d" % (2 * j + 1), t[:, 512:1024], True))

    def pspair(self):
        if self.ps_rr % 2:
            self.ps_rr += 1
        j = (self.ps_rr % 6) // 2
        self.ps_rr += 2
        return self.PS[2 * j], self.PS[2 * j + 1], self.PSP[j]

    def ps(self):
        b = self.PS[self.ps_rr % 6]
        self.ps_rr += 1
        return b

    def psacc(self, i):
        return self.PS[6 + (i % 2)]

    def _deps(self, e, reads, writes):
        d = {}
        own = "c_" + e

        def add(tokdict, is_read_set):
            for key, (sem, val) in tokdict.items():
                if key == own and (e == "pe" or is_read_set):
                    continue
                if d.get(key, (None, 0))[1] < val:
                    d[key] = (sem, val)

        for b in reads:
            add(b.w, False)
        for b in writes:
            add(b.w, False)
            add(b.r, True)
        return d

    def _wait(self, e, d):
        seen = self.seen[e]
        for key, (sem, val) in d.items():
            if seen.get(key, 0) >= val:
                continue
            self.prog[e].append(("w", sem, val))
            seen[key] = val

    def I(self, e, name, *args, rd=(), wr=(), **kw):
        d = self._deps(e, rd, wr)
        self._wait(e, d)
        self.cnt[e] += 1
        self.prog[e].append(("i", name, args, kw, self.sem[e], 1))
        key = "c_" + e
        tok = (self.sem[e], self.cnt[e])
        for b in rd:
            b.r[key] = tok
        for b in wr:
            b.w[key] = tok

    def dma(self, e, out_ap, in_ap, dst, src, **kw):
        if dst.dsem is None:
            dst.dsem = self.dfree.pop()
        i = dst.dsem
        d = self._deps(e, [src], [dst])
        self._wait(e, d)
        self.dval[i] += 16
        self.prog[e].append(("i", "dma_start", (), dict(out=out_ap, in_=in_ap, **kw), self.dsems[i], 16))
        key = "d%d" % i
        tok = (self.dsems[i], self.dval[i])
        src.r[key] = tok
        dst.w[key] = tok

    def drain(self):
        d = {}
        for i, v in enumerate(self.dval):
            if v:
                d["d%d" % i] = (self.dsems[i], v)
        self._wait("sp", d)

    def emit(self):
        self.drain()
        prog = self.prog
        with self.nc.Block() as block:
            def mk(e):
                def body(g):
                    for it in prog[e]:
                        if it[0] == "w":
                            g.wait_ge(it[1], it[2])
                        else:
                            getattr(g, it[1])(*it[2], **it[3]).then_inc(it[4], it[5])
                return body
            block.tensor(mk("pe"))
            block.scalar(mk("act"))
            block.vector(mk("dve"))
            block.gpsimd(mk("pool"))
            block.sync(mk("sp"))
        self.prog = {k: [] for k in self.ENG}
        for b in self.phase_bufs:
            if b.dsem is not None:
                self.dfree.append(b.dsem)
                b.dsem = None
        self.phase_bufs = []

    class _Phase:
        def __init__(self, s):
            self.s = s

        def __enter__(self):
            self.es = ExitStack()
            self.es.__enter__()
            self.s.es = self.es
            return self

        def __exit__(self, *a):
            if a[0] is None:
                self.s.emit()
            self.s.es = self.s.ges
            return self.es.__exit__(*a)

    def phase(self):
        return Sched._Phase(self)


def dap(buf, off, dims):
    return bass.AP(buf.t, off, [list(d) for d in dims])


class Builder:
    def __init__(self, dbg=None):
        self.dbg = dbg or {}
        self.nc = bass.Bass("TRN2", target_bir_lowering=False)
        self.outs = []
        self.pre = {}
        self.scoped = []

    def declare(self, s):
        I = lambda n, sh, dt=F32: s.dram(n, sh, dt, kind="ExternalInput")
        self.x = I("x", [NB, L, D])
        self.ctx = I("ctx", [NB, LC, D])
        self.cT = I("cT", [128, 8, 3])
        self.w_ada = I("w_ada", [DEPTH, D, 6 * D])
        self.b_ada = I("b_ada", [DEPTH, 6 * D])
        self.g_mixc = I("g_mixc", [DEPTH, 128, 8])
        self.g_ffnc = I("g_ffnc", [DEPTH, 128, 8])
        self.w_in = I("w_in", [DEPTH, D, 3088])
        self.w_out = I("w_out", [DEPTH, D, D])
        self.w_up = I("w_up", [DEPTH, D, 2 * DFF])
        self.w_down = I("w_down", [DEPTH, DFF, D])
        self.na_qg = I("na_qg", [DEPTH, 64])
        self.na_kg = I("na_kg", [DEPTH, 64])
        self.rpb_t = I("rpb_t", [DEPTH, 4, 15, 64, 64])
        self.na_mask = I("na_mask", [128, 64])
        self.df_qg = I("df_qg", [DEPTH, 32])
        self.df_kg = I("df_kg", [DEPTH, 32])
        self.df_lam = I("df_lam", [DEPTH, 4, 32])
        self.df_subln = I("df_subln", [DEPTH, 64])
        self.rope_cos = I("rope_cos", [L, 32])
        self.rope_sin = I("rope_sin", [L, 32])
        self.conv_wc = I("conv_wc", [DEPTH, 128, 8, 5])
        self.conv_bc = I("conv_bc", [DEPTH, 128, 8])
        self.dt_bias = I("dt_bias", [DEPTH, 16])
        self.a_log = I("a_log", [DEPTH, 16])
        self.ssm_d = I("ssm_d", [DEPTH, 8])
        self.ssm_norm = I("ssm_norm", [DEPTH, 512])
        self.fconv_wc = I("fconv_wc", [DEPTH, 128, NF, 3])
        self.fconv_bc = I("fconv_bc", [DEPTH, 128, NF])
        self.ident_d = I("ident", [128, 128])
        self.tri_d = I("tri", [5, 128, 128])
        self.negm_d = I("negm", [128, 2, 4, 128])
        self.sel_d = I("sel", [128, 2, 8, 128])
        self.out = s.dram("out", [NB, L, D], F32, kind="ExternalOutput")
        dk = "ExternalOutput" if self.dbg.get("dump") else "Internal"
        self.modrow_d = s.dram("modrow_d", [DEPTH, 3, 6 * D], F32, kind=dk)
        self.XA = [s.dram("XA%d" % l, [NB, T, D], F32, kind=dk) for l in range(DEPTH)]
        self.XB = s.dram("XB0", [NB, T, D], F32, kind=dk)
        self.Yd = s.dram("Yd", [NB, T, D], F32, kind=dk)
        self.Zd = s.dram("Zd", [T, 512], F32, kind=dk)
        self.ATd = s.dram("ATd", [NF, 128, T], BF16, kind=dk)
        self.hTd = s.dram("hTd", [128, 8, T], BF16, kind=dk) if self.dbg.get("dump") else None
        self.Yin = I("Yin", [DEPTH, NB, T, D]) if self.dbg.get("feed_y") else None

    def build(self):
        nc = self.nc
        with ExitStack() as es:
            s = Sched(nc, es)
            self.s = s
            self.declare(s)
            s.psum_init()
            self.ident = s.sb("ident", [128, 128], F32, persist=True)
            self.identb = s.sb("identb", [128, 128], BF16, persist=True)
            self.hT = s.sb("hT", [128, 8, T], BF16, persist=True)
            self.AB = [[s.sb("AB%d%d" % (l, i), [128, 8, 3], F32, persist=True) for i in range(4)] for l in range(DEPTH)]
            self.BT = s.sb("BT", [128, 4, 14, 64], BF16, persist=True)
            ph = self.dbg.get("phases")
            with s.phase():
                s.dma("sp", self.ident[:], self.ident_d.t.ap(), self.ident, self.ident_d)
                s.I("act", "activation", out=self.identb[:], in_=self.ident[:], func=AF.Copy, rd=[self.ident], wr=[self.identb])
                self.phase_adaln()
            for l in self.dbg.get("layers", range(DEPTH)):
                last = l == DEPTH - 1
                nt = 16 if last else 18
                if ph is None or "na" in ph:
                    with s.phase():
                        self.phase_na_bias(l)
                for b in range(NB):
                    if l == 0:
                        src = lambda t, b=b: (self.x.t.ap()[b, t * 128:(t + 1) * 128, :], self.x) if t < 16 else \
                            (self.ctx.t.ap()[b, (t - 16) * 128:(t - 15) * 128, :], self.ctx)
                    else:
                        src = lambda t, b=b: (self.XB.t.ap()[b, t * 128:(t + 1) * 128, :], self.XB)
                    full = ph is None
                    with ExitStack() as sc1:
                        wdf_buf = self.alloc_scoped(sc1, "wdf", [128, 8, 768]) if full else None
                        with s.phase():
                            if full:
                                self.pre["wna"] = self.w_na(l)
                            self.phase_norm(src, self.AB[l][0], self.AB[l][1], b, 18)
                            if self.dbg.get("dump") and l == self.dbg.get("dump_l", 0) and b == 0 and self.dbg.get("dump_h") == "mix":
                                s.dma("sp", self.hTd.t.ap(), self.hT[:], self.hTd, self.hT)
                            if ph is None or "na" in ph:
                                if full:
                                    self.pre["wdf"] = self.w_df(l, wdf_buf)
                                self.phase_na(l, b)
                        if ph is None or "df" in ph:
                            with s.phase():
                                self.phase_df(l, b)
                        self.release_scoped()
                    if ph is None or "ssm" in ph:
                        self.phase_ssm(l, b)
                    if ph is None or "out" in ph:
                        with s.phase():
                            self.phase_out(l, b, src, nt)
                    if ph is None or "ffn" in ph:
                        srcA = lambda t, b=b, l=l: (self.XA[l].t.ap()[b, t * 128:(t + 1) * 128, :], self.XA[l])
                        with s.phase():
                            self.phase_norm(srcA, self.AB[l][2], self.AB[l][3], b, nt)
                        with ExitStack() as sc2:
                            wdn_buf = self.alloc_scoped(sc2, "wdn", [128, NF, D])
                            with s.phase():
                                self.pre["wdn"] = self.w_dn(l, wdn_buf)
                                self.phase_ffn_up(l, b, nt)
                            with s.phase():
                                self.phase_ffn_down(l, b, srcA, nt)
                            self.release_scoped()
        return nc

    def phase_adaln(self):
        s = self.s
        cT = s.sb("cT", [128, 8, 3], F32)
        siluT = s.sb("siluT", [128, 8, 3], F32)
        s.dma("sp", cT[:], self.cT.t.ap(), cT, self.cT)
        s.I("act", "activation", out=siluT[:], in_=cT[:], func=AF.Silu, rd=[cT], wr=[siluT])
        wa = [s.sb("wa%d" % i, [128, 8, 512], F32) for i in range(3)]
        for l in range(DEPTH):
            brow = s.sb("brow%d" % l, [3, 6 * D], F32)
            modrow = s.sb("modrow%d" % l, [3, 6 * D], F32)
            s.dma("sp", brow[:], dap(self.b_ada, l * 6 * D, [[0, 3], [1, 6 * D]]), brow, self.b_ada)
            for j in range(12):
                w = wa[j % 3]
                s.dma("sp", w[:], self.w_ada.t.ap()[l, :, j * 512:(j + 1) * 512].rearrange("(k p) n -> p k n", p=128), w, self.w_ada)
                pm = s.ps()
                for k in range(8):
                    s.I("pe", "matmul", pm[0:3, :], siluT[:, k, :], w[:, k, :], start=(k == 0), stop=(k == 7), rd=[siluT, w], wr=[pm])
                s.I("dve", "tensor_tensor", out=modrow[:, j * 512:(j + 1) * 512], in0=pm[0:3, :], in1=brow[:, j * 512:(j + 1) * 512], op=ALU.add,
                    rd=[pm, brow], wr=[modrow])
            s.dma("sp", self.modrow_d.t.ap()[l], modrow[:], self.modrow_d, modrow)
            pT = s.ps()
            for c in range(48):
                s.I("pe", "transpose", pT[:, c * 3:(c + 1) * 3], modrow[0:3, c * 128:(c + 1) * 128], self.ident[0:3, 0:3], rd=[modrow, self.ident], wr=[pT])
            modcol = s.sb("modcol%d" % l, [128, 48, 3], F32)
            s.I("dve", "tensor_copy", modcol[:].rearrange("p a b -> p (a b)"), pT[:, 0:144], rd=[pT], wr=[modcol])
            gm = s.sb("gm%d" % l, [128, 8, 1], F32)
            gf = s.sb("gf%d" % l, [128, 8, 1], F32)
            s.dma("sp", gm[:, :, 0], self.g_mixc.t.ap()[l], gm, self.g_mixc)
            s.dma("sp", gf[:, :, 0], self.g_ffnc.t.ap()[l], gf, self.g_ffnc)
            A1, B1, A2, B2 = self.AB[l]
            for (A, Bv, g, sc0, sh0) in ((A1, B1, gm, 8, 0), (A2, B2, gf, 32, 24)):
                s.I("dve", "scalar_tensor_tensor", out=A[:], in0=modcol[:, sc0:sc0 + 8, :], scalar=1.0, in1=g[:].to_broadcast([128, 8, 3]),
                    op0=ALU.add, op1=ALU.mult, rd=[modcol, g], wr=[A])
                s.I("dve", "tensor_copy", Bv[:], modcol[:, sh0:sh0 + 8, :], rd=[modcol], wr=[Bv])

    def phase_norm(self, src, A, Bv, b, ntiles):
        s = self.s
        xt = [s.sb("nxt%d" % i, [128, D], F32) for i in range(4)]
        xn = [s.sb("nxn%d" % i, [128, D], F32) for i in range(8)]
        junks = [s.sb("njunk%d" % i, [128, D], BF16) for i in range(2)]
        sts = [s.sb("nst%d" % i, [128, 1, 3], F32) for i in range(8)]
        ngroups = (ntiles + 3) // 4
        for g in range(ngroups):
            tiles = list(range(4 * g, min(4 * g + 4, ntiles)))
            j = b if tiles[0] < 16 else 2
            for ti, t in enumerate(tiles):
                ap, sbuf = src(t)
                x_ = xt[t % 4]
                n_ = xn[t % 8]
                s.dma("sp", x_[:], ap, x_, sbuf)
                st, junk = sts[t % 8], junks[t % 2]
                s.I("act", "activation", out=junk[:], in_=x_[:], func=AF.Square, accum_out=st[:, 0, 0:1], rd=[x_], wr=[junk, st])
                s.I("act", "activation", out=st[:, 0, 1:2], in_=st[:, 0, 0:1], func=AF.Sqrt, scale=1.0 / D, bias=EPS, rd=[st], wr=[st])
                s.I("dve", "reciprocal", st[:, 0, 2:3], st[:, 0, 1:2], rd=[st], wr=[st])
                s.I("dve", "tensor_scalar", out=n_[:], in0=x_[:], scalar1=st[:, 0, 2:3], scalar2=None, op0=ALU.mult, rd=[x_, st], wr=[n_])
            n = len(tiles) * 128
            for c in range(8):
                p = s.ps()
                for ti, t in enumerate(tiles):
                    n_ = xn[t % 8]
                    s.I("pe", "transpose", p[:, ti * 128:(ti + 1) * 128], n_[:, c * 128:(c + 1) * 128], self.ident[:], rd=[n_, self.ident], wr=[p])
                if c % 2 == 0:
                    s.I("act", "activation", out=self.hT[:, c, tiles[0] * 128:tiles[0] * 128 + n], in_=p[:, 0:n], func=AF.Identity,
                        scale=A[:, c, j:j + 1], bias=Bv[:, c, j:j + 1], rd=[p, A, Bv], wr=[self.hT])
                else:
                    s.I("dve", "tensor_scalar", out=self.hT[:, c, tiles[0] * 128:tiles[0] * 128 + n], in0=p[:, 0:n], scalar1=A[:, c, j:j + 1],
                        scalar2=Bv[:, c, j:j + 1], op0=ALU.mult, op1=ALU.add, rd=[p, A, Bv], wr=[self.hT])

    def load_w(self, name, src_buf, src_ap, shape, scope=None):
        s = self.s
        if name in self.pre:
            return self.pre.pop(name)
        if scope is None:
            w = s.sb(name, shape, BF16)
        else:
            w = scope
        s.dma("pool", w[:], src_ap, w, src_buf)
        return w

    def alloc_scoped(self, es, name, shape):
        s = self.s
        s.uid += 1
        t = es.enter_context(self.nc.sbuf_tensor("%s_%d" % (name, s.uid), list(shape), BF16))
        w = Buf(name, t, True)
        self.scoped.append(w)
        return w

    def release_scoped(self):
        for b in self.scoped:
            if b.dsem is not None:
                self.s.dfree.append(b.dsem)
                b.dsem = None
        self.scoped = []

    def w_na(self, l, scope=None):
        return self.load_w("wna", self.w_in, self.w_in.t.ap()[l, :, 0:768].rearrange("(k p) n -> p k n", p=128), [128, 8, 768], scope)

    def w_df(self, l, scope=None):
        return self.load_w("wdf", self.w_in, self.w_in.t.ap()[l, :, 768:1536].rearrange("(k p) n -> p k n", p=128), [128, 8, 768], scope)

    def w_dn(self, l, scope=None):
        return self.load_w("wdn", self.w_down, self.w_down.t.ap()[l].rearrange("(f p) n -> p f n", p=128), [128, NF, D], scope)

    def bcast_row(self, name, src_buf, off, n, dt=F32, parts=128):
        s = self.s
        t = s.sb(name, [parts, n], dt)
        s.dma("sp", t[:], dap(src_buf, off, [[0, parts], [1, n]]), t, src_buf)
        return t

    def phase_out(self, l, b, src, ntiles):
        s = self.s
        w = self.load_w("wout", self.w_out, self.w_out.t.ap()[l].rearrange("(k p) n -> p k n", p=128), [128, 8, D])
        G = {}
        G[b] = self.bcast_row("g1b", self.modrow_d, (l * 3 + b) * 6 * D + 2 * D, D)
        if ntiles > 16:
            G[2] = self.bcast_row("g1c", self.modrow_d, (l * 3 + 2) * 6 * D + 2 * D, D)
        yt = [s.sb("oyt%d" % i, [128, D], F32) for i in range(8)]
        xr = [s.sb("oxr%d" % i, [128, D], F32) for i in range(8)]
        yT = [s.sb("oyT%d" % i, [128, 8, 512], BF16) for i in range(2)]
        tmp = [s.sb("otmp%d" % i, [128, D], F32) for i in range(2)]
        xo = [s.sb("oxo%d" % i, [128, D], F32) for i in range(2)]
        ngroups = (ntiles + 3) // 4

        def loads(g):
            for t in range(4 * g, min(4 * g + 4, ntiles)):
                y_ = yt[t % 8]
                if self.Yin is not None:
                    s.dma("sp", y_[:], self.Yin.t.ap()[l, b, t * 128:(t + 1) * 128, :], y_, self.Yin)
                else:
                    s.dma("sp", y_[:], self.Yd.t.ap()[b, t * 128:(t + 1) * 128, :], y_, self.Yd)
                ap, sbuf = src(t)
                s.dma("sp", xr[t % 8][:], ap, xr[t % 8], sbuf)

        loads(0)
        for g in range(ngroups):
            if g + 1 < ngroups:
                loads(g + 1)
            tiles = list(range(4 * g, min(4 * g + 4, ntiles)))
            j = b if tiles[0] < 16 else 2
            yT_ = yT[g % 2]
            n = len(tiles) * 128
            for c in range(8):
                p = s.ps()
                for ti, t in enumerate(tiles):
                    s.I("pe", "transpose", p[:, ti * 128:(ti + 1) * 128], yt[t % 8][:, c * 128:(c + 1) * 128], self.ident[:], rd=[yt[t % 8], self.ident], wr=[p])
                s.I("act", "activation", out=yT_[:, c, 0:n], in_=p[:, 0:n], func=AF.Copy, rd=[p], wr=[yT_])
            for ti, t in enumerate(tiles):
                tm = tmp[t % 2]
                xo_ = xo[t % 2]
                for hf in range(2):
                    p = s.ps()
                    for k in range(8):
                        s.I("pe", "matmul", p[:, :], yT_[:, k, ti * 128:(ti + 1) * 128], w[:, k, hf * 512:(hf + 1) * 512], start=(k == 0), stop=(k == 7),
                            rd=[yT_, w], wr=[p])
                    s.I("dve", "tensor_tensor", out=tm[:, hf * 512:(hf + 1) * 512], in0=p[:, :], in1=G[j][:, hf * 512:(hf + 1) * 512], op=ALU.mult,
                        rd=[p, G[j]], wr=[tm])
                s.I("pool", "tensor_tensor", out=xo_[:], in0=tm[:], in1=xr[t % 8][:], op=ALU.add, rd=[tm, xr[t % 8]], wr=[xo_])
                s.dma("sp", self.XA[l].t.ap()[b, t * 128:(t + 1) * 128, :], xo_[:], self.XA[l], xo_)

    def phase_ffn_up(self, l, b, ntiles):
        s = self.s
        groups = [(0, 512), (512, 512), (1024, 512), (1536, 512)]
        if ntiles > 16:
            groups.append((2048, 256))
        ntok = ntiles * 128
        wu = [s.sb("wu%d" % i, [128, 8, 256], BF16) for i in range(3)]
        Gl = [s.sb("fGl%d" % i, [128, L + 2], F32) for i in range(2)]
        Gc = [s.sb("fGc%d" % i, [128, LC + 2], F32) for i in range(2)]
        V = [s.sb("fV%d" % i, [128, T], F32) for i in range(2)]
        acc = [s.sb("facc%d" % i, [128, T], F32) for i in range(2)]
        at = [s.sb("fat%d" % i, [128, T], BF16) for i in range(2)]
        cw = s.sb("fcw", [128, NF, 3], F32)
        cb = s.sb("fcb", [128, NF], F32)
        s.dma("sp", cw[:], self.fconv_wc.t.ap()[l], cw, self.fconv_wc)
        s.dma("sp", cb[:], self.fconv_bc.t.ap()[l], cb, self.fconv_bc)
        for i in range(2):
            s.I("pool", "memset", Gl[i][:], 0.0, wr=[Gl[i]])
            s.I("pool", "memset", Gc[i][:], 0.0, wr=[Gc[i]])
        wup = self.w_up.t.ap()[l]

        def loadw(f):
            w = wu[f % 3]
            s.dma("pool", w[:, :, 0:128], wup[:, f * 128:(f + 1) * 128].rearrange("(k p) n -> p k n", p=128), w, self.w_up)
            s.dma("pool", w[:, :, 128:256], wup[:, DFF + f * 128:DFF + (f + 1) * 128].rearrange("(k p) n -> p k n", p=128), w, self.w_up)

        loadw(0)
        pend = []
        for f in range(NF):
            if f + 1 < NF:
                loadw(f + 1)
            w = wu[f % 3]
            gl, gc, v, a, o = Gl[f % 2], Gc[f % 2], V[f % 2], acc[f % 2], at[f % 2]
            for gi, (t0, n) in enumerate(groups):
                if gi == 2 and pend:
                    pend.pop(0)()
                pa = s.ps()
                pb = s.ps()
                for k in range(8):
                    s.I("pe", "matmul", pa[:, 0:n], w[:, k, 0:128], self.hT[:, k, t0:t0 + n], start=(k == 0), stop=(k == 7), rd=[w, self.hT], wr=[pa])
                for k in range(8):
                    s.I("pe", "matmul", pb[:, 0:n], w[:, k, 128:256], self.hT[:, k, t0:t0 + n], start=(k == 0), stop=(k == 7), rd=[w, self.hT], wr=[pb])
                if t0 < L:
                    s.I("act", "activation", out=gl[:, 1 + t0:1 + t0 + n], in_=pa[:, 0:n], func=AF.Copy, rd=[pa], wr=[gl])
                else:
                    s.I("act", "activation", out=gc[:, 1:1 + n], in_=pa[:, 0:n], func=AF.Copy, rd=[pa], wr=[gc])
                s.I("act", "activation", out=v[:, t0:t0 + n], in_=pb[:, 0:n], func=AF.Copy, rd=[pb], wr=[v])
            segs = [(gl, 0, L)] + ([(gc, L, LC)] if ntiles > 16 else [])
            for (gb, o0, n) in segs:
                s.I("dve", "tensor_scalar", out=a[:, o0:o0 + n], in0=gb[:, 0:n], scalar1=cw[:, f, 0:1], scalar2=None, op0=ALU.mult, rd=[gb, cw], wr=[a])
                for k in (1, 2):
                    s.I("dve", "scalar_tensor_tensor", out=a[:, o0:o0 + n], in0=gb[:, k:k + n], scalar=cw[:, f, k:k + 1], in1=a[:, o0:o0 + n],
                        op0=ALU.mult, op1=ALU.add, rd=[gb, cw, a], wr=[a])

            def tail(f=f, a=a, v=v, o=o):
                s.I("act", "activation", out=a[:, 0:ntok], in_=a[:, 0:ntok], func=AF.Silu, bias=cb[:, f:f + 1], scale=1.0, rd=[a, cb], wr=[a])
                s.I("pool", "tensor_tensor", out=o[:, 0:ntok], in0=a[:, 0:ntok], in1=v[:, 0:ntok], op=ALU.mult, rd=[a, v], wr=[o])
                s.dma("sp", self.ATd.t.ap()[f, :, 0:ntok], o[:, 0:ntok], self.ATd, o)
            pend.append(tail)
        while pend:
            pend.pop(0)()

    def phase_ffn_down(self, l, b, src, ntiles):
        s = self.s
        last = l == DEPTH - 1
        w = self.w_dn(l)
        G = {}
        G[b] = self.bcast_row("g2b", self.modrow_d, (l * 3 + b) * 6 * D + 5 * D, D)
        if ntiles > 16:
            G[2] = self.bcast_row("g2c", self.modrow_d, (l * 3 + 2) * 6 * D + 5 * D, D)
        aT = [s.sb("daT%d" % i, [128, NF, 512], BF16) for i in range(2)]
        xr = [s.sb("dxr%d" % i, [128, D], F32) for i in range(8)]
        tmp = [s.sb("dtmp%d" % i, [128, D], F32) for i in range(2)]
        xo = [s.sb("dxo%d" % i, [128, D], F32) for i in range(2)]
        ngroups = (ntiles + 3) // 4

        def loads(g):
            tiles = list(range(4 * g, min(4 * g + 4, ntiles)))
            n = len(tiles) * 128
            a_ = aT[g % 2]
            s.dma("sp", a_[:, :, 0:n], self.ATd.t.ap()[:, :, tiles[0] * 128:tiles[0] * 128 + n].rearrange("f p t -> p f t"), a_, self.ATd)
            for t in tiles:
                ap, sbuf = src(t)
                s.dma("sp", xr[t % 8][:], ap, xr[t % 8], sbuf)

        loads(0)
        for g in range(ngroups):
            if g + 1 < ngroups:
                loads(g + 1)
            tiles = list(range(4 * g, min(4 * g + 4, ntiles)))
            j = b if tiles[0] < 16 else 2
            a_ = aT[g % 2]
            for ti, t in enumerate(tiles):
                tm = tmp[t % 2]
                xo_ = xo[t % 2]
                for hf in range(2):
                    p = s.ps()
                    for f in range(NF):
                        s.I("pe", "matmul", p[:, :], a_[:, f, ti * 128:(ti + 1) * 128], w[:, f, hf * 512:(hf + 1) * 512], start=(f == 0), stop=(f == NF - 1),
                            rd=[a_, w], wr=[p])
                    s.I("dve", "tensor_tensor", out=tm[:, hf * 512:(hf + 1) * 512], in0=p[:, :], in1=G[j][:, hf * 512:(hf + 1) * 512], op=ALU.mult,
                        rd=[p, G[j]], wr=[tm])
                s.I("pool", "tensor_tensor", out=xo_[:], in0=tm[:], in1=xr[t % 8][:], op=ALU.add, rd=[tm, xr[t % 8]], wr=[xo_])
                if last:
                    s.dma("sp", self.out.t.ap()[b, t * 128:(t + 1) * 128, :], xo_[:], self.out, xo_)
                else:
                    s.dma("sp", self.XB.t.ap()[b, t * 128:(t + 1) * 128, :], xo_[:], self.XB, xo_)

    def group_norm(self, p, ngrp, gd, gains, out, sq, st, view):
        s = self.s
        n = ngrp * gd
        s.I("act", "activation", out=sq[:, 0:n], in_=p[:, 0:n], func=AF.Square, rd=[p], wr=[sq])
        s.I("dve", "tensor_reduce", out=st[:, 0:ngrp, 0], in_=sq[:, 0:n].rearrange("p (g d) -> p g d", d=gd), axis=AX.X, op=ALU.add, rd=[sq], wr=[st])
        s.I("act", "activation", out=st[:, 0:ngrp, 1], in_=st[:, 0:ngrp, 0], func=AF.Sqrt, scale=1.0 / gd, bias=EPS, rd=[st], wr=[st])
        s.I("dve", "reciprocal", st[:, 0:ngrp, 2], st[:, 0:ngrp, 1], rd=[st], wr=[st])
        s.I("dve", "tensor_tensor", out=sq[:, 0:n].rearrange("p (g d) -> p g d", d=gd), in0=p[:, 0:n].rearrange("p (g d) -> p g d", d=gd),
            in1=st[:, 0:ngrp, 2:3].to_broadcast([128, ngrp, gd]), op=ALU.mult, rd=[p, st], wr=[sq])
        s.I("pool", "tensor_tensor", out=out, in0=view(sq[:, 0:n]), in1=gains, op=ALU.mult, rd=[sq] + self._gn_rd, wr=self._gn_wr)

    def phase_na_bias(self, l):
        s = self.s
        bt32 = s.sb("bt32", [128, 4, 14, 64], F32)
        mk = s.sb("namask", [128, 64], F32)
        s.dma("sp", mk[:], self.na_mask.t.ap(), mk, self.na_mask)
        for h in range(4):
            for half in range(2):
                s.dma("sp", bt32[64 * half:64 * half + 64, h, :, :], self.rpb_t.t.ap()[l, h, half:half + 14].rearrange("d k q -> k d q"), bt32, self.rpb_t)
        s.I("dve", "tensor_tensor", out=bt32[:].rearrange("p h d q -> p (h d) q"), in0=bt32[:].rearrange("p h d q -> p (h d) q"),
            in1=mk[:].rearrange("p (o q) -> p o q", o=1).to_broadcast([128, 56, 64]), op=ALU.add, rd=[bt32, mk], wr=[bt32])
        s.I("dve", "tensor_scalar", out=self.BT[:].rearrange("p h d q -> p (h d q)"), in0=bt32[:].rearrange("p h d q -> p (h d q)"), scalar1=8.0, scalar2=None,
            op0=ALU.mult, rd=[bt32], wr=[self.BT])

    def phase_na(self, l, b):
        s = self.s
        last = l == DEPTH - 1
        w = self.w_na(l)
        gains = s.sb("nagain", [128, 2, 1, 64], F32)
        s.dma("sp", gains[:, 0, 0, :], dap(self.na_qg, l * 64, [[0, 128], [1, 64]]), gains, self.na_qg)
        s.dma("sp", gains[:, 1, 0, :], dap(self.na_kg, l * 64, [[0, 128], [1, 64]]), gains, self.na_kg)
        QKT = s.sb("naQKT", [128, 6, T], BF16)
        kz = [s.sb("nakz%d" % i, [128, 2, 256], F32) for i in range(2)]
        for i in range(2):
            s.I("pool", "memset", kz[i][:], 0.0, wr=[kz[i]])
        VE = s.sb("naVE", [128, 18, 4, 65], BF16)
        VO = s.sb("naVO", [128, 15, 4, 65], BF16)
        s.I("pool", "memset", VE[:], 1.0, wr=[VE])
        s.I("pool", "memset", VO[:], 1.0, wr=[VO])
        qn = [s.sb("naqn%d" % i, [128, 2, 4, 64], F32) for i in range(2)]
        gsq = [s.sb("nagsq%d" % i, [128, 512], F32) for i in range(2)]
        gst = [s.sb("nagst%d" % i, [128, 16, 3], F32) for i in range(2)]
        pending = None
        for t in range(18):
            pq = s.ps()
            pv = s.ps()
            for k in range(8):
                s.I("pe", "matmul", pq[:, :], self.hT[:, k, t * 128:(t + 1) * 128], w[:, k, 0:512], start=(k == 0), stop=(k == 7), rd=[self.hT, w], wr=[pq])
            for k in range(8):
                s.I("pe", "matmul", pv[:, 0:256], self.hT[:, k, t * 128:(t + 1) * 128], w[:, k, 512:768], start=(k == 0), stop=(k == 7), rd=[self.hT, w], wr=[pv])
            q_ = qn[t % 2]
            self._gn_rd = [gains]
            self._gn_wr = [q_]
            self.group_norm(pq, 8, 64, gains[:].to_broadcast([128, 2, 4, 64]), q_[:], gsq[t % 2], gst[t % 2],
                            lambda ap: ap.rearrange("p (a h d) -> p a h d", a=2, h=4))
            kz_ = kz[t % 2]
            for hs in range(2):
                s.I("pool", "tensor_copy", kz_[:, hs, :].rearrange("p (pr h2 d) -> p pr h2 d", pr=2, h2=2)[:, :, hs, :],
                    q_[:, 1, :, :].rearrange("p (pr h2) d -> p pr h2 d", pr=2)[:, :, hs, :], rd=[q_], wr=[kz_])
            s.I("dve", "tensor_copy", VE[:, t, :, 0:64], pv[:, 0:256].rearrange("p (h d) -> p h d", h=4), rd=[pv], wr=[VE])

            def tail(t=t, q_=q_, kz_=kz_):
                pt = s.ps()
                pt2 = s.ps()
                qf = q_[:].rearrange("p a h d -> p (a h d)")
                for cc in range(2):
                    s.I("pe", "transpose", pt[:, cc * 128:(cc + 1) * 128], qf[:, cc * 128:(cc + 1) * 128], self.ident[:], rd=[q_, self.ident], wr=[pt])
                for hs in range(2):
                    for pr in range(2):
                        s.I("pe", "transpose", pt2[:, (hs * 2 + pr) * 128:(hs * 2 + pr + 1) * 128], kz_[:, hs, pr * 128:(pr + 1) * 128], self.ident[:], rd=[kz_, self.ident], wr=[pt2])
                s.I("act", "activation", out=QKT[:, 0:2, t * 128:(t + 1) * 128], in_=pt[:, 0:256].rearrange("p (c t) -> p c t", c=2), func=AF.Copy, rd=[pt], wr=[QKT])
                s.I("act", "activation", out=QKT[:, 2:6, t * 128:(t + 1) * 128], in_=pt2[:, :].rearrange("p (c t) -> p c t", c=4), func=AF.Copy, rd=[pt2], wr=[QKT])
            if pending is not None:
                pending()
            pending = tail
        pending()
        for i in range(15):
            pv = s.ps()
            for k in range(8):
                s.I("pe", "matmul", pv[:, 0:256], self.hT[:, k, 64 + i * 128:64 + (i + 1) * 128], w[:, k, 512:768], start=(k == 0), stop=(k == 7), rd=[self.hT, w], wr=[pv])
            s.I("dve", "tensor_copy", VO[:, i, :, 0:64], pv[:, 0:256].rearrange("p (h d) -> p h d", h=4), rd=[pv], wr=[VO])
        PT = [s.sb("naPT%d" % i, [128, 6, 64], BF16) for i in range(4)]
        rec = [s.sb("narec%d" % i, [128, 4, 1], F32) for i in range(2)]
        yo = [s.sb("nayo%d" % i, [128, 4, 64], F32) for i in range(2)]
        def s_part(r, h, P_):
            R0 = min(max(r - 4, 0), 24)
            pair = h // 2
            pS = s.ps()
            q_ap = QKT[:, pair, r * 64:(r + 1) * 64]
            kk = 2 + 2 * (h % 2) + pair
            for ci in range(4):
                kr = R0 + 2 * ci
                d = kr - r + 7
                s.I("pe", "matmul", pS[:, ci * 64:(ci + 1) * 64], QKT[:, kk, kr * 64:kr * 64 + 128], q_ap, start=True, stop=False,
                    rd=[QKT], wr=[pS])
                s.I("pe", "matmul", pS[:, ci * 64:(ci + 1) * 64], self.identb[:], self.BT[:, h, d, :], start=False, stop=True,
                    rd=[self.identb, self.BT], wr=[pS])
            for cc in range(2):
                s.I("pe", "matmul", pS[:, (4 + cc) * 64:(5 + cc) * 64], QKT[:, kk, L + cc * 128:L + (cc + 1) * 128], q_ap,
                    start=True, stop=True, rd=[QKT], wr=[pS])
            s.I("act", "activation", out=P_[:].rearrange("p c q -> p (c q)"), in_=pS[:, 0:384], func=AF.Exp, scale=0.125, rd=[pS], wr=[P_])

        def pv_part(r, h, P_):
            R0 = min(max(r - 4, 0), 24)
            rp, rr = r // 2, r % 2
            po = s.psacc(rp)
            for c in range(6):
                if c < 4:
                    kr = R0 + 2 * c
                    vb, v_ap = (VE, VE[:, kr // 2, h, :]) if kr % 2 == 0 else (VO, VO[:, (kr - 1) // 2, h, :])
                else:
                    vb, v_ap = VE, VE[:, 16 + (c - 4), h, :]
                s.I("pe", "matmul", po[64 * rr:64 * rr + 64, h * 65:(h + 1) * 65], P_[:, c, :], v_ap, start=(c == 0), stop=(c == 5), rd=[P_, vb], wr=[po])

        def row_finish(rp):
            po = s.psacc(rp)
            rc, y_ = rec[rp % 2], yo[rp % 2]
            pov = po[:, 0:260].rearrange("p (h e) -> p h e", e=65)
            s.I("dve", "reciprocal", rc[:], pov[:, :, 64:65], rd=[po], wr=[rc])
            s.I("dve", "tensor_tensor", out=y_[:], in0=pov[:, :, 0:64], in1=rc[:].to_broadcast([128, 4, 64]), op=ALU.mult, rd=[po, rc], wr=[y_])
            s.dma("sp", self.Yd.t.ap()[b, rp * 128:(rp + 1) * 128, 0:256], y_[:].rearrange("p h d -> p (h d)"), self.Yd, y_)

        seq = [(r, h) for r in range(32) for h in range(4)]
        SK = 2
        for i in range(len(seq) + SK):
            if i < len(seq):
                s_part(seq[i][0], seq[i][1], PT[i % 4])
            j = i - SK
            if j >= 0:
                pv_part(seq[j][0], seq[j][1], PT[j % 4])
                if seq[j][0] % 2 == 1 and seq[j][1] == 3:
                    row_finish(seq[j][0] // 2)
        if not last:
            PTc = [s.sb("naPTc%d" % i, [128, 2, 256], BF16) for i in range(2)]
            pos = [s.psacc(0), s.psacc(1)]
            for h in range(4):
                pair, base = h // 2, 64 * (h % 2)
                pS = s.ps()
                for cc in range(2):
                    s.I("pe", "matmul", pS[:, cc * 256:(cc + 1) * 256], QKT[:, 2 + 2 * (h % 2) + pair, L + cc * 128:L + (cc + 1) * 128],
                        QKT[:, pair, L:L + 256], start=True, stop=True, rd=[QKT], wr=[pS])
                P_ = PTc[h % 2]
                s.I("act", "activation", out=P_[:].rearrange("p c q -> p (c q)"), in_=pS[:, :], func=AF.Exp, scale=0.125, rd=[pS], wr=[P_])
                for qt in range(2):
                    for cc in range(2):
                        s.I("pe", "matmul", pos[qt][:, h * 65:(h + 1) * 65], P_[:, cc, qt * 128:(qt + 1) * 128], VE[:, 16 + cc, h, :], start=(cc == 0), stop=(cc == 1),
                            rd=[P_, VE], wr=[pos[qt]])
            for qt in range(2):
                rc, y_ = rec[qt], yo[qt]
                pov = pos[qt][:, 0:260].rearrange("p (h e) -> p h e", e=65)
                s.I("dve", "reciprocal", rc[:], pov[:, :, 64:65], rd=[pos[qt]], wr=[rc])
                s.I("dve", "tensor_tensor", out=y_[:], in0=pov[:, :, 0:64], in1=rc[:].to_broadcast([128, 4, 64]), op=ALU.mult, rd=[pos[qt], rc], wr=[y_])
                s.dma("sp", self.Yd.t.ap()[b, L + qt * 128:L + (qt + 1) * 128, 0:256], y_[:].rearrange("p h d -> p (h d)"), self.Yd, y_)

    def phase_df(self, l, b):
        s = self.s
        last = l == DEPTH - 1
        lam_init = 0.8 - 0.6 * math.exp(-0.3 * l)
        w = self.w_df(l)
        gains = s.sb("dfgain", [128, 2, 1, 32], F32)
        s.dma("sp", gains[:, 0, 0, :], dap(self.df_qg, l * 32, [[0, 128], [1, 32]]), gains, self.df_qg)
        s.dma("sp", gains[:, 1, 0, :], dap(self.df_kg, l * 32, [[0, 128], [1, 32]]), gains, self.df_kg)
        COS = s.sb("dfcos", [128, 16, 32], F32)
        SIN = s.sb("dfsin", [128, 16, 32], F32)
        s.dma("sp", COS[:], self.rope_cos.t.ap().rearrange("(t p) d -> p t d", p=128), COS, self.rope_cos)
        s.dma("sp", SIN[:], self.rope_sin.t.ap().rearrange("(t p) d -> p t d", p=128), SIN, self.rope_sin)
        lv = s.sb("dflv", [128, 4, 32], F32)
        s.dma("sp", lv[:].rearrange("p a d -> p (a d)"), dap(self.df_lam, l * 128, [[0, 128], [1, 128]]), lv, self.df_lam)
        lp = s.sb("dflp", [128, 2, 32], F32)
        ls = s.sb("dfls", [128, 8], F32)
        s.I("dve", "tensor_tensor", out=lp[:, 0, :], in0=lv[:, 0, :], in1=lv[:, 1, :], op=ALU.mult, rd=[lv], wr=[lp])
        s.I("dve", "tensor_tensor", out=lp[:, 1, :], in0=lv[:, 2, :], in1=lv[:, 3, :], op=ALU.mult, rd=[lv], wr=[lp])
        s.I("dve", "tensor_reduce", out=ls[:, 0:2], in_=lp[:], axis=AX.X, op=ALU.add, rd=[lp], wr=[ls])
        s.I("act", "activation", out=ls[:, 2:4], in_=ls[:, 0:2], func=AF.Exp, rd=[ls], wr=[ls])
        s.I("dve", "scalar_tensor_tensor", out=ls[:, 4:5], in0=ls[:, 3:4], scalar=-lam_init, in1=ls[:, 2:3], op0=ALU.add, op1=ALU.subtract, rd=[ls], wr=[ls])
        neglam = ls[:, 4:5]
        sub = s.sb("dfsub", [128, 1, 64], F32)
        s.dma("sp", sub[:, 0, :], dap(self.df_subln, l * 64, [[0, 128], [1, 64]]), sub, self.df_subln)
        s.I("dve", "tensor_scalar", out=sub[:], in0=sub[:], scalar1=1.0 - lam_init, scalar2=None, op0=ALU.mult, rd=[sub], wr=[sub])

        QZ = s.sb("dfQZ", [128, 2, 2, T], BF16)
        KZ = s.sb("dfKZ", [128, 2, 2, T], BF16)
        VD = s.sb("dfVD", [128, 18, 4, 65], BF16)
        s.I("pool", "memset", VD[:], 1.0, wr=[VD])
        qn = [s.sb("dfqn%d" % i, [128, 16, 32], F32) for i in range(2)]
        gsq = [s.sb("dfgsq%d" % i, [128, 512], F32) for i in range(1)] * 2
        gst = [s.sb("dfgst%d" % i, [128, 16, 3], F32) for i in range(2)]
        t1 = [s.sb("dft1%d" % i, [128, 16, 32], F32) for i in range(1)] * 2
        t2 = [s.sb("dft2%d" % i, [128, 16, 32], F32) for i in range(1)] * 2
        qz = [s.sb("dfqz%d" % i, [128, 4, 256], F32) for i in range(2)]
        for i in range(2):
            s.I("pool", "memset", qz[i][:], 0.0, wr=[qz[i]])
        pending = None
        for t in range(18):
            pq = s.ps()
            pv = s.ps()
            for k in range(8):
                s.I("pe", "matmul", pq[:, :], self.hT[:, k, t * 128:(t + 1) * 128], w[:, k, 0:512], start=(k == 0), stop=(k == 7), rd=[self.hT, w], wr=[pq])
            for k in range(8):
                s.I("pe", "matmul", pv[:, 0:256], self.hT[:, k, t * 128:(t + 1) * 128], w[:, k, 512:768], start=(k == 0), stop=(k == 7), rd=[self.hT, w], wr=[pv])
            q_ = qn[t % 2]
            self._gn_rd = [gains]
            self._gn_wr = [q_]
            self.group_norm(pq, 16, 32, gains[:].to_broadcast([128, 2, 8, 32]), q_[:].rearrange("p (a h) d -> p a h d", a=2), gsq[t % 2], gst[t % 2],
                            lambda ap: ap.rearrange("p (a h d) -> p a h d", a=2, h=8))
            z_ = qz[t % 2]
            if t < 16:
                a_, b_ = t1[t % 2], t2[t % 2]
                s.I("pool", "tensor_tensor", out=a_[:], in0=q_[:], in1=COS[:, t:t + 1, :].to_broadcast([128, 16, 32]), op=ALU.mult, rd=[q_, COS], wr=[a_])
                q5 = q_[:].rearrange("p g (a f e) -> p g a f e", a=2, f=2)
                b5 = b_[:].rearrange("p g (a f e) -> p g a f e", a=2, f=2)
                s5 = SIN[:, t:t + 1, :].to_broadcast([128, 16, 32]).rearrange("p g (a f e) -> p g a f e", a=2, f=2)
                s.I("dve", "tensor_tensor", out=b5[:, :, :, 0, :], in0=q5[:, :, :, 1, :], in1=s5[:, :, :, 0, :], op=ALU.mult, rd=[q_, SIN], wr=[b_])
                s.I("dve", "tensor_tensor", out=b5[:, :, :, 1, :], in0=q5[:, :, :, 0, :], in1=s5[:, :, :, 1, :], op=ALU.mult, rd=[q_, SIN], wr=[b_])
                srcs = (a_, b_)
            else:
                srcs = (q_,)

            def comb(out_ap, sel):
                if len(srcs) == 2:
                    s.I("pool", "tensor_tensor", out=out_ap, in0=sel(srcs[0]), in1=sel(srcs[1]), op=ALU.add, rd=list(srcs), wr=[z_])
                else:
                    s.I("pool", "tensor_copy", out_ap, sel(srcs[0]), rd=list(srcs), wr=[z_])
            for m in range(2):
                comb(z_[:, m, :].rearrange("p (h m d) -> p h m d", h=4, m=2)[:, :, m, :],
                     lambda bf: bf[:, 0:8, :].rearrange("p (h m) d -> p h m d", m=2)[:, :, m, :])
            for hs in range(2):
                comb(z_[:, 2 + hs, :].rearrange("p (pr h2 e) -> p pr h2 e", pr=2, h2=2)[:, :, hs, :],
                     lambda bf: bf[:, 8:16, :].rearrange("p (pr h2 m) d -> p pr h2 (m d)", pr=2, h2=2)[:, :, hs, :])
            s.I("dve", "tensor_copy", VD[:, t, :, 0:64], pv[:, 0:256].rearrange("p (h d) -> p h d", h=4), rd=[pv], wr=[VD])

            def tail(t=t, z_=z_):
                pt = s.ps()
                pt2 = s.ps()
                for m in range(2):
                    for pr in range(2):
                        s.I("pe", "transpose", pt[:, (m * 2 + pr) * 128:(m * 2 + pr + 1) * 128], z_[:, m, pr * 128:(pr + 1) * 128], self.ident[:], rd=[z_, self.ident], wr=[pt])
                for hs in range(2):
                    for pr in range(2):
                        s.I("pe", "transpose", pt2[:, (hs * 2 + pr) * 128:(hs * 2 + pr + 1) * 128], z_[:, 2 + hs, pr * 128:(pr + 1) * 128], self.ident[:], rd=[z_, self.ident], wr=[pt2])
                s.I("act", "activation", out=QZ[:, :, :, t * 128:(t + 1) * 128], in_=pt[:, :].rearrange("p (m c t) -> p m c t", m=2, c=2), func=AF.Copy, rd=[pt], wr=[QZ])
                s.I("act", "activation", out=KZ[:, :, :, t * 128:(t + 1) * 128], in_=pt2[:, :].rearrange("p (a c t) -> p a c t", a=2, c=2), func=AF.Copy, rd=[pt2], wr=[KZ])
            if pending is not None:
                pending()
            pending = tail
        pending()
        PT = [[s.sb("dfPT%d%d" % (i, m), [128, 18, 512], BF16) for m in range(2)] for i in range(2)]
        rec = s.sb("dfrec", [128, 2, 4, 1], F32)
        o0 = s.sb("dfo0", [128, 4, 64], F32)
        o1 = s.sb("dfo1", [128, 4, 64], F32)
        yd = [s.sb("dfyd%d" % i, [128, 4, 64], F32) for i in range(2)]
        sq = s.sb("dfsq2", [128, 4, 64], F32)
        st = s.sb("dfst2", [128, 4, 3], F32)
        epsb = s.sb("dfeps", [128, 1], F32)
        s.I("pool", "memset", epsb[:], EPS, wr=[epsb])
        blocks = [(qb * 512, 512, list(range(18))) for qb in range(4)]
        if not last:
            blocks.append((L, 256, [16, 17]))
        items = [(h, q0, nq, kcs) for h in range(4) for (q0, nq, kcs) in blocks]

        def s_steps(idx):
            h, q0, nq, kcs = items[idx]
            pair = h // 2
            P_ = PT[idx % 2]
            out = []
            for m in range(2):
                for ki in range(0, len(kcs), 2):
                    def f(m=m, kc=kcs[ki]):
                        pa, pb, pp = s.pspair()
                        for o, pbuf in ((0, pa), (1, pb)):
                            s.I("pe", "matmul", pbuf[:, 0:nq], KZ[:, h % 2, pair, (kc + o) * 128:(kc + o + 1) * 128], QZ[:, m, pair, q0:q0 + nq], start=True, stop=True,
                                rd=[KZ, QZ], wr=[pbuf])
                        s.I("act", "activation", out=P_[m][:, kc:kc + 2, 0:nq], in_=pp[:, :].rearrange("p (c q) -> p c q", c=2)[:, :, 0:nq], func=AF.Exp, scale=32.0 ** -0.5,
                            rd=[pa, pb], wr=[P_[m]])
                    out.append(f)
            return out

        def pv_steps(idx):
            h, q0, nq, kcs = items[idx]
            P_ = PT[idx % 2]
            po = [s.psacc(0), s.psacc(1)]
            out = []
            for m in range(2):
                for qs in range(nq // 128):
                    for i, kc in enumerate(kcs):
                        def f(m=m, qs=qs, i=i, kc=kc):
                            s.I("pe", "matmul", po[m][:, qs * 65:(qs + 1) * 65], P_[m][:, kc, qs * 128:(qs + 1) * 128], VD[:, kc, h, :], start=(i == 0), stop=(i == len(kcs) - 1),
                                rd=[P_[m], VD], wr=[po[m]])
                        out.append(f)
            return out

        def finish(idx):
            h, q0, nq, kcs = items[idx]
            po = [s.psacc(0), s.psacc(1)]
            nqs = nq // 128
            y_ = yd[idx % 2]
            pv0 = po[0][:, 0:nqs * 65].rearrange("p (q e) -> p q e", e=65)
            pv1 = po[1][:, 0:nqs * 65].rearrange("p (q e) -> p q e", e=65)
            s.I("dve", "reciprocal", rec[:, 0, 0:nqs, :], pv0[:, :, 64:65], rd=[po[0]], wr=[rec])
            s.I("dve", "reciprocal", rec[:, 1, 0:nqs, :], pv1[:, :, 64:65], rd=[po[1]], wr=[rec])
            s.I("dve", "tensor_tensor", out=o0[:, 0:nqs, :], in0=pv0[:, :, 0:64], in1=rec[:, 0, 0:nqs, :].to_broadcast([128, nqs, 64]), op=ALU.mult, rd=[po[0], rec], wr=[o0])
            s.I("dve", "tensor_tensor", out=o1[:, 0:nqs, :], in0=pv1[:, :, 0:64], in1=rec[:, 1, 0:nqs, :].to_broadcast([128, nqs, 64]), op=ALU.mult, rd=[po[1], rec], wr=[o1])
            s.I("dve", "scalar_tensor_tensor", out=o0[:, 0:nqs, :], in0=o1[:, 0:nqs, :], scalar=neglam, in1=o0[:, 0:nqs, :], op0=ALU.mult, op1=ALU.add, rd=[o1, o0, ls], wr=[o0])
            s.I("pool", "tensor_tensor", out=sq[:, 0:nqs, :], in0=o0[:, 0:nqs, :], in1=o0[:, 0:nqs, :], op=ALU.mult, rd=[o0], wr=[sq])
            s.I("dve", "tensor_reduce", out=st[:, 0:nqs, 0], in_=sq[:, 0:nqs, :], axis=AX.X, op=ALU.add, rd=[sq], wr=[st])
            s.I("act", "activation", out=st[:, 0:nqs, 1], in_=st[:, 0:nqs, 0], func=AF.Ln, scale=1.0 / 64, bias=epsb[:, 0:1], rd=[st, epsb], wr=[st])
            s.I("act", "activation", out=st[:, 0:nqs, 2], in_=st[:, 0:nqs, 1], func=AF.Exp, scale=-0.5, rd=[st], wr=[st])
            s.I("dve", "tensor_tensor", out=sq[:, 0:nqs, :], in0=o0[:, 0:nqs, :], in1=st[:, 0:nqs, 2:3].to_broadcast([128, nqs, 64]), op=ALU.mult, rd=[o0, st], wr=[sq])
            s.I("pool", "tensor_tensor", out=y_[:, 0:nqs, :], in0=sq[:, 0:nqs, :], in1=sub[:].to_broadcast([128, nqs, 64]), op=ALU.mult, rd=[sq, sub], wr=[y_])
            s.dma("sp", self.Yd.t.ap()[b, q0:q0 + nq, 256 + h * 64:256 + (h + 1) * 64].rearrange("(q p) d -> p q d", p=128), y_[:, 0:nqs, :], self.Yd, y_)


        for f in s_steps(0):
            f()
        for idx in range(len(items)):
            A = s_steps(idx + 1) if idx + 1 < len(items) else []
            B = pv_steps(idx)
            ratio = max(1, len(B) // max(1, len(A)))
            ai = 0
            for bi, fb in enumerate(B):
                if bi % ratio == 0 and ai < len(A):
                    A[ai]()
                    ai += 1
                fb()
            while ai < len(A):
                A[ai]()
                ai += 1
            finish(idx)

    def phase_ssm(self, l, b):
        s = self.s
        last = l == DEPTH - 1
        ntl = 16 if last else 18
        with ExitStack() as mid:
            def msb(name, shape, dt):
                s.uid += 1
                t = mid.enter_context(self.nc.sbuf_tensor("%s_%d" % (name, s.uid), list(shape), dt))
                return Buf(name, t, True)
            XS = msb("ssXS", [128, 18, 512], F32)
            BMT = msb("ssBMT", [128, 2, T], BF16)
            CMT = msb("ssCMT", [128, 2, T], BF16)
            BM = msb("ssBM", [128, 18, 2, 128], BF16)
            DT = msb("ssDT", [128, 18, 16], F32)
            LA = msb("ssLA", [128, 18, 16], F32)
            with s.phase():
                self.ssm_prep(l, b, XS, BMT, CMT, BM, DT, LA)
            with s.phase():
                self.ssm_scan(l, b, XS, BMT, CMT, BM, DT, LA, ntl)
            for bb in (XS, BMT, CMT, BM, DT, LA):
                if bb.dsem is not None:
                    s.dfree.append(bb.dsem)
                    bb.dsem = None

    def ssm_prep(self, l, b, XS, BMT, CMT, BM, DT, LA):
        s = self.s
        w = self.load_w("wss", self.w_in, self.w_in.t.ap()[l, :, 1536:3088].rearrange("(k p) n -> p k n", p=128), [128, 8, 1552])
        dtb = self.bcast_row("ssdtb", self.dt_bias, l * 16, 16)
        alog = self.bcast_row("ssalog", self.a_log, l * 16, 16)
        A = s.sb("ssA", [128, 16], F32)
        s.I("act", "activation", out=A[:], in_=alog[:], func=AF.Exp, rd=[alog], wr=[A])
        s.I("dve", "tensor_scalar", out=A[:], in0=A[:], scalar1=-1.0, scalar2=None, op0=ALU.mult, rd=[A], wr=[A])
        cw = s.sb("sscw", [128, 8, 5], F32)
        cb = s.sb("sscb", [128, 8], F32)
        s.dma("sp", cw[:], self.conv_wc.t.ap()[l], cw, self.conv_wc)
        s.dma("sp", cb[:], self.conv_bc.t.ap()[l], cb, self.conv_bc)
        zs = [s.sb("sszs%d" % i, [128, 512], F32) for i in range(2)]
        tmp = s.sb("sstmp", [128, 18, 16], F32)
        for t in range(18):
            pz = s.ps()
            pd = s.ps()
            for k in range(8):
                s.I("pe", "matmul", pz[:, :], self.hT[:, k, t * 128:(t + 1) * 128], w[:, k, 0:512], start=(k == 0), stop=(k == 7), rd=[self.hT, w], wr=[pz])
            for k in range(8):
                s.I("pe", "matmul", pd[:, 0:16], self.hT[:, k, t * 128:(t + 1) * 128], w[:, k, 1536:1552], start=(k == 0), stop=(k == 7), rd=[self.hT, w], wr=[pd])
            z_ = zs[t % 2]
            s.I("act", "activation", out=z_[:], in_=pz[:, :], func=AF.Silu, rd=[pz], wr=[z_])
            s.dma("sp", self.Zd.t.ap()[t * 128:(t + 1) * 128, :], z_[:], self.Zd, z_)
            s.I("dve", "tensor_tensor", out=tmp[:, t, :], in0=pd[:, 0:16], in1=dtb[:], op=ALU.add, rd=[pd, dtb], wr=[tmp])
        s.I("act", "activation", out=tmp[:], in_=tmp[:], func=AF.Exp, rd=[tmp], wr=[tmp])
        s.I("act", "activation", out=DT[:], in_=tmp[:], func=AF.Ln, bias=1.0, scale=1.0, rd=[tmp], wr=[DT])
        s.I("dve", "tensor_tensor", out=LA[:], in0=DT[:], in1=A[:].rearrange("p (o d) -> p o d", o=1).to_broadcast([128, 18, 16]), op=ALU.mult, rd=[DT, A], wr=[LA])
        Gl = [s.sb("ssGl%d" % i, [128, L + 4], F32) for i in range(2)]
        Gc = [s.sb("ssGc%d" % i, [128, LC + 4], F32) for i in range(2)]
        acc = [s.sb("ssacc%d" % i, [128, T], F32) for i in range(2)]
        for i in range(2):
            s.I("pool", "memset", Gl[i][:], 0.0, wr=[Gl[i]])
            s.I("pool", "memset", Gc[i][:], 0.0, wr=[Gc[i]])
        groups = [(0, 512), (512, 512), (1024, 512), (1536, 512), (2048, 256)]

        def A1(c):
            gl, gc = Gl[c % 2], Gc[c % 2]
            for (t0, n) in groups:
                p = s.ps()
                for k in range(8):
                    s.I("pe", "matmul", p[:, 0:n], w[:, k, 512 + c * 128:512 + (c + 1) * 128], self.hT[:, k, t0:t0 + n], start=(k == 0), stop=(k == 7), rd=[w, self.hT], wr=[p])
                if t0 < L:
                    s.I("act", "activation", out=gl[:, 2 + t0:2 + t0 + n], in_=p[:, 0:n], func=AF.Copy, rd=[p], wr=[gl])
                else:
                    s.I("act", "activation", out=gc[:, 2:2 + n], in_=p[:, 0:n], func=AF.Copy, rd=[p], wr=[gc])

        def A2(c):
            gl, gc, a = Gl[c % 2], Gc[c % 2], acc[c % 2]
            for (gb, o0, n) in ((gl, 0, L), (gc, L, LC)):
                s.I("dve", "tensor_scalar", out=a[:, o0:o0 + n], in0=gb[:, 0:n], scalar1=cw[:, c, 0:1], scalar2=None, op0=ALU.mult, rd=[gb, cw], wr=[a])
                for k in range(1, 5):
                    s.I("dve", "scalar_tensor_tensor", out=a[:, o0:o0 + n], in0=gb[:, k:k + n], scalar=cw[:, c, k:k + 1], in1=a[:, o0:o0 + n],
                        op0=ALU.mult, op1=ALU.add, rd=[gb, cw, a], wr=[a])
            if c < 6:
                s.I("act", "activation", out=a[:], in_=a[:], func=AF.Silu, bias=cb[:, c:c + 1], scale=1.0, rd=[a, cb], wr=[a])
            else:
                s.I("act", "activation", out=CMT[:, c - 6, :], in_=a[:], func=AF.Silu, bias=cb[:, c:c + 1], scale=1.0, rd=[a, cb], wr=[CMT])

        def Bst(c):
            a = acc[c % 2]
            if c >= 6:
                return
            if c in (4, 5):
                s.I("pool", "tensor_copy", BMT[:, c - 4, :], a[:], rd=[a], wr=[BMT])
            for g4 in range(5):
                tiles = list(range(4 * g4, min(4 * g4 + 4, 18)))
                p = s.ps()
                for ti, t in enumerate(tiles):
                    s.I("pe", "transpose", p[:, ti * 128:(ti + 1) * 128], a[:, t * 128:(t + 1) * 128], self.ident[:], rd=[a, self.ident], wr=[p])
                if c < 4:
                    dst, dbuf = XS[:, tiles[0]:tiles[0] + len(tiles), c * 128:(c + 1) * 128], XS
                else:
                    dst, dbuf = BM[:, tiles[0]:tiles[0] + len(tiles), c - 4, :], BM
                s.I("act", "activation", out=dst, in_=p[:, 0:len(tiles) * 128].rearrange("p (t d) -> p t d", d=128), func=AF.Copy, rd=[p], wr=[dbuf])

        A1(0)
        A1(1)
        A2(0)
        for c in range(1, 8):
            if c + 1 < 8:
                A1(c + 1)
            Bst(c - 1)
            A2(c)
        Bst(7)

    def ssm_scan(self, l, b, XS, BMT, CMT, BM, DT, LA, ntl):
        s = self.s
        TRI = s.sb("ssTRI", [128, 5, 128], F32)
        s.dma("sp", TRI[:], self.tri_d.t.ap().rearrange("a p n -> p a n"), TRI, self.tri_d)
        SL, SU, LE, GE, ON = (TRI[:, i, :] for i in range(5))
        Y = s.sb("ssY", [128, 18, 512], F32)
        s.I("pool", "memset", Y[:], 0.0, wr=[Y])
        ST = [[s.sb("ssST%d%d" % (d, g), [128, 4, 64], F32) for g in range(2)] for d in range(2)]
        STb = [[s.sb("ssSTb%d%d" % (d, g), [128, 256], BF16) for g in range(2)] for d in range(2)]
        for d in range(2):
            for g in range(2):
                s.I("pool", "memset", ST[d][g][:], 0.0, wr=[ST[d][g]])
                s.I("pool", "memset", STb[d][g][:], 0.0, wr=[STb[d][g]])
        NEGM = s.sb("ssNEGM", [128, 2, 4, 128], BF16)
        s.dma("pool", NEGM[:], self.negm_d.t.ap(), NEGM, self.negm_d)
        E12 = s.sb("ssE12", [128, 2, 8, 128], BF16)
        s.dma("pool", E12[:], self.sel_d.t.ap(), E12, self.sel_d)
        EX = [s.sb("ssEX%d" % i, [128, 3, 8], F32) for i in range(3)]
        CST = [s.sb("ssCST%d" % i, [128, 128], BF16) for i in range(3)]
        R1 = [s.sb("ssR1%d" % i, [8, 128], F32) for i in range(3)]
        for i in range(3):
            s.I("pool", "memset", CST[i][:], 0.0, wr=[CST[i]])
        XDT = [s.sb("ssXDT%d" % i, [128, 8, 64], BF16) for i in range(3)]
        XDD = [s.sb("ssXDD%d" % i, [128, 8, 64], BF16) for i in range(3)]
        cbs = [s.sb("sscbs%d" % i, [128, 1, 128], F32) for i in range(4)]
        Lt = [s.sb("ssLt%d" % i, [128, 4, 128], F32) for i in range(2)] * 2
        Wm = [s.sb("ssW%d" % i, [128, 4, 128], BF16) for i in range(4)]
        tm1 = [s.sb("sstm1%d" % i, [128, 4, 64], F32) for i in range(4)]
        yt = [s.sb("ssyt%d" % i, [128, 4, 64], F32) for i in range(4)]
        tm2 = [s.sb("sstm2%d" % i, [128, 4, 64], F32) for i in range(4)]
        order = [(0, 16), (1, 17), (0, 17), (1, 16)]
        for i in range(16):
            order.append((0, i))
            order.append((1, 15 - i))
        ctxs = {}

        def stageA(k):
            dr, t = order[k]
            want_y = t < ntl
            ex, cst, xdt, xdd = EX[k % 3], CST[k % 3], XDT[k % 3], XDD[k % 3]
            la8 = LA[:, t, dr * 8:(dr + 1) * 8]
            pe_ = s.ps()
            m1, m2 = (SL, LE) if dr == 0 else (SU, GE)
            s.I("pe", "matmul", pe_[:, 0:8], m1, la8, start=True, stop=True, rd=[TRI, LA], wr=[pe_])
            s.I("pe", "matmul", pe_[:, 8:16], m2, la8, start=True, stop=True, rd=[TRI, LA], wr=[pe_])
            s.I("pe", "matmul", pe_[:, 16:24], ON, la8, start=True, stop=True, rd=[TRI, LA], wr=[pe_])
            s.I("act", "activation", out=ex[:].rearrange("p a h -> p (a h)"), in_=pe_[:, 0:24], func=AF.Exp, rd=[pe_], wr=[ex])
            s.I("dve", "tensor_tensor", out=xdt[:], in0=XS[:, t, :].rearrange("p (h d) -> p h d", d=64),
                in1=DT[:, t, dr * 8:(dr + 1) * 8].rearrange("p (h o) -> p h o", o=1).to_broadcast([128, 8, 64]), op=ALU.mult, rd=[XS, DT], wr=[xdt])
            s.I("dve", "tensor_tensor", out=xdd[:], in0=xdt[:], in1=ex[:, 0, :].rearrange("p (h o) -> p h o", o=1).to_broadcast([128, 8, 64]), op=ALU.mult,
                rd=[xdt, ex], wr=[xdd])
            cb2 = []
            if want_y:
                pcs = s.ps()
                s.I("pe", "matmul", pcs[0:8, 0:128], la8, m2, start=True, stop=True, rd=[LA, TRI], wr=[pcs])
                r1 = R1[k % 3]
                s.I("act", "activation", out=cst[0:8, :], in_=pcs[0:8, 0:128], func=AF.Copy, rd=[pcs], wr=[cst])
                s.I("dve", "tensor_tensor", out=r1[:], in0=pcs[0:8, 0:128], in1=cst[0:8, :], op=ALU.subtract, rd=[pcs, cst], wr=[r1])
                s.I("act", "activation", out=cst[32:40, :], in_=r1[:], func=AF.Copy, rd=[r1], wr=[cst])
                for g in range(2):
                    cb_ = cbs[(2 * k + g) % 4]
                    pc = s.ps()
                    s.I("pe", "matmul", pc[:, 0:128], BMT[:, g, t * 128:(t + 1) * 128], CMT[:, g, t * 128:(t + 1) * 128], start=True, stop=True, rd=[BMT, CMT], wr=[pc])
                    s.I("act", "activation", out=cb_[:, 0, :], in_=pc[:, 0:128], func=AF.Copy, rd=[pc], wr=[cb_])
                    cb2.append(cb_)
            ctxs[k] = (ex, cst, xdt, xdd, cb2)

        def stageB(k):
            dr, t = order[k]
            want_y = t < ntl
            ex, cst, xdt, xdd, cb2 = ctxs[k]
            ws = []
            if want_y:
                for g in range(2):
                    lt, wm = Lt[(2 * k + g) % 4], Wm[(2 * k + g) % 4]
                    pd_ = s.ps()
                    s.I("pe", "matmul", pd_[:, :], cst[:], E12[:, 1, g * 4:(g + 1) * 4, :].rearrange("p e i -> p (e i)"), start=True, stop=False, rd=[E12, cst], wr=[pd_])
                    s.I("pe", "matmul", pd_[:, :], self.identb[:], NEGM[:, dr, :, :].rearrange("p e i -> p (e i)"), start=False, stop=False, rd=[self.identb, NEGM], wr=[pd_])
                    for e in range(4):
                        h8 = g * 4 + e
                        s.I("pe", "matmul", pd_[:, e * 128:(e + 1) * 128], E12[:, 0, h8, :], cst[:], start=False, stop=(e == 3), rd=[E12, cst], wr=[pd_])
                    s.I("act", "activation", out=lt[:].rearrange("p e i -> p (e i)"), in_=pd_[:, :], func=AF.Exp, rd=[pd_], wr=[lt])
                    s.I("dve", "tensor_tensor", out=wm[:], in0=lt[:], in1=cb2[g][:].to_broadcast([128, 4, 128]), op=ALU.mult, rd=[lt, cb2[g]], wr=[wm])
                    ws.append(wm)
            ctxs[k] = (ex, cst, xdt, xdd, ws)

        def stageC(k):
            dr, t = order[k]
            want_y = t < ntl
            ex, cst, xdt, xdd, ws = ctxs.pop(k)
            for g in range(2):
                if want_y:
                    py = s.psacc(g)
                    for e in range(4):
                        s.I("pe", "matmul", py[:, e * 64:(e + 1) * 64], ws[g][:, e, :], xdt[:, g * 4 + e, :], start=True, stop=True, rd=[ws[g], xdt], wr=[py])
                    po = s.ps()
                    s.I("pe", "matmul", po[:, 0:256], CMT[:, g, t * 128:(t + 1) * 128], STb[dr][g][:], start=True, stop=True, rd=[CMT, STb[dr][g]], wr=[po])
                    a_, y_ = tm1[(2 * k + g) % 4], yt[(2 * k + g) % 4]
                    s.I("dve", "tensor_tensor", out=a_[:], in0=po[:, 0:256].rearrange("p (h d) -> p h d", d=64),
                        in1=ex[:, 1, g * 4:(g + 1) * 4].rearrange("p (h o) -> p h o", o=1).to_broadcast([128, 4, 64]), op=ALU.mult, rd=[po, ex], wr=[a_])
                    s.I("dve", "tensor_tensor", out=y_[:], in0=py[:, 0:256].rearrange("p (h d) -> p h d", d=64), in1=a_[:], op=ALU.add, rd=[py, a_], wr=[y_])
                    yv = Y[:, t, g * 256:(g + 1) * 256].rearrange("p (h d) -> p h d", d=64)
                    s.I("pool", "tensor_tensor", out=yv, in0=yv, in1=y_[:], op=ALU.add, rd=[Y, y_], wr=[Y])
                pst = s.ps()
                s.I("pe", "matmul", pst[:, 0:256], BM[:, t, g, :], xdd[:, g * 4:(g + 1) * 4, :], start=True, stop=True, rd=[BM, xdd], wr=[pst])
                c_ = tm2[(2 * k + g) % 4]
                s.I("pool", "tensor_tensor", out=c_[:], in0=ST[dr][g][:], in1=ex[:, 2, g * 4:(g + 1) * 4].rearrange("p (h o) -> p h o", o=1).to_broadcast([128, 4, 64]),
                    op=ALU.mult, rd=[ST[dr][g], ex], wr=[c_])
                s.I("dve", "tensor_tensor", out=ST[dr][g][:], in0=pst[:, 0:256].rearrange("p (h d) -> p h d", d=64), in1=c_[:], op=ALU.add, rd=[pst, c_], wr=[ST[dr][g]])
                s.I("act", "activation", out=STb[dr][g][:], in_=ST[dr][g][:].rearrange("p h d -> p (h d)"), func=AF.Copy, rd=[ST[dr][g]], wr=[STb[dr][g]])

        n = len(order)
        for k in range(n + 2):
            if k < n:
                stageA(k)
            if 1 <= k <= n:
                stageB(k - 1)
            if k >= 2:
                stageC(k - 2)
        dsk = self.bcast_row("ssdsk", self.ssm_d, l * 8, 8)
        ng = self.bcast_row("ssng", self.ssm_norm, l * 512, 512)
        zt = [s.sb("sszt%d" % i, [128, 512], F32) for i in range(3)]
        u = [s.sb("ssu%d" % i, [128, 512], F32) for i in range(3)]
        junks = [s.sb("ssjunk%d" % i, [128, 512], BF16) for i in range(1)] * 2
        sts = [s.sb("ssfst%d" % i, [128, 1, 3], F32) for i in range(4)]
        def zload(t):
            s.dma("sp", zt[t % 3][:], self.Zd.t.ap()[t * 128:(t + 1) * 128, :], zt[t % 3], self.Zd)
        zload(0)
        zload(1)
        for t in range(ntl):
            z_, u_ = zt[t % 3], u[t % 3]
            st, junk = sts[t % 4], junks[t % 2]
            if t + 2 < ntl:
                zload(t + 2)
            s.I("dve", "tensor_tensor", out=u_[:].rearrange("p (h d) -> p h d", d=64), in0=XS[:, t, :].rearrange("p (h d) -> p h d", d=64),
                in1=dsk[:].rearrange("p (h o) -> p h o", o=1).to_broadcast([128, 8, 64]), op=ALU.mult, rd=[XS, dsk], wr=[u_])
            s.I("pool", "tensor_tensor", out=u_[:], in0=u_[:], in1=Y[:, t, :], op=ALU.add, rd=[u_, Y], wr=[u_])
            s.I("dve", "tensor_tensor", out=u_[:], in0=u_[:], in1=z_[:], op=ALU.mult, rd=[u_, z_], wr=[u_])
            s.I("act", "activation", out=junk[:], in_=u_[:], func=AF.Square, accum_out=st[:, 0, 0:1], rd=[u_], wr=[junk, st])
            s.I("act", "activation", out=st[:, 0, 1:2], in_=st[:, 0, 0:1], func=AF.Sqrt, scale=1.0 / 512, bias=EPS, rd=[st], wr=[st])
            s.I("dve", "reciprocal", st[:, 0, 2:3], st[:, 0, 1:2], rd=[st], wr=[st])
            s.I("dve", "scalar_tensor_tensor", out=u_[:], in0=u_[:], scalar=st[:, 0, 2:3], in1=ng[:], op0=ALU.mult, op1=ALU.mult, rd=[u_, st, ng], wr=[u_])
            s.dma("sp", self.Yd.t.ap()[b, t * 128:(t + 1) * 128, 512:1024], u_[:], self.Yd, u_)


def _consts():
    ident = np.eye(128, dtype=np.float32)
    i = np.arange(128)
    SL = (i[:, None] > i[None, :]).astype(np.float32)
    SU = (i[:, None] < i[None, :]).astype(np.float32)
    LE = (i[:, None] <= i[None, :]).astype(np.float32)
    GE = (i[:, None] >= i[None, :]).astype(np.float32)
    ON = np.ones((128, 128), np.float32)
    tri = np.stack([SL, SU, LE, GE, ON]).astype(np.float32)
    qc = np.arange(64)
    ws = np.clip(qc - 8, 0, 48)
    kc = np.arange(64)
    ok = (kc[:, None] >= ws[None, :]) & (kc[:, None] < ws[None, :] + 16)
    m = np.where(ok, 0.0, NEG).astype(np.float32)
    na_mask = np.concatenate([m, m], axis=0)
    per_axis = 16
    inv_freq = (10000.0 ** (-np.arange(0, per_axis, 2, dtype=np.float32) / per_axis)).astype(np.float32)
    t = np.arange(L)
    pos = np.stack([t // 64, t % 64], axis=-1).astype(np.float32)
    ang = pos[:, :, None] * inv_freq
    ang = np.concatenate([ang, ang], axis=-1).reshape(L, 32)
    cos = np.cos(ang).astype(np.float32)
    sin = np.sin(ang).astype(np.float32).reshape(L, 2, 2, 8).copy()
    sin[:, :, 0, :] *= -1.0
    negm = np.stack([np.where(i[None, :] < i[:, None], NEG, 0.0), np.where(i[None, :] > i[:, None], NEG, 0.0)], axis=1).astype(np.float32)
    negm = np.ascontiguousarray(np.broadcast_to(negm[:, :, None, :], (128, 2, 4, 128))).astype(np.float32)
    sel = np.zeros((128, 2, 8, 128), np.float32)
    for h in range(8):
        sel[h, 0, h, :] = 1.0
        sel[32 + h, 0, h, :] = 1.0
    sel[:, 1] = -sel[:, 0]
    return ident, tri, na_mask, cos, sin.reshape(L, 32), negm, sel


def make_in_maps(inp):
    f = lambda a: np.ascontiguousarray(np.asarray(a, dtype=np.float32))
    ident, tri, na_mask, cos, sin, negm, sel = _consts()
    colv = lambda v, n: f(np.asarray(v).reshape(DEPTH, n, 128).transpose(0, 2, 1))
    idx = np.clip(np.arange(64)[:, None] - np.arange(64)[None, :], -15, 15) + 15
    shared = {
        "w_ada": f(inp["w_ada"]), "b_ada": f(inp["b_ada"]),
        "g_mixc": colv(inp["g_mix"], 8), "g_ffnc": colv(inp["g_ffn"], 8),
        "w_in": f(inp["w_in"]), "w_out": f(inp["w_out"]), "w_up": f(inp["ffn_w_up"]), "w_down": f(inp["ffn_w_down"]),
        "na_qg": f(inp["na_q_gain"]), "na_kg": f(inp["na_k_gain"]),
        "rpb_t": f(np.asarray(inp["na_rpb"])[..., idx]), "na_mask": na_mask,
        "df_qg": f(inp["df_q_gain"]), "df_kg": f(inp["df_k_gain"]), "df_lam": f(inp["df_lambda"]), "df_subln": f(inp["df_subln"]),
        "rope_cos": cos, "rope_sin": sin,
        "conv_wc": f(np.asarray(inp["ssm_conv_w"]).reshape(DEPTH, 5, 8, 128).transpose(0, 3, 2, 1)),
        "conv_bc": colv(inp["ssm_conv_b"], 8),
        "dt_bias": f(np.asarray(inp["ssm_dt_bias"]).reshape(DEPTH, 16)), "a_log": f(np.asarray(inp["ssm_a_log"]).reshape(DEPTH, 16)),
        "ssm_d": f(inp["ssm_d"]), "ssm_norm": f(inp["ssm_norm"]),
        "fconv_wc": f(np.asarray(inp["ffn_conv_w"]).reshape(DEPTH, 3, NF, 128).transpose(0, 3, 2, 1)),
        "fconv_bc": colv(inp["ffn_conv_b"], NF),
        "ident": ident, "tri": tri, "negm": negm, "sel": sel,
    }
    maps = []
    x, c, ctx, c_ctx = (np.asarray(inp[k]) for k in ("x", "c", "ctx", "c_ctx"))
    for i in range(NCORES):
        sl = slice(i * NB, (i + 1) * NB)
        c3 = np.concatenate([c[sl], c_ctx[None, :]], axis=0)
        cT = f(c3.reshape(3, 8, 128).transpose(2, 1, 0))
        m = dict(shared)
        m.update({"x": f(x[sl]), "ctx": f(ctx[sl]), "cT": cT})
        maps.append(m)
    return maps


_NC_CACHE = {}


def kernel(**inputs):
    if "nc" not in _NC_CACHE:
        _NC_CACHE["nc"] = Builder().build()
    nc = _NC_CACHE["nc"]
    maps = make_in_maps(inputs)
    res = run_bass_kernel_spmd(nc, maps, core_ids=list(range(NCORES)))
    return np.concatenate([np.asarray(r["out"]) for r in res.results], axis=0).astype(np.float32)
```
